# Optimizing a Trainium2 kernel written in Bass

```python
import jax
import jax.numpy as jnp
from jax import lax
import numpy as np

D_MODEL = 1024
BATCH = 16
SEQ = 4096
DEPTH = 4

N_MIXERS = 4
BLOCK_Q = 128
LN_EPS = 1e-5
RMS_EPS = 1e-6
DN_ALPHA = (2 * DEPTH) ** 0.25
DN_BETA = (8 * DEPTH) ** -0.25
ROPE_THETA = 500000.0
ROPE_FRACTION = 4

FOX_HEADS = 8
FOX_HEAD_DIM = D_MODEL // FOX_HEADS

DSA_HEADS = 8
DSA_HEAD_DIM = D_MODEL // DSA_HEADS
DSA_ROPE_DIM = DSA_HEAD_DIM // ROPE_FRACTION
DSA_NOPE_DIM = DSA_HEAD_DIM - DSA_ROPE_DIM
DSA_KV_RANK = 128
IDX_HEADS = 8
IDX_DIM = 64
IDX_ROPE_DIM = IDX_DIM // ROPE_FRACTION
TOPK_MAX = 256

RWKV_HEAD_DIM = 64
RWKV_HEADS = D_MODEL // RWKV_HEAD_DIM
RWKV_DECAY_LORA = 64
RWKV_AAA_LORA = 64
RWKV_GN_EPS = 64e-5

RET_HEADS = 4
RET_HEAD_DIM = D_MODEL // RET_HEADS
RET_CHUNK = 128
RET_THETA = 10000.0

kernel_name = 'hybrid_fox_dsa_rwkv7_retnet_trunk'


def _layer_norm(x, g, b, eps):
    xf = x.astype(jnp.float32)
    xc = xf - jnp.mean(xf, -1, keepdims=True)
    var = jnp.mean(xc * xc, -1, keepdims=True)
    return (xc * lax.rsqrt(var + eps) * g.astype(jnp.float32) + b.astype(jnp.float32)).astype(x.dtype)


def _rms_norm(x, g, eps):
    xf = x.astype(jnp.float32)
    return (xf * lax.rsqrt(jnp.mean(xf * xf, -1, keepdims=True) + eps) * g.astype(jnp.float32)).astype(x.dtype)


def _split(a, widths):
    idx = [int(v) for v in np.cumsum(widths)[:-1]]
    return jnp.split(a, idx, axis=-1)


def _rope_tables(seq_len, rot_dim, theta):
    inv = 1.0 / (theta ** (jnp.arange(0, rot_dim, 2, dtype=jnp.float32) / rot_dim))
    ang = jnp.arange(seq_len, dtype=jnp.float32)[:, None] * inv[None, :]
    return jnp.cos(ang), jnp.sin(ang)


def _apply_rope(x, cos, sin):
    half = cos.shape[-1]
    bshape = (cos.shape[0],) + (1,) * (x.ndim - 3) + (half,)
    c = cos.reshape(bshape).astype(x.dtype)
    s = sin.reshape(bshape).astype(x.dtype)
    x1, x2, rest = x[..., :half], x[..., half:2 * half], x[..., 2 * half:]
    return jnp.concatenate([x1 * c - x2 * s, x2 * c + x1 * s, rest], axis=-1)


def _to_blocks(a):
    B, S = a.shape[0], a.shape[1]
    a = a.reshape((B, S // BLOCK_Q, BLOCK_Q) + a.shape[2:])
    return jnp.moveaxis(a, 1, 0)


def _from_blocks(a):
    a = jnp.moveaxis(a, 0, 1)
    return a.reshape((a.shape[0], a.shape[1] * a.shape[2]) + a.shape[3:])


def _fox_mixer(x, w_in, b_f, w_out):
    B, S, D = x.shape
    H, dh = FOX_HEADS, FOX_HEAD_DIM
    q, k, v, f_logit, gate = _split(x @ w_in, [H * dh, H * dh, H * dh, H, D])
    q = q.reshape(B, S, H, dh)
    k = k.reshape(B, S, H, dh)
    v = v.reshape(B, S, H, dh)
    log_f = jax.nn.log_sigmoid((f_logit + b_f).astype(jnp.float32))
    cum = jnp.cumsum(log_f, axis=1)
    cum_t = jnp.transpose(cum, (0, 2, 1))
    kpos = jnp.arange(S)
    starts = jnp.arange(S // BLOCK_Q) * BLOCK_Q
    scale = dh ** -0.5

    def block(args):
        q_b, c_b, t0 = args
        s = jnp.einsum('bqhd,bshd->bhqs', q_b, k).astype(jnp.float32) * scale
        s = s + jnp.transpose(c_b, (0, 2, 1))[..., None] - cum_t[:, :, None, :]
        qpos = t0 + jnp.arange(BLOCK_Q)
        s = jnp.where(kpos[None, :] <= qpos[:, None], s, -jnp.inf)
        p = jax.nn.softmax(s, axis=-1).astype(v.dtype)
        return jnp.einsum('bhqs,bshd->bqhd', p, v)

    o = _from_blocks(lax.map(block, (_to_blocks(q), _to_blocks(cum), starts)))
    return (jax.nn.silu(gate) * o.reshape(B, S, D)) @ w_out


def _dsa_mixer(x, cos_a, sin_a, cos_i, sin_i, w_in, kv_norm_g, w_uk, w_uv, w_out):
    B, S, D = x.shape
    H, dh, dr, dc = DSA_HEADS, DSA_HEAD_DIM, DSA_ROPE_DIM, DSA_KV_RANK
    HI, di = IDX_HEADS, IDX_DIM
    q, ckv, kr, qi, ki, wi, gate = _split(x @ w_in, [H * dh, dc, dr, HI * di, di, HI, D])
    q = _apply_rope(q.reshape(B, S, H, dh), cos_a, sin_a)
    q_rope, q_nope = q[..., :dr], q[..., dr:]
    ckv = _rms_norm(ckv, kv_norm_g, RMS_EPS)
    kr = _apply_rope(kr, cos_a, sin_a)
    q_lat = jnp.einsum('bshn,hcn->bshc', q_nope, w_uk)
    q_full = jnp.concatenate([q_lat, q_rope], axis=-1)
    kv_lat = jnp.concatenate([ckv, kr], axis=-1)
    qi = _apply_rope(qi.reshape(B, S, HI, di), cos_i, sin_i)
    ki = _apply_rope(ki, cos_i, sin_i)
    wi = wi.astype(jnp.float32) * HI ** -0.5
    k_sel = min(TOPK_MAX, S // 4)
    kpos = jnp.arange(S)
    starts = jnp.arange(S // BLOCK_Q) * BLOCK_Q

    def block(args):
        qi_b, wi_b, qf_b, t0 = args
        qpos = t0 + jnp.arange(BLOCK_Q)
        isc = jnp.einsum('bqhd,bsd->bqhs', qi_b, ki).astype(jnp.float32) * di ** -0.5
        isc = jnp.einsum('bqhs,bqh->bqs', jax.nn.relu(isc), wi_b)
        isc = jnp.where(kpos[None, None, :] <= qpos[None, :, None], isc, -jnp.inf)
        _, idx = lax.top_k(isc, k_sel)
        valid = idx <= qpos[None, :, None]
        sel = jax.vmap(lambda kv_b, i_b: kv_b[i_b])(kv_lat, idx)
        logits = jnp.einsum('bqhc,bqkc->bhqk', qf_b, sel).astype(jnp.float32) * dh ** -0.5
        logits = jnp.where(valid[:, None], logits, -jnp.inf)
        p = jax.nn.softmax(logits, axis=-1).astype(sel.dtype)
        o_lat = jnp.einsum('bhqk,bqkc->bqhc', p, sel[..., :dc])
        return jnp.einsum('bqhc,hcd->bqhd', o_lat, w_uv)

    o = _from_blocks(lax.map(block, (_to_blocks(qi), _to_blocks(wi), _to_blocks(q_full), starts)))
    return (jax.nn.silu(gate) * o.reshape(B, S, D)) @ w_out


def _rwkv7_mixer(x, mu, w_in, w0, w_lora_a, w_lora_b, a0, a_lora_a, a_lora_b, k_k, k_a, r_k, gn_g, gn_b, w_out):
    B, S, D = x.shape
    H, N = RWKV_HEADS, RWKV_HEAD_DIM
    f32 = jnp.float32
    xx = jnp.pad(x, ((0, 0), (1, 0), (0, 0)))[:, :S] - x
    w_r, w_k, w_v, w_g = jnp.split(w_in, 4, axis=-1)
    r = (x + xx * mu[0]) @ w_r
    xw = x + xx * mu[1]
    k = (x + xx * mu[2]) @ w_k
    v = (x + xx * mu[3]) @ w_v
    xa = x + xx * mu[4]
    g = (x + xx * mu[5]) @ w_g
    w_log = -jax.nn.softplus(-(w0 + jnp.tanh(xw @ w_lora_a) @ w_lora_b)) - 0.5
    decay = jnp.exp(-jnp.exp(w_log.astype(f32)))
    a = jax.nn.sigmoid(a0 + (xa @ a_lora_a) @ a_lora_b)
    kk = (k * k_k).reshape(B, S, H, N).astype(f32)
    kk = kk * lax.rsqrt(jnp.sum(kk * kk, -1, keepdims=True) + 1e-12)
    k = k * (1 + (a - 1) * k_a)
    r = r.reshape(B, S, H, N)
    k = k.reshape(B, S, H, N)
    v = v.reshape(B, S, H, N)
    a = a.reshape(B, S, H, N)
    decay = decay.reshape(B, S, H, N)

    def tm(t):
        return jnp.moveaxis(t.astype(f32), 1, 0)

    def step(state, inp):
        r_t, w_t, k_t, v_t, kk_t, a_t = inp
        sa = jnp.einsum('bhvk,bhk->bhv', state, -kk_t)
        state = (state * w_t[:, :, None, :] + sa[..., None] * (kk_t * a_t)[:, :, None, :]
                 + v_t[..., None] * k_t[:, :, None, :])
        return state, jnp.einsum('bhvk,bhk->bhv', state, r_t)

    state0 = jnp.zeros((B, H, N, N), f32)
    _, ys = lax.scan(step, state0, (tm(r), tm(decay), tm(k), tm(v), tm(kk), tm(a)))
    y = jnp.moveaxis(ys, 0, 1)
    y = _layer_norm(y, gn_g.reshape(H, N), gn_b.reshape(H, N), RWKV_GN_EPS)
    bonus = jnp.sum(r.astype(f32) * k.astype(f32) * r_k, -1, keepdims=True) * v.astype(f32)
    y = (y + bonus).reshape(B, S, D).astype(x.dtype)
    return (jax.nn.silu(g) * y) @ w_out


def _retnet_mixer(x, cos_r, sin_r, w_in, gn_g, w_out):
    B, S, D = x.shape
    H, dk, C = RET_HEADS, RET_HEAD_DIM, RET_CHUNK
    f32 = jnp.float32
    q, k, v, g = jnp.split(x @ w_in, 4, axis=-1)
    q = _apply_rope(q.reshape(B, S, H, dk), cos_r, sin_r)
    k = _apply_rope(k.reshape(B, S, H, dk), cos_r, sin_r) * dk ** -0.5
    v = v.reshape(B, S, H, dk)
    log_g = jnp.log1p(-(2.0 ** (-5.0 - jnp.arange(H, dtype=f32))))
    pos = jnp.arange(C, dtype=f32)
    diff = pos[:, None] - pos[None, :]
    d_mask = jnp.where(diff[None] >= 0, jnp.exp(jnp.maximum(diff, 0.0)[None] * log_g[:, None, None]), 0.0)
    xi = jnp.exp((pos[None, :] + 1.0) * log_g[:, None])
    zeta = jnp.exp((C - 1.0 - pos[None, :]) * log_g[:, None])
    g_c = jnp.exp(C * log_g)
    nc = S // C

    def chunks(t):
        return jnp.transpose(t.reshape(B, nc, C, H, dk), (1, 0, 3, 2, 4)).astype(f32)

    def step(R, inp):
        q_c, k_c, v_c = inp
        inner = jnp.einsum('bhnd,bhmd->bhnm', q_c, k_c) * d_mask[None]
        o = (jnp.einsum('bhnm,bhme->bhne', inner, v_c)
             + jnp.einsum('bhnd,bhde->bhne', q_c, R) * xi[None, :, :, None])
        R = R * g_c[None, :, None, None] + jnp.einsum('bhmd,bhme->bhde', k_c * zeta[None, :, :, None], v_c)
        return R, o

    R0 = jnp.zeros((B, H, dk, dk), f32)
    _, o = lax.scan(step, R0, (chunks(q), chunks(k), chunks(v)))
    o = jnp.transpose(o, (1, 0, 3, 2, 4)).reshape(B, S, H, dk)
    o = _rms_norm(o, gn_g.reshape(H, dk), RMS_EPS).reshape(B, S, D).astype(x.dtype)
    return (jax.nn.silu(g) * o) @ w_out


def setup_inputs(seed: int = 0) -> dict:
    key = jax.random.key(seed)
    keys = iter(jax.random.split(key, 40))
    D = D_MODEL
    f32 = jnp.float32

    def nrm(shape, scale):
        return jax.random.normal(next(keys), shape, f32) * scale

    def unif(shape, lo, hi):
        return jax.random.uniform(next(keys), shape, f32, lo, hi)

    w_s = D ** -0.5
    o_s = D ** -0.5 * DN_BETA
    fox_width = 3 * FOX_HEADS * FOX_HEAD_DIM + FOX_HEADS + D
    dsa_width = (DSA_HEADS * DSA_HEAD_DIM + DSA_KV_RANK + DSA_ROPE_DIM
                 + IDX_HEADS * IDX_DIM + IDX_DIM + IDX_HEADS + D)
    return {
        'x': nrm((BATCH, SEQ, D), 1.0),
        'ln_g': 1.0 + nrm((DEPTH, D), 0.02),
        'ln_b': nrm((DEPTH, D), 0.02),
        'fox_w_in': nrm((D, fox_width), w_s),
        'fox_b_f': unif((FOX_HEADS,), 1.0, 5.0),
        'fox_w_out': nrm((D, D), o_s),
        'dsa_w_in': nrm((D, dsa_width), w_s),
        'dsa_kv_norm_g': 1.0 + nrm((DSA_KV_RANK,), 0.02),
        'dsa_w_uk': nrm((DSA_HEADS, DSA_KV_RANK, DSA_NOPE_DIM), DSA_KV_RANK ** -0.5),
        'dsa_w_uv': nrm((DSA_HEADS, DSA_KV_RANK, DSA_HEAD_DIM), DSA_KV_RANK ** -0.5),
        'dsa_w_out': nrm((D, D), o_s),
        'rwkv_mu': unif((6, D), 0.0, 1.0),
        'rwkv_w_in': nrm((D, 4 * D), w_s),
        'rwkv_w0': unif((D,), -6.0, -1.0),
        'rwkv_w_lora_a': nrm((D, RWKV_DECAY_LORA), w_s),
        'rwkv_w_lora_b': nrm((RWKV_DECAY_LORA, D), 0.5 * RWKV_DECAY_LORA ** -0.5),
        'rwkv_a0': nrm((D,), 0.5),
        'rwkv_a_lora_a': nrm((D, RWKV_AAA_LORA), w_s),
        'rwkv_a_lora_b': nrm((RWKV_AAA_LORA, D), 0.5 * RWKV_AAA_LORA ** -0.5),
        'rwkv_k_k': 0.85 + nrm((D,), 0.05),
        'rwkv_k_a': 1.0 + nrm((D,), 0.05),
        'rwkv_r_k': nrm((RWKV_HEADS, RWKV_HEAD_DIM), 0.1),
        'rwkv_gn_g': 1.0 + nrm((D,), 0.02),
        'rwkv_gn_b': nrm((D,), 0.02),
        'rwkv_w_out': nrm((D, D), o_s),
        'ret_w_in': nrm((D, 4 * D), w_s),
        'ret_gn_g': 1.0 + nrm((D,), 0.02),
        'ret_w_out': nrm((D, D), o_s),
    }


def reference(x, ln_g, ln_b, fox_w_in, fox_b_f, fox_w_out, dsa_w_in, dsa_kv_norm_g, dsa_w_uk, dsa_w_uv,
              dsa_w_out, rwkv_mu, rwkv_w_in, rwkv_w0, rwkv_w_lora_a, rwkv_w_lora_b, rwkv_a0, rwkv_a_lora_a,
              rwkv_a_lora_b, rwkv_k_k, rwkv_k_a, rwkv_r_k, rwkv_gn_g, rwkv_gn_b, rwkv_w_out, ret_w_in,
              ret_gn_g, ret_w_out):
    S = x.shape[1]
    cos_a, sin_a = _rope_tables(S, DSA_ROPE_DIM, ROPE_THETA)
    cos_i, sin_i = _rope_tables(S, IDX_ROPE_DIM, ROPE_THETA)
    cos_r, sin_r = _rope_tables(S, RET_HEAD_DIM, RET_THETA)
    mixers = (
        lambda h: _fox_mixer(h, fox_w_in, fox_b_f, fox_w_out),
        lambda h: _dsa_mixer(h, cos_a, sin_a, cos_i, sin_i, dsa_w_in, dsa_kv_norm_g, dsa_w_uk, dsa_w_uv, dsa_w_out),
        lambda h: _rwkv7_mixer(h, rwkv_mu, rwkv_w_in, rwkv_w0, rwkv_w_lora_a, rwkv_w_lora_b, rwkv_a0,
                               rwkv_a_lora_a, rwkv_a_lora_b, rwkv_k_k, rwkv_k_a, rwkv_r_k, rwkv_gn_g,
                               rwkv_gn_b, rwkv_w_out),
        lambda h: _retnet_mixer(h, cos_r, sin_r, ret_w_in, ret_gn_g, ret_w_out),
    )
    for i in range(DEPTH):
        x = _layer_norm(DN_ALPHA * x + mixers[i % N_MIXERS](x), ln_g[i], ln_b[i], LN_EPS)
    return x
```

```python
import numpy as np
from contextlib import ExitStack
import concourse.bass as bass
import concourse.mybir as mybir
from concourse.bass_utils import run_bass_kernel_spmd

F32 = mybir.dt.float32
ALU = mybir.AluOpType
AF = mybir.ActivationFunctionType
AX = mybir.AxisListType


class Res:
    __slots__ = ("w", "r", "dsem", "excl")

    def __init__(self):
        self.w = None
        self.r = {}
        self.dsem = None
        self.excl = False


class V:
    __slots__ = ("ap", "res")

    def __init__(self, ap, res):
        self.ap = ap
        self.res = res


class Tile:
    def __init__(self, handle, res=None):
        self.h = handle
        self.res = res if res is not None else Res()

    def __getitem__(self, key):
        return V(self.h[key], self.res)

    def v(self, ap):
        return V(ap, self.res)


class Sched:
    ENG = ("tensor", "vector", "scalar", "gpsimd", "sync")

    def __init__(self, nc, stack, n_dma_sems=80):
        self.nc = nc
        self.stack = stack
        self.e = {"tensor": nc.tensor, "vector": nc.vector, "scalar": nc.scalar,
                  "gpsimd": nc.gpsimd, "sync": nc.sync}
        self.sem = {}
        self.tot = {}
        for e in self.ENG:
            self.sem[e] = stack.enter_context(nc.semaphore("s_" + e))
            self.tot[e] = 0
        self.bar = stack.enter_context(nc.semaphore("s_bar"))
        self.bar_n = 0
        self.free_dma = []
        for i in range(n_dma_sems):
            k = "d%d" % i
            self.sem[k] = stack.enter_context(nc.semaphore(k))
            self.tot[k] = 0
            self.free_dma.append(k)
        self.known = {e: {} for e in self.ENG}
        self.n_ins = 0
        self.n_wait = 0

    def tile(self, stack, shape, name, dtype=F32, dma=False):
        self.n_tiles = getattr(self, "n_tiles", 0) + 1
        h = stack.enter_context(self.nc.sbuf_tensor("%s_%d" % (name, self.n_tiles), list(shape), dtype))
        t = Tile(h)
        if dma:
            t.res.dsem = self.free_dma.pop()
            stack.callback(self.free_dma.append, t.res.dsem)
        return t

    def psum(self, stack, name, shape=(128, 512), dtype=F32):
        h = stack.enter_context(self.nc.psum_tensor(name, list(shape), dtype))
        t = Tile(h)
        t.res.excl = True
        return t

    def _waits(self, eng, reads, writes):
        deps = {}
        for v in reads:
            if v is None or v.res is None:
                continue
            w = v.res.w
            if w is not None:
                if deps.get(w[0], 0) < w[1]:
                    deps[w[0]] = w[1]
        for v in writes:
            if v is None or v.res is None:
                continue
            w = v.res.w
            if w is not None:
                if deps.get(w[0], 0) < w[1]:
                    deps[w[0]] = w[1]
            for k, val in v.res.r.items():
                if deps.get(k, 0) < val:
                    deps[k] = val
        kn = self.known[eng]
        for k, val in deps.items():
            if k[0] == "d":
                val = self.tot[k]
            elif eng == "tensor" and k == "tensor":
                continue
            if kn.get(k, 0) >= val:
                continue
            kn[k] = val
            self.e[eng].wait_ge(self.sem[k], val)
            self.n_wait += 1

    def emit(self, eng, reads, writes, fn):
        xr = [v for v in reads if v is not None and v.res is not None and v.res.excl]
        if xr:
            writes = list(writes) + xr
        self._waits(eng, reads, writes)
        ins = fn(self.e[eng])
        self.tot[eng] += 1
        n = self.tot[eng]
        ins.then_inc(self.sem[eng], 1)
        self.n_ins += 1
        for v in reads:
            if v is not None and v.res is not None:
                if v.res.r.get(eng, 0) < n:
                    v.res.r[eng] = n
        for v in writes:
            if v is not None and v.res is not None:
                v.res.w = (eng, n)
                v.res.r = {}

    def dma(self, out, in_, q="sync", **kw):
        self._waits(q, [in_], [out])
        k = None
        if out.res is not None and out.res.dsem is not None:
            k = out.res.dsem
        elif in_.res is not None and in_.res.dsem is not None:
            k = in_.res.dsem
        assert k is not None, "dma needs a resource with a dma semaphore"
        ins = self.e[q].dma_start(out=out.ap, in_=in_.ap, **kw)
        self.tot[k] += 16
        n = self.tot[k]
        ins.then_inc(self.sem[k], 16)
        self.n_ins += 1
        if in_.res is not None:
            if in_.res.r.get(k, 0) < n:
                in_.res.r[k] = n
        if out.res is not None:
            out.res.w = (k, n)
            out.res.r = {}

    def barrier(self):
        sy = self.e["sync"]
        kn = self.known["sync"]
        for k, val in self.tot.items():
            if val > 0 and kn.get(k, 0) < val:
                sy.wait_ge(self.sem[k], val)
                kn[k] = val
        self.bar_n += 1
        sy.sem_inc(self.bar, 1)
        for e in self.ENG:
            if e != "sync":
                self.e[e].wait_ge(self.bar, self.bar_n)
            for k, val in self.tot.items():
                self.known[e][k] = val

    def mm(self, out, lhsT, rhs, start=True, stop=True):
        self.emit("tensor", [lhsT, rhs], [out],
                  lambda e: e.matmul(out.ap, lhsT.ap, rhs.ap, start=start, stop=stop))

    def tr(self, out, in_, ident):
        self.emit("tensor", [in_, ident], [out],
                  lambda e: e.transpose(out.ap, in_.ap, ident.ap))

    def act(self, out, in_, func, bias=None, scale=1.0, eng="scalar"):
        rd = [in_]
        kw = {}
        if isinstance(bias, V):
            rd.append(bias)
            kw["bias"] = bias.ap
        elif bias is not None:
            kw["bias"] = bias
        if isinstance(scale, V):
            rd.append(scale)
            kw["scale"] = scale.ap
        else:
            kw["scale"] = scale
        self.emit(eng, rd, [out], lambda e: e.activation(out.ap, in_.ap, func, **kw))

    def tt(self, out, in0, in1, op, eng="vector"):
        self.emit(eng, [in0, in1], [out],
                  lambda e: e.tensor_tensor(out.ap, in0.ap, in1.ap, op))

    def ts(self, out, in0, s1, s2=None, op0=ALU.mult, op1=None, eng="vector"):
        rd = [in0]
        a1 = s1
        a2 = s2
        if isinstance(s1, V):
            rd.append(s1)
            a1 = s1.ap
        if isinstance(s2, V):
            rd.append(s2)
            a2 = s2.ap
        if op1 is None:
            self.emit(eng, rd, [out], lambda e: e.tensor_scalar(out.ap, in0.ap, a1, None, op0))
        else:
            self.emit(eng, rd, [out], lambda e: e.tensor_scalar(out.ap, in0.ap, a1, a2, op0, op1))

    def stt(self, out, in0, scalar, in1, op0, op1, eng="vector"):
        rd = [in0, in1]
        a = scalar
        if isinstance(scalar, V):
            rd.append(scalar)
            a = scalar.ap
        self.emit(eng, rd, [out],
                  lambda e: e.scalar_tensor_tensor(out.ap, in0.ap, a, in1.ap, op0, op1))

    def copy(self, out, in_, eng="vector"):
        if eng == "scalar":
            self.emit(eng, [in_], [out], lambda e: e.copy(out.ap, in_.ap))
        else:
            self.emit(eng, [in_], [out], lambda e: e.tensor_copy(out.ap, in_.ap))

    def red(self, out, in_, op=ALU.add, axis=AX.X, eng="vector"):
        self.emit(eng, [in_], [out], lambda e: e.tensor_reduce(out.ap, in_.ap, axis, op))

    def memset(self, out, val, eng="vector"):
        self.emit(eng, [], [out], lambda e: e.memset(out.ap, val))

    def recip(self, out, in_):
        self.emit("vector", [in_], [out], lambda e: e.reciprocal(out.ap, in_.ap))

    def vmax(self, out, in_):
        self.emit("vector", [in_], [out], lambda e: e.max(out.ap, in_.ap))

    def match_replace(self, out, in_to_replace, in_values, imm):
        self.emit("vector", [in_to_replace, in_values], [out],
                  lambda e: e.match_replace(out.ap, in_to_replace.ap, in_values.ap, imm))

D = 1024
SEQ = 4096
NSEQ = 2
TOK = NSEQ * SEQ
NB = SEQ // 128
LN_EPS = 1e-5
DN_ALPHA = (2 * 4) ** 0.25


def dv(ap):
    return V(ap, None)


class K:
    def __init__(self, nc, st, weights_meta, consts_meta, nblk=TOK // 128):
        self.nc = nc
        self.st = st
        self.S = Sched(nc, st)
        self.nblk = nblk
        self.din = {}
        for name, shape in list(weights_meta.items()) + list(consts_meta.items()):
            self.din[name] = nc.dram_tensor(name, list(shape), F32, kind="ExternalInput").ap()
        self.x_in = nc.dram_tensor("x", [TOK, D], F32, kind="ExternalInput").ap()
        self.out = nc.dram_tensor("out", [TOK, D], F32, kind="ExternalOutput").ap()
        self.o_s = nc.dram_tensor("o_s", [TOK, D], F32).ap()
        self.g_s = nc.dram_tensor("g_s", [TOK, D], F32).ap()
        S = self.S
        self.ps = [S.psum(st, "psb%d" % i) for i in range(8)]
        self.ps_i = 0
        self.pa_i = 0
        self.ident = S.tile(st, [128, 128], "ident", dma=True)
        S.dma(self.ident[:, :], dv(self.din["ident"][:, :]))
        self._eps = {}
        for val in (LN_EPS, 1.0, 1e-6, 64e-5, 1e-12, 0.0):
            t = S.tile(st, [128, 1], "eps")
            S.memset(t[:, :], float(val))
            self._eps[val] = t

    def psum(self):
        p = self.ps[2 + self.ps_i % 6]
        self.ps_i += 1
        return p

    def psum_acc(self):
        p = self.ps[self.pa_i % 2]
        self.pa_i += 1
        return p

    def stop(self, n):
        import os
        if int(os.environ.get("KSTOP", "99")) == n:
            self.S.barrier()
            return True
        return False

    def scratch(self, name, shape):
        return self.nc.dram_tensor(name, list(shape), F32).ap()

    def load_xT(self, src, t0, xtok, xT, nb=4):
        S = self.S
        S.dma(xtok[:, 0:nb, :], dv(src[t0:t0 + nb * 128, :].rearrange("(b p) d -> p b d", p=128)))
        for kc in range(8):
            ps = self.psum()
            for b in range(nb):
                S.tr(ps[:, b * 128:(b + 1) * 128], xtok[:, b, kc * 128:(kc + 1) * 128], self.ident[:, :])
            S.copy(xT[:, kc, 0:nb * 128], ps[:, 0:nb * 128], eng=("vector" if kc % 2 == 0 else "scalar"))

    def load_w(self, wt, w_ap, c0, n, q="sync"):
        self.S.dma(wt[:, :, 0:n], dv(w_ap.rearrange("(kc p) n -> p kc n", p=128)[:, :, c0:c0 + n]), q=q)

    def mm_feat(self, ps, wt, c0, m, xT, ntok=512):
        for kc in range(8):
            self.S.mm(ps[0:m, 0:ntok], wt[:, kc, c0:c0 + m], xT[:, kc, 0:ntok], start=(kc == 0), stop=(kc == 7))

    def mm_tok(self, ps, xT, blk, wt, c0, n):
        for kc in range(8):
            self.S.mm(ps[:, 0:n], xT[:, kc, blk * 128:(blk + 1) * 128], wt[:, kc, c0:c0 + n], start=(kc == 0), stop=(kc == 7))

    def phase_out(self, layer, x_src, w_out, post=None):
        S = self.S
        with ExitStack() as st:
            wo = S.tile(st, [128, 8, 1024], "wo", dma=True)
            S.dma(wo[:, :, :], dv(w_out.rearrange("(kc p) n -> p kc n", p=128)))
            lng = S.tile(st, [128, 1024], "lng", dma=True)
            lnb = S.tile(st, [128, 1024], "lnb", dma=True)
            S.dma(lng[:, :], dv(self.din["ln_g"][layer:layer + 1, :].partition_broadcast(128)))
            S.dma(lnb[:, :], dv(self.din["ln_b"][layer:layer + 1, :].partition_broadcast(128)))
            NBUF = 2
            ot = [S.tile(st, [128, 1024], "po_o%d" % i, dma=True) for i in range(NBUF)]
            gt = [S.tile(st, [128, 1024], "po_g%d" % i, dma=True) for i in range(NBUF)]
            xt = [S.tile(st, [128, 1024], "po_x%d" % i, dma=True) for i in range(NBUF)]
            zT = [S.tile(st, [128, 8, 128], "po_zT%d" % i) for i in range(NBUF)]
            yt = [S.tile(st, [128, 1024], "po_y%d" % i, dma=True) for i in range(NBUF)]
            sq = S.tile(st, [128, 1024], "po_sq")
            stat = [S.tile(st, [128, 8], "po_st%d" % i) for i in range(NBUF)]
            extra = post[0](st) if post is not None else None
            for blk in range(self.nblk):
                i = blk % NBUF
                r0 = blk * 128
                S.dma(ot[i][:, :], dv(self.o_s[r0:r0 + 128, :]))
                S.dma(gt[i][:, :], dv(self.g_s[r0:r0 + 128, :]))
                S.dma(xt[i][:, :], dv(x_src[r0:r0 + 128, :]))
                if post is not None:
                    post[1](extra, blk, ot[i], gt[i])
                S.act(gt[i][:, :], gt[i][:, :], AF.Silu)
                S.tt(ot[i][:, :], ot[i][:, :], gt[i][:, :], ALU.mult, eng="gpsimd")
                for half in range(2):
                    ps = self.psum()
                    for j in range(4):
                        kc = half * 4 + j
                        S.tr(ps[:, j * 128:(j + 1) * 128], ot[i][:, kc * 128:(kc + 1) * 128], self.ident[:, :])
                    S.copy(zT[i][:, half * 4:(half + 1) * 4, :],
                           ps.v(ps.h[:, :].rearrange("p (j t) -> p j t", t=128)), eng=("vector" if half == 0 else "scalar"))
                y = yt[i]
                for half in range(2):
                    ps = self.psum()
                    for kc in range(8):
                        S.mm(ps[:, :], zT[i][:, kc, :], wo[:, kc, half * 512:(half + 1) * 512], start=(kc == 0), stop=(kc == 7))
                    S.stt(y[:, half * 512:(half + 1) * 512], xt[i][:, half * 512:(half + 1) * 512], DN_ALPHA, ps[:, :], ALU.mult, ALU.add)
                sv = stat[i]
                S.red(sv[:, 0:1], y[:, :])
                S.ts(sv[:, 1:2], sv[:, 0:1], -1.0 / D, None, op0=ALU.mult)
                S.ts(y[:, :], y[:, :], sv[:, 1:2], None, op0=ALU.add)
                S.tt(sq[:, :], y[:, :], y[:, :], ALU.mult, eng="gpsimd")
                S.red(sv[:, 2:3], sq[:, :])
                S.act(sv[:, 3:4], sv[:, 2:3], AF.Sqrt, bias=self.eps_tile(LN_EPS), scale=1.0 / D)
                S.recip(sv[:, 4:5], sv[:, 3:4])
                S.stt(y[:, :], y[:, :], sv[:, 4:5], lng[:, :], ALU.mult, ALU.mult)
                S.tt(y[:, :], y[:, :], lnb[:, :], ALU.add, eng="gpsimd")
                S.dma(dv(self.out[r0:r0 + 128, :]), y[:, :], q="gpsimd")
            S.barrier()

    def eps_tile(self, val):
        return self._eps[val][:, 0:1]

    def fox_alloc(self):
        self.fx_qT = self.scratch("fx_qT", [8, 128, TOK])
        self.fx_kT = self.scratch("fx_kT", [8, 128, TOK])
        self.fx_v = self.scratch("fx_v", [TOK, D])
        self.fx_c = self.scratch("fx_c", [NSEQ, 128, NB * 8])
        self.fx_cr = self.scratch("fx_cr", [NSEQ, 128, NB * 8])

    def fox_proj(self, x_src):
        S = self.S
        w = self.din["fox_w_in"]
        with ExitStack() as st:
            xtok = [S.tile(st, [128, 4, 1024], "fp_xtok%d" % i, dma=True) for i in range(2)]
            xT = [S.tile(st, [128, 8, 512], "fp_xT%d" % i) for i in range(2)]
            wt = [S.tile(st, [128, 8, 512], "fp_w%d" % i, dma=True) for i in range(3)]
            wf = S.tile(st, [128, 8, 8], "fp_wf", dma=True)
            self.load_w(wf, w, 3072, 8)
            stg = [S.tile(st, [128, 512], "fp_stg%d" % i, dma=True) for i in range(4)]
            lf = S.tile(st, [128, NB, 8], "fp_lf")
            bf = S.tile(st, [128, 8], "fp_bf", dma=True)
            S.dma(bf[:, :], dv(self.din["fox_b_f"].rearrange("(o n) -> o n", o=1).partition_broadcast(128)))
            tri = S.tile(st, [128, 128], "fp_tri", dma=True)
            S.dma(tri[:, :], dv(self.din["mask_le"][:, :]))
            ones = S.tile(st, [128, 128], "fp_ones")
            S.memset(ones[:, :], 1.0)
            cw = S.tile(st, [128, NB, 8], "fp_cw", dma=True)
            cr = S.tile(st, [128, NB, 8], "fp_cr", dma=True)
            wi = 0
            si = 0
            ntile = self.nblk // 4
            for ti in range(ntile):
                t0 = ti * 512
                seq, tl = divmod(ti, NB // 4)
                xk = xtok[ti % 2]
                xt_ = xT[ti % 2]
                self.load_xT(x_src, t0, xk, xt_)
                for which, dst in ((0, self.fx_qT), (1, self.fx_kT)):
                    for grp in range(2):
                        wtile = wt[wi % 3]
                        wi += 1
                        self.load_w(wtile, w, which * 1024 + grp * 512, 512)
                        for j in range(4):
                            h = grp * 4 + j
                            ps = self.psum()
                            self.mm_feat(ps, wtile, j * 128, 128, xt_)
                            sg = stg[si % 4]
                            si += 1
                            S.copy(sg[:, :], ps[:, :], eng=("vector" if si % 2 == 0 else "scalar"))
                            S.dma(dv(dst[h, :, t0:t0 + 512]), sg[:, :], q="gpsimd")
                for c0, dst in ((2048, self.fx_v), (3080, self.g_s)):
                    for grp in range(2):
                        wtile = wt[wi % 3]
                        wi += 1
                        self.load_w(wtile, w, c0 + grp * 512, 512)
                        for b in range(4):
                            ps = self.psum()
                            self.mm_tok(ps, xt_, b, wtile, 0, 512)
                            sg = stg[si % 4]
                            si += 1
                            S.copy(sg[:, :], ps[:, :], eng=("vector" if si % 2 == 0 else "scalar"))
                            S.dma(dv(dst[t0 + b * 128:t0 + (b + 1) * 128, grp * 512:(grp + 1) * 512]), sg[:, :], q="gpsimd")
                ps = self.psum()
                for b in range(4):
                    for kc in range(8):
                        S.mm(ps[:, b * 8:(b + 1) * 8], xt_[:, kc, b * 128:(b + 1) * 128], wf[:, kc, :], start=(kc == 0), stop=(kc == 7))
                for b in range(4):
                    S.tt(lf[:, tl * 4 + b, :], ps[:, b * 8:(b + 1) * 8], bf[:, :], ALU.add)
                if tl == NB // 4 - 1 or ti == ntile - 1:
                    lf2 = lf.v(lf.h[:, :, :].rearrange("p b h -> p (b h)"))
                    S.act(lf2, lf2, AF.Exp, scale=-1.0)
                    S.act(lf2, lf2, AF.Ln, bias=self.eps_tile(1.0), scale=1.0)
                    S.ts(lf2, lf2, -1.0, None, op0=ALU.mult)
                    psw = self.psum()
                    S.mm(psw[:, 0:NB * 8], tri[:, :], lf2)
                    pst = self.psum()
                    S.mm(pst[:, 0:NB * 8], ones[:, :], lf2)
                    S.memset(cr[:, 0, :], 0.0)
                    for b in range(1, NB):
                        S.tt(cr[:, b, :], cr[:, b - 1, :], pst[:, (b - 1) * 8:b * 8], ALU.add)
                    S.tt(cw.v(cw.h[:, :, :].rearrange("p b h -> p (b h)")), psw[:, 0:NB * 8],
                         cr.v(cr.h[:, :, :].rearrange("p b h -> p (b h)")), ALU.add)
                    S.dma(dv(self.fx_c[seq, :, :]), cw.v(cw.h[:, :, :].rearrange("p b h -> p (b h)")), q="gpsimd")
                    S.dma(dv(self.fx_cr[seq, :, :]), cr.v(cr.h[:, :, :].rearrange("p b h -> p (b h)")), q="gpsimd")
            S.barrier()

    def fox_mix(self):
        S = self.S
        nseq = max(1, self.nblk // NB)
        nb = min(NB, self.nblk)
        SCALE = 128 ** -0.5
        with ExitStack() as st:
            KT = [S.tile(st, [128, SEQ], "fm_KT%d" % i, dma=True) for i in range(2)]
            QT = [S.tile(st, [128, SEQ], "fm_QT%d" % i, dma=True) for i in range(2)]
            VA = [S.tile(st, [128, NB, 132], "fm_VA%d" % i, dma=True) for i in range(2)]
            OS = [S.tile(st, [128, NB, 128], "fm_OS%d" % i, dma=True) for i in range(2)]
            cw = S.tile(st, [128, NB, 8], "fm_cw", dma=True)
            cr = S.tile(st, [128, NB, 8], "fm_cr", dma=True)
            bias = [S.tile(st, [128, NB, NB], "fm_bias%d" % i) for i in range(2)]
            PT = [S.tile(st, [128, 128], "fm_PT%d" % i) for i in range(12)]
            rc = [S.tile(st, [128, 1], "fm_rc%d" % i) for i in range(2)]
            mle = S.tile(st, [128, 128], "fm_mle", dma=True)
            S.dma(mle[:, :], dv(self.din["mask_le"][:, :]))
            for i in range(2):
                S.memset(VA[i][:, :, 128:129], 1.0)
            it = 0
            pti = 0
            for seq in range(nseq):
                S.dma(cw.v(cw.h[:, :, :].rearrange("p b h -> p (b h)")), dv(self.fx_c[seq, :, :]))
                S.dma(cr.v(cr.h[:, :, :].rearrange("p b h -> p (b h)")), dv(self.fx_cr[seq, :, :]))
                for h in range(8):
                    i = it % 2
                    it += 1
                    c0 = seq * SEQ
                    S.dma(KT[i][:, 0:nb * 128], dv(self.fx_kT[h, :, c0:c0 + nb * 128]))
                    S.dma(QT[i][:, 0:nb * 128], dv(self.fx_qT[h, :, c0:c0 + nb * 128]))
                    S.dma(VA[i][:, 0:nb, 0:128],
                          dv(self.fx_v[c0:c0 + nb * 128, h * 128:(h + 1) * 128].rearrange("(b p) d -> p b d", p=128)))
                    bt = bias[i]
                    for qb in range(nb):
                        S.ts(bt[:, qb, 0:qb + 1], cw[:, 0:qb + 1, h], -1.0, cr[:, qb, h:h + 1], op0=ALU.mult, op1=ALU.add)
                    for qb in range(nb):
                        pso = self.psum_acc()
                        nk = qb + 1
                        for g0 in range(0, nk, 4):
                            g1 = min(nk, g0 + 4)
                            pss = self.psum()
                            pts = [PT[(pti + j) % 12] for j in range(4)]
                            pti += 4
                            for kb in range(g0, g1):
                                S.mm(pss[:, (kb - g0) * 128:(kb - g0 + 1) * 128], KT[i][:, kb * 128:(kb + 1) * 128],
                                     QT[i][:, qb * 128:(qb + 1) * 128])
                            for kb in range(g0, g1):
                                S.act(pts[kb - g0][:, :], pss[:, (kb - g0) * 128:(kb - g0 + 1) * 128], AF.Exp,
                                      bias=bt[:, qb, kb:kb + 1], scale=SCALE)
                            if g1 == nk:
                                S.tt(pts[qb - g0][:, :], pts[qb - g0][:, :], mle[:, :], ALU.mult, eng="gpsimd")
                            for kb in range(g0, g1):
                                S.mm(pso[:, 0:129], pts[kb - g0][:, :], VA[i][:, kb, 0:129], start=(kb == 0), stop=(kb == nk - 1))
                        r = rc[qb % 2]
                        S.recip(r[:, :], pso[:, 128:129])
                        S.ts(OS[i][:, qb, :], pso[:, 0:128], r[:, 0:1], None, op0=ALU.mult)
                    S.dma(dv(self.o_s[c0:c0 + nb * 128, h * 128:(h + 1) * 128].rearrange("(b p) d -> p b d", p=128)),
                          OS[i][:, 0:nb, :], q="gpsimd")
            S.barrier()

    def dsa_alloc(self):
        self.ds_qlT = self.scratch("ds_qlT", [8, 128, TOK])
        self.ds_qcT = self.scratch("ds_qcT", [8, 96, TOK])
        self.ds_kcT = self.scratch("ds_kcT", [96, TOK])
        self.ds_ckv = self.scratch("ds_ckv", [TOK, 128])
        self.ds_ckvT = self.scratch("ds_ckvT", [128, TOK])
        self.ds_wi = self.scratch("ds_wi", [TOK, 16])

    def dsa_proj(self, x_src):
        S = self.S
        w = self.din["dsa_w_in"]
        wr = w.rearrange("(kc p) n -> p kc n", p=128)
        with ExitStack() as st:
            xtok = [S.tile(st, [128, 4, 1024], "dp_xtok%d" % i, dma=True) for i in range(2)]
            xT = [S.tile(st, [128, 8, 512], "dp_xT%d" % i) for i in range(2)]
            wt = [S.tile(st, [128, 8, 512], "dp_w%d" % i, dma=True) for i in range(2)]
            wsm = S.tile(st, [128, 8, 744], "dp_wsm", dma=True)
            S.dma(wsm[:, :, :], dv(wr[:, :, 1024:1768]))
            wpq = S.tile(st, [128, 8, 8, 32], "dp_wpq", dma=True)
            wpi = S.tile(st, [128, 8, 8, 32], "dp_wpi", dma=True)
            wpk = S.tile(st, [128, 8, 64], "dp_wpk", dma=True)
            S.memset(wpi[:, :, :, :], 0.0)
            S.memset(wpk[:, :, :], 0.0)
            for h in range(8):
                S.dma(wpq[:, :, h, 0:16], dv(wr[:, :, h * 128 + 16:h * 128 + 32]))
                S.dma(wpq[:, :, h, 16:32], dv(wr[:, :, h * 128:h * 128 + 16]))
                S.dma(wpi[:, :, h, 0:8], dv(wr[:, :, 1184 + h * 64 + 8:1184 + h * 64 + 16]))
                S.dma(wpi[:, :, h, 8:16], dv(wr[:, :, 1184 + h * 64:1184 + h * 64 + 8]))
            S.dma(wpk[:, :, 0:16], dv(wr[:, :, 1152 + 16:1152 + 32]))
            S.dma(wpk[:, :, 16:32], dv(wr[:, :, 1152:1152 + 16]))
            S.dma(wpk[:, :, 32:40], dv(wr[:, :, 1696 + 8:1696 + 16]))
            S.dma(wpk[:, :, 40:48], dv(wr[:, :, 1696:1696 + 8]))
            wuk = S.tile(st, [128, 8, 96], "dp_wuk", dma=True)
            S.dma(wuk[:, :, :], dv(self.din["dsa_w_uk"].rearrange("h c n -> c h n")))
            wukT = S.tile(st, [96, 8, 128], "dp_wukT")
            for h in range(8):
                ps = self.psum()
                S.tr(ps[0:96, 0:128], wuk[:, h, :], self.ident[:, :])
                S.copy(wukT[:, h, :], ps[0:96, 0:128])
            if self.stop(1):
                return
            kvg = S.tile(st, [128, 128], "dp_kvg", dma=True)
            S.dma(kvg[:, :], dv(self.din["dsa_kv_norm_g"].rearrange("(o n) -> o n", o=1).partition_broadcast(128)))
            ropeA = [S.tile(st, [32, 2, 512], "dp_ropeA%d" % i, dma=True) for i in range(2)]
            ropeI = [S.tile(st, [32, 2, 512], "dp_ropeI%d" % i, dma=True) for i in range(2)]
            stg = [S.tile(st, [128, 512], "dp_stg%d" % i, dma=True) for i in range(4)]
            qn = [S.tile(st, [96, 512], "dp_qn%d" % i) for i in range(2)]
            qc = [S.tile(st, [96, 512], "dp_qc%d" % i, dma=True) for i in range(3)]
            t32 = [S.tile(st, [32, 512], "dp_t32%d" % i) for i in range(2)]
            csb = [S.tile(st, [128, 128], "dp_csb%d" % i, dma=True) for i in range(2)]
            csq = S.tile(st, [128, 128], "dp_csq")
            cst = [S.tile(st, [128, 8], "dp_cst%d" % i) for i in range(2)]
            cT = [S.tile(st, [128, 128], "dp_cT%d" % i, dma=True) for i in range(2)]
            wis = [S.tile(st, [128, 16], "dp_wi%d" % i, dma=True) for i in range(2)]
            wi_ = 0
            si = 0
            qi_ = 0
            ntile = self.nblk // 4
            CI = (64 ** -0.5) * (8 ** -0.5)
            for ti in range(ntile):
                t0 = ti * 512
                p0 = t0 % SEQ
                xk = xtok[ti % 2]
                xt_ = xT[ti % 2]
                self.load_xT(x_src, t0, xk, xt_)
                rA = ropeA[ti % 2]
                rI = ropeI[ti % 2]
                S.dma(rA[:, 0, :], dv(self.din["dsa_ropeA_c"][:, p0:p0 + 512]))
                S.dma(rA[:, 1, :], dv(self.din["dsa_ropeA_s"][:, p0:p0 + 512]))
                S.dma(rI[:, 0, :], dv(self.din["dsa_ropeI_c"][:, p0:p0 + 512]))
                S.dma(rI[:, 1, :], dv(self.din["dsa_ropeI_s"][:, p0:p0 + 512]))

                def rope32(dst, psA, psB):
                    a, b2 = t32
                    S.tt(a[:, :], psA[0:32, :], rA[:, 0, :], ALU.mult)
                    S.tt(b2[:, :], psB[0:32, :], rA[:, 1, :], ALU.mult)
                    S.tt(dst, a[:, :], b2[:, :], ALU.add, eng="gpsimd")

                if self.stop(2):
                    return
                for grp in range(2):
                    wtile = wt[wi_ % 2]
                    wi_ += 1
                    self.load_w(wtile, w, grp * 512, 512)
                    for j in range(4):
                        h = grp * 4 + j
                        ps = self.psum()
                        self.mm_feat(ps, wtile, j * 128 + 32, 96, xt_)
                        qnt = qn[h % 2]
                        S.copy(qnt[:, :], ps[0:96, :], eng="scalar")
                        ps2 = self.psum()
                        S.mm(ps2[:, :], wukT[:, h, :], qnt[:, :])
                        sg = stg[si % 4]
                        si += 1
                        S.copy(sg[:, :], ps2[:, :])
                        S.dma(dv(self.ds_qlT[h, :, t0:t0 + 512]), sg[:, :], q="gpsimd")
                        if self.stop(21):
                            return
                        qct = qc[qi_ % 3]
                        qi_ += 1
                        psA = self.psum()
                        self.mm_feat(psA, wtile, j * 128, 32, xt_)
                        psB = self.psum()
                        for kc in range(8):
                            S.mm(psB[0:32, :], wpq[:, kc, h, :], xt_[:, kc, :], start=(kc == 0), stop=(kc == 7))
                        rope32(qct[0:32, :], psA, psB)
                        S.dma(dv(self.ds_qcT[h, 64:96, t0:t0 + 512]), qct[0:32, :], q="gpsimd")
                        if self.stop(22):
                            return
                        qct = qc[qi_ % 3]
                        qi_ += 1
                        psC = self.psum()
                        self.mm_feat(psC, wsm, 160 + h * 64, 64, xt_)
                        psD = self.psum()
                        for kc in range(8):
                            S.mm(psD[0:32, :], wpi[:, kc, h, :], xt_[:, kc, :], start=(kc == 0), stop=(kc == 7))
                        if self.stop(23):
                            return
                        S.copy(qct[0:64, :], psC[0:64, :], eng="scalar")
                        if self.stop(24):
                            return
                        a, b2 = t32
                        S.tt(a[0:32, :], psC[0:32, :], rI[:, 0, :], ALU.mult)
                        S.tt(b2[0:32, :], psD[0:32, :], rI[:, 1, :], ALU.mult)
                        if self.stop(25):
                            return
                        S.tt(qct[0:32, :], a[0:32, :], b2[0:32, :], ALU.add, eng="gpsimd")
                        if self.stop(26):
                            return
                        S.dma(dv(self.ds_qcT[h, 0:64, t0:t0 + 512]), qct[0:64, :], q="gpsimd")
                if self.stop(3):
                    return
                qct = qc[qi_ % 3]
                qi_ += 1
                psA = self.psum()
                self.mm_feat(psA, wsm, 128, 32, xt_)
                psB = self.psum()
                for kc in range(8):
                    S.mm(psB[0:32, :], wpk[:, kc, 0:32], xt_[:, kc, :], start=(kc == 0), stop=(kc == 7))
                rope32(qct[0:32, :], psA, psB)
                S.dma(dv(self.ds_kcT[64:96, t0:t0 + 512]), qct[0:32, :], q="gpsimd")
                qct = qc[qi_ % 3]
                qi_ += 1
                psC = self.psum()
                self.mm_feat(psC, wsm, 672, 64, xt_)
                psD = self.psum()
                for kc in range(8):
                    S.mm(psD[0:32, :], wpk[:, kc, 32:64], xt_[:, kc, :], start=(kc == 0), stop=(kc == 7))
                S.copy(qct[0:64, :], psC[0:64, :], eng="scalar")
                a, b2 = t32
                S.tt(a[0:32, :], psC[0:32, :], rI[:, 0, :], ALU.mult)
                S.tt(b2[0:32, :], psD[0:32, :], rI[:, 1, :], ALU.mult)
                S.tt(qct[0:32, :], a[0:32, :], b2[0:32, :], ALU.add, eng="gpsimd")
                S.dma(dv(self.ds_kcT[0:64, t0:t0 + 512]), qct[0:64, :], q="gpsimd")
                if self.stop(4):
                    return
                for b in range(4):
                    r0 = t0 + b * 128
                    ps = self.psum()
                    self.mm_tok(ps, xt_, b, wsm, 0, 128)
                    c = csb[b % 2]
                    sv = cst[b % 2]
                    S.copy(c[:, :], ps[:, 0:128], eng="scalar")
                    S.tt(csq[:, :], c[:, :], c[:, :], ALU.mult, eng="gpsimd")
                    S.red(sv[:, 0:1], csq[:, :])
                    S.act(sv[:, 1:2], sv[:, 0:1], AF.Sqrt, bias=self.eps_tile(1e-6), scale=1.0 / 128)
                    S.recip(sv[:, 2:3], sv[:, 1:2])
                    S.stt(c[:, :], c[:, :], sv[:, 2:3], kvg[:, :], ALU.mult, ALU.mult)
                    S.dma(dv(self.ds_ckv[r0:r0 + 128, :]), c[:, :], q="gpsimd")
                    psT = self.psum()
                    S.tr(psT[:, 0:128], c[:, :], self.ident[:, :])
                    ct = cT[b % 2]
                    S.copy(ct[:, :], psT[:, 0:128], eng="scalar")
                    S.dma(dv(self.ds_ckvT[:, r0:r0 + 128]), ct[:, :], q="gpsimd")
                    psw = self.psum()
                    self.mm_tok(psw, xt_, b, wsm, 736, 8)
                    wv = wis[b % 2]
                    S.copy(wv[:, 8:16], psw[:, 0:8])
                    S.stt(wv[:, 0:8], wv[:, 8:16], -1.0, wv[:, 8:16], ALU.mult, ALU.max)
                    S.ts(wv[:, 0:8], wv[:, 0:8], CI, None, op0=ALU.mult)
                    S.act(wv[:, 8:16], wv[:, 8:16], AF.Sign)
                    S.dma(dv(self.ds_wi[r0:r0 + 128, :]), wv[:, :], q="gpsimd")
                if self.stop(5):
                    return
                for grp in range(2):
                    wtile = wt[wi_ % 2]
                    wi_ += 1
                    self.load_w(wtile, w, 1768 + grp * 512, 512)
                    for b in range(4):
                        ps = self.psum()
                        self.mm_tok(ps, xt_, b, wtile, 0, 512)
                        sg = stg[si % 4]
                        si += 1
                        S.copy(sg[:, :], ps[:, :], eng=("vector" if si % 2 == 0 else "scalar"))
                        S.dma(dv(self.g_s[t0 + b * 128:t0 + (b + 1) * 128, grp * 512:(grp + 1) * 512]), sg[:, :], q="gpsimd")
            S.barrier()

    def dsa_mix(self):
        S = self.S
        nseq = max(1, self.nblk // NB)
        nb = min(NB, self.nblk)
        SCALE = 128 ** -0.5
        with ExitStack() as st:
            kc_ = S.tile(st, [96, SEQ], "dm_kc", dma=True)
            ckvT = S.tile(st, [128, SEQ], "dm_ckvT", dma=True)
            ckvA = S.tile(st, [128, NB, 132], "dm_ckvA", dma=True)
            S.memset(ckvA[:, :, 128:129], 1.0)
            wuv = S.tile(st, [128, 8, 128], "dm_wuv", dma=True)
            S.dma(wuv[:, :, :], dv(self.din["dsa_w_uv"].rearrange("h c d -> c h d")))
            negm = S.tile(st, [128, 128], "dm_negm", dma=True)
            S.dma(negm[:, :], dv(self.din["negmask"][:, :]))
            qc = [S.tile(st, [96, 8, 128], "dm_qc%d" % i, dma=True) for i in range(2)]
            ql = [S.tile(st, [128, 8, 128], "dm_ql%d" % i, dma=True) for i in range(2)]
            wi = [S.tile(st, [128, 16], "dm_wi%d" % i, dma=True) for i in range(2)]
            I_ = [S.tile(st, [128, SEQ], "dm_I%d" % i) for i in range(2)]
            Wk = S.tile(st, [128, SEQ], "dm_Wk")
            MT = [S.tile(st, [128, NB, 128], "dm_MT%d" % i) for i in range(2)]
            m8 = [S.tile(st, [128, 8], "dm_m8%d" % i) for i in range(2)]
            tmp = [S.tile(st, [128, 512], "dm_tmp%d" % i) for i in range(3)]
            PT = [S.tile(st, [128, 512], "dm_PT%d" % i) for i in range(3)]
            rc = [S.tile(st, [128, 1], "dm_rc%d" % i) for i in range(2)]
            olat = [S.tile(st, [128, 128], "dm_ol%d" % i) for i in range(2)]
            olT = [S.tile(st, [128, 128], "dm_olT%d" % i) for i in range(2)]
            osb = [S.tile(st, [128, 1024], "dm_osb%d" % i, dma=True) for i in range(2)]
            tmi = 0
            pti = 0
            hi = 0
            for seq in range(nseq):
                c0 = seq * SEQ
                S.dma(kc_[:, 0:nb * 128], dv(self.ds_kcT[:, c0:c0 + nb * 128]))
                S.dma(ckvT[:, 0:nb * 128], dv(self.ds_ckvT[:, c0:c0 + nb * 128]))
                S.dma(ckvA[:, 0:nb, 0:128], dv(self.ds_ckv[c0:c0 + nb * 128, :].rearrange("(b p) d -> p b d", p=128)))
                for qb in range(nb):
                    i = qb % 2
                    r0 = c0 + qb * 128
                    nk = qb + 1
                    L = nk * 128
                    S.dma(qc[i][:, :, :], dv(self.ds_qcT[:, :, r0:r0 + 128].rearrange("h d t -> d h t")))
                    S.dma(ql[i][:, :, :], dv(self.ds_qlT[:, :, r0:r0 + 128].rearrange("h d t -> d h t")))
                    S.dma(wi[i][:, :], dv(self.ds_wi[r0:r0 + 128, :]))
                    It = I_[i]
                    for k0 in range(0, L, 512):
                        n = min(512, L - k0)
                        for h in range(8):
                            ps = self.psum()
                            S.mm(ps[:, 0:n], qc[i][0:64, h, :], kc_[0:64, k0:k0 + n])
                            t = tmp[tmi % 3]
                            tmi += 1
                            S.act(t[:, 0:n], ps[:, 0:n], AF.Relu, scale=wi[i][:, h:h + 1])
                            if h == 0:
                                S.ts(It[:, k0:k0 + n], t[:, 0:n], wi[i][:, 8:9], None, op0=ALU.mult)
                            else:
                                S.stt(It[:, k0:k0 + n], t[:, 0:n], wi[i][:, 8 + h:9 + h], It[:, k0:k0 + n], ALU.mult, ALU.add)
                    S.tt(It[:, qb * 128:L], It[:, qb * 128:L], negm[:, :], ALU.add, eng="gpsimd")
                    if qb >= 2:
                        src = It
                        for r in range(32):
                            m = m8[r % 2]
                            S.vmax(m[:, :], src[:, 0:L])
                            if r < 31:
                                S.match_replace(Wk[:, 0:L], m[:, :], src[:, 0:L], -1e30)
                                src = Wk
                        S.ts(Wk[:, 0:L], It[:, 0:L], m8[1][:, 7:8], None, op0=ALU.is_ge)
                    else:
                        S.ts(Wk[:, 0:L], It[:, 0:L], -1e29, None, op0=ALU.is_ge)
                    mt = MT[i]
                    for g0 in range(0, nk, 4):
                        g1 = min(nk, g0 + 4)
                        ps = self.psum()
                        for kb in range(g0, g1):
                            S.tr(ps[:, (kb - g0) * 128:(kb - g0 + 1) * 128], Wk[:, kb * 128:(kb + 1) * 128], self.ident[:, :])
                        S.copy(mt.v(mt.h[:, g0:g1, :].rearrange("p a t -> p (a t)")), ps[:, 0:(g1 - g0) * 128],
                               eng=("scalar" if (g0 // 4) % 2 == 0 else "gpsimd") if False else "scalar")
                    ob = osb[i]
                    for h in range(8):
                        pso = self.psum_acc()
                        for g0 in range(0, nk, 4):
                            g1 = min(nk, g0 + 4)
                            n = (g1 - g0) * 128
                            pss = self.psum()
                            for kb in range(g0, g1):
                                o_ = pss[:, (kb - g0) * 128:(kb - g0 + 1) * 128]
                                S.mm(o_, ckvT[:, kb * 128:(kb + 1) * 128], ql[i][:, h, :], start=True, stop=False)
                                S.mm(o_, kc_[64:96, kb * 128:(kb + 1) * 128], qc[i][64:96, h, :], start=False, stop=True)
                            pt = PT[pti % 3]
                            pti += 1
                            S.act(pt[:, 0:n], pss[:, 0:n], AF.Exp, scale=SCALE)
                            S.tt(pt[:, 0:n], pt[:, 0:n], mt.v(mt.h[:, g0:g1, :].rearrange("p a t -> p (a t)")), ALU.mult, eng="gpsimd")
                            for kb in range(g0, g1):
                                S.mm(pso[:, 0:129], pt[:, (kb - g0) * 128:(kb - g0 + 1) * 128], ckvA[:, kb, 0:129],
                                     start=(kb == 0), stop=(kb == nk - 1))
                        r = rc[hi % 2]
                        ol = olat[hi % 2]
                        olt = olT[hi % 2]
                        hi += 1
                        S.recip(r[:, :], pso[:, 128:129])
                        S.ts(ol[:, :], pso[:, 0:128], r[:, 0:1], None, op0=ALU.mult)
                        psT = self.psum()
                        S.tr(psT[:, 0:128], ol[:, :], self.ident[:, :])
                        S.copy(olt[:, :], psT[:, 0:128], eng="scalar")
                        ps2 = self.psum()
                        S.mm(ps2[:, 0:128], olt[:, :], wuv[:, h, :])
                        S.copy(ob[:, h * 128:(h + 1) * 128], ps2[:, 0:128], eng="scalar")
                    S.dma(dv(self.o_s[r0:r0 + 128, :]), ob[:, :], q="gpsimd")
            S.barrier()

    def rwkv_alloc(self):
        self.rw_r = self.scratch("rw_r", [TOK, D])
        self.rw_k = self.scratch("rw_k", [TOK, D])
        self.rw_v = self.scratch("rw_v", [TOK, D])
        self.rw_lw = self.scratch("rw_lw", [TOK, D])
        self.rw_a = self.scratch("rw_a", [TOK, D])

    def bcast_row(self, st, name, ap1d):
        t = self.S.tile(st, [128, 1024], name, dma=True)
        self.S.dma(t[:, :], dv(ap1d.rearrange("(o n) -> o n", o=1).partition_broadcast(128)))
        return t

    def rwkv_proj(self, x_src):
        S = self.S
        w = self.din["rwkv_w_in"]
        with ExitStack() as st:
            xtok = [S.tile(st, [128, 4, 1024], "wp_xtok%d" % i, dma=True) for i in range(2)]
            xT = [S.tile(st, [128, 8, 512], "wp_xT%d" % i) for i in range(2)]
            dT = S.tile(st, [128, 8, 512], "wp_dT")
            xm = [S.tile(st, [128, 8, 512], "wp_xm%d" % i) for i in range(2)]
            halo = S.tile(st, [128, 8, 1], "wp_halo")
            wt = [S.tile(st, [128, 8, 512], "wp_w%d" % i, dma=True) for i in range(2)]
            stg = [S.tile(st, [128, 512], "wp_stg%d" % i, dma=True) for i in range(4)]
            mu48 = S.tile(st, [48, 128], "wp_mu48", dma=True)
            S.dma(mu48[:, :], dv(self.din["rwkv_mu"].rearrange("i (kc p) -> (i kc) p", p=128)))
            muT = S.tile(st, [128, 48], "wp_muT")
            ps = self.psum()
            S.tr(ps[:, 0:48], mu48[:, :], self.ident[0:48, 0:48])
            S.copy(muT[:, :], ps[:, 0:48])
            wla = S.tile(st, [128, 8, 64], "wp_wla", dma=True)
            ala = S.tile(st, [128, 8, 64], "wp_ala", dma=True)
            S.dma(wla[:, :, :], dv(self.din["rwkv_w_lora_a"].rearrange("(kc p) n -> p kc n", p=128)))
            S.dma(ala[:, :, :], dv(self.din["rwkv_a_lora_a"].rearrange("(kc p) n -> p kc n", p=128)))
            wlb = S.tile(st, [64, 1024], "wp_wlb", dma=True)
            alb = S.tile(st, [64, 1024], "wp_alb", dma=True)
            S.dma(wlb[:, :], dv(self.din["rwkv_w_lora_b"][:, :]))
            S.dma(alb[:, :], dv(self.din["rwkv_a_lora_b"][:, :]))
            w0 = self.bcast_row(st, "wp_w0", self.din["rwkv_w0"])
            a0 = self.bcast_row(st, "wp_a0", self.din["rwkv_a0"])
            hl = [S.tile(st, [64, 512], "wp_hl%d" % i) for i in range(2)]
            wi_ = 0
            si = 0
            xi_ = 0
            ntile = self.nblk // 4
            NEG = -float(np.exp(-0.5))
            for ti in range(ntile):
                t0 = ti * 512
                xk = xtok[ti % 2]
                xt_ = xT[ti % 2]
                if t0 % SEQ == 0:
                    S.memset(halo[:, :, :], 0.0)
                self.load_xT(x_src, t0, xk, xt_)
                S.tt(dT[:, :, 1:512], xt_[:, :, 0:511], xt_[:, :, 1:512], ALU.subtract)
                S.tt(dT[:, :, 0:1], halo[:, :, :], xt_[:, :, 0:1], ALU.subtract)
                S.copy(halo[:, :, :], xt_[:, :, 511:512], eng="gpsimd")

                def mix(i):
                    nonlocal xi_
                    t = xm[xi_ % 2]
                    xi_ += 1
                    for kc in range(8):
                        S.stt(t[:, kc, :], dT[:, kc, :], muT[:, i * 8 + kc:i * 8 + kc + 1], xt_[:, kc, :], ALU.mult, ALU.add)
                    return t

                for i, c0, dst in ((0, 0, self.rw_r), (2, 1024, self.rw_k), (3, 2048, self.rw_v), (5, 3072, self.g_s)):
                    xmt = mix(i)
                    for grp in range(2):
                        wtile = wt[wi_ % 2]
                        wi_ += 1
                        self.load_w(wtile, w, c0 + grp * 512, 512)
                        for b in range(4):
                            ps = self.psum()
                            self.mm_tok(ps, xmt, b, wtile, 0, 512)
                            sg = stg[si % 4]
                            si += 1
                            S.copy(sg[:, :], ps[:, :], eng=("vector" if si % 2 == 0 else "scalar"))
                            S.dma(dv(dst[t0 + b * 128:t0 + (b + 1) * 128, grp * 512:(grp + 1) * 512]), sg[:, :], q="gpsimd")
                for i, la, lb, bias_t, dst, mul in ((1, wla, wlb, w0, self.rw_lw, NEG), (4, ala, alb, a0, self.rw_a, None)):
                    xmt = mix(i)
                    ps = self.psum()
                    for kc in range(8):
                        S.mm(ps[0:64, :], la[:, kc, :], xmt[:, kc, :], start=(kc == 0), stop=(kc == 7))
                    h_ = hl[i % 2]
                    if i == 1:
                        S.act(h_[:, :], ps[0:64, :], AF.Tanh)
                    else:
                        S.copy(h_[:, :], ps[0:64, :], eng="scalar")
                    for b in range(4):
                        for half in range(2):
                            ps2 = self.psum()
                            S.mm(ps2[:, :], h_[:, b * 128:(b + 1) * 128], lb[:, half * 512:(half + 1) * 512])
                            sg = stg[si % 4]
                            si += 1
                            S.tt(sg[:, :], ps2[:, :], bias_t[:, half * 512:(half + 1) * 512], ALU.add)
                            S.act(sg[:, :], sg[:, :], AF.Sigmoid)
                            if mul is not None:
                                S.ts(sg[:, :], sg[:, :], mul, None, op0=ALU.mult, eng="gpsimd")
                            S.dma(dv(dst[t0 + b * 128:t0 + (b + 1) * 128, half * 512:(half + 1) * 512]), sg[:, :], q="gpsimd")
            S.barrier()

    def rwkv_mix(self):
        S = self.S
        nseq = max(1, self.nblk // NB)
        nb = min(NB, self.nblk)

        def v3(t):
            return t.v(t.h[:, :].rearrange("p (a c) -> p a c", c=64))

        with ExitStack() as st:
            kk_c = self.bcast_row(st, "wm_kk", self.din["rwkv_k_k"])
            ka_c = self.bcast_row(st, "wm_ka", self.din["rwkv_k_a"])
            rk_c = self.bcast_row(st, "wm_rk", self.din["rwkv_r_k"].rearrange("h n -> (h n)"))
            gg_c = self.bcast_row(st, "wm_gg", self.din["rwkv_gn_g"])
            gb_c = self.bcast_row(st, "wm_gb", self.din["rwkv_gn_b"])
            tri = S.tile(st, [128, 128], "wm_tri", dma=True)
            mlt = S.tile(st, [128, 128], "wm_mlt", dma=True)
            mgt = S.tile(st, [128, 128], "wm_mgt", dma=True)
            S.dma(tri[:, :], dv(self.din["mask_le"][:, :]))
            S.dma(mlt[:, :], dv(self.din["mask_lt"][:, :]))
            S.dma(mgt[:, :], dv(self.din["mask_gt"][:, :]))
            ones = S.tile(st, [128, 128], "wm_ones")
            S.memset(ones[:, :], 1.0)
            NIN = 2
            r_ = [S.tile(st, [128, 1024], "wm_r%d" % i, dma=True) for i in range(NIN)]
            k_ = [S.tile(st, [128, 1024], "wm_k%d" % i, dma=True) for i in range(NIN)]
            v_ = [S.tile(st, [128, 1024], "wm_v%d" % i, dma=True) for i in range(NIN)]
            lw_ = [S.tile(st, [128, 1024], "wm_lw%d" % i, dma=True) for i in range(NIN)]
            a_ = [S.tile(st, [128, 1024], "wm_a%d" % i, dma=True) for i in range(NIN)]
            kk = S.tile(st, [128, 1024], "wm_kkn")
            km = S.tile(st, [128, 1024], "wm_km")
            kka = S.tile(st, [128, 1024], "wm_kka")
            cum = S.tile(st, [128, 1024], "wm_cum")
            e1 = S.tile(st, [128, 1024], "wm_e1")
            e2 = S.tile(st, [128, 1024], "wm_e2")
            tq = S.tile(st, [128, 1024], "wm_tq")
            Ab = S.tile(st, [128, 1024], "wm_Ab")
            Bb = S.tile(st, [128, 1024], "wm_Bb")
            Kb = S.tile(st, [128, 1024], "wm_Kb")
            Rb = S.tile(st, [128, 1024], "wm_Rb")
            Bt = S.tile(st, [128, 1024], "wm_Bt")
            Kt = S.tile(st, [128, 1024], "wm_Kt")
            AbT = S.tile(st, [64, 16, 128], "wm_AbT")
            BbT = S.tile(st, [64, 16, 128], "wm_BbT")
            KbT = S.tile(st, [64, 16, 128], "wm_KbT")
            RbT = S.tile(st, [64, 16, 128], "wm_RbT")
            gC = S.tile(st, [64, 16], "wm_gC")
            sm = S.tile(st, [128, 64], "wm_sm")
            ST = [S.tile(st, [64, 64], "wm_ST%d" % h) for h in range(16)]
            ysb = [S.tile(st, [128, 1024], "wm_y%d" % i, dma=True) for i in range(2)]
            NS = 2
            P_ = [[S.tile(st, [128, 128], "wm_P%d_%d" % (s, i)) for i in range(2)] for s in range(NS)]
            PT_ = [[S.tile(st, [128, 128], "wm_PT%d_%d" % (s, i)) for i in range(2)] for s in range(NS)]
            W_ = [[S.tile(st, [128, 128], "wm_W%d_%d" % (s, i)) for i in range(2)] for s in range(NS)]
            ArbT = [S.tile(st, [128, 128], "wm_ArbT%d" % s) for s in range(NS)]
            ArkT = [S.tile(st, [128, 128], "wm_ArkT%d" % s) for s in range(NS)]
            AakT = [S.tile(st, [128, 128], "wm_AakT%d" % s) for s in range(NS)]
            MT = [S.tile(st, [64, 64], "wm_MT%d" % s) for s in range(NS)]
            GT = [S.tile(st, [64, 128], "wm_GT%d" % s) for s in range(NS)]
            hs = 0
            ci = 0
            for seq in range(nseq):
                for h in range(16):
                    S.memset(ST[h][:, :], 0.0)
                for c in range(nb):
                    i = ci % NIN
                    ci += 1
                    r0 = seq * SEQ + c * 128
                    rt, kt, vt, lwt, at = r_[i], k_[i], v_[i], lw_[i], a_[i]
                    S.dma(rt[:, :], dv(self.rw_r[r0:r0 + 128, :]))
                    S.dma(kt[:, :], dv(self.rw_k[r0:r0 + 128, :]))
                    S.dma(vt[:, :], dv(self.rw_v[r0:r0 + 128, :]))
                    S.dma(lwt[:, :], dv(self.rw_lw[r0:r0 + 128, :]))
                    S.dma(at[:, :], dv(self.rw_a[r0:r0 + 128, :]))
                    S.tt(kk[:, :], kt[:, :], kk_c[:, :], ALU.mult, eng="gpsimd")
                    S.tt(tq[:, :], kk[:, :], kk[:, :], ALU.mult, eng="gpsimd")
                    S.red(sm[:, 0:16], v3(tq))
                    S.act(sm[:, 0:16], sm[:, 0:16], AF.Sqrt, bias=self.eps_tile(1e-12), scale=1.0)
                    S.recip(sm[:, 16:32], sm[:, 0:16])
                    S.tt(v3(kk), v3(kk), sm.v(sm.h[:, 16:32].unsqueeze(2).to_broadcast([128, 16, 64])), ALU.mult)
                    S.stt(km[:, :], at[:, :], -1.0, ka_c[:, :], ALU.add, ALU.mult)
                    S.stt(km[:, :], km[:, :], 1.0, kt[:, :], ALU.add, ALU.mult)
                    S.tt(kka[:, :], kk[:, :], at[:, :], ALU.mult, eng="gpsimd")
                    S.tt(tq[:, :], rt[:, :], km[:, :], ALU.mult, eng="gpsimd")
                    S.tt(tq[:, :], tq[:, :], rk_c[:, :], ALU.mult, eng="gpsimd")
                    S.red(sm[:, 32:48], v3(tq))
                    for half in range(2):
                        hsl = slice(half * 512, (half + 1) * 512)
                        psc = self.psum()
                        S.mm(psc[:, :], tri[:, :], lwt[:, hsl])
                        S.copy(cum[:, hsl], psc[:, :], eng="scalar")
                        pst = self.psum()
                        S.mm(pst[:, :], ones[:, :], lwt[:, hsl])
                        S.tt(e2[:, hsl], pst[:, :], cum[:, hsl], ALU.subtract)
                    S.act(e2[:, :], e2[:, :], AF.Exp)
                    S.tt(Bt[:, :], kka[:, :], e2[:, :], ALU.mult, eng="gpsimd")
                    S.tt(Kt[:, :], km[:, :], e2[:, :], ALU.mult, eng="gpsimd")
                    S.act(e1[:, :], cum[:, :], AF.Exp)
                    S.tt(Rb[:, :], rt[:, :], e1[:, :], ALU.mult, eng="gpsimd")
                    S.act(e1[:, :], cum[:, :], AF.Exp, scale=-1.0)
                    S.tt(Bb[:, :], kka[:, :], e1[:, :], ALU.mult, eng="gpsimd")
                    S.tt(Kb[:, :], km[:, :], e1[:, :], ALU.mult, eng="gpsimd")
                    S.tt(e2[:, :], cum[:, :], lwt[:, :], ALU.subtract)
                    S.act(e2[:, :], e2[:, :], AF.Exp)
                    S.stt(Ab[:, :], kk[:, :], -1.0, e2[:, :], ALU.mult, ALU.mult)
                    psg = self.psum()
                    for h in range(16):
                        S.mm(psg[0:64, h:h + 1], lwt[:, h * 64:(h + 1) * 64], ones[:, 0:1])
                    S.act(gC[:, :], psg[0:64, 0:16], AF.Exp)
                    for src, dstT in ((Ab, AbT), (Bb, BbT), (Kb, KbT), (Rb, RbT)):
                        for g in range(4):
                            pT = self.psum()
                            for j in range(4):
                                h = g * 4 + j
                                S.tr(pT[0:64, j * 128:(j + 1) * 128], src[:, h * 64:(h + 1) * 64], self.ident[:, :])
                            S.copy(dstT.v(dstT.h[:, g * 4:(g + 1) * 4, :].rearrange("p a t -> p (a t)")), pT[0:64, :],
                                   eng=("scalar" if g % 2 == 0 else "vector"))
                    y = ysb[c % 2]
                    for h in range(16):
                        s = hs % NS
                        hs += 1
                        hc = slice(h * 64, (h + 1) * 64)
                        P, PT, W = P_[s], PT_[s], W_[s]
                        ps1 = self.psum()
                        S.mm(ps1[:, 0:128], AbT[:, h, :], BbT[:, h, :])
                        S.tt(P[0][:, :], ps1[:, 0:128], mgt[:, :], ALU.mult)
                        ps2 = self.psum()
                        S.mm(ps2[:, 0:128], BbT[:, h, :], AbT[:, h, :])
                        S.mm(ps2[:, 128:256], BbT[:, h, :], RbT[:, h, :])
                        S.tt(PT[0][:, :], ps2[:, 0:128], mlt[:, :], ALU.mult)
                        S.tt(ArbT[s][:, :], ps2[:, 128:256], tri[:, :], ALU.mult)
                        ps3 = self.psum()
                        S.mm(ps3[:, 0:128], KbT[:, h, :], AbT[:, h, :])
                        S.mm(ps3[:, 128:256], KbT[:, h, :], RbT[:, h, :])
                        S.tt(AakT[s][:, :], ps3[:, 0:128], mlt[:, :], ALU.mult)
                        S.tt(ArkT[s][:, :], ps3[:, 128:256], tri[:, :], ALU.mult)
                        ps4 = self.psum()
                        S.mm(ps4[:, 0:64], AakT[s][:, :], vt[:, hc])
                        S.copy(W[0][:, 0:64], Ab[:, hc], eng="gpsimd")
                        S.copy(W[0][:, 64:128], ps4[:, 0:64], eng="scalar")
                        for lv in range(7):
                            a, b = lv % 2, (lv + 1) % 2
                            psw = self.psum()
                            S.mm(psw[:, 0:128], PT[a][:, :], W[a][:, :])
                            S.tt(W[b][:, :], W[a][:, :], psw[:, 0:128], ALU.add)
                            if lv < 6:
                                psq = self.psum()
                                S.mm(psq[:, 0:128], P[a][:, :], PT[a][:, :])
                                if lv < 5:
                                    S.mm(psq[:, 128:256], PT[a][:, :], P[a][:, :])
                                S.copy(PT[b][:, :], psq[:, 0:128], eng="scalar")
                                if lv < 5:
                                    S.copy(P[b][:, :], psq[:, 128:256], eng="scalar")
                        Wf = W[1]
                        At_ = Wf[:, 0:64]
                        U0 = Wf[:, 64:128]
                        psM = self.psum()
                        S.mm(psM[0:64, 0:64], At_, Bt[:, hc])
                        S.copy(MT[s][:, :], psM[0:64, 0:64], eng="scalar")
                        psG = self.psum()
                        S.mm(psG[0:64, 0:128], At_, ArbT[s][:, :])
                        S.tt(GT[s][:, :], psG[0:64, 0:128], RbT[:, h, :], ALU.add)
                        psY = self.psum()
                        S.mm(psY[:, 0:64], ArbT[s][:, :], U0, start=True, stop=False)
                        S.mm(psY[:, 0:64], ArkT[s][:, :], vt[:, hc], start=False, stop=False)
                        S.mm(psY[:, 0:64], GT[s][:, :], ST[h][:, :], start=False, stop=True)
                        S.copy(y[:, hc], psY[:, 0:64], eng="scalar")
                        psS = self.psum()
                        S.mm(psS[0:64, 0:64], Bt[:, hc], U0, start=True, stop=False)
                        S.mm(psS[0:64, 0:64], Kt[:, hc], vt[:, hc], start=False, stop=False)
                        S.mm(psS[0:64, 0:64], MT[s][:, :], ST[h][:, :], start=False, stop=True)
                        S.stt(ST[h][:, :], ST[h][:, :], gC[:, h:h + 1], psS[0:64, 0:64], ALU.mult, ALU.add)
                    S.red(sm[:, 0:16], v3(y))
                    S.ts(sm[:, 0:16], sm[:, 0:16], -1.0 / 64, None, op0=ALU.mult)
                    S.tt(v3(y), v3(y), sm.v(sm.h[:, 0:16].unsqueeze(2).to_broadcast([128, 16, 64])), ALU.add)
                    S.tt(tq[:, :], y[:, :], y[:, :], ALU.mult, eng="gpsimd")
                    S.red(sm[:, 16:32], v3(tq))
                    S.act(sm[:, 16:32], sm[:, 16:32], AF.Sqrt, bias=self.eps_tile(64e-5), scale=1.0 / 64)
                    S.recip(sm[:, 48:64], sm[:, 16:32])
                    S.tt(v3(y), v3(y), sm.v(sm.h[:, 48:64].unsqueeze(2).to_broadcast([128, 16, 64])), ALU.mult)
                    S.tt(y[:, :], y[:, :], gg_c[:, :], ALU.mult, eng="gpsimd")
                    S.tt(y[:, :], y[:, :], gb_c[:, :], ALU.add, eng="gpsimd")
                    S.tt(v3(tq), v3(vt), sm.v(sm.h[:, 32:48].unsqueeze(2).to_broadcast([128, 16, 64])), ALU.mult)
                    S.tt(y[:, :], y[:, :], tq[:, :], ALU.add, eng="gpsimd")
                    S.dma(dv(self.o_s[r0:r0 + 128, :]), y[:, :], q="gpsimd")
            S.barrier()

    def ret_alloc(self):
        self.rt_qT = self.scratch("rt_qT", [4, 2, 128, TOK])
        self.rt_kT = self.scratch("rt_kT", [4, 2, 128, TOK])
        self.rt_v = self.scratch("rt_v", [TOK, D])

    def ret_proj(self, x_src):
        S = self.S
        w = self.din["ret_w_in"]
        with ExitStack() as st:
            xtok = [S.tile(st, [128, 4, 1024], "rp_xtok%d" % i, dma=True) for i in range(2)]
            xT = [S.tile(st, [128, 8, 512], "rp_xT%d" % i) for i in range(2)]
            wt = [S.tile(st, [128, 8, 512], "rp_w%d" % i, dma=True) for i in range(3)]
            stg = [S.tile(st, [128, 512], "rp_stg%d" % i, dma=True) for i in range(4)]
            cs = [S.tile(st, [128, 2, 512], "rp_cs%d" % i, dma=True) for i in range(2)]
            tmp = [S.tile(st, [128, 512], "rp_tmp%d" % i) for i in range(4)]
            wi = 0
            si = 0
            ntile = self.nblk // 4
            for ti in range(ntile):
                t0 = ti * 512
                p0 = t0 % SEQ
                xk = xtok[ti % 2]
                xt_ = xT[ti % 2]
                self.load_xT(x_src, t0, xk, xt_)
                c = cs[ti % 2]
                S.dma(c[:, 0, :], dv(self.din["ret_cosT"][:, p0:p0 + 512]))
                S.dma(c[:, 1, :], dv(self.din["ret_sinT"][:, p0:p0 + 512]))
                for which, dst, scl in ((0, self.rt_qT, 1.0), (1, self.rt_kT, 256 ** -0.5)):
                    for grp in range(2):
                        wtile = wt[wi % 3]
                        wi += 1
                        self.load_w(wtile, w, which * 1024 + grp * 512, 512)
                        for j in range(2):
                            h = grp * 2 + j
                            ps1 = self.psum()
                            self.mm_feat(ps1, wtile, j * 256, 128, xt_)
                            ps2 = self.psum()
                            self.mm_feat(ps2, wtile, j * 256 + 128, 128, xt_)
                            a, b2, c3, d4 = tmp
                            S.stt(a[:, :], ps1[:, :], scl, c[:, 0, :], ALU.mult, ALU.mult)
                            S.stt(b2[:, :], ps2[:, :], scl, c[:, 1, :], ALU.mult, ALU.mult)
                            S.stt(c3[:, :], ps2[:, :], scl, c[:, 0, :], ALU.mult, ALU.mult)
                            S.stt(d4[:, :], ps1[:, :], scl, c[:, 1, :], ALU.mult, ALU.mult)
                            s1 = stg[si % 4]
                            s2 = stg[(si + 1) % 4]
                            si += 2
                            S.tt(s1[:, :], a[:, :], b2[:, :], ALU.subtract, eng="gpsimd")
                            S.tt(s2[:, :], c3[:, :], d4[:, :], ALU.add, eng="gpsimd")
                            S.dma(dv(dst[h, 0, :, t0:t0 + 512]), s1[:, :], q="gpsimd")
                            S.dma(dv(dst[h, 1, :, t0:t0 + 512]), s2[:, :], q="gpsimd")
                for c0, dst in ((2048, self.rt_v), (3072, self.g_s)):
                    for grp in range(2):
                        wtile = wt[wi % 3]
                        wi += 1
                        self.load_w(wtile, w, c0 + grp * 512, 512)
                        for b in range(4):
                            ps = self.psum()
                            self.mm_tok(ps, xt_, b, wtile, 0, 512)
                            sg = stg[si % 4]
                            si += 1
                            S.copy(sg[:, :], ps[:, :], eng=("vector" if si % 2 == 0 else "scalar"))
                            S.dma(dv(dst[t0 + b * 128:t0 + (b + 1) * 128, grp * 512:(grp + 1) * 512]), sg[:, :], q="gpsimd")
            S.barrier()

    def ret_mix(self):
        S = self.S
        nseq = max(1, self.nblk // NB)
        nb = min(NB, self.nblk)
        gs = [1.0 - 2.0 ** (-5.0 - h) for h in range(4)]
        with ExitStack() as st:
            QT = [[S.tile(st, [128, 2, 512], "rm_QT%d_%d" % (h, i), dma=True) for i in range(2)] for h in range(4)]
            KT = [[S.tile(st, [128, 2, 512], "rm_KT%d_%d" % (h, i), dma=True) for i in range(2)] for h in range(4)]
            VV = [[S.tile(st, [128, 4, 256], "rm_V%d_%d" % (h, i), dma=True) for i in range(2)] for h in range(4)]
            R = [S.tile(st, [128, 2, 256], "rm_R%d" % h) for h in range(4)]
            dpT = [S.tile(st, [128, 128], "rm_dp%d" % h, dma=True) for h in range(4)]
            xz = S.tile(st, [128, 8], "rm_xz", dma=True)
            S.dma(xz[:, :], dv(self.din["ret_xz"][:, :]))
            for h in range(4):
                S.dma(dpT[h][:, :], dv(self.din["ret_dpT"][h, :, :]))
            inT = [S.tile(st, [128, 128], "rm_inT%d" % i) for i in range(4)]
            kz = [S.tile(st, [128, 256], "rm_kz%d" % i) for i in range(4)]
            osb = [S.tile(st, [128, 256], "rm_o%d" % i, dma=True) for i in range(4)]
            cnt = 0
            for seq in range(nseq):
                for h in range(4):
                    S.memset(R[h][:, :, :], 0.0)
                for grp in range(nb // 4):
                    t0 = seq * SEQ + grp * 512
                    par = grp % 2
                    for h in range(4):
                        S.dma(QT[h][par][:, :, :], dv(self.rt_qT[h, :, :, t0:t0 + 512].rearrange("c p t -> p c t")))
                        S.dma(KT[h][par][:, :, :], dv(self.rt_kT[h, :, :, t0:t0 + 512].rearrange("c p t -> p c t")))
                        S.dma(VV[h][par][:, :, :],
                              dv(self.rt_v[t0:t0 + 512, h * 256:(h + 1) * 256].rearrange("(c p) e -> p c e", p=128)))
                    for n in range(4):
                        cs = slice(n * 128, (n + 1) * 128)
                        for h in range(4):
                            q_, k_, v_ = QT[h][par], KT[h][par], VV[h][par]
                            it = inT[cnt % 4]
                            kzt = kz[cnt % 4]
                            ot = osb[cnt % 4]
                            cnt += 1
                            ps_in = self.psum()
                            for dc in range(2):
                                S.mm(ps_in[:, 0:128], k_[:, dc, cs], q_[:, dc, cs], start=(dc == 0), stop=(dc == 1))
                            S.tt(it[:, :], ps_in[:, 0:128], dpT[h][:, :], ALU.mult)
                            ps_o = self.psum()
                            S.mm(ps_o[:, 0:256], it[:, :], v_[:, n, :], start=True, stop=False)
                            for dc in range(2):
                                S.mm(ps_o[:, 0:256], q_[:, dc, cs], R[h][:, dc, :], start=False, stop=(dc == 1))
                            S.act(ot[:, :], ps_o[:, 0:256], AF.Copy, scale=xz[:, h:h + 1])
                            r0 = t0 + n * 128
                            S.dma(dv(self.o_s[r0:r0 + 128, h * 256:(h + 1) * 256]), ot[:, :], q="gpsimd")
                            ps_k = self.psum()
                            for dc in range(2):
                                S.tr(ps_k[:, dc * 128:(dc + 1) * 128], k_[:, dc, cs], self.ident[:, :])
                            S.act(kzt[:, :], ps_k[:, 0:256], AF.Copy, scale=xz[:, 4 + h:5 + h])
                            ps_r = self.psum()
                            for dc in range(2):
                                S.mm(ps_r[:, dc * 256:(dc + 1) * 256], kzt[:, dc * 128:(dc + 1) * 128], v_[:, n, :])
                            Rf = R[h].v(R[h].h[:, :, :].rearrange("p c e -> p (c e)"))
                            S.stt(Rf, Rf, float(gs[h] ** 128), ps_r[:, :], ALU.mult, ALU.add)
            S.barrier()

    def ret_post_alloc(self, st):
        S = self.S
        gng = S.tile(st, [128, 1024], "rpo_g", dma=True)
        S.dma(gng[:, :], dv(self.din["ret_gn_g"].rearrange("(o n) -> o n", o=1).partition_broadcast(128)))
        sq = S.tile(st, [128, 1024], "rpo_sq")
        ss = [S.tile(st, [128, 8], "rpo_ss%d" % i) for i in range(2)]
        return (gng, sq, ss)

    def ret_post(self, extra, blk, ot, gt):
        S = self.S
        gng, sq, ss = extra
        s = ss[blk % 2]
        S.tt(sq[:, :], ot[:, :], ot[:, :], ALU.mult, eng="gpsimd")
        S.red(s[:, 0:4], sq.v(sq.h[:, :].rearrange("p (a c) -> p a c", c=256)))
        S.act(s[:, 0:4], s[:, 0:4], AF.Sqrt, bias=self.eps_tile(1e-6), scale=1.0 / 256)
        S.recip(s[:, 4:8], s[:, 0:4])
        o3 = ot.v(ot.h[:, :].rearrange("p (a c) -> p a c", c=256))
        S.tt(o3, o3, s.v(s.h[:, 4:8].unsqueeze(2).to_broadcast([128, 4, 256])), ALU.mult)
        S.tt(ot[:, :], ot[:, :], gng[:, :], ALU.mult, eng="gpsimd")

    def run_layers(self, layers):
        first = True
        for L in layers:
            src = self.x_in if first else self.out
            first = False
            if L == 0:
                self.fox_alloc()
                self.fox_proj(src)
                self.fox_mix()
                self.phase_out(0, src, self.din["fox_w_out"])
            elif L == 1:
                self.dsa_alloc()
                import os
                dbg = os.environ.get("KDBG", "")
                if "noproj" not in dbg:
                    self.dsa_proj(src)
                if "nomix" not in dbg:
                    self.dsa_mix()
                if "noout" not in dbg:
                    self.phase_out(1, src, self.din["dsa_w_out"])
            elif L == 2:
                self.rwkv_alloc()
                self.rwkv_proj(src)
                self.rwkv_mix()
                self.phase_out(2, src, self.din["rwkv_w_out"])
            elif L == 3:
                self.ret_alloc()
                self.ret_proj(src)
                self.ret_mix()
                self.phase_out(3, src, self.din["ret_w_out"], post=(self.ret_post_alloc, self.ret_post))
        self.S.barrier()


WEIGHT_SHAPES = {
    'ln_g': (4, 1024), 'ln_b': (4, 1024), 'fox_w_in': (1024, 4104), 'fox_b_f': (8,), 'fox_w_out': (1024, 1024),
    'dsa_w_in': (1024, 2792), 'dsa_kv_norm_g': (128,), 'dsa_w_uk': (8, 128, 96), 'dsa_w_uv': (8, 128, 128),
    'dsa_w_out': (1024, 1024), 'rwkv_mu': (6, 1024), 'rwkv_w_in': (1024, 4096), 'rwkv_w0': (1024,),
    'rwkv_w_lora_a': (1024, 64), 'rwkv_w_lora_b': (64, 1024), 'rwkv_a0': (1024,), 'rwkv_a_lora_a': (1024, 64),
    'rwkv_a_lora_b': (64, 1024), 'rwkv_k_k': (1024,), 'rwkv_k_a': (1024,), 'rwkv_r_k': (16, 64),
    'rwkv_gn_g': (1024,), 'rwkv_gn_b': (1024,), 'rwkv_w_out': (1024, 1024), 'ret_w_in': (1024, 4096),
    'ret_gn_g': (1024,), 'ret_w_out': (1024, 1024),
}


def make_consts():
    c = {}
    c["ident"] = np.eye(128, dtype=np.float32)
    i = np.arange(128)
    c["mask_le"] = (i[:, None] <= i[None, :]).astype(np.float32)
    c["mask_lt"] = (i[:, None] < i[None, :]).astype(np.float32)
    c["mask_ge"] = (i[:, None] >= i[None, :]).astype(np.float32)
    c["mask_gt"] = (i[:, None] > i[None, :]).astype(np.float32)
    inv = 1.0 / (10000.0 ** (np.arange(0, 256, 2, dtype=np.float32) / np.float32(256)))
    ang = np.arange(4096, dtype=np.float32)[:, None] * inv[None, :].astype(np.float32)
    c["ret_cosT"] = np.ascontiguousarray(np.cos(ang).T.astype(np.float32))
    c["ret_sinT"] = np.ascontiguousarray(np.sin(ang).T.astype(np.float32))
    lg = np.log1p(-(2.0 ** (-5.0 - np.arange(4, dtype=np.float64))))
    pos = np.arange(128, dtype=np.float64)
    dp = np.zeros((4, 128, 128), np.float32)
    xz = np.zeros((128, 8), np.float32)
    for h in range(4):
        dp[h] = (np.exp(-(pos[:, None] + 1.0) * lg[h]) * (pos[:, None] <= pos[None, :])).astype(np.float32)
        xz[:, h] = np.exp((pos + 1.0) * lg[h])
        xz[:, 4 + h] = np.exp((127.0 - pos) * lg[h])
    c["ret_dpT"] = dp
    c["ret_xz"] = xz
    def rope_fm(rot):
        inv = 1.0 / (np.float32(500000.0) ** (np.arange(0, rot, 2, dtype=np.float32) / np.float32(rot)))
        ang = np.arange(4096, dtype=np.float32)[:, None] * inv[None, :].astype(np.float32)
        cs, sn = np.cos(ang).T.astype(np.float32), np.sin(ang).T.astype(np.float32)
        return np.ascontiguousarray(np.concatenate([cs, cs], 0)), np.ascontiguousarray(np.concatenate([-sn, sn], 0))
    c["dsa_ropeA_c"], c["dsa_ropeA_s"] = rope_fm(32)
    ic, isn = rope_fm(16)
    c["dsa_ropeI_c"] = np.ascontiguousarray(np.concatenate([ic, np.ones_like(ic)], 0))
    c["dsa_ropeI_s"] = np.ascontiguousarray(np.concatenate([isn, np.zeros_like(isn)], 0))
    c["negmask"] = np.where(i[None, :] <= i[:, None], 0.0, -1e30).astype(np.float32)
    return c


_CACHE = {}


def build_program(layers=(0, 1, 2, 3), nblk=TOK // 128):
    key = (tuple(layers), nblk)
    if key in _CACHE:
        return _CACHE[key]
    consts = make_consts()
    nc = bass.Bass("TRN2", target_bir_lowering=False)
    with ExitStack() as st:
        k = K(nc, st, WEIGHT_SHAPES, {n: a.shape for n, a in consts.items()}, nblk=nblk)
        k.run_layers(layers)
        print("instructions", k.S.n_ins, "waits", k.S.n_wait)
    _CACHE[key] = (nc, consts)
    return nc, consts


def kernel(**inputs):
    x = np.ascontiguousarray(np.asarray(inputs["x"], dtype=np.float32))
    nc, consts = build_program()
    base = {n: np.ascontiguousarray(np.asarray(inputs[n], dtype=np.float32)) for n in WEIGHT_SHAPES}
    base.update(consts)
    in_maps = []
    for c in range(8):
        m = dict(base)
        m["x"] = x[2 * c:2 * c + 2].reshape(TOK, D)
        in_maps.append(m)
    res = run_bass_kernel_spmd(nc, in_maps, core_ids=list(range(8)))
    out = np.stack([r["out"].reshape(NSEQ, SEQ, D) for r in res.results], axis=0).reshape(16, SEQ, D)
    return out.astype(np.float32)
```

```python
import numpy as np
from contextlib import ExitStack
import concourse.bass as bass
import concourse.mybir as mybir
from concourse.bass_utils import run_bass_kernel_spmd

F32 = mybir.dt.float32
ALU = mybir.AluOpType
AF = mybir.ActivationFunctionType
AX = mybir.AxisListType


class Res:
    __slots__ = ("w", "r", "dsem", "excl")

    def __init__(self):
        self.w = None
        self.r = {}
        self.dsem = None
        self.excl = False


class V:
    __slots__ = ("ap", "res")

    def __init__(self, ap, res):
        self.ap = ap
        self.res = res


class Tile:
    def __init__(self, handle, res=None):
        self.h = handle
        self.res = res if res is not None else Res()

    def __getitem__(self, key):
        return V(self.h[key], self.res)

    def v(self, ap):
        return V(ap, self.res)


class Sched:
    ENG = ("tensor", "vector", "scalar", "gpsimd", "sync")

    def __init__(self, nc, stack, n_dma_sems=92):
        self.nc = nc
        self.stack = stack
        self.e = {"tensor": nc.tensor, "vector": nc.vector, "scalar": nc.scalar,
                  "gpsimd": nc.gpsimd, "sync": nc.sync}
        self.sem = {}
        self.tot = {}
        for e in self.ENG:
            self.sem[e] = stack.enter_context(nc.semaphore("s_" + e))
            self.tot[e] = 0
        self.bar = stack.enter_context(nc.semaphore("s_bar"))
        self.bar_n = 0
        self.free_dma = []
        for i in range(n_dma_sems):
            k = "d%d" % i
            self.sem[k] = stack.enter_context(nc.semaphore(k))
            self.tot[k] = 0
            self.free_dma.append(k)
        self.known = {e: {} for e in self.ENG}
        self.n_ins = 0
        self.n_wait = 0

    def tile(self, stack, shape, name, dtype=F32, dma=False):
        self.n_tiles = getattr(self, "n_tiles", 0) + 1
        h = stack.enter_context(self.nc.sbuf_tensor("%s_%d" % (name, self.n_tiles), list(shape), dtype))
        t = Tile(h)
        if dma:
            t.res.dsem = {}
            stack.callback(self._release_dsems, t.res)
        return t

    def _release_dsems(self, res):
        for k in res.dsem.values():
            self.free_dma.append(k)
        res.dsem = {}

    def _dsem(self, res, q):
        kind = "sw" if q == "gpsimd" else "hw"
        k = res.dsem.get(kind)
        if k is None:
            k = self.free_dma.pop()
            res.dsem[kind] = k
        return k

    def psum(self, stack, name, shape=(128, 512), dtype=F32):
        h = stack.enter_context(self.nc.psum_tensor(name, list(shape), dtype))
        t = Tile(h)
        t.res.excl = True
        return t

    def _waits(self, eng, reads, writes):
        deps = {}
        for v in reads:
            if v is None or v.res is None:
                continue
            w = v.res.w
            if w is not None:
                if deps.get(w[0], 0) < w[1]:
                    deps[w[0]] = w[1]
        for v in writes:
            if v is None or v.res is None:
                continue
            w = v.res.w
            if w is not None:
                if deps.get(w[0], 0) < w[1]:
                    deps[w[0]] = w[1]
            for k, val in v.res.r.items():
                if deps.get(k, 0) < val:
                    deps[k] = val
        kn = self.known[eng]
        for k, val in deps.items():
            if k[0] == "d":
                val = self.tot[k]
            elif eng == "tensor" and k == "tensor":
                continue
            if kn.get(k, 0) >= val:
                continue
            kn[k] = val
            self.e[eng].wait_ge(self.sem[k], val)
            self.n_wait += 1

    def emit(self, eng, reads, writes, fn):
        xr = [v for v in reads if v is not None and v.res is not None and v.res.excl]
        if xr:
            writes = list(writes) + xr
        self._waits(eng, reads, writes)
        ins = fn(self.e[eng])
        self.tot[eng] += 1
        n = self.tot[eng]
        ins.then_inc(self.sem[eng], 1)
        self.n_ins += 1
        for v in reads:
            if v is not None and v.res is not None:
                if v.res.r.get(eng, 0) < n:
                    v.res.r[eng] = n
        for v in writes:
            if v is not None and v.res is not None:
                v.res.w = (eng, n)
                v.res.r = {}

    def dma(self, out, in_, q="sync", **kw):
        self._waits(q, [in_], [out])
        k = None
        if out.res is not None and out.res.dsem is not None:
            k = self._dsem(out.res, q)
        elif in_.res is not None and in_.res.dsem is not None:
            k = self._dsem(in_.res, q)
        assert k is not None, "dma needs a resource with a dma semaphore"
        ins = self.e[q].dma_start(out=out.ap, in_=in_.ap, **kw)
        self.tot[k] += 16
        n = self.tot[k]
        ins.then_inc(self.sem[k], 16)
        self.n_ins += 1
        if in_.res is not None:
            if in_.res.r.get(k, 0) < n:
                in_.res.r[k] = n
        if out.res is not None:
            out.res.w = (k, n)
            out.res.r = {}

    def barrier(self):
        sy = self.e["sync"]
        kn = self.known["sync"]
        for k, val in self.tot.items():
            if val > 0 and kn.get(k, 0) < val:
                sy.wait_ge(self.sem[k], val)
                kn[k] = val
        self.bar_n += 1
        sy.sem_inc(self.bar, 1)
        for e in self.ENG:
            if e != "sync":
                self.e[e].wait_ge(self.bar, self.bar_n)
            for k, val in self.tot.items():
                self.known[e][k] = val

    def mm(self, out, lhsT, rhs, start=True, stop=True):
        self.emit("tensor", [lhsT, rhs], [out],
                  lambda e: e.matmul(out.ap, lhsT.ap, rhs.ap, start=start, stop=stop))

    def tr(self, out, in_, ident):
        self.emit("tensor", [in_, ident], [out],
                  lambda e: e.transpose(out.ap, in_.ap, ident.ap))

    def act(self, out, in_, func, bias=None, scale=1.0, eng="scalar"):
        rd = [in_]
        kw = {}
        if isinstance(bias, V):
            rd.append(bias)
            kw["bias"] = bias.ap
        elif bias is not None:
            kw["bias"] = bias
        if isinstance(scale, V):
            rd.append(scale)
            kw["scale"] = scale.ap
        else:
            kw["scale"] = scale
        self.emit(eng, rd, [out], lambda e: e.activation(out.ap, in_.ap, func, **kw))

    def tt(self, out, in0, in1, op, eng="vector"):
        self.emit(eng, [in0, in1], [out],
                  lambda e: e.tensor_tensor(out.ap, in0.ap, in1.ap, op))

    def ts(self, out, in0, s1, s2=None, op0=ALU.mult, op1=None, eng="vector"):
        rd = [in0]
        a1 = s1
        a2 = s2
        if isinstance(s1, V):
            rd.append(s1)
            a1 = s1.ap
        if isinstance(s2, V):
            rd.append(s2)
            a2 = s2.ap
        if op1 is None:
            self.emit(eng, rd, [out], lambda e: e.tensor_scalar(out.ap, in0.ap, a1, None, op0))
        else:
            self.emit(eng, rd, [out], lambda e: e.tensor_scalar(out.ap, in0.ap, a1, a2, op0, op1))

    def stt(self, out, in0, scalar, in1, op0, op1, eng="vector"):
        rd = [in0, in1]
        a = scalar
        if isinstance(scalar, V):
            rd.append(scalar)
            a = scalar.ap
        self.emit(eng, rd, [out],
                  lambda e: e.scalar_tensor_tensor(out.ap, in0.ap, a, in1.ap, op0, op1))

    def copy(self, out, in_, eng="vector"):
        if eng == "scalar":
            self.emit(eng, [in_], [out], lambda e: e.copy(out.ap, in_.ap))
        else:
            self.emit(eng, [in_], [out], lambda e: e.tensor_copy(out.ap, in_.ap))

    def red(self, out, in_, op=ALU.add, axis=AX.X, eng="vector"):
        self.emit(eng, [in_], [out], lambda e: e.tensor_reduce(out.ap, in_.ap, axis, op))

    def memset(self, out, val, eng="vector"):
        self.emit(eng, [], [out], lambda e: e.memset(out.ap, val))

    def recip(self, out, in_):
        self.emit("vector", [in_], [out], lambda e: e.reciprocal(out.ap, in_.ap))

    def vmax(self, out, in_):
        self.emit("vector", [in_], [out], lambda e: e.max(out.ap, in_.ap))

    def match_replace(self, out, in_to_replace, in_values, imm):
        self.emit("vector", [in_to_replace, in_values], [out],
                  lambda e: e.match_replace(out.ap, in_to_replace.ap, in_values.ap, imm))

D = 1024
SEQ = 4096
NSEQ = 2
TOK = NSEQ * SEQ
NB = SEQ // 128
LN_EPS = 1e-5
DN_ALPHA = (2 * 4) ** 0.25


def dv(ap):
    return V(ap, None)


class K:
    def __init__(self, nc, st, weights_meta, consts_meta, nblk=TOK // 128):
        self.nc = nc
        self.st = st
        self.S = Sched(nc, st)
        self.nblk = nblk
        self.din = {}
        for name, shape in list(weights_meta.items()) + list(consts_meta.items()):
            self.din[name] = nc.dram_tensor(name, list(shape), F32, kind="ExternalInput").ap()
        self.x_in = nc.dram_tensor("x", [TOK, D], F32, kind="ExternalInput").ap()
        self.out = nc.dram_tensor("out", [TOK, D], F32, kind="ExternalOutput").ap()
        self.o_s = nc.dram_tensor("o_s", [TOK, D], F32).ap()
        self.g_s = nc.dram_tensor("g_s", [TOK, D], F32).ap()
        S = self.S
        self.ps = [S.psum(st, "psb%d" % i) for i in range(8)]
        self.ps_i = 0
        self.pa_i = 0
        self.ident = S.tile(st, [128, 128], "ident", dma=True)
        S.dma(self.ident[:, :], dv(self.din["ident"][:, :]))
        self._eps = {}
        for val in (LN_EPS, 1.0, 1e-6, 64e-5, 1e-12, 0.0):
            t = S.tile(st, [128, 1], "eps")
            S.memset(t[:, :], float(val))
            self._eps[val] = t

    def psum(self):
        p = self.ps[2 + self.ps_i % 6]
        self.ps_i += 1
        return p

    def psum_acc(self):
        p = self.ps[self.pa_i % 2]
        self.pa_i += 1
        return p

    def stop(self, n):
        import os
        if int(os.environ.get("KSTOP", "99")) == n:
            self.S.barrier()
            return True
        return False

    def scratch(self, name, shape):
        return self.nc.dram_tensor(name, list(shape), F32).ap()

    def load_xT(self, src, t0, xtok, xT, nb=4):
        S = self.S
        S.dma(xtok[:, 0:nb, :], dv(src[t0:t0 + nb * 128, :].rearrange("(b p) d -> p b d", p=128)))
        for kc in range(8):
            ps = self.psum()
            for b in range(nb):
                S.tr(ps[:, b * 128:(b + 1) * 128], xtok[:, b, kc * 128:(kc + 1) * 128], self.ident[:, :])
            S.copy(xT[:, kc, 0:nb * 128], ps[:, 0:nb * 128], eng=("vector" if kc % 2 == 0 else "scalar"))

    def load_w(self, wt, w_ap, c0, n, q="sync"):
        self.S.dma(wt[:, :, 0:n], dv(w_ap.rearrange("(kc p) n -> p kc n", p=128)[:, :, c0:c0 + n]), q=q)

    def mm_feat(self, ps, wt, c0, m, xT, ntok=512):
        for kc in range(8):
            self.S.mm(ps[0:m, 0:ntok], wt[:, kc, c0:c0 + m], xT[:, kc, 0:ntok], start=(kc == 0), stop=(kc == 7))

    def mm_tok(self, ps, xT, blk, wt, c0, n):
        for kc in range(8):
            self.S.mm(ps[:, 0:n], xT[:, kc, blk * 128:(blk + 1) * 128], wt[:, kc, c0:c0 + n], start=(kc == 0), stop=(kc == 7))

    def phase_out(self, layer, x_src, w_out, post=None):
        S = self.S
        with ExitStack() as st:
            wo = S.tile(st, [128, 8, 1024], "wo", dma=True)
            S.dma(wo[:, :, :], dv(w_out.rearrange("(kc p) n -> p kc n", p=128)))
            lng = S.tile(st, [128, 1024], "lng", dma=True)
            lnb = S.tile(st, [128, 1024], "lnb", dma=True)
            S.dma(lng[:, :], dv(self.din["ln_g"][layer:layer + 1, :].partition_broadcast(128)))
            S.dma(lnb[:, :], dv(self.din["ln_b"][layer:layer + 1, :].partition_broadcast(128)))
            NBUF = 2
            ot = [S.tile(st, [128, 1024], "po_o%d" % i, dma=True) for i in range(NBUF)]
            gt = [S.tile(st, [128, 1024], "po_g%d" % i, dma=True) for i in range(NBUF)]
            xt = [S.tile(st, [128, 1024], "po_x%d" % i, dma=True) for i in range(NBUF)]
            zT = [S.tile(st, [128, 8, 128], "po_zT%d" % i) for i in range(NBUF)]
            yt = [S.tile(st, [128, 1024], "po_y%d" % i, dma=True) for i in range(NBUF)]
            sq = S.tile(st, [128, 1024], "po_sq")
            stat = [S.tile(st, [128, 8], "po_st%d" % i) for i in range(NBUF)]
            extra = post[0](st) if post is not None else None
            for blk in range(self.nblk):
                i = blk % NBUF
                r0 = blk * 128
                S.dma(ot[i][:, :], dv(self.o_s[r0:r0 + 128, :]))
                S.dma(gt[i][:, :], dv(self.g_s[r0:r0 + 128, :]))
                S.dma(xt[i][:, :], dv(x_src[r0:r0 + 128, :]))
                if post is not None:
                    post[1](extra, blk, ot[i], gt[i])
                S.act(gt[i][:, :], gt[i][:, :], AF.Silu)
                S.tt(ot[i][:, :], ot[i][:, :], gt[i][:, :], ALU.mult, eng="gpsimd")
                for half in range(2):
                    ps = self.psum()
                    for j in range(4):
                        kc = half * 4 + j
                        S.tr(ps[:, j * 128:(j + 1) * 128], ot[i][:, kc * 128:(kc + 1) * 128], self.ident[:, :])
                    S.copy(zT[i][:, half * 4:(half + 1) * 4, :],
                           ps.v(ps.h[:, :].rearrange("p (j t) -> p j t", t=128)), eng=("vector" if half == 0 else "scalar"))
                y = yt[i]
                for half in range(2):
                    ps = self.psum()
                    for kc in range(8):
                        S.mm(ps[:, :], zT[i][:, kc, :], wo[:, kc, half * 512:(half + 1) * 512], start=(kc == 0), stop=(kc == 7))
                    S.stt(y[:, half * 512:(half + 1) * 512], xt[i][:, half * 512:(half + 1) * 512], DN_ALPHA, ps[:, :], ALU.mult, ALU.add)
                sv = stat[i]
                S.red(sv[:, 0:1], y[:, :])
                S.ts(sv[:, 1:2], sv[:, 0:1], -1.0 / D, None, op0=ALU.mult)
                S.ts(y[:, :], y[:, :], sv[:, 1:2], None, op0=ALU.add)
                S.tt(sq[:, :], y[:, :], y[:, :], ALU.mult, eng="gpsimd")
                S.red(sv[:, 2:3], sq[:, :])
                S.act(sv[:, 3:4], sv[:, 2:3], AF.Sqrt, bias=self.eps_tile(LN_EPS), scale=1.0 / D)
                S.recip(sv[:, 4:5], sv[:, 3:4])
                S.stt(y[:, :], y[:, :], sv[:, 4:5], lng[:, :], ALU.mult, ALU.mult)
                S.tt(y[:, :], y[:, :], lnb[:, :], ALU.add, eng="gpsimd")
                S.dma(dv(self.out[r0:r0 + 128, :]), y[:, :], q="gpsimd")
            S.barrier()

    def eps_tile(self, val):
        return self._eps[val][:, 0:1]

    def fox_alloc(self):
        self.fx_qT = self.scratch("fx_qT", [8, 128, TOK])
        self.fx_kT = self.scratch("fx_kT", [8, 128, TOK])
        self.fx_v = self.scratch("fx_v", [TOK, D])
        self.fx_c = self.scratch("fx_c", [NSEQ, 128, NB * 8])
        self.fx_cr = self.scratch("fx_cr", [NSEQ, 128, NB * 8])

    def fox_proj(self, x_src):
        S = self.S
        w = self.din["fox_w_in"]
        with ExitStack() as st:
            xtok = [S.tile(st, [128, 4, 1024], "fp_xtok%d" % i, dma=True) for i in range(2)]
            xT = [S.tile(st, [128, 8, 512], "fp_xT%d" % i) for i in range(2)]
            wt = [S.tile(st, [128, 8, 512], "fp_w%d" % i, dma=True) for i in range(3)]
            wf = S.tile(st, [128, 8, 8], "fp_wf", dma=True)
            self.load_w(wf, w, 3072, 8)
            stg = [S.tile(st, [128, 512], "fp_stg%d" % i, dma=True) for i in range(4)]
            lf = S.tile(st, [128, NB, 8], "fp_lf")
            bf = S.tile(st, [128, 8], "fp_bf", dma=True)
            S.dma(bf[:, :], dv(self.din["fox_b_f"].rearrange("(o n) -> o n", o=1).partition_broadcast(128)))
            tri = S.tile(st, [128, 128], "fp_tri", dma=True)
            S.dma(tri[:, :], dv(self.din["mask_le"][:, :]))
            ones = S.tile(st, [128, 128], "fp_ones")
            S.memset(ones[:, :], 1.0)
            cw = S.tile(st, [128, NB, 8], "fp_cw", dma=True)
            cr = S.tile(st, [128, NB, 8], "fp_cr", dma=True)
            wi = 0
            si = 0
            ntile = self.nblk // 4
            for ti in range(ntile):
                t0 = ti * 512
                seq, tl = divmod(ti, NB // 4)
                xk = xtok[ti % 2]
                xt_ = xT[ti % 2]
                self.load_xT(x_src, t0, xk, xt_)
                for which, dst in ((0, self.fx_qT), (1, self.fx_kT)):
                    for grp in range(2):
                        wtile = wt[wi % 3]
                        wi += 1
                        self.load_w(wtile, w, which * 1024 + grp * 512, 512)
                        for j in range(4):
                            h = grp * 4 + j
                            ps = self.psum()
                            self.mm_feat(ps, wtile, j * 128, 128, xt_)
                            sg = stg[si % 4]
                            si += 1
                            S.copy(sg[:, :], ps[:, :], eng=("vector" if si % 2 == 0 else "scalar"))
                            S.dma(dv(dst[h, :, t0:t0 + 512]), sg[:, :], q="gpsimd")
                for c0, dst in ((2048, self.fx_v), (3080, self.g_s)):
                    for grp in range(2):
                        wtile = wt[wi % 3]
                        wi += 1
                        self.load_w(wtile, w, c0 + grp * 512, 512)
                        for b in range(4):
                            ps = self.psum()
                            self.mm_tok(ps, xt_, b, wtile, 0, 512)
                            sg = stg[si % 4]
                            si += 1
                            S.copy(sg[:, :], ps[:, :], eng=("vector" if si % 2 == 0 else "scalar"))
                            S.dma(dv(dst[t0 + b * 128:t0 + (b + 1) * 128, grp * 512:(grp + 1) * 512]), sg[:, :], q="gpsimd")
                ps = self.psum()
                for b in range(4):
                    for kc in range(8):
                        S.mm(ps[:, b * 8:(b + 1) * 8], xt_[:, kc, b * 128:(b + 1) * 128], wf[:, kc, :], start=(kc == 0), stop=(kc == 7))
                for b in range(4):
                    S.tt(lf[:, tl * 4 + b, :], ps[:, b * 8:(b + 1) * 8], bf[:, :], ALU.add)
                if tl == NB // 4 - 1 or ti == ntile - 1:
                    lf2 = lf.v(lf.h[:, :, :].rearrange("p b h -> p (b h)"))
                    S.act(lf2, lf2, AF.Exp, scale=-1.0)
                    S.act(lf2, lf2, AF.Ln, bias=self.eps_tile(1.0), scale=1.0)
                    S.ts(lf2, lf2, -1.0, None, op0=ALU.mult)
                    psw = self.psum()
                    S.mm(psw[:, 0:NB * 8], tri[:, :], lf2)
                    pst = self.psum()
                    S.mm(pst[:, 0:NB * 8], ones[:, :], lf2)
                    S.memset(cr[:, 0, :], 0.0)
                    for b in range(1, NB):
                        S.tt(cr[:, b, :], cr[:, b - 1, :], pst[:, (b - 1) * 8:b * 8], ALU.add)
                    S.tt(cw.v(cw.h[:, :, :].rearrange("p b h -> p (b h)")), psw[:, 0:NB * 8],
                         cr.v(cr.h[:, :, :].rearrange("p b h -> p (b h)")), ALU.add)
                    S.dma(dv(self.fx_c[seq, :, :]), cw.v(cw.h[:, :, :].rearrange("p b h -> p (b h)")), q="gpsimd")
                    S.dma(dv(self.fx_cr[seq, :, :]), cr.v(cr.h[:, :, :].rearrange("p b h -> p (b h)")), q="gpsimd")
            S.barrier()

    def fox_mix(self):
        S = self.S
        nseq = max(1, self.nblk // NB)
        nb = min(NB, self.nblk)
        SCALE = 128 ** -0.5
        with ExitStack() as st:
            KT = [S.tile(st, [128, SEQ], "fm_KT%d" % i, dma=True) for i in range(2)]
            QT = [S.tile(st, [128, SEQ], "fm_QT%d" % i, dma=True) for i in range(2)]
            VA = [S.tile(st, [128, NB, 132], "fm_VA%d" % i, dma=True) for i in range(2)]
            OS = [S.tile(st, [128, NB, 128], "fm_OS%d" % i, dma=True) for i in range(2)]
            cw = S.tile(st, [128, NB, 8], "fm_cw", dma=True)
            cr = S.tile(st, [128, NB, 8], "fm_cr", dma=True)
            bias = [S.tile(st, [128, NB, NB], "fm_bias%d" % i) for i in range(2)]
            PT = [S.tile(st, [128, 128], "fm_PT%d" % i) for i in range(12)]
            rc = [S.tile(st, [128, 1], "fm_rc%d" % i) for i in range(2)]
            mle = S.tile(st, [128, 128], "fm_mle", dma=True)
            S.dma(mle[:, :], dv(self.din["mask_le"][:, :]))
            for i in range(2):
                S.memset(VA[i][:, :, 128:129], 1.0)
            it = 0
            pti = 0
            for seq in range(nseq):
                S.dma(cw.v(cw.h[:, :, :].rearrange("p b h -> p (b h)")), dv(self.fx_c[seq, :, :]))
                S.dma(cr.v(cr.h[:, :, :].rearrange("p b h -> p (b h)")), dv(self.fx_cr[seq, :, :]))
                for h in range(8):
                    i = it % 2
                    it += 1
                    c0 = seq * SEQ
                    S.dma(KT[i][:, 0:nb * 128], dv(self.fx_kT[h, :, c0:c0 + nb * 128]))
                    S.dma(QT[i][:, 0:nb * 128], dv(self.fx_qT[h, :, c0:c0 + nb * 128]))
                    S.dma(VA[i][:, 0:nb, 0:128],
                          dv(self.fx_v[c0:c0 + nb * 128, h * 128:(h + 1) * 128].rearrange("(b p) d -> p b d", p=128)))
                    bt = bias[i]
                    for qb in range(nb):
                        S.ts(bt[:, qb, 0:qb + 1], cw[:, 0:qb + 1, h], -1.0, cr[:, qb, h:h + 1], op0=ALU.mult, op1=ALU.add)
                    for qb in range(nb):
                        pso = self.psum_acc()
                        nk = qb + 1
                        for g0 in range(0, nk, 4):
                            g1 = min(nk, g0 + 4)
                            pss = self.psum()
                            pts = [PT[(pti + j) % 12] for j in range(4)]
                            pti += 4
                            for kb in range(g0, g1):
                                S.mm(pss[:, (kb - g0) * 128:(kb - g0 + 1) * 128], KT[i][:, kb * 128:(kb + 1) * 128],
                                     QT[i][:, qb * 128:(qb + 1) * 128])
                            for kb in range(g0, g1):
                                S.act(pts[kb - g0][:, :], pss[:, (kb - g0) * 128:(kb - g0 + 1) * 128], AF.Exp,
                                      bias=bt[:, qb, kb:kb + 1], scale=SCALE)
                            if g1 == nk:
                                S.tt(pts[qb - g0][:, :], pts[qb - g0][:, :], mle[:, :], ALU.mult, eng="gpsimd")
                            for kb in range(g0, g1):
                                S.mm(pso[:, 0:129], pts[kb - g0][:, :], VA[i][:, kb, 0:129], start=(kb == 0), stop=(kb == nk - 1))
                        r = rc[qb % 2]
                        S.recip(r[:, :], pso[:, 128:129])
                        S.ts(OS[i][:, qb, :], pso[:, 0:128], r[:, 0:1], None, op0=ALU.mult)
                    S.dma(dv(self.o_s[c0:c0 + nb * 128, h * 128:(h + 1) * 128].rearrange("(b p) d -> p b d", p=128)),
                          OS[i][:, 0:nb, :], q="gpsimd")
            S.barrier()

    def dsa_alloc(self):
        self.ds_qlT = self.scratch("ds_qlT", [8, 128, TOK])
        self.ds_qcT = self.scratch("ds_qcT", [8, 96, TOK])
        self.ds_kcT = self.scratch("ds_kcT", [96, TOK])
        self.ds_ckv = self.scratch("ds_ckv", [TOK, 128])
        self.ds_ckvT = self.scratch("ds_ckvT", [128, TOK])
        self.ds_wi = self.scratch("ds_wi", [TOK, 16])

    def dsa_proj(self, x_src):
        S = self.S
        w = self.din["dsa_w_in"]
        wr = w.rearrange("(kc p) n -> p kc n", p=128)
        with ExitStack() as st:
            xtok = [S.tile(st, [128, 4, 1024], "dp_xtok%d" % i, dma=True) for i in range(2)]
            xT = [S.tile(st, [128, 8, 512], "dp_xT%d" % i) for i in range(2)]
            wt = [S.tile(st, [128, 8, 512], "dp_w%d" % i, dma=True) for i in range(2)]
            wsm = S.tile(st, [128, 8, 744], "dp_wsm", dma=True)
            S.dma(wsm[:, :, :], dv(wr[:, :, 1024:1768]))
            wpq = S.tile(st, [128, 8, 8, 32], "dp_wpq", dma=True)
            wpi = S.tile(st, [128, 8, 8, 32], "dp_wpi", dma=True)
            wpk = S.tile(st, [128, 8, 64], "dp_wpk", dma=True)
            S.memset(wpi[:, :, :, :], 0.0)
            S.memset(wpk[:, :, :], 0.0)
            for h in range(8):
                S.dma(wpq[:, :, h, 0:16], dv(wr[:, :, h * 128 + 16:h * 128 + 32]))
                S.dma(wpq[:, :, h, 16:32], dv(wr[:, :, h * 128:h * 128 + 16]))
                S.dma(wpi[:, :, h, 0:8], dv(wr[:, :, 1184 + h * 64 + 8:1184 + h * 64 + 16]))
                S.dma(wpi[:, :, h, 8:16], dv(wr[:, :, 1184 + h * 64:1184 + h * 64 + 8]))
            S.dma(wpk[:, :, 0:16], dv(wr[:, :, 1152 + 16:1152 + 32]))
            S.dma(wpk[:, :, 16:32], dv(wr[:, :, 1152:1152 + 16]))
            S.dma(wpk[:, :, 32:40], dv(wr[:, :, 1696 + 8:1696 + 16]))
            S.dma(wpk[:, :, 40:48], dv(wr[:, :, 1696:1696 + 8]))
            wuk = S.tile(st, [128, 8, 96], "dp_wuk", dma=True)
            S.dma(wuk[:, :, :], dv(self.din["dsa_w_uk"].rearrange("h c n -> c h n")))
            wukT = S.tile(st, [96, 8, 128], "dp_wukT")
            for h in range(8):
                ps = self.psum()
                S.tr(ps[0:96, 0:128], wuk[:, h, :], self.ident[:, :])
                S.copy(wukT[:, h, :], ps[0:96, 0:128])
            if self.stop(1):
                return
            kvg = S.tile(st, [128, 128], "dp_kvg", dma=True)
            S.dma(kvg[:, :], dv(self.din["dsa_kv_norm_g"].rearrange("(o n) -> o n", o=1).partition_broadcast(128)))
            ropeA = [S.tile(st, [32, 2, 512], "dp_ropeA%d" % i, dma=True) for i in range(2)]
            ropeI = [S.tile(st, [32, 2, 512], "dp_ropeI%d" % i, dma=True) for i in range(2)]
            stg = [S.tile(st, [128, 512], "dp_stg%d" % i, dma=True) for i in range(4)]
            qn = [S.tile(st, [96, 512], "dp_qn%d" % i) for i in range(2)]
            qc = [S.tile(st, [96, 512], "dp_qc%d" % i, dma=True) for i in range(3)]
            t32 = [S.tile(st, [32, 512], "dp_t32%d" % i) for i in range(2)]
            csb = [S.tile(st, [128, 128], "dp_csb%d" % i, dma=True) for i in range(2)]
            csq = S.tile(st, [128, 128], "dp_csq")
            cst = [S.tile(st, [128, 8], "dp_cst%d" % i) for i in range(2)]
            cT = [S.tile(st, [128, 128], "dp_cT%d" % i, dma=True) for i in range(2)]
            wis = [S.tile(st, [128, 16], "dp_wi%d" % i, dma=True) for i in range(2)]
            wi_ = 0
            si = 0
            qi_ = 0
            ntile = self.nblk // 4
            CI = (64 ** -0.5) * (8 ** -0.5)
            for ti in range(ntile):
                t0 = ti * 512
                p0 = t0 % SEQ
                xk = xtok[ti % 2]
                xt_ = xT[ti % 2]
                self.load_xT(x_src, t0, xk, xt_)
                rA = ropeA[ti % 2]
                rI = ropeI[ti % 2]
                S.dma(rA[:, 0, :], dv(self.din["dsa_ropeA_c"][:, p0:p0 + 512]))
                S.dma(rA[:, 1, :], dv(self.din["dsa_ropeA_s"][:, p0:p0 + 512]))
                S.dma(rI[:, 0, :], dv(self.din["dsa_ropeI_c"][:, p0:p0 + 512]))
                S.dma(rI[:, 1, :], dv(self.din["dsa_ropeI_s"][:, p0:p0 + 512]))

                def rope32(dst, psA, psB):
                    a, b2 = t32
                    S.tt(a[:, :], psA[0:32, :], rA[:, 0, :], ALU.mult)
                    S.tt(b2[:, :], psB[0:32, :], rA[:, 1, :], ALU.mult)
                    S.tt(dst, a[:, :], b2[:, :], ALU.add, eng="gpsimd")

                if self.stop(2):
                    return
                for grp in range(2):
                    wtile = wt[wi_ % 2]
                    wi_ += 1
                    self.load_w(wtile, w, grp * 512, 512)
                    for j in range(4):
                        h = grp * 4 + j
                        ps = self.psum()
                        self.mm_feat(ps, wtile, j * 128 + 32, 96, xt_)
                        qnt = qn[h % 2]
                        S.copy(qnt[:, :], ps[0:96, :], eng="scalar")
                        ps2 = self.psum()
                        S.mm(ps2[:, :], wukT[:, h, :], qnt[:, :])
                        sg = stg[si % 4]
                        si += 1
                        S.copy(sg[:, :], ps2[:, :])
                        S.dma(dv(self.ds_qlT[h, :, t0:t0 + 512]), sg[:, :], q="gpsimd")
                        if self.stop(21):
                            return
                        qct = qc[qi_ % 3]
                        qi_ += 1
                        psA = self.psum()
                        self.mm_feat(psA, wtile, j * 128, 32, xt_)
                        psB = self.psum()
                        for kc in range(8):
                            S.mm(psB[0:32, :], wpq[:, kc, h, :], xt_[:, kc, :], start=(kc == 0), stop=(kc == 7))
                        rope32(qct[0:32, :], psA, psB)
                        S.dma(dv(self.ds_qcT[h, 64:96, t0:t0 + 512]), qct[0:32, :], q="gpsimd")
                        if self.stop(22):
                            return
                        qct = qc[qi_ % 3]
                        qi_ += 1
                        psC = self.psum()
                        self.mm_feat(psC, wsm, 160 + h * 64, 64, xt_)
                        psD = self.psum()
                        for kc in range(8):
                            S.mm(psD[0:32, :], wpi[:, kc, h, :], xt_[:, kc, :], start=(kc == 0), stop=(kc == 7))
                        if self.stop(23):
                            return
                        S.copy(qct[0:64, :], psC[0:64, :], eng="scalar")
                        if self.stop(24):
                            return
                        a, b2 = t32
                        S.tt(a[0:32, :], psC[0:32, :], rI[:, 0, :], ALU.mult)
                        S.tt(b2[0:32, :], psD[0:32, :], rI[:, 1, :], ALU.mult)
                        if self.stop(25):
                            return
                        S.tt(qct[0:32, :], a[0:32, :], b2[0:32, :], ALU.add, eng="gpsimd")
                        if self.stop(26):
                            return
                        S.dma(dv(self.ds_qcT[h, 0:64, t0:t0 + 512]), qct[0:64, :], q="gpsimd")
                if self.stop(3):
                    return
                qct = qc[qi_ % 3]
                qi_ += 1
                psA = self.psum()
                self.mm_feat(psA, wsm, 128, 32, xt_)
                psB = self.psum()
                for kc in range(8):
                    S.mm(psB[0:32, :], wpk[:, kc, 0:32], xt_[:, kc, :], start=(kc == 0), stop=(kc == 7))
                rope32(qct[0:32, :], psA, psB)
                S.dma(dv(self.ds_kcT[64:96, t0:t0 + 512]), qct[0:32, :], q="gpsimd")
                qct = qc[qi_ % 3]
                qi_ += 1
                psC = self.psum()
                self.mm_feat(psC, wsm, 672, 64, xt_)
                psD = self.psum()
                for kc in range(8):
                    S.mm(psD[0:32, :], wpk[:, kc, 32:64], xt_[:, kc, :], start=(kc == 0), stop=(kc == 7))
                S.copy(qct[0:64, :], psC[0:64, :], eng="scalar")
                a, b2 = t32
                S.tt(a[0:32, :], psC[0:32, :], rI[:, 0, :], ALU.mult)
                S.tt(b2[0:32, :], psD[0:32, :], rI[:, 1, :], ALU.mult)
                S.tt(qct[0:32, :], a[0:32, :], b2[0:32, :], ALU.add, eng="gpsimd")
                S.dma(dv(self.ds_kcT[0:64, t0:t0 + 512]), qct[0:64, :], q="gpsimd")
                if self.stop(4):
                    return
                for b in range(4):
                    r0 = t0 + b * 128
                    ps = self.psum()
                    self.mm_tok(ps, xt_, b, wsm, 0, 128)
                    c = csb[b % 2]
                    sv = cst[b % 2]
                    S.copy(c[:, :], ps[:, 0:128], eng="scalar")
                    S.tt(csq[:, :], c[:, :], c[:, :], ALU.mult, eng="gpsimd")
                    S.red(sv[:, 0:1], csq[:, :])
                    S.act(sv[:, 1:2], sv[:, 0:1], AF.Sqrt, bias=self.eps_tile(1e-6), scale=1.0 / 128)
                    S.recip(sv[:, 2:3], sv[:, 1:2])
                    S.stt(c[:, :], c[:, :], sv[:, 2:3], kvg[:, :], ALU.mult, ALU.mult)
                    S.dma(dv(self.ds_ckv[r0:r0 + 128, :]), c[:, :], q="gpsimd")
                    psT = self.psum()
                    S.tr(psT[:, 0:128], c[:, :], self.ident[:, :])
                    ct = cT[b % 2]
                    S.copy(ct[:, :], psT[:, 0:128], eng="scalar")
                    S.dma(dv(self.ds_ckvT[:, r0:r0 + 128]), ct[:, :], q="gpsimd")
                    psw = self.psum()
                    self.mm_tok(psw, xt_, b, wsm, 736, 8)
                    wv = wis[b % 2]
                    S.copy(wv[:, 8:16], psw[:, 0:8])
                    S.stt(wv[:, 0:8], wv[:, 8:16], -1.0, wv[:, 8:16], ALU.mult, ALU.max)
                    S.ts(wv[:, 0:8], wv[:, 0:8], CI, None, op0=ALU.mult)
                    S.act(wv[:, 8:16], wv[:, 8:16], AF.Sign)
                    S.dma(dv(self.ds_wi[r0:r0 + 128, :]), wv[:, :], q="gpsimd")
                if self.stop(5):
                    return
                for grp in range(2):
                    wtile = wt[wi_ % 2]
                    wi_ += 1
                    self.load_w(wtile, w, 1768 + grp * 512, 512)
                    for b in range(4):
                        ps = self.psum()
                        self.mm_tok(ps, xt_, b, wtile, 0, 512)
                        sg = stg[si % 4]
                        si += 1
                        S.copy(sg[:, :], ps[:, :], eng=("vector" if si % 2 == 0 else "scalar"))
                        S.dma(dv(self.g_s[t0 + b * 128:t0 + (b + 1) * 128, grp * 512:(grp + 1) * 512]), sg[:, :], q="gpsimd")
            S.barrier()

    def dsa_mix(self):
        S = self.S
        nseq = max(1, self.nblk // NB)
        nb = min(NB, self.nblk)
        SCALE = 128 ** -0.5
        with ExitStack() as st:
            kc_ = S.tile(st, [96, SEQ], "dm_kc", dma=True)
            ckvT = S.tile(st, [128, SEQ], "dm_ckvT", dma=True)
            ckvA = S.tile(st, [128, NB, 132], "dm_ckvA", dma=True)
            S.memset(ckvA[:, :, 128:129], 1.0)
            wuv = S.tile(st, [128, 8, 128], "dm_wuv", dma=True)
            S.dma(wuv[:, :, :], dv(self.din["dsa_w_uv"].rearrange("h c d -> c h d")))
            negm = S.tile(st, [128, 128], "dm_negm", dma=True)
            S.dma(negm[:, :], dv(self.din["negmask"][:, :]))
            qc = [S.tile(st, [96, 8, 128], "dm_qc%d" % i, dma=True) for i in range(2)]
            ql = [S.tile(st, [128, 8, 128], "dm_ql%d" % i, dma=True) for i in range(2)]
            wi = [S.tile(st, [128, 16], "dm_wi%d" % i, dma=True) for i in range(2)]
            I_ = [S.tile(st, [128, SEQ], "dm_I%d" % i) for i in range(2)]
            Wk = S.tile(st, [128, SEQ], "dm_Wk")
            MT = [S.tile(st, [128, NB, 128], "dm_MT%d" % i) for i in range(2)]
            m8 = [S.tile(st, [128, 8], "dm_m8%d" % i) for i in range(2)]
            tmp = [S.tile(st, [128, 512], "dm_tmp%d" % i) for i in range(3)]
            PT = [S.tile(st, [128, 512], "dm_PT%d" % i) for i in range(3)]
            rc = [S.tile(st, [128, 1], "dm_rc%d" % i) for i in range(2)]
            olat = [S.tile(st, [128, 128], "dm_ol%d" % i) for i in range(2)]
            olT = [S.tile(st, [128, 128], "dm_olT%d" % i) for i in range(2)]
            osb = [S.tile(st, [128, 1024], "dm_osb%d" % i, dma=True) for i in range(2)]
            cnt = {"tmi": 0, "pti": 0, "hi": 0}

            def prep(seq, qb):
                i = qb % 2
                c0 = seq * SEQ
                r0 = c0 + qb * 128
                L = (qb + 1) * 128
                S.dma(qc[i][:, :, :], dv(self.ds_qcT[:, :, r0:r0 + 128].rearrange("h d t -> d h t")))
                S.dma(ql[i][:, :, :], dv(self.ds_qlT[:, :, r0:r0 + 128].rearrange("h d t -> d h t")))
                S.dma(wi[i][:, :], dv(self.ds_wi[r0:r0 + 128, :]))
                It = I_[i]
                for k0 in range(0, L, 512):
                    n = min(512, L - k0)
                    for h in range(8):
                        ps = self.psum()
                        S.mm(ps[:, 0:n], qc[i][0:64, h, :], kc_[0:64, k0:k0 + n])
                        t = tmp[cnt["tmi"] % 3]
                        cnt["tmi"] += 1
                        S.act(t[:, 0:n], ps[:, 0:n], AF.Relu, scale=wi[i][:, h:h + 1])
                        if h == 0:
                            S.ts(It[:, k0:k0 + n], t[:, 0:n], wi[i][:, 8:9], None, op0=ALU.mult)
                        else:
                            S.stt(It[:, k0:k0 + n], t[:, 0:n], wi[i][:, 8 + h:9 + h], It[:, k0:k0 + n], ALU.mult, ALU.add)
                S.tt(It[:, qb * 128:L], It[:, qb * 128:L], negm[:, :], ALU.add, eng="gpsimd")

            def topk_rounds(qb, ra, rb):
                if qb < 2:
                    return
                It = I_[qb % 2]
                L = (qb + 1) * 128
                for r in range(ra, rb):
                    src = It if r == 0 else Wk
                    m = m8[r % 2]
                    S.vmax(m[:, :], src[:, 0:L])
                    if r < 31:
                        S.match_replace(Wk[:, 0:L], m[:, :], src[:, 0:L], -1e30)

            def finish(qb):
                i = qb % 2
                It = I_[i]
                nk = qb + 1
                L = nk * 128
                if qb >= 2:
                    S.ts(Wk[:, 0:L], It[:, 0:L], m8[1][:, 7:8], None, op0=ALU.is_ge)
                else:
                    S.ts(Wk[:, 0:L], It[:, 0:L], -1e29, None, op0=ALU.is_ge)
                mt = MT[i]
                for g0 in range(0, nk, 4):
                    g1 = min(nk, g0 + 4)
                    ps = self.psum()
                    for kb in range(g0, g1):
                        S.tr(ps[:, (kb - g0) * 128:(kb - g0 + 1) * 128], Wk[:, kb * 128:(kb + 1) * 128], self.ident[:, :])
                    S.copy(mt.v(mt.h[:, g0:g1, :].rearrange("p a t -> p (a t)")), ps[:, 0:(g1 - g0) * 128], eng="scalar")

            def attn_head(seq, qb, h):
                i = qb % 2
                nk = qb + 1
                mt = MT[i]
                ob = osb[i]
                pso = self.psum_acc()
                for g0 in range(0, nk, 4):
                    g1 = min(nk, g0 + 4)
                    n = (g1 - g0) * 128
                    pss = self.psum()
                    for kb in range(g0, g1):
                        o_ = pss[:, (kb - g0) * 128:(kb - g0 + 1) * 128]
                        S.mm(o_, ckvT[:, kb * 128:(kb + 1) * 128], ql[i][:, h, :], start=True, stop=False)
                        S.mm(o_, kc_[64:96, kb * 128:(kb + 1) * 128], qc[i][64:96, h, :], start=False, stop=True)
                    pt = PT[cnt["pti"] % 3]
                    cnt["pti"] += 1
                    S.act(pt[:, 0:n], pss[:, 0:n], AF.Exp, scale=SCALE)
                    S.tt(pt[:, 0:n], pt[:, 0:n], mt.v(mt.h[:, g0:g1, :].rearrange("p a t -> p (a t)")), ALU.mult, eng="gpsimd")
                    for kb in range(g0, g1):
                        S.mm(pso[:, 0:129], pt[:, (kb - g0) * 128:(kb - g0 + 1) * 128], ckvA[:, kb, 0:129],
                             start=(kb == 0), stop=(kb == nk - 1))
                hi = cnt["hi"]
                cnt["hi"] += 1
                r = rc[hi % 2]
                ol = olat[hi % 2]
                olt = olT[hi % 2]
                S.recip(r[:, :], pso[:, 128:129])
                S.ts(ol[:, :], pso[:, 0:128], r[:, 0:1], None, op0=ALU.mult)
                psT = self.psum()
                S.tr(psT[:, 0:128], ol[:, :], self.ident[:, :])
                S.copy(olt[:, :], psT[:, 0:128], eng="scalar")
                ps2 = self.psum()
                S.mm(ps2[:, 0:128], olt[:, :], wuv[:, h, :])
                S.copy(ob[:, h * 128:(h + 1) * 128], ps2[:, 0:128], eng="scalar")
                if h == 7:
                    r0 = seq * SEQ + qb * 128
                    S.dma(dv(self.o_s[r0:r0 + 128, :]), ob[:, :], q="gpsimd")

            for seq in range(nseq):
                c0 = seq * SEQ
                S.dma(kc_[:, 0:nb * 128], dv(self.ds_kcT[:, c0:c0 + nb * 128]))
                S.dma(ckvT[:, 0:nb * 128], dv(self.ds_ckvT[:, c0:c0 + nb * 128]))
                S.dma(ckvA[:, 0:nb, 0:128], dv(self.ds_ckv[c0:c0 + nb * 128, :].rearrange("(b p) d -> p b d", p=128)))
                prep(seq, 0)
                topk_rounds(0, 0, 32)
                finish(0)
                for qb in range(nb):
                    nxt = qb + 1 < nb
                    if nxt:
                        prep(seq, qb + 1)
                    for h in range(8):
                        attn_head(seq, qb, h)
                        if nxt:
                            topk_rounds(qb + 1, 4 * h, 4 * h + 4)
                    if nxt:
                        finish(qb + 1)
            S.barrier()

    def rwkv_alloc(self):
        self.rw_r = self.scratch("rw_r", [TOK, D])
        self.rw_k = self.scratch("rw_k", [TOK, D])
        self.rw_v = self.scratch("rw_v", [TOK, D])
        self.rw_lw = self.scratch("rw_lw", [TOK, D])
        self.rw_a = self.scratch("rw_a", [TOK, D])

    def bcast_row(self, st, name, ap1d):
        t = self.S.tile(st, [128, 1024], name, dma=True)
        self.S.dma(t[:, :], dv(ap1d.rearrange("(o n) -> o n", o=1).partition_broadcast(128)))
        return t

    def rwkv_proj(self, x_src):
        S = self.S
        w = self.din["rwkv_w_in"]
        with ExitStack() as st:
            xtok = [S.tile(st, [128, 4, 1024], "wp_xtok%d" % i, dma=True) for i in range(2)]
            xT = [S.tile(st, [128, 8, 512], "wp_xT%d" % i) for i in range(2)]
            dT = S.tile(st, [128, 8, 512], "wp_dT")
            xm = [S.tile(st, [128, 8, 512], "wp_xm%d" % i) for i in range(2)]
            halo = S.tile(st, [128, 8, 1], "wp_halo")
            wt = [S.tile(st, [128, 8, 512], "wp_w%d" % i, dma=True) for i in range(2)]
            stg = [S.tile(st, [128, 512], "wp_stg%d" % i, dma=True) for i in range(4)]
            mu48 = S.tile(st, [48, 128], "wp_mu48", dma=True)
            S.dma(mu48[:, :], dv(self.din["rwkv_mu"].rearrange("i (kc p) -> (i kc) p", p=128)))
            muT = S.tile(st, [128, 48], "wp_muT")
            ps = self.psum()
            S.tr(ps[:, 0:48], mu48[:, :], self.ident[0:48, 0:48])
            S.copy(muT[:, :], ps[:, 0:48])
            wla = S.tile(st, [128, 8, 64], "wp_wla", dma=True)
            ala = S.tile(st, [128, 8, 64], "wp_ala", dma=True)
            S.dma(wla[:, :, :], dv(self.din["rwkv_w_lora_a"].rearrange("(kc p) n -> p kc n", p=128)))
            S.dma(ala[:, :, :], dv(self.din["rwkv_a_lora_a"].rearrange("(kc p) n -> p kc n", p=128)))
            wlb = S.tile(st, [64, 1024], "wp_wlb", dma=True)
            alb = S.tile(st, [64, 1024], "wp_alb", dma=True)
            S.dma(wlb[:, :], dv(self.din["rwkv_w_lora_b"][:, :]))
            S.dma(alb[:, :], dv(self.din["rwkv_a_lora_b"][:, :]))
            w0 = self.bcast_row(st, "wp_w0", self.din["rwkv_w0"])
            a0 = self.bcast_row(st, "wp_a0", self.din["rwkv_a0"])
            hl = [S.tile(st, [64, 512], "wp_hl%d" % i) for i in range(2)]
            wi_ = 0
            si = 0
            xi_ = 0
            ntile = self.nblk // 4
            NEG = -float(np.exp(-0.5))
            for ti in range(ntile):
                t0 = ti * 512
                xk = xtok[ti % 2]
                xt_ = xT[ti % 2]
                if t0 % SEQ == 0:
                    S.memset(halo[:, :, :], 0.0)
                self.load_xT(x_src, t0, xk, xt_)
                S.tt(dT[:, :, 1:512], xt_[:, :, 0:511], xt_[:, :, 1:512], ALU.subtract)
                S.tt(dT[:, :, 0:1], halo[:, :, :], xt_[:, :, 0:1], ALU.subtract)
                S.copy(halo[:, :, :], xt_[:, :, 511:512], eng="gpsimd")

                def mix(i):
                    nonlocal xi_
                    t = xm[xi_ % 2]
                    xi_ += 1
                    for kc in range(8):
                        S.stt(t[:, kc, :], dT[:, kc, :], muT[:, i * 8 + kc:i * 8 + kc + 1], xt_[:, kc, :], ALU.mult, ALU.add)
                    return t

                for i, c0, dst in ((0, 0, self.rw_r), (2, 1024, self.rw_k), (3, 2048, self.rw_v), (5, 3072, self.g_s)):
                    xmt = mix(i)
                    for grp in range(2):
                        wtile = wt[wi_ % 2]
                        wi_ += 1
                        self.load_w(wtile, w, c0 + grp * 512, 512)
                        for b in range(4):
                            ps = self.psum()
                            self.mm_tok(ps, xmt, b, wtile, 0, 512)
                            sg = stg[si % 4]
                            si += 1
                            S.copy(sg[:, :], ps[:, :], eng=("vector" if si % 2 == 0 else "scalar"))
                            S.dma(dv(dst[t0 + b * 128:t0 + (b + 1) * 128, grp * 512:(grp + 1) * 512]), sg[:, :], q="gpsimd")
                for i, la, lb, bias_t, dst, mul in ((1, wla, wlb, w0, self.rw_lw, NEG), (4, ala, alb, a0, self.rw_a, None)):
                    xmt = mix(i)
                    ps = self.psum()
                    for kc in range(8):
                        S.mm(ps[0:64, :], la[:, kc, :], xmt[:, kc, :], start=(kc == 0), stop=(kc == 7))
                    h_ = hl[i % 2]
                    if i == 1:
                        S.act(h_[:, :], ps[0:64, :], AF.Tanh)
                    else:
                        S.copy(h_[:, :], ps[0:64, :], eng="scalar")
                    for b in range(4):
                        for half in range(2):
                            ps2 = self.psum()
                            S.mm(ps2[:, :], h_[:, b * 128:(b + 1) * 128], lb[:, half * 512:(half + 1) * 512])
                            sg = stg[si % 4]
                            si += 1
                            S.tt(sg[:, :], ps2[:, :], bias_t[:, half * 512:(half + 1) * 512], ALU.add)
                            S.act(sg[:, :], sg[:, :], AF.Sigmoid)
                            if mul is not None:
                                S.ts(sg[:, :], sg[:, :], mul, None, op0=ALU.mult, eng="gpsimd")
                            S.dma(dv(dst[t0 + b * 128:t0 + (b + 1) * 128, half * 512:(half + 1) * 512]), sg[:, :], q="gpsimd")
            S.barrier()

    def rwkv_mix(self):
        S = self.S
        nseq = max(1, self.nblk // NB)
        nb = min(NB, self.nblk)

        def v3(t):
            return t.v(t.h[:, :].rearrange("p (a c) -> p a c", c=64))

        with ExitStack() as st:
            kk_c = self.bcast_row(st, "wm_kk", self.din["rwkv_k_k"])
            ka_c = self.bcast_row(st, "wm_ka", self.din["rwkv_k_a"])
            rk_c = self.bcast_row(st, "wm_rk", self.din["rwkv_r_k"].rearrange("h n -> (h n)"))
            gg_c = self.bcast_row(st, "wm_gg", self.din["rwkv_gn_g"])
            gb_c = self.bcast_row(st, "wm_gb", self.din["rwkv_gn_b"])
            tri = S.tile(st, [128, 128], "wm_tri", dma=True)
            mlt = S.tile(st, [128, 128], "wm_mlt", dma=True)
            mgt = S.tile(st, [128, 128], "wm_mgt", dma=True)
            S.dma(tri[:, :], dv(self.din["mask_le"][:, :]))
            S.dma(mlt[:, :], dv(self.din["mask_lt"][:, :]))
            S.dma(mgt[:, :], dv(self.din["mask_gt"][:, :]))
            ones = S.tile(st, [128, 128], "wm_ones")
            S.memset(ones[:, :], 1.0)
            NIN = 2
            r_ = [S.tile(st, [128, 1024], "wm_r%d" % i, dma=True) for i in range(NIN)]
            k_ = [S.tile(st, [128, 1024], "wm_k%d" % i, dma=True) for i in range(NIN)]
            v_ = [S.tile(st, [128, 1024], "wm_v%d" % i, dma=True) for i in range(NIN)]
            lw_ = [S.tile(st, [128, 1024], "wm_lw%d" % i, dma=True) for i in range(NIN)]
            a_ = [S.tile(st, [128, 1024], "wm_a%d" % i, dma=True) for i in range(NIN)]
            kk = S.tile(st, [128, 1024], "wm_kkn")
            km = S.tile(st, [128, 1024], "wm_km")
            kka = S.tile(st, [128, 1024], "wm_kka")
            cum = S.tile(st, [128, 1024], "wm_cum")
            e1 = S.tile(st, [128, 1024], "wm_e1")
            e1x = e1
            e2 = S.tile(st, [128, 1024], "wm_e2")
            Ab = S.tile(st, [128, 1024], "wm_Ab")
            Bb = S.tile(st, [128, 1024], "wm_Bb")
            Kb = S.tile(st, [128, 1024], "wm_Kb")
            Rb = S.tile(st, [128, 1024], "wm_Rb")
            Bt = S.tile(st, [128, 1024], "wm_Bt")
            Kt = S.tile(st, [128, 1024], "wm_Kt")
            AbT = S.tile(st, [64, 16, 128], "wm_AbT")
            BbT = S.tile(st, [64, 16, 128], "wm_BbT")
            KbT = S.tile(st, [64, 16, 128], "wm_KbT")
            RbT = S.tile(st, [64, 16, 128], "wm_RbT")
            gC = S.tile(st, [64, 16], "wm_gC")
            sm = S.tile(st, [128, 64], "wm_sm")
            ST = [S.tile(st, [64, 64], "wm_ST%d" % h) for h in range(16)]
            ysb = [S.tile(st, [128, 1024], "wm_y%d" % i, dma=True) for i in range(2)]
            G = 8
            P_ = [[S.tile(st, [128, 128], "wm_P%d_%d" % (s, i)) for i in range(2)] for s in range(G)]
            PT_ = [[S.tile(st, [128, 128], "wm_PT%d_%d" % (s, i)) for i in range(2)] for s in range(G)]
            W_ = [[S.tile(st, [128, 128], "wm_W%d_%d" % (s, i)) for i in range(2)] for s in range(G)]
            ArbT = [S.tile(st, [128, 128], "wm_ArbT%d" % s) for s in range(G)]
            ArkT = [S.tile(st, [128, 128], "wm_ArkT%d" % s) for s in range(G)]
            MT = [S.tile(st, [64, 64], "wm_MT%d" % s) for s in range(G)]
            GT = [S.tile(st, [64, 128], "wm_GT%d" % s) for s in range(G)]
            hs = 0
            ci = 0
            for seq in range(nseq):
                for h in range(16):
                    S.memset(ST[h][:, :], 0.0)
                for c in range(nb):
                    i = ci % NIN
                    ci += 1
                    r0 = seq * SEQ + c * 128
                    rt, kt, vt, lwt, at = r_[i], k_[i], v_[i], lw_[i], a_[i]
                    S.dma(rt[:, :], dv(self.rw_r[r0:r0 + 128, :]))
                    S.dma(kt[:, :], dv(self.rw_k[r0:r0 + 128, :]))
                    S.dma(vt[:, :], dv(self.rw_v[r0:r0 + 128, :]))
                    S.dma(lwt[:, :], dv(self.rw_lw[r0:r0 + 128, :]))
                    S.dma(at[:, :], dv(self.rw_a[r0:r0 + 128, :]))
                    S.tt(kk[:, :], kt[:, :], kk_c[:, :], ALU.mult, eng="gpsimd")
                    S.tt(e1x[:, :], kk[:, :], kk[:, :], ALU.mult, eng="gpsimd")
                    S.red(sm[:, 0:16], v3(e1x))
                    S.act(sm[:, 0:16], sm[:, 0:16], AF.Sqrt, bias=self.eps_tile(1e-12), scale=1.0)
                    S.recip(sm[:, 16:32], sm[:, 0:16])
                    S.tt(v3(kk), v3(kk), sm.v(sm.h[:, 16:32].unsqueeze(2).to_broadcast([128, 16, 64])), ALU.mult)
                    S.stt(km[:, :], at[:, :], -1.0, ka_c[:, :], ALU.add, ALU.mult)
                    S.stt(km[:, :], km[:, :], 1.0, kt[:, :], ALU.add, ALU.mult)
                    S.tt(kka[:, :], kk[:, :], at[:, :], ALU.mult, eng="gpsimd")
                    S.tt(e1x[:, :], rt[:, :], km[:, :], ALU.mult, eng="gpsimd")
                    S.tt(e1x[:, :], e1x[:, :], rk_c[:, :], ALU.mult, eng="gpsimd")
                    S.red(sm[:, 32:48], v3(e1x))
                    for half in range(2):
                        hsl = slice(half * 512, (half + 1) * 512)
                        psc = self.psum()
                        S.mm(psc[:, :], tri[:, :], lwt[:, hsl])
                        S.copy(cum[:, hsl], psc[:, :], eng="scalar")
                        pst = self.psum()
                        S.mm(pst[:, :], ones[:, :], lwt[:, hsl])
                        S.tt(e2[:, hsl], pst[:, :], cum[:, hsl], ALU.subtract)
                    S.act(e2[:, :], e2[:, :], AF.Exp)
                    S.tt(Bt[:, :], kka[:, :], e2[:, :], ALU.mult, eng="gpsimd")
                    S.tt(Kt[:, :], km[:, :], e2[:, :], ALU.mult, eng="gpsimd")
                    S.act(e1[:, :], cum[:, :], AF.Exp)
                    S.tt(Rb[:, :], rt[:, :], e1[:, :], ALU.mult, eng="gpsimd")
                    S.act(e1[:, :], cum[:, :], AF.Exp, scale=-1.0)
                    S.tt(Bb[:, :], kka[:, :], e1[:, :], ALU.mult, eng="gpsimd")
                    S.tt(Kb[:, :], km[:, :], e1[:, :], ALU.mult, eng="gpsimd")
                    S.tt(e2[:, :], cum[:, :], lwt[:, :], ALU.subtract)
                    S.act(e2[:, :], e2[:, :], AF.Exp)
                    S.stt(Ab[:, :], kk[:, :], -1.0, e2[:, :], ALU.mult, ALU.mult)
                    psg = self.psum()
                    for h in range(16):
                        S.mm(psg[0:64, h:h + 1], lwt[:, h * 64:(h + 1) * 64], ones[:, 0:1])
                    S.act(gC[:, :], psg[0:64, 0:16], AF.Exp)
                    for src, dstT in ((Ab, AbT), (Bb, BbT), (Kb, KbT), (Rb, RbT)):
                        for g in range(4):
                            pT = self.psum()
                            for j in range(4):
                                h = g * 4 + j
                                S.tr(pT[0:64, j * 128:(j + 1) * 128], src[:, h * 64:(h + 1) * 64], self.ident[:, :])
                            S.copy(dstT.v(dstT.h[:, g * 4:(g + 1) * 4, :].rearrange("p a t -> p (a t)")), pT[0:64, :],
                                   eng=("scalar" if g % 2 == 0 else "vector"))
                    y = ysb[c % 2]
                    for hg in range(16 // G):
                        H = [(hg * G + j, j) for j in range(G)]
                        for h, s in H:
                            ps1 = self.psum()
                            S.mm(ps1[:, 0:128], AbT[:, h, :], BbT[:, h, :])
                            S.tt(P_[s][0][:, :], ps1[:, 0:128], mgt[:, :], ALU.mult)
                        for h, s in H:
                            ps2 = self.psum()
                            S.mm(ps2[:, 0:128], BbT[:, h, :], AbT[:, h, :])
                            S.mm(ps2[:, 128:256], BbT[:, h, :], RbT[:, h, :])
                            S.tt(PT_[s][0][:, :], ps2[:, 0:128], mlt[:, :], ALU.mult)
                            S.tt(ArbT[s][:, :], ps2[:, 128:256], tri[:, :], ALU.mult)
                        for h, s in H:
                            ps3 = self.psum()
                            S.mm(ps3[:, 0:128], KbT[:, h, :], AbT[:, h, :])
                            S.mm(ps3[:, 128:256], KbT[:, h, :], RbT[:, h, :])
                            S.tt(PT_[s][1][:, :], ps3[:, 0:128], mlt[:, :], ALU.mult)
                            S.tt(ArkT[s][:, :], ps3[:, 128:256], tri[:, :], ALU.mult)
                        for h, s in H:
                            hc = slice(h * 64, (h + 1) * 64)
                            ps4 = self.psum()
                            S.mm(ps4[:, 0:64], PT_[s][1][:, :], vt[:, hc])
                            S.copy(W_[s][0][:, 0:64], Ab[:, hc], eng="gpsimd")
                            S.copy(W_[s][0][:, 64:128], ps4[:, 0:64], eng="scalar")
                        for lv in range(7):
                            a, b = lv % 2, (lv + 1) % 2
                            for h, s in H:
                                psw = self.psum()
                                S.mm(psw[:, 0:128], PT_[s][a][:, :], W_[s][a][:, :])
                                S.tt(W_[s][b][:, :], W_[s][a][:, :], psw[:, 0:128], ALU.add)
                            if lv < 6:
                                for h, s in H:
                                    psq = self.psum()
                                    S.mm(psq[:, 0:128], P_[s][a][:, :], PT_[s][a][:, :])
                                    if lv < 5:
                                        S.mm(psq[:, 128:256], PT_[s][a][:, :], P_[s][a][:, :])
                                    S.copy(PT_[s][b][:, :], psq[:, 0:128], eng="scalar")
                                    if lv < 5:
                                        S.copy(P_[s][b][:, :], psq[:, 128:256], eng="scalar")
                        for h, s in H:
                            hc = slice(h * 64, (h + 1) * 64)
                            psM = self.psum()
                            S.mm(psM[0:64, 0:64], W_[s][1][:, 0:64], Bt[:, hc])
                            S.copy(MT[s][:, :], psM[0:64, 0:64], eng="scalar")
                        for h, s in H:
                            psG = self.psum()
                            S.mm(psG[0:64, 0:128], W_[s][1][:, 0:64], ArbT[s][:, :])
                            S.tt(GT[s][:, :], psG[0:64, 0:128], RbT[:, h, :], ALU.add)
                        for h, s in H:
                            hc = slice(h * 64, (h + 1) * 64)
                            psY = self.psum()
                            S.mm(psY[:, 0:64], ArbT[s][:, :], W_[s][1][:, 64:128], start=True, stop=False)
                            S.mm(psY[:, 0:64], ArkT[s][:, :], vt[:, hc], start=False, stop=False)
                            S.mm(psY[:, 0:64], GT[s][:, :], ST[h][:, :], start=False, stop=True)
                            S.copy(y[:, hc], psY[:, 0:64], eng="scalar")
                        for h, s in H:
                            hc = slice(h * 64, (h + 1) * 64)
                            psS = self.psum()
                            S.mm(psS[0:64, 0:64], Bt[:, hc], W_[s][1][:, 64:128], start=True, stop=False)
                            S.mm(psS[0:64, 0:64], Kt[:, hc], vt[:, hc], start=False, stop=False)
                            S.mm(psS[0:64, 0:64], MT[s][:, :], ST[h][:, :], start=False, stop=True)
                            S.stt(ST[h][:, :], ST[h][:, :], gC[:, h:h + 1], psS[0:64, 0:64], ALU.mult, ALU.add)
                    S.red(sm[:, 0:16], v3(y))
                    S.ts(sm[:, 0:16], sm[:, 0:16], -1.0 / 64, None, op0=ALU.mult)
                    S.tt(v3(y), v3(y), sm.v(sm.h[:, 0:16].unsqueeze(2).to_broadcast([128, 16, 64])), ALU.add)
                    S.tt(e1x[:, :], y[:, :], y[:, :], ALU.mult, eng="gpsimd")
                    S.red(sm[:, 16:32], v3(e1x))
                    S.act(sm[:, 16:32], sm[:, 16:32], AF.Sqrt, bias=self.eps_tile(64e-5), scale=1.0 / 64)
                    S.recip(sm[:, 48:64], sm[:, 16:32])
                    S.tt(v3(y), v3(y), sm.v(sm.h[:, 48:64].unsqueeze(2).to_broadcast([128, 16, 64])), ALU.mult)
                    S.tt(y[:, :], y[:, :], gg_c[:, :], ALU.mult, eng="gpsimd")
                    S.tt(y[:, :], y[:, :], gb_c[:, :], ALU.add, eng="gpsimd")
                    S.tt(v3(e1x), v3(vt), sm.v(sm.h[:, 32:48].unsqueeze(2).to_broadcast([128, 16, 64])), ALU.mult)
                    S.tt(y[:, :], y[:, :], e1x[:, :], ALU.add, eng="gpsimd")
                    S.dma(dv(self.o_s[r0:r0 + 128, :]), y[:, :], q="gpsimd")
            S.barrier()

    def ret_alloc(self):
        self.rt_qT = self.scratch("rt_qT", [4, 2, 128, TOK])
        self.rt_kT = self.scratch("rt_kT", [4, 2, 128, TOK])
        self.rt_v = self.scratch("rt_v", [TOK, D])

    def ret_proj(self, x_src):
        S = self.S
        w = self.din["ret_w_in"]
        with ExitStack() as st:
            xtok = [S.tile(st, [128, 4, 1024], "rp_xtok%d" % i, dma=True) for i in range(2)]
            xT = [S.tile(st, [128, 8, 512], "rp_xT%d" % i) for i in range(2)]
            wt = [S.tile(st, [128, 8, 512], "rp_w%d" % i, dma=True) for i in range(3)]
            stg = [S.tile(st, [128, 512], "rp_stg%d" % i, dma=True) for i in range(4)]
            cs = [S.tile(st, [128, 2, 512], "rp_cs%d" % i, dma=True) for i in range(2)]
            tmp = [S.tile(st, [128, 512], "rp_tmp%d" % i) for i in range(4)]
            wi = 0
            si = 0
            ntile = self.nblk // 4
            for ti in range(ntile):
                t0 = ti * 512
                p0 = t0 % SEQ
                xk = xtok[ti % 2]
                xt_ = xT[ti % 2]
                self.load_xT(x_src, t0, xk, xt_)
                c = cs[ti % 2]
                S.dma(c[:, 0, :], dv(self.din["ret_cosT"][:, p0:p0 + 512]))
                S.dma(c[:, 1, :], dv(self.din["ret_sinT"][:, p0:p0 + 512]))
                for which, dst, scl in ((0, self.rt_qT, 1.0), (1, self.rt_kT, 256 ** -0.5)):
                    for grp in range(2):
                        wtile = wt[wi % 3]
                        wi += 1
                        self.load_w(wtile, w, which * 1024 + grp * 512, 512)
                        for j in range(2):
                            h = grp * 2 + j
                            ps1 = self.psum()
                            self.mm_feat(ps1, wtile, j * 256, 128, xt_)
                            ps2 = self.psum()
                            self.mm_feat(ps2, wtile, j * 256 + 128, 128, xt_)
                            a, b2, c3, d4 = tmp
                            S.stt(a[:, :], ps1[:, :], scl, c[:, 0, :], ALU.mult, ALU.mult)
                            S.stt(b2[:, :], ps2[:, :], scl, c[:, 1, :], ALU.mult, ALU.mult)
                            S.stt(c3[:, :], ps2[:, :], scl, c[:, 0, :], ALU.mult, ALU.mult)
                            S.stt(d4[:, :], ps1[:, :], scl, c[:, 1, :], ALU.mult, ALU.mult)
                            s1 = stg[si % 4]
                            s2 = stg[(si + 1) % 4]
                            si += 2
                            S.tt(s1[:, :], a[:, :], b2[:, :], ALU.subtract, eng="gpsimd")
                            S.tt(s2[:, :], c3[:, :], d4[:, :], ALU.add, eng="gpsimd")
                            S.dma(dv(dst[h, 0, :, t0:t0 + 512]), s1[:, :], q="gpsimd")
                            S.dma(dv(dst[h, 1, :, t0:t0 + 512]), s2[:, :], q="gpsimd")
                for c0, dst in ((2048, self.rt_v), (3072, self.g_s)):
                    for grp in range(2):
                        wtile = wt[wi % 3]
                        wi += 1
                        self.load_w(wtile, w, c0 + grp * 512, 512)
                        for b in range(4):
                            ps = self.psum()
                            self.mm_tok(ps, xt_, b, wtile, 0, 512)
                            sg = stg[si % 4]
                            si += 1
                            S.copy(sg[:, :], ps[:, :], eng=("vector" if si % 2 == 0 else "scalar"))
                            S.dma(dv(dst[t0 + b * 128:t0 + (b + 1) * 128, grp * 512:(grp + 1) * 512]), sg[:, :], q="gpsimd")
            S.barrier()

    def ret_mix(self):
        S = self.S
        nseq = max(1, self.nblk // NB)
        nb = min(NB, self.nblk)
        gs = [1.0 - 2.0 ** (-5.0 - h) for h in range(4)]
        with ExitStack() as st:
            QT = [[S.tile(st, [128, 2, 512], "rm_QT%d_%d" % (h, i), dma=True) for i in range(2)] for h in range(4)]
            KT = [[S.tile(st, [128, 2, 512], "rm_KT%d_%d" % (h, i), dma=True) for i in range(2)] for h in range(4)]
            VV = [[S.tile(st, [128, 4, 256], "rm_V%d_%d" % (h, i), dma=True) for i in range(2)] for h in range(4)]
            R = [S.tile(st, [128, 2, 256], "rm_R%d" % h) for h in range(4)]
            dpT = [S.tile(st, [128, 128], "rm_dp%d" % h, dma=True) for h in range(4)]
            xz = S.tile(st, [128, 8], "rm_xz", dma=True)
            S.dma(xz[:, :], dv(self.din["ret_xz"][:, :]))
            for h in range(4):
                S.dma(dpT[h][:, :], dv(self.din["ret_dpT"][h, :, :]))
            inT = [S.tile(st, [128, 128], "rm_inT%d" % i) for i in range(4)]
            kz = [S.tile(st, [128, 256], "rm_kz%d" % i) for i in range(4)]
            osb = [S.tile(st, [128, 256], "rm_o%d" % i, dma=True) for i in range(4)]
            cnt = 0
            for seq in range(nseq):
                for h in range(4):
                    S.memset(R[h][:, :, :], 0.0)
                for grp in range(nb // 4):
                    t0 = seq * SEQ + grp * 512
                    par = grp % 2
                    for h in range(4):
                        S.dma(QT[h][par][:, :, :], dv(self.rt_qT[h, :, :, t0:t0 + 512].rearrange("c p t -> p c t")))
                        S.dma(KT[h][par][:, :, :], dv(self.rt_kT[h, :, :, t0:t0 + 512].rearrange("c p t -> p c t")))
                        S.dma(VV[h][par][:, :, :],
                              dv(self.rt_v[t0:t0 + 512, h * 256:(h + 1) * 256].rearrange("(c p) e -> p c e", p=128)))
                    for n in range(4):
                        cs = slice(n * 128, (n + 1) * 128)
                        for h in range(4):
                            q_, k_, v_ = QT[h][par], KT[h][par], VV[h][par]
                            it = inT[cnt % 4]
                            kzt = kz[cnt % 4]
                            ot = osb[cnt % 4]
                            cnt += 1
                            ps_in = self.psum()
                            for dc in range(2):
                                S.mm(ps_in[:, 0:128], k_[:, dc, cs], q_[:, dc, cs], start=(dc == 0), stop=(dc == 1))
                            S.tt(it[:, :], ps_in[:, 0:128], dpT[h][:, :], ALU.mult)
                            ps_o = self.psum()
                            S.mm(ps_o[:, 0:256], it[:, :], v_[:, n, :], start=True, stop=False)
                            for dc in range(2):
                                S.mm(ps_o[:, 0:256], q_[:, dc, cs], R[h][:, dc, :], start=False, stop=(dc == 1))
                            S.act(ot[:, :], ps_o[:, 0:256], AF.Copy, scale=xz[:, h:h + 1])
                            r0 = t0 + n * 128
                            S.dma(dv(self.o_s[r0:r0 + 128, h * 256:(h + 1) * 256]), ot[:, :], q="gpsimd")
                            ps_k = self.psum()
                            for dc in range(2):
                                S.tr(ps_k[:, dc * 128:(dc + 1) * 128], k_[:, dc, cs], self.ident[:, :])
                            S.act(kzt[:, :], ps_k[:, 0:256], AF.Copy, scale=xz[:, 4 + h:5 + h])
                            ps_r = self.psum()
                            for dc in range(2):
                                S.mm(ps_r[:, dc * 256:(dc + 1) * 256], kzt[:, dc * 128:(dc + 1) * 128], v_[:, n, :])
                            Rf = R[h].v(R[h].h[:, :, :].rearrange("p c e -> p (c e)"))
                            S.stt(Rf, Rf, float(gs[h] ** 128), ps_r[:, :], ALU.mult, ALU.add)
            S.barrier()

    def ret_post_alloc(self, st):
        S = self.S
        gng = S.tile(st, [128, 1024], "rpo_g", dma=True)
        S.dma(gng[:, :], dv(self.din["ret_gn_g"].rearrange("(o n) -> o n", o=1).partition_broadcast(128)))
        sq = S.tile(st, [128, 1024], "rpo_sq")
        ss = [S.tile(st, [128, 8], "rpo_ss%d" % i) for i in range(2)]
        return (gng, sq, ss)

    def ret_post(self, extra, blk, ot, gt):
        S = self.S
        gng, sq, ss = extra
        s = ss[blk % 2]
        S.tt(sq[:, :], ot[:, :], ot[:, :], ALU.mult, eng="gpsimd")
        S.red(s[:, 0:4], sq.v(sq.h[:, :].rearrange("p (a c) -> p a c", c=256)))
        S.act(s[:, 0:4], s[:, 0:4], AF.Sqrt, bias=self.eps_tile(1e-6), scale=1.0 / 256)
        S.recip(s[:, 4:8], s[:, 0:4])
        o3 = ot.v(ot.h[:, :].rearrange("p (a c) -> p a c", c=256))
        S.tt(o3, o3, s.v(s.h[:, 4:8].unsqueeze(2).to_broadcast([128, 4, 256])), ALU.mult)
        S.tt(ot[:, :], ot[:, :], gng[:, :], ALU.mult, eng="gpsimd")

    def run_layers(self, layers):
        first = True
        for L in layers:
            src = self.x_in if first else self.out
            first = False
            if L == 0:
                self.fox_alloc()
                self.fox_proj(src)
                self.fox_mix()
                self.phase_out(0, src, self.din["fox_w_out"])
            elif L == 1:
                self.dsa_alloc()
                import os
                dbg = os.environ.get("KDBG", "")
                if "noproj" not in dbg:
                    self.dsa_proj(src)
                if "nomix" not in dbg:
                    self.dsa_mix()
                if "noout" not in dbg:
                    self.phase_out(1, src, self.din["dsa_w_out"])
            elif L == 2:
                self.rwkv_alloc()
                self.rwkv_proj(src)
                self.rwkv_mix()
                self.phase_out(2, src, self.din["rwkv_w_out"])
            elif L == 3:
                self.ret_alloc()
                self.ret_proj(src)
                self.ret_mix()
                self.phase_out(3, src, self.din["ret_w_out"], post=(self.ret_post_alloc, self.ret_post))
        self.S.barrier()


WEIGHT_SHAPES = {
    'ln_g': (4, 1024), 'ln_b': (4, 1024), 'fox_w_in': (1024, 4104), 'fox_b_f': (8,), 'fox_w_out': (1024, 1024),
    'dsa_w_in': (1024, 2792), 'dsa_kv_norm_g': (128,), 'dsa_w_uk': (8, 128, 96), 'dsa_w_uv': (8, 128, 128),
    'dsa_w_out': (1024, 1024), 'rwkv_mu': (6, 1024), 'rwkv_w_in': (1024, 4096), 'rwkv_w0': (1024,),
    'rwkv_w_lora_a': (1024, 64), 'rwkv_w_lora_b': (64, 1024), 'rwkv_a0': (1024,), 'rwkv_a_lora_a': (1024, 64),
    'rwkv_a_lora_b': (64, 1024), 'rwkv_k_k': (1024,), 'rwkv_k_a': (1024,), 'rwkv_r_k': (16, 64),
    'rwkv_gn_g': (1024,), 'rwkv_gn_b': (1024,), 'rwkv_w_out': (1024, 1024), 'ret_w_in': (1024, 4096),
    'ret_gn_g': (1024,), 'ret_w_out': (1024, 1024),
}


def make_consts():
    c = {}
    c["ident"] = np.eye(128, dtype=np.float32)
    i = np.arange(128)
    c["mask_le"] = (i[:, None] <= i[None, :]).astype(np.float32)
    c["mask_lt"] = (i[:, None] < i[None, :]).astype(np.float32)
    c["mask_ge"] = (i[:, None] >= i[None, :]).astype(np.float32)
    c["mask_gt"] = (i[:, None] > i[None, :]).astype(np.float32)
    inv = 1.0 / (10000.0 ** (np.arange(0, 256, 2, dtype=np.float32) / np.float32(256)))
    ang = np.arange(4096, dtype=np.float32)[:, None] * inv[None, :].astype(np.float32)
    c["ret_cosT"] = np.ascontiguousarray(np.cos(ang).T.astype(np.float32))
    c["ret_sinT"] = np.ascontiguousarray(np.sin(ang).T.astype(np.float32))
    lg = np.log1p(-(2.0 ** (-5.0 - np.arange(4, dtype=np.float64))))
    pos = np.arange(128, dtype=np.float64)
    dp = np.zeros((4, 128, 128), np.float32)
    xz = np.zeros((128, 8), np.float32)
    for h in range(4):
        dp[h] = (np.exp(-(pos[:, None] + 1.0) * lg[h]) * (pos[:, None] <= pos[None, :])).astype(np.float32)
        xz[:, h] = np.exp((pos + 1.0) * lg[h])
        xz[:, 4 + h] = np.exp((127.0 - pos) * lg[h])
    c["ret_dpT"] = dp
    c["ret_xz"] = xz
    def rope_fm(rot):
        inv = 1.0 / (np.float32(500000.0) ** (np.arange(0, rot, 2, dtype=np.float32) / np.float32(rot)))
        ang = np.arange(4096, dtype=np.float32)[:, None] * inv[None, :].astype(np.float32)
        cs, sn = np.cos(ang).T.astype(np.float32), np.sin(ang).T.astype(np.float32)
        return np.ascontiguousarray(np.concatenate([cs, cs], 0)), np.ascontiguousarray(np.concatenate([-sn, sn], 0))
    c["dsa_ropeA_c"], c["dsa_ropeA_s"] = rope_fm(32)
    ic, isn = rope_fm(16)
    c["dsa_ropeI_c"] = np.ascontiguousarray(np.concatenate([ic, np.ones_like(ic)], 0))
    c["dsa_ropeI_s"] = np.ascontiguousarray(np.concatenate([isn, np.zeros_like(isn)], 0))
    c["negmask"] = np.where(i[None, :] <= i[:, None], 0.0, -1e30).astype(np.float32)
    return c


_CACHE = {}


def build_program(layers=(0, 1, 2, 3), nblk=TOK // 128):
    key = (tuple(layers), nblk)
    if key in _CACHE:
        return _CACHE[key]
    consts = make_consts()
    nc = bass.Bass("TRN2", target_bir_lowering=False)
    with ExitStack() as st:
        k = K(nc, st, WEIGHT_SHAPES, {n: a.shape for n, a in consts.items()}, nblk=nblk)
        k.run_layers(layers)
        print("instructions", k.S.n_ins, "waits", k.S.n_wait)
    _CACHE[key] = (nc, consts)
    return nc, consts


def kernel(**inputs):
    x = np.ascontiguousarray(np.asarray(inputs["x"], dtype=np.float32))
    nc, consts = build_program()
    base = {n: np.ascontiguousarray(np.asarray(inputs[n], dtype=np.float32)) for n in WEIGHT_SHAPES}
    base.update(consts)
    in_maps = []
    for c in range(8):
        m = dict(base)
        m["x"] = x[2 * c:2 * c + 2].reshape(TOK, D)
        in_maps.append(m)
    res = run_bass_kernel_spmd(nc, in_maps, core_ids=list(range(8)))
    out = np.stack([r["out"].reshape(NSEQ, SEQ, D) for r in res.results], axis=0).reshape(16, SEQ, D)
    return out.astype(np.float32)
```

```python
import numpy as np
from contextlib import ExitStack
import concourse.bass as bass
import concourse.mybir as mybir
from concourse.bass_utils import run_bass_kernel_spmd

F32 = mybir.dt.float32
ALU = mybir.AluOpType
AF = mybir.ActivationFunctionType
AX = mybir.AxisListType


class Res:
    __slots__ = ("w", "r", "dsem", "excl")

    def __init__(self):
        self.w = None
        self.r = {}
        self.dsem = None
        self.excl = False


class V:
    __slots__ = ("ap", "res")

    def __init__(self, ap, res):
        self.ap = ap
        self.res = res


class Tile:
    def __init__(self, handle, res=None):
        self.h = handle
        self.res = res if res is not None else Res()

    def __getitem__(self, key):
        return V(self.h[key], self.res)

    def v(self, ap):
        return V(ap, self.res)


class Sched:
    ENG = ("tensor", "vector", "scalar", "gpsimd", "sync")

    def __init__(self, nc, stack, n_dma_sems=92):
        self.nc = nc
        self.stack = stack
        self.e = {"tensor": nc.tensor, "vector": nc.vector, "scalar": nc.scalar,
                  "gpsimd": nc.gpsimd, "sync": nc.sync}
        self.sem = {}
        self.tot = {}
        for e in self.ENG:
            self.sem[e] = stack.enter_context(nc.semaphore("s_" + e))
            self.tot[e] = 0
        self.bar = stack.enter_context(nc.semaphore("s_bar"))
        self.bar_n = 0
        self.free_dma = []
        for i in range(n_dma_sems):
            k = "d%d" % i
            self.sem[k] = stack.enter_context(nc.semaphore(k))
            self.tot[k] = 0
            self.free_dma.append(k)
        self.known = {e: {} for e in self.ENG}
        self.n_ins = 0
        self.n_wait = 0

    def tile(self, stack, shape, name, dtype=F32, dma=False):
        self.n_tiles = getattr(self, "n_tiles", 0) + 1
        h = stack.enter_context(self.nc.sbuf_tensor("%s_%d" % (name, self.n_tiles), list(shape), dtype))
        t = Tile(h)
        if dma:
            t.res.dsem = {}
            stack.callback(self._release_dsems, t.res)
        return t

    def _release_dsems(self, res):
        for k in res.dsem.values():
            self.free_dma.append(k)
        res.dsem = {}

    def _dsem(self, res, q):
        kind = "sw" if q == "gpsimd" else "hw"
        k = res.dsem.get(kind)
        if k is None:
            k = self.free_dma.pop()
            res.dsem[kind] = k
        return k

    def psum(self, stack, name, shape=(128, 512), dtype=F32):
        h = stack.enter_context(self.nc.psum_tensor(name, list(shape), dtype))
        t = Tile(h)
        t.res.excl = True
        return t

    def _waits(self, eng, reads, writes):
        deps = {}
        for v in reads:
            if v is None or v.res is None:
                continue
            w = v.res.w
            if w is not None:
                if deps.get(w[0], 0) < w[1]:
                    deps[w[0]] = w[1]
        for v in writes:
            if v is None or v.res is None:
                continue
            w = v.res.w
            if w is not None:
                if deps.get(w[0], 0) < w[1]:
                    deps[w[0]] = w[1]
            for k, val in v.res.r.items():
                if deps.get(k, 0) < val:
                    deps[k] = val
        kn = self.known[eng]
        for k, val in deps.items():
            if k[0] == "d":
                val = self.tot[k]
            elif eng == "tensor" and k == "tensor":
                continue
            if kn.get(k, 0) >= val:
                continue
            kn[k] = val
            self.e[eng].wait_ge(self.sem[k], val)
            self.n_wait += 1

    def emit(self, eng, reads, writes, fn):
        xr = [v for v in reads if v is not None and v.res is not None and v.res.excl]
        if xr:
            writes = list(writes) + xr
        self._waits(eng, reads, writes)
        ins = fn(self.e[eng])
        self.tot[eng] += 1
        n = self.tot[eng]
        ins.then_inc(self.sem[eng], 1)
        self.n_ins += 1
        for v in reads:
            if v is not None and v.res is not None:
                if v.res.r.get(eng, 0) < n:
                    v.res.r[eng] = n
        for v in writes:
            if v is not None and v.res is not None:
                v.res.w = (eng, n)
                v.res.r = {}

    def dma(self, out, in_, q="sync", **kw):
        self._waits(q, [in_], [out])
        k = None
        if out.res is not None and out.res.dsem is not None:
            k = self._dsem(out.res, q)
        elif in_.res is not None and in_.res.dsem is not None:
            k = self._dsem(in_.res, q)
        assert k is not None, "dma needs a resource with a dma semaphore"
        ins = self.e[q].dma_start(out=out.ap, in_=in_.ap, **kw)
        self.tot[k] += 16
        n = self.tot[k]
        ins.then_inc(self.sem[k], 16)
        self.n_ins += 1
        if in_.res is not None:
            if in_.res.r.get(k, 0) < n:
                in_.res.r[k] = n
        if out.res is not None:
            out.res.w = (k, n)
            out.res.r = {}

    def barrier(self):
        sy = self.e["sync"]
        kn = self.known["sync"]
        for k, val in self.tot.items():
            if val > 0 and kn.get(k, 0) < val:
                sy.wait_ge(self.sem[k], val)
                kn[k] = val
        self.bar_n += 1
        sy.sem_inc(self.bar, 1)
        for e in self.ENG:
            if e != "sync":
                self.e[e].wait_ge(self.bar, self.bar_n)
            for k, val in self.tot.items():
                self.known[e][k] = val

    def mm(self, out, lhsT, rhs, start=True, stop=True):
        self.emit("tensor", [lhsT, rhs], [out],
                  lambda e: e.matmul(out.ap, lhsT.ap, rhs.ap, start=start, stop=stop))

    def tr(self, out, in_, ident):
        self.emit("tensor", [in_, ident], [out],
                  lambda e: e.transpose(out.ap, in_.ap, ident.ap))

    def act(self, out, in_, func, bias=None, scale=1.0, eng="scalar"):
        rd = [in_]
        kw = {}
        if isinstance(bias, V):
            rd.append(bias)
            kw["bias"] = bias.ap
        elif bias is not None:
            kw["bias"] = bias
        if isinstance(scale, V):
            rd.append(scale)
            kw["scale"] = scale.ap
        else:
            kw["scale"] = scale
        self.emit(eng, rd, [out], lambda e: e.activation(out.ap, in_.ap, func, **kw))

    def tt(self, out, in0, in1, op, eng="vector"):
        self.emit(eng, [in0, in1], [out],
                  lambda e: e.tensor_tensor(out.ap, in0.ap, in1.ap, op))

    def ts(self, out, in0, s1, s2=None, op0=ALU.mult, op1=None, eng="vector"):
        rd = [in0]
        a1 = s1
        a2 = s2
        if isinstance(s1, V):
            rd.append(s1)
            a1 = s1.ap
        if isinstance(s2, V):
            rd.append(s2)
            a2 = s2.ap
        if op1 is None:
            self.emit(eng, rd, [out], lambda e: e.tensor_scalar(out.ap, in0.ap, a1, None, op0))
        else:
            self.emit(eng, rd, [out], lambda e: e.tensor_scalar(out.ap, in0.ap, a1, a2, op0, op1))

    def stt(self, out, in0, scalar, in1, op0, op1, eng="vector"):
        rd = [in0, in1]
        a = scalar
        if isinstance(scalar, V):
            rd.append(scalar)
            a = scalar.ap
        self.emit(eng, rd, [out],
                  lambda e: e.scalar_tensor_tensor(out.ap, in0.ap, a, in1.ap, op0, op1))

    def copy(self, out, in_, eng="vector"):
        if eng == "scalar":
            self.emit(eng, [in_], [out], lambda e: e.copy(out.ap, in_.ap))
        else:
            self.emit(eng, [in_], [out], lambda e: e.tensor_copy(out.ap, in_.ap))

    def red(self, out, in_, op=ALU.add, axis=AX.X, eng="vector"):
        self.emit(eng, [in_], [out], lambda e: e.tensor_reduce(out.ap, in_.ap, axis, op))

    def memset(self, out, val, eng="vector"):
        self.emit(eng, [], [out], lambda e: e.memset(out.ap, val))

    def recip(self, out, in_):
        self.emit("vector", [in_], [out], lambda e: e.reciprocal(out.ap, in_.ap))

    def vmax(self, out, in_):
        self.emit("vector", [in_], [out], lambda e: e.max(out.ap, in_.ap))

    def match_replace(self, out, in_to_replace, in_values, imm):
        self.emit("vector", [in_to_replace, in_values], [out],
                  lambda e: e.match_replace(out.ap, in_to_replace.ap, in_values.ap, imm))

D = 1024
SEQ = 4096
NSEQ = 2
TOK = NSEQ * SEQ
NB = SEQ // 128
LN_EPS = 1e-5
DN_ALPHA = (2 * 4) ** 0.25


def dv(ap):
    return V(ap, None)


class K:
    def __init__(self, nc, st, weights_meta, consts_meta, nblk=TOK // 128):
        self.nc = nc
        self.st = st
        self.S = Sched(nc, st)
        self.nblk = nblk
        self.din = {}
        for name, shape in list(weights_meta.items()) + list(consts_meta.items()):
            self.din[name] = nc.dram_tensor(name, list(shape), F32, kind="ExternalInput").ap()
        self.x_in = nc.dram_tensor("x", [TOK, D], F32, kind="ExternalInput").ap()
        self.out = nc.dram_tensor("out", [TOK, D], F32, kind="ExternalOutput").ap()
        self.o_s = nc.dram_tensor("o_s", [TOK, D], F32).ap()
        self.g_s = nc.dram_tensor("g_s", [TOK, D], F32).ap()
        S = self.S
        self.ps = [S.psum(st, "psb%d" % i) for i in range(8)]
        self.ps_i = 0
        self.pa_i = 0
        self.ident = S.tile(st, [128, 128], "ident", dma=True)
        S.dma(self.ident[:, :], dv(self.din["ident"][:, :]))
        self._eps = {}
        for val in (LN_EPS, 1.0, 1e-6, 64e-5, 1e-12, 0.0, -1.0):
            t = S.tile(st, [128, 1], "eps")
            S.memset(t[:, :], float(val))
            self._eps[val] = t

    def psum(self):
        p = self.ps[2 + self.ps_i % 6]
        self.ps_i += 1
        return p

    def psum_acc(self):
        p = self.ps[self.pa_i % 2]
        self.pa_i += 1
        return p

    def stop(self, n):
        import os
        if int(os.environ.get("KSTOP", "99")) == n:
            self.S.barrier()
            return True
        return False

    def scratch(self, name, shape):
        return self.nc.dram_tensor(name, list(shape), F32).ap()

    def load_xT(self, src, t0, xtok, xT, nb=4):
        S = self.S
        S.dma(xtok[:, 0:nb, :], dv(src[t0:t0 + nb * 128, :].rearrange("(b p) d -> p b d", p=128)))
        for kc in range(8):
            ps = self.psum()
            for b in range(nb):
                S.tr(ps[:, b * 128:(b + 1) * 128], xtok[:, b, kc * 128:(kc + 1) * 128], self.ident[:, :])
            S.copy(xT[:, kc, 0:nb * 128], ps[:, 0:nb * 128], eng=("vector" if kc % 2 == 0 else "scalar"))

    def load_w(self, wt, w_ap, c0, n, q="sync"):
        self.S.dma(wt[:, :, 0:n], dv(w_ap.rearrange("(kc p) n -> p kc n", p=128)[:, :, c0:c0 + n]), q=q)

    def mm_feat(self, ps, wt, c0, m, xT, ntok=512):
        for kc in range(8):
            self.S.mm(ps[0:m, 0:ntok], wt[:, kc, c0:c0 + m], xT[:, kc, 0:ntok], start=(kc == 0), stop=(kc == 7))

    def mm_tok(self, ps, xT, blk, wt, c0, n):
        for kc in range(8):
            self.S.mm(ps[:, 0:n], xT[:, kc, blk * 128:(blk + 1) * 128], wt[:, kc, c0:c0 + n], start=(kc == 0), stop=(kc == 7))

    def phase_out(self, layer, x_src, w_out, post=None):
        S = self.S
        with ExitStack() as st:
            wo = S.tile(st, [128, 8, 1024], "wo", dma=True)
            S.dma(wo[:, :, :], dv(w_out.rearrange("(kc p) n -> p kc n", p=128)))
            lng = S.tile(st, [128, 1024], "lng", dma=True)
            lnb = S.tile(st, [128, 1024], "lnb", dma=True)
            S.dma(lng[:, :], dv(self.din["ln_g"][layer:layer + 1, :].partition_broadcast(128)))
            S.dma(lnb[:, :], dv(self.din["ln_b"][layer:layer + 1, :].partition_broadcast(128)))
            NBUF = 2
            ot = [S.tile(st, [128, 1024], "po_o%d" % i, dma=True) for i in range(NBUF)]
            gt = [S.tile(st, [128, 1024], "po_g%d" % i, dma=True) for i in range(NBUF)]
            xt = [S.tile(st, [128, 1024], "po_x%d" % i, dma=True) for i in range(NBUF)]
            zT = [S.tile(st, [128, 8, 128], "po_zT%d" % i) for i in range(NBUF)]
            yt = [S.tile(st, [128, 1024], "po_y%d" % i, dma=True) for i in range(NBUF)]
            sq = S.tile(st, [128, 1024], "po_sq")
            stat = [S.tile(st, [128, 8], "po_st%d" % i) for i in range(NBUF)]
            extra = post[0](st) if post is not None else None
            for blk in range(self.nblk):
                i = blk % NBUF
                r0 = blk * 128
                S.dma(ot[i][:, :], dv(self.o_s[r0:r0 + 128, :]))
                S.dma(gt[i][:, :], dv(self.g_s[r0:r0 + 128, :]))
                S.dma(xt[i][:, :], dv(x_src[r0:r0 + 128, :]))
                if post is not None:
                    post[1](extra, blk, ot[i], gt[i])
                S.act(gt[i][:, :], gt[i][:, :], AF.Silu)
                S.tt(ot[i][:, :], ot[i][:, :], gt[i][:, :], ALU.mult, eng="gpsimd")
                for half in range(2):
                    ps = self.psum()
                    for j in range(4):
                        kc = half * 4 + j
                        S.tr(ps[:, j * 128:(j + 1) * 128], ot[i][:, kc * 128:(kc + 1) * 128], self.ident[:, :])
                    S.copy(zT[i][:, half * 4:(half + 1) * 4, :],
                           ps.v(ps.h[:, :].rearrange("p (j t) -> p j t", t=128)), eng=("vector" if half == 0 else "scalar"))
                y = yt[i]
                for half in range(2):
                    ps = self.psum()
                    for kc in range(8):
                        S.mm(ps[:, :], zT[i][:, kc, :], wo[:, kc, half * 512:(half + 1) * 512], start=(kc == 0), stop=(kc == 7))
                    S.stt(y[:, half * 512:(half + 1) * 512], xt[i][:, half * 512:(half + 1) * 512], DN_ALPHA, ps[:, :], ALU.mult, ALU.add)
                sv = stat[i]
                S.red(sv[:, 0:1], y[:, :])
                S.ts(sv[:, 1:2], sv[:, 0:1], -1.0 / D, None, op0=ALU.mult)
                S.ts(y[:, :], y[:, :], sv[:, 1:2], None, op0=ALU.add)
                S.tt(sq[:, :], y[:, :], y[:, :], ALU.mult, eng="gpsimd")
                S.red(sv[:, 2:3], sq[:, :])
                S.act(sv[:, 3:4], sv[:, 2:3], AF.Sqrt, bias=self.eps_tile(LN_EPS), scale=1.0 / D)
                S.recip(sv[:, 4:5], sv[:, 3:4])
                S.stt(y[:, :], y[:, :], sv[:, 4:5], lng[:, :], ALU.mult, ALU.mult)
                S.tt(y[:, :], y[:, :], lnb[:, :], ALU.add, eng="gpsimd")
                S.dma(dv(self.out[r0:r0 + 128, :]), y[:, :], q="gpsimd")
            S.barrier()

    def eps_tile(self, val):
        return self._eps[val][:, 0:1]

    def fox_alloc(self):
        self.fx_qT = self.scratch("fx_qT", [8, 128, TOK])
        self.fx_kT = self.scratch("fx_kT", [8, 128, TOK])
        self.fx_v = self.scratch("fx_v", [TOK, D])
        self.fx_c = self.scratch("fx_c", [NSEQ, 128, NB * 8])
        self.fx_cr = self.scratch("fx_cr", [NSEQ, 128, NB * 8])

    def fox_proj(self, x_src):
        S = self.S
        w = self.din["fox_w_in"]
        with ExitStack() as st:
            xtok = [S.tile(st, [128, 4, 1024], "fp_xtok%d" % i, dma=True) for i in range(2)]
            xT = [S.tile(st, [128, 8, 512], "fp_xT%d" % i) for i in range(2)]
            wt = [S.tile(st, [128, 8, 512], "fp_w%d" % i, dma=True) for i in range(3)]
            wf = S.tile(st, [128, 8, 8], "fp_wf", dma=True)
            self.load_w(wf, w, 3072, 8)
            stg = [S.tile(st, [128, 512], "fp_stg%d" % i, dma=True) for i in range(4)]
            lf = S.tile(st, [128, NB, 8], "fp_lf")
            bf = S.tile(st, [128, 8], "fp_bf", dma=True)
            S.dma(bf[:, :], dv(self.din["fox_b_f"].rearrange("(o n) -> o n", o=1).partition_broadcast(128)))
            tri = S.tile(st, [128, 128], "fp_tri", dma=True)
            S.dma(tri[:, :], dv(self.din["mask_le"][:, :]))
            ones = S.tile(st, [128, 128], "fp_ones")
            S.memset(ones[:, :], 1.0)
            cw = S.tile(st, [128, NB, 8], "fp_cw", dma=True)
            cr = S.tile(st, [128, NB, 8], "fp_cr", dma=True)
            wi = 0
            si = 0
            ntile = self.nblk // 4
            for ti in range(ntile):
                t0 = ti * 512
                seq, tl = divmod(ti, NB // 4)
                xk = xtok[ti % 2]
                xt_ = xT[ti % 2]
                self.load_xT(x_src, t0, xk, xt_)
                for which, dst in ((0, self.fx_qT), (1, self.fx_kT)):
                    for grp in range(2):
                        wtile = wt[wi % 3]
                        wi += 1
                        self.load_w(wtile, w, which * 1024 + grp * 512, 512)
                        for j in range(4):
                            h = grp * 4 + j
                            ps = self.psum()
                            self.mm_feat(ps, wtile, j * 128, 128, xt_)
                            sg = stg[si % 4]
                            si += 1
                            S.copy(sg[:, :], ps[:, :], eng=("vector" if si % 2 == 0 else "scalar"))
                            S.dma(dv(dst[h, :, t0:t0 + 512]), sg[:, :], q="gpsimd")
                for c0, dst in ((2048, self.fx_v), (3080, self.g_s)):
                    for grp in range(2):
                        wtile = wt[wi % 3]
                        wi += 1
                        self.load_w(wtile, w, c0 + grp * 512, 512)
                        for b in range(4):
                            ps = self.psum()
                            self.mm_tok(ps, xt_, b, wtile, 0, 512)
                            sg = stg[si % 4]
                            si += 1
                            S.copy(sg[:, :], ps[:, :], eng=("vector" if si % 2 == 0 else "scalar"))
                            S.dma(dv(dst[t0 + b * 128:t0 + (b + 1) * 128, grp * 512:(grp + 1) * 512]), sg[:, :], q="gpsimd")
                ps = self.psum()
                for b in range(4):
                    for kc in range(8):
                        S.mm(ps[:, b * 8:(b + 1) * 8], xt_[:, kc, b * 128:(b + 1) * 128], wf[:, kc, :], start=(kc == 0), stop=(kc == 7))
                for b in range(4):
                    S.tt(lf[:, tl * 4 + b, :], ps[:, b * 8:(b + 1) * 8], bf[:, :], ALU.add)
                if tl == NB // 4 - 1 or ti == ntile - 1:
                    lf2 = lf.v(lf.h[:, :, :].rearrange("p b h -> p (b h)"))
                    S.act(lf2, lf2, AF.Exp, scale=-1.0)
                    S.act(lf2, lf2, AF.Ln, bias=self.eps_tile(1.0), scale=1.0)
                    S.ts(lf2, lf2, -1.0, None, op0=ALU.mult)
                    psw = self.psum()
                    S.mm(psw[:, 0:NB * 8], tri[:, :], lf2)
                    pst = self.psum()
                    S.mm(pst[:, 0:NB * 8], ones[:, :], lf2)
                    S.memset(cr[:, 0, :], 0.0)
                    for b in range(1, NB):
                        S.tt(cr[:, b, :], cr[:, b - 1, :], pst[:, (b - 1) * 8:b * 8], ALU.add)
                    S.tt(cw.v(cw.h[:, :, :].rearrange("p b h -> p (b h)")), psw[:, 0:NB * 8],
                         cr.v(cr.h[:, :, :].rearrange("p b h -> p (b h)")), ALU.add)
                    S.dma(dv(self.fx_c[seq, :, :]), cw.v(cw.h[:, :, :].rearrange("p b h -> p (b h)")), q="gpsimd")
                    S.dma(dv(self.fx_cr[seq, :, :]), cr.v(cr.h[:, :, :].rearrange("p b h -> p (b h)")), q="gpsimd")
            S.barrier()

    def fox_mix(self):
        S = self.S
        nseq = max(1, self.nblk // NB)
        nb = min(NB, self.nblk)
        SCALE = 128 ** -0.5
        with ExitStack() as st:
            KT = [S.tile(st, [128, SEQ], "fm_KT%d" % i, dma=True) for i in range(2)]
            QT = [S.tile(st, [128, SEQ], "fm_QT%d" % i, dma=True) for i in range(2)]
            VA = [S.tile(st, [128, NB, 132], "fm_VA%d" % i, dma=True) for i in range(2)]
            OS = [S.tile(st, [128, NB, 128], "fm_OS%d" % i, dma=True) for i in range(2)]
            cw = S.tile(st, [128, NB, 8], "fm_cw", dma=True)
            cr = S.tile(st, [128, NB, 8], "fm_cr", dma=True)
            bias = [S.tile(st, [128, NB, NB], "fm_bias%d" % i) for i in range(2)]
            PT = [S.tile(st, [128, 128], "fm_PT%d" % i) for i in range(12)]
            rc = [S.tile(st, [128, 1], "fm_rc%d" % i) for i in range(2)]
            mle = S.tile(st, [128, 128], "fm_mle", dma=True)
            S.dma(mle[:, :], dv(self.din["mask_le"][:, :]))
            for i in range(2):
                S.memset(VA[i][:, :, 128:129], 1.0)
            it = 0
            pti = 0
            for seq in range(nseq):
                S.dma(cw.v(cw.h[:, :, :].rearrange("p b h -> p (b h)")), dv(self.fx_c[seq, :, :]))
                S.dma(cr.v(cr.h[:, :, :].rearrange("p b h -> p (b h)")), dv(self.fx_cr[seq, :, :]))
                for h in range(8):
                    i = it % 2
                    it += 1
                    c0 = seq * SEQ
                    S.dma(KT[i][:, 0:nb * 128], dv(self.fx_kT[h, :, c0:c0 + nb * 128]))
                    S.dma(QT[i][:, 0:nb * 128], dv(self.fx_qT[h, :, c0:c0 + nb * 128]))
                    S.dma(VA[i][:, 0:nb, 0:128],
                          dv(self.fx_v[c0:c0 + nb * 128, h * 128:(h + 1) * 128].rearrange("(b p) d -> p b d", p=128)))
                    bt = bias[i]
                    for qb in range(nb):
                        S.ts(bt[:, qb, 0:qb + 1], cw[:, 0:qb + 1, h], -1.0, cr[:, qb, h:h + 1], op0=ALU.mult, op1=ALU.add)
                    groups = []
                    for qb in range(nb):
                        nk = qb + 1
                        for g0 in range(0, nk, 4):
                            groups.append((qb, g0, min(nk, g0 + 4)))
                    state = {}

                    def stage1(gi):
                        nonlocal pti
                        qb, g0, g1 = groups[gi]
                        pss = self.psum()
                        pts = [PT[(pti + j) % 12] for j in range(4)]
                        pti += 4
                        for kb in range(g0, g1):
                            S.mm(pss[:, (kb - g0) * 128:(kb - g0 + 1) * 128], KT[i][:, kb * 128:(kb + 1) * 128],
                                 QT[i][:, qb * 128:(qb + 1) * 128])
                        for kb in range(g0, g1):
                            S.act(pts[kb - g0][:, :], pss[:, (kb - g0) * 128:(kb - g0 + 1) * 128], AF.Exp,
                                  bias=bt[:, qb, kb:kb + 1], scale=SCALE)
                        if g1 == qb + 1:
                            S.tt(pts[qb - g0][:, :], pts[qb - g0][:, :], mle[:, :], ALU.mult, eng="gpsimd")
                        state[gi] = pts

                    def stage2(gi):
                        qb, g0, g1 = groups[gi]
                        nk = qb + 1
                        pts = state.pop(gi)
                        if g0 == 0:
                            state["pso"] = self.psum_acc()
                        pso = state["pso"]
                        for kb in range(g0, g1):
                            S.mm(pso[:, 0:129], pts[kb - g0][:, :], VA[i][:, kb, 0:129], start=(kb == 0), stop=(kb == nk - 1))
                        if g1 == nk:
                            r = rc[qb % 2]
                            S.recip(r[:, :], pso[:, 128:129])
                            S.ts(OS[i][:, qb, :], pso[:, 0:128], r[:, 0:1], None, op0=ALU.mult)

                    LA = 2
                    for gi in range(len(groups) + LA):
                        if gi < len(groups):
                            stage1(gi)
                        if gi - LA >= 0:
                            stage2(gi - LA)
                    S.dma(dv(self.o_s[c0:c0 + nb * 128, h * 128:(h + 1) * 128].rearrange("(b p) d -> p b d", p=128)),
                          OS[i][:, 0:nb, :], q="gpsimd")
            S.barrier()

    def dsa_alloc(self):
        self.ds_qlT = self.scratch("ds_qlT", [8, 128, TOK])
        self.ds_qcT = self.scratch("ds_qcT", [8, 96, TOK])
        self.ds_kcT = self.scratch("ds_kcT", [96, TOK])
        self.ds_ckv = self.scratch("ds_ckv", [TOK, 128])
        self.ds_ckvT = self.scratch("ds_ckvT", [128, TOK])
        self.ds_wi = self.scratch("ds_wi", [TOK, 16])

    def dsa_proj(self, x_src):
        S = self.S
        w = self.din["dsa_w_in"]
        wr = w.rearrange("(kc p) n -> p kc n", p=128)
        with ExitStack() as st:
            xtok = [S.tile(st, [128, 4, 1024], "dp_xtok%d" % i, dma=True) for i in range(2)]
            xT = [S.tile(st, [128, 8, 512], "dp_xT%d" % i) for i in range(2)]
            wt = [S.tile(st, [128, 8, 512], "dp_w%d" % i, dma=True) for i in range(2)]
            wsm = S.tile(st, [128, 8, 744], "dp_wsm", dma=True)
            S.dma(wsm[:, :, :], dv(wr[:, :, 1024:1768]))
            wpq = S.tile(st, [128, 8, 8, 32], "dp_wpq", dma=True)
            wpi = S.tile(st, [128, 8, 8, 32], "dp_wpi", dma=True)
            wpk = S.tile(st, [128, 8, 64], "dp_wpk", dma=True)
            S.memset(wpi[:, :, :, :], 0.0)
            S.memset(wpk[:, :, :], 0.0)
            for h in range(8):
                S.dma(wpq[:, :, h, 0:16], dv(wr[:, :, h * 128 + 16:h * 128 + 32]))
                S.dma(wpq[:, :, h, 16:32], dv(wr[:, :, h * 128:h * 128 + 16]))
                S.dma(wpi[:, :, h, 0:8], dv(wr[:, :, 1184 + h * 64 + 8:1184 + h * 64 + 16]))
                S.dma(wpi[:, :, h, 8:16], dv(wr[:, :, 1184 + h * 64:1184 + h * 64 + 8]))
            S.dma(wpk[:, :, 0:16], dv(wr[:, :, 1152 + 16:1152 + 32]))
            S.dma(wpk[:, :, 16:32], dv(wr[:, :, 1152:1152 + 16]))
            S.dma(wpk[:, :, 32:40], dv(wr[:, :, 1696 + 8:1696 + 16]))
            S.dma(wpk[:, :, 40:48], dv(wr[:, :, 1696:1696 + 8]))
            wuk = S.tile(st, [128, 8, 96], "dp_wuk", dma=True)
            S.dma(wuk[:, :, :], dv(self.din["dsa_w_uk"].rearrange("h c n -> c h n")))
            wukT = S.tile(st, [96, 8, 128], "dp_wukT")
            for h in range(8):
                ps = self.psum()
                S.tr(ps[0:96, 0:128], wuk[:, h, :], self.ident[:, :])
                S.copy(wukT[:, h, :], ps[0:96, 0:128])
            if self.stop(1):
                return
            kvg = S.tile(st, [128, 128], "dp_kvg", dma=True)
            S.dma(kvg[:, :], dv(self.din["dsa_kv_norm_g"].rearrange("(o n) -> o n", o=1).partition_broadcast(128)))
            ropeA = [S.tile(st, [32, 2, 512], "dp_ropeA%d" % i, dma=True) for i in range(2)]
            ropeI = [S.tile(st, [32, 2, 512], "dp_ropeI%d" % i, dma=True) for i in range(2)]
            stg = [S.tile(st, [128, 512], "dp_stg%d" % i, dma=True) for i in range(4)]
            qn = [S.tile(st, [96, 512], "dp_qn%d" % i) for i in range(2)]
            qc = [S.tile(st, [96, 512], "dp_qc%d" % i, dma=True) for i in range(3)]
            t32 = [S.tile(st, [32, 512], "dp_t32%d" % i) for i in range(2)]
            csb = [S.tile(st, [128, 128], "dp_csb%d" % i, dma=True) for i in range(2)]
            csq = S.tile(st, [128, 128], "dp_csq")
            cst = [S.tile(st, [128, 8], "dp_cst%d" % i) for i in range(2)]
            cT = [S.tile(st, [128, 128], "dp_cT%d" % i, dma=True) for i in range(2)]
            wis = [S.tile(st, [128, 16], "dp_wi%d" % i, dma=True) for i in range(2)]
            wi_ = 0
            si = 0
            qi_ = 0
            ntile = self.nblk // 4
            CI = (64 ** -0.5) * (8 ** -0.5)
            for ti in range(ntile):
                t0 = ti * 512
                p0 = t0 % SEQ
                xk = xtok[ti % 2]
                xt_ = xT[ti % 2]
                self.load_xT(x_src, t0, xk, xt_)
                rA = ropeA[ti % 2]
                rI = ropeI[ti % 2]
                S.dma(rA[:, 0, :], dv(self.din["dsa_ropeA_c"][:, p0:p0 + 512]))
                S.dma(rA[:, 1, :], dv(self.din["dsa_ropeA_s"][:, p0:p0 + 512]))
                S.dma(rI[:, 0, :], dv(self.din["dsa_ropeI_c"][:, p0:p0 + 512]))
                S.dma(rI[:, 1, :], dv(self.din["dsa_ropeI_s"][:, p0:p0 + 512]))

                def rope32(dst, psA, psB):
                    a, b2 = t32
                    S.tt(a[:, :], psA[0:32, :], rA[:, 0, :], ALU.mult)
                    S.tt(b2[:, :], psB[0:32, :], rA[:, 1, :], ALU.mult)
                    S.tt(dst, a[:, :], b2[:, :], ALU.add, eng="gpsimd")

                if self.stop(2):
                    return
                for grp in range(2):
                    wtile = wt[wi_ % 2]
                    wi_ += 1
                    self.load_w(wtile, w, grp * 512, 512)
                    for j in range(4):
                        h = grp * 4 + j
                        ps = self.psum()
                        self.mm_feat(ps, wtile, j * 128 + 32, 96, xt_)
                        qnt = qn[h % 2]
                        S.copy(qnt[:, :], ps[0:96, :], eng="scalar")
                        ps2 = self.psum()
                        S.mm(ps2[:, :], wukT[:, h, :], qnt[:, :])
                        sg = stg[si % 4]
                        si += 1
                        S.copy(sg[:, :], ps2[:, :])
                        S.dma(dv(self.ds_qlT[h, :, t0:t0 + 512]), sg[:, :], q="gpsimd")
                        if self.stop(21):
                            return
                        qct = qc[qi_ % 3]
                        qi_ += 1
                        psA = self.psum()
                        self.mm_feat(psA, wtile, j * 128, 32, xt_)
                        psB = self.psum()
                        for kc in range(8):
                            S.mm(psB[0:32, :], wpq[:, kc, h, :], xt_[:, kc, :], start=(kc == 0), stop=(kc == 7))
                        rope32(qct[0:32, :], psA, psB)
                        S.dma(dv(self.ds_qcT[h, 64:96, t0:t0 + 512]), qct[0:32, :], q="gpsimd")
                        if self.stop(22):
                            return
                        qct = qc[qi_ % 3]
                        qi_ += 1
                        psC = self.psum()
                        self.mm_feat(psC, wsm, 160 + h * 64, 64, xt_)
                        psD = self.psum()
                        for kc in range(8):
                            S.mm(psD[0:32, :], wpi[:, kc, h, :], xt_[:, kc, :], start=(kc == 0), stop=(kc == 7))
                        if self.stop(23):
                            return
                        S.copy(qct[0:64, :], psC[0:64, :], eng="scalar")
                        if self.stop(24):
                            return
                        a, b2 = t32
                        S.tt(a[0:32, :], psC[0:32, :], rI[:, 0, :], ALU.mult)
                        S.tt(b2[0:32, :], psD[0:32, :], rI[:, 1, :], ALU.mult)
                        if self.stop(25):
                            return
                        S.tt(qct[0:32, :], a[0:32, :], b2[0:32, :], ALU.add, eng="gpsimd")
                        if self.stop(26):
                            return
                        S.dma(dv(self.ds_qcT[h, 0:64, t0:t0 + 512]), qct[0:64, :], q="gpsimd")
                if self.stop(3):
                    return
                qct = qc[qi_ % 3]
                qi_ += 1
                psA = self.psum()
                self.mm_feat(psA, wsm, 128, 32, xt_)
                psB = self.psum()
                for kc in range(8):
                    S.mm(psB[0:32, :], wpk[:, kc, 0:32], xt_[:, kc, :], start=(kc == 0), stop=(kc == 7))
                rope32(qct[0:32, :], psA, psB)
                S.dma(dv(self.ds_kcT[64:96, t0:t0 + 512]), qct[0:32, :], q="gpsimd")
                qct = qc[qi_ % 3]
                qi_ += 1
                psC = self.psum()
                self.mm_feat(psC, wsm, 672, 64, xt_)
                psD = self.psum()
                for kc in range(8):
                    S.mm(psD[0:32, :], wpk[:, kc, 32:64], xt_[:, kc, :], start=(kc == 0), stop=(kc == 7))
                S.copy(qct[0:64, :], psC[0:64, :], eng="scalar")
                a, b2 = t32
                S.tt(a[0:32, :], psC[0:32, :], rI[:, 0, :], ALU.mult)
                S.tt(b2[0:32, :], psD[0:32, :], rI[:, 1, :], ALU.mult)
                S.tt(qct[0:32, :], a[0:32, :], b2[0:32, :], ALU.add, eng="gpsimd")
                S.dma(dv(self.ds_kcT[0:64, t0:t0 + 512]), qct[0:64, :], q="gpsimd")
                if self.stop(4):
                    return
                for b in range(4):
                    r0 = t0 + b * 128
                    ps = self.psum()
                    self.mm_tok(ps, xt_, b, wsm, 0, 128)
                    c = csb[b % 2]
                    sv = cst[b % 2]
                    S.copy(c[:, :], ps[:, 0:128], eng="scalar")
                    S.tt(csq[:, :], c[:, :], c[:, :], ALU.mult, eng="gpsimd")
                    S.red(sv[:, 0:1], csq[:, :])
                    S.act(sv[:, 1:2], sv[:, 0:1], AF.Sqrt, bias=self.eps_tile(1e-6), scale=1.0 / 128)
                    S.recip(sv[:, 2:3], sv[:, 1:2])
                    S.stt(c[:, :], c[:, :], sv[:, 2:3], kvg[:, :], ALU.mult, ALU.mult)
                    S.dma(dv(self.ds_ckv[r0:r0 + 128, :]), c[:, :], q="gpsimd")
                    psT = self.psum()
                    S.tr(psT[:, 0:128], c[:, :], self.ident[:, :])
                    ct = cT[b % 2]
                    S.copy(ct[:, :], psT[:, 0:128], eng="scalar")
                    S.dma(dv(self.ds_ckvT[:, r0:r0 + 128]), ct[:, :], q="gpsimd")
                    psw = self.psum()
                    self.mm_tok(psw, xt_, b, wsm, 736, 8)
                    wv = wis[b % 2]
                    S.copy(wv[:, 8:16], psw[:, 0:8])
                    S.stt(wv[:, 0:8], wv[:, 8:16], -1.0, wv[:, 8:16], ALU.mult, ALU.max)
                    S.ts(wv[:, 0:8], wv[:, 0:8], CI, None, op0=ALU.mult)
                    S.act(wv[:, 8:16], wv[:, 8:16], AF.Sign)
                    S.dma(dv(self.ds_wi[r0:r0 + 128, :]), wv[:, :], q="gpsimd")
                if self.stop(5):
                    return
                for grp in range(2):
                    wtile = wt[wi_ % 2]
                    wi_ += 1
                    self.load_w(wtile, w, 1768 + grp * 512, 512)
                    for b in range(4):
                        ps = self.psum()
                        self.mm_tok(ps, xt_, b, wtile, 0, 512)
                        sg = stg[si % 4]
                        si += 1
                        S.copy(sg[:, :], ps[:, :], eng=("vector" if si % 2 == 0 else "scalar"))
                        S.dma(dv(self.g_s[t0 + b * 128:t0 + (b + 1) * 128, grp * 512:(grp + 1) * 512]), sg[:, :], q="gpsimd")
            S.barrier()

    def dsa_mix(self):
        S = self.S
        nseq = max(1, self.nblk // NB)
        nb = min(NB, self.nblk)
        SCALE = 128 ** -0.5
        with ExitStack() as st:
            kc_ = S.tile(st, [96, SEQ], "dm_kc", dma=True)
            ckvT = S.tile(st, [128, SEQ], "dm_ckvT", dma=True)
            ckvA = S.tile(st, [128, NB, 132], "dm_ckvA", dma=True)
            S.memset(ckvA[:, :, 128:129], 1.0)
            wuv = S.tile(st, [128, 8, 128], "dm_wuv", dma=True)
            S.dma(wuv[:, :, :], dv(self.din["dsa_w_uv"].rearrange("h c d -> c h d")))
            negm = S.tile(st, [128, 128], "dm_negm", dma=True)
            S.dma(negm[:, :], dv(self.din["negmask"][:, :]))
            qc = [S.tile(st, [96, 8, 128], "dm_qc%d" % i, dma=True) for i in range(2)]
            ql = [S.tile(st, [128, 8, 128], "dm_ql%d" % i, dma=True) for i in range(2)]
            wi = [S.tile(st, [128, 16], "dm_wi%d" % i, dma=True) for i in range(2)]
            I_ = [S.tile(st, [128, SEQ], "dm_I%d" % i) for i in range(2)]
            Wk = S.tile(st, [128, SEQ], "dm_Wk")
            MT = [S.tile(st, [128, NB, 128], "dm_MT%d" % i) for i in range(2)]
            m8 = [S.tile(st, [128, 8], "dm_m8%d" % i) for i in range(2)]
            tmp = [S.tile(st, [128, 512], "dm_tmp%d" % i) for i in range(3)]
            PT = [S.tile(st, [128, 512], "dm_PT%d" % i) for i in range(3)]
            rc = [S.tile(st, [128, 1], "dm_rc%d" % i) for i in range(2)]
            raws = [S.tile(st, [128, 132], "dm_raw%d" % i) for i in range(2)]
            olat = [S.tile(st, [128, 128], "dm_ol%d" % i) for i in range(2)]
            olT = [S.tile(st, [128, 128], "dm_olT%d" % i) for i in range(2)]
            osb = [S.tile(st, [128, 1024], "dm_osb%d" % i, dma=True) for i in range(2)]
            cnt = {"tmi": 0, "pti": 0, "hi": 0}

            def prep(seq, qb):
                i = qb % 2
                c0 = seq * SEQ
                r0 = c0 + qb * 128
                L = (qb + 1) * 128
                S.dma(qc[i][:, :, :], dv(self.ds_qcT[:, :, r0:r0 + 128].rearrange("h d t -> d h t")))
                S.dma(ql[i][:, :, :], dv(self.ds_qlT[:, :, r0:r0 + 128].rearrange("h d t -> d h t")))
                S.dma(wi[i][:, :], dv(self.ds_wi[r0:r0 + 128, :]))
                It = I_[i]
                for k0 in range(0, L, 512):
                    n = min(512, L - k0)
                    for h in range(8):
                        ps = self.psum()
                        S.mm(ps[:, 0:n], qc[i][0:64, h, :], kc_[0:64, k0:k0 + n])
                        t = tmp[cnt["tmi"] % 3]
                        cnt["tmi"] += 1
                        S.act(t[:, 0:n], ps[:, 0:n], AF.Relu, scale=wi[i][:, h:h + 1])
                        if h == 0:
                            S.ts(It[:, k0:k0 + n], t[:, 0:n], wi[i][:, 8:9], None, op0=ALU.mult)
                        else:
                            S.stt(It[:, k0:k0 + n], t[:, 0:n], wi[i][:, 8 + h:9 + h], It[:, k0:k0 + n], ALU.mult, ALU.add)
                S.tt(It[:, qb * 128:L], It[:, qb * 128:L], negm[:, :], ALU.add, eng="gpsimd")

            def topk_rounds(qb, ra, rb):
                if qb < 2:
                    return
                It = I_[qb % 2]
                L = (qb + 1) * 128
                for r in range(ra, rb):
                    src = It if r == 0 else Wk
                    m = m8[r % 2]
                    S.vmax(m[:, :], src[:, 0:L])
                    if r < 31:
                        S.match_replace(Wk[:, 0:L], m[:, :], src[:, 0:L], -1e30)

            def finish(qb):
                i = qb % 2
                It = I_[i]
                nk = qb + 1
                L = nk * 128
                if qb >= 2:
                    S.ts(Wk[:, 0:L], It[:, 0:L], m8[1][:, 7:8], None, op0=ALU.is_ge)
                else:
                    S.ts(Wk[:, 0:L], It[:, 0:L], -1e29, None, op0=ALU.is_ge)
                mt = MT[i]
                for g0 in range(0, nk, 4):
                    g1 = min(nk, g0 + 4)
                    ps = self.psum()
                    for kb in range(g0, g1):
                        S.tr(ps[:, (kb - g0) * 128:(kb - g0 + 1) * 128], Wk[:, kb * 128:(kb + 1) * 128], self.ident[:, :])
                    S.copy(mt.v(mt.h[:, g0:g1, :].rearrange("p a t -> p (a t)")), ps[:, 0:(g1 - g0) * 128], eng="scalar")

            def a_stage1(qb, h, g0, g1):
                i = qb % 2
                mt = MT[i]
                n = (g1 - g0) * 128
                pss = self.psum()
                for kb in range(g0, g1):
                    o_ = pss[:, (kb - g0) * 128:(kb - g0 + 1) * 128]
                    S.mm(o_, ckvT[:, kb * 128:(kb + 1) * 128], ql[i][:, h, :], start=True, stop=False)
                    S.mm(o_, kc_[64:96, kb * 128:(kb + 1) * 128], qc[i][64:96, h, :], start=False, stop=True)
                pt = PT[cnt["pti"] % 3]
                cnt["pti"] += 1
                S.act(pt[:, 0:n], pss[:, 0:n], AF.Exp, scale=SCALE)
                S.tt(pt[:, 0:n], pt[:, 0:n], mt.v(mt.h[:, g0:g1, :].rearrange("p a t -> p (a t)")), ALU.mult, eng="gpsimd")
                return pt

            def a_stage2(seq, qb, h, g0, g1, pt):
                i = qb % 2
                nk = qb + 1
                ob = osb[i]
                if g0 == 0:
                    cnt["pso"] = self.psum_acc()
                pso = cnt["pso"]
                for kb in range(g0, g1):
                    S.mm(pso[:, 0:129], pt[:, (kb - g0) * 128:(kb - g0 + 1) * 128], ckvA[:, kb, 0:129],
                         start=(kb == 0), stop=(kb == nk - 1))
                if g1 != nk:
                    return
                hi = cnt["hi"]
                cnt["hi"] += 1
                r = rc[hi % 2]
                ol = olat[hi % 2]
                olt = olT[hi % 2]
                raw = raws[hi % 2]
                S.copy(raw[:, 0:1], pso[:, 128:129], eng="scalar")
                S.tt(raw[:, 1:2], raw[:, 0:1], self.eps_tile(-1.0), ALU.pow, eng="gpsimd")
                S.act(ol[:, :], pso[:, 0:128], AF.Copy, scale=raw[:, 1:2])
                psT = self.psum()
                S.tr(psT[:, 0:128], ol[:, :], self.ident[:, :])
                S.copy(olt[:, :], psT[:, 0:128], eng="scalar")
                ps2 = self.psum()
                S.mm(ps2[:, 0:128], olt[:, :], wuv[:, h, :])
                S.copy(ob[:, h * 128:(h + 1) * 128], ps2[:, 0:128], eng="scalar")
                if h == 7:
                    r0 = seq * SEQ + qb * 128
                    S.dma(dv(self.o_s[r0:r0 + 128, :]), ob[:, :], q="gpsimd")

            def attention(seq, qb, nxt):
                nk = qb + 1
                groups = [(h, g0, min(nk, g0 + 4)) for h in range(8) for g0 in range(0, nk, 4)]
                LA = 2
                pend = {}
                for gi in range(len(groups) + LA):
                    if gi < len(groups):
                        h, g0, g1 = groups[gi]
                        pend[gi] = a_stage1(qb, h, g0, g1)
                    gj = gi - LA
                    if gj >= 0:
                        h, g0, g1 = groups[gj]
                        a_stage2(seq, qb, h, g0, g1, pend.pop(gj))
                        if g1 == nk and nxt:
                            topk_rounds(qb + 1, 4 * h, 4 * h + 4)

            for seq in range(nseq):
                c0 = seq * SEQ
                S.dma(kc_[:, 0:nb * 128], dv(self.ds_kcT[:, c0:c0 + nb * 128]))
                S.dma(ckvT[:, 0:nb * 128], dv(self.ds_ckvT[:, c0:c0 + nb * 128]))
                S.dma(ckvA[:, 0:nb, 0:128], dv(self.ds_ckv[c0:c0 + nb * 128, :].rearrange("(b p) d -> p b d", p=128)))
                prep(seq, 0)
                topk_rounds(0, 0, 32)
                finish(0)
                for qb in range(nb):
                    nxt = qb + 1 < nb
                    if nxt:
                        prep(seq, qb + 1)
                    attention(seq, qb, nxt)
                    if nxt:
                        finish(qb + 1)
            S.barrier()

    def rwkv_alloc(self):
        self.rw_r = self.scratch("rw_r", [TOK, D])
        self.rw_k = self.scratch("rw_k", [TOK, D])
        self.rw_v = self.scratch("rw_v", [TOK, D])
        self.rw_lw = self.scratch("rw_lw", [TOK, D])
        self.rw_a = self.scratch("rw_a", [TOK, D])

    def bcast_row(self, st, name, ap1d):
        t = self.S.tile(st, [128, 1024], name, dma=True)
        self.S.dma(t[:, :], dv(ap1d.rearrange("(o n) -> o n", o=1).partition_broadcast(128)))
        return t

    def rwkv_proj(self, x_src):
        S = self.S
        w = self.din["rwkv_w_in"]
        with ExitStack() as st:
            xtok = [S.tile(st, [128, 4, 1024], "wp_xtok%d" % i, dma=True) for i in range(2)]
            xT = [S.tile(st, [128, 8, 512], "wp_xT%d" % i) for i in range(2)]
            dT = S.tile(st, [128, 8, 512], "wp_dT")
            xm = [S.tile(st, [128, 8, 512], "wp_xm%d" % i) for i in range(2)]
            halo = S.tile(st, [128, 8, 1], "wp_halo")
            wt = [S.tile(st, [128, 8, 512], "wp_w%d" % i, dma=True) for i in range(2)]
            stg = [S.tile(st, [128, 512], "wp_stg%d" % i, dma=True) for i in range(4)]
            mu48 = S.tile(st, [48, 128], "wp_mu48", dma=True)
            S.dma(mu48[:, :], dv(self.din["rwkv_mu"].rearrange("i (kc p) -> (i kc) p", p=128)))
            muT = S.tile(st, [128, 48], "wp_muT")
            ps = self.psum()
            S.tr(ps[:, 0:48], mu48[:, :], self.ident[0:48, 0:48])
            S.copy(muT[:, :], ps[:, 0:48])
            wla = S.tile(st, [128, 8, 64], "wp_wla", dma=True)
            ala = S.tile(st, [128, 8, 64], "wp_ala", dma=True)
            S.dma(wla[:, :, :], dv(self.din["rwkv_w_lora_a"].rearrange("(kc p) n -> p kc n", p=128)))
            S.dma(ala[:, :, :], dv(self.din["rwkv_a_lora_a"].rearrange("(kc p) n -> p kc n", p=128)))
            wlb = S.tile(st, [64, 1024], "wp_wlb", dma=True)
            alb = S.tile(st, [64, 1024], "wp_alb", dma=True)
            S.dma(wlb[:, :], dv(self.din["rwkv_w_lora_b"][:, :]))
            S.dma(alb[:, :], dv(self.din["rwkv_a_lora_b"][:, :]))
            w0 = self.bcast_row(st, "wp_w0", self.din["rwkv_w0"])
            a0 = self.bcast_row(st, "wp_a0", self.din["rwkv_a0"])
            hl = [S.tile(st, [64, 512], "wp_hl%d" % i) for i in range(2)]
            wi_ = 0
            si = 0
            xi_ = 0
            ntile = self.nblk // 4
            NEG = -float(np.exp(-0.5))
            for ti in range(ntile):
                t0 = ti * 512
                xk = xtok[ti % 2]
                xt_ = xT[ti % 2]
                if t0 % SEQ == 0:
                    S.memset(halo[:, :, :], 0.0)
                self.load_xT(x_src, t0, xk, xt_)
                S.tt(dT[:, :, 1:512], xt_[:, :, 0:511], xt_[:, :, 1:512], ALU.subtract)
                S.tt(dT[:, :, 0:1], halo[:, :, :], xt_[:, :, 0:1], ALU.subtract)
                S.copy(halo[:, :, :], xt_[:, :, 511:512], eng="gpsimd")

                def mix(i):
                    nonlocal xi_
                    t = xm[xi_ % 2]
                    xi_ += 1
                    for kc in range(8):
                        S.stt(t[:, kc, :], dT[:, kc, :], muT[:, i * 8 + kc:i * 8 + kc + 1], xt_[:, kc, :], ALU.mult, ALU.add)
                    return t

                for i, c0, dst in ((0, 0, self.rw_r), (2, 1024, self.rw_k), (3, 2048, self.rw_v), (5, 3072, self.g_s)):
                    xmt = mix(i)
                    for grp in range(2):
                        wtile = wt[wi_ % 2]
                        wi_ += 1
                        self.load_w(wtile, w, c0 + grp * 512, 512)
                        for b in range(4):
                            ps = self.psum()
                            self.mm_tok(ps, xmt, b, wtile, 0, 512)
                            sg = stg[si % 4]
                            si += 1
                            S.copy(sg[:, :], ps[:, :], eng=("vector" if si % 2 == 0 else "scalar"))
                            S.dma(dv(dst[t0 + b * 128:t0 + (b + 1) * 128, grp * 512:(grp + 1) * 512]), sg[:, :], q="gpsimd")
                for i, la, lb, bias_t, dst, mul in ((1, wla, wlb, w0, self.rw_lw, NEG), (4, ala, alb, a0, self.rw_a, None)):
                    xmt = mix(i)
                    ps = self.psum()
                    for kc in range(8):
                        S.mm(ps[0:64, :], la[:, kc, :], xmt[:, kc, :], start=(kc == 0), stop=(kc == 7))
                    h_ = hl[i % 2]
                    if i == 1:
                        S.act(h_[:, :], ps[0:64, :], AF.Tanh)
                    else:
                        S.copy(h_[:, :], ps[0:64, :], eng="scalar")
                    for b in range(4):
                        for half in range(2):
                            ps2 = self.psum()
                            S.mm(ps2[:, :], h_[:, b * 128:(b + 1) * 128], lb[:, half * 512:(half + 1) * 512])
                            sg = stg[si % 4]
                            si += 1
                            S.tt(sg[:, :], ps2[:, :], bias_t[:, half * 512:(half + 1) * 512], ALU.add)
                            S.act(sg[:, :], sg[:, :], AF.Sigmoid)
                            if mul is not None:
                                S.ts(sg[:, :], sg[:, :], mul, None, op0=ALU.mult, eng="gpsimd")
                            S.dma(dv(dst[t0 + b * 128:t0 + (b + 1) * 128, half * 512:(half + 1) * 512]), sg[:, :], q="gpsimd")
            S.barrier()

    def rwkv_mix(self):
        S = self.S
        nseq = max(1, self.nblk // NB)
        nb = min(NB, self.nblk)

        def v3(t):
            return t.v(t.h[:, :].rearrange("p (a c) -> p a c", c=64))

        with ExitStack() as st:
            kk_c = self.bcast_row(st, "wm_kk", self.din["rwkv_k_k"])
            ka_c = self.bcast_row(st, "wm_ka", self.din["rwkv_k_a"])
            rk_c = self.bcast_row(st, "wm_rk", self.din["rwkv_r_k"].rearrange("h n -> (h n)"))
            gg_c = self.bcast_row(st, "wm_gg", self.din["rwkv_gn_g"])
            gb_c = self.bcast_row(st, "wm_gb", self.din["rwkv_gn_b"])
            tri = S.tile(st, [128, 128], "wm_tri", dma=True)
            mlt = S.tile(st, [128, 128], "wm_mlt", dma=True)
            mgt = S.tile(st, [128, 128], "wm_mgt", dma=True)
            S.dma(tri[:, :], dv(self.din["mask_le"][:, :]))
            S.dma(mlt[:, :], dv(self.din["mask_lt"][:, :]))
            S.dma(mgt[:, :], dv(self.din["mask_gt"][:, :]))
            ones = S.tile(st, [128, 128], "wm_ones")
            S.memset(ones[:, :], 1.0)
            NIN = 2
            r_ = [S.tile(st, [128, 1024], "wm_r%d" % i, dma=True) for i in range(NIN)]
            k_ = [S.tile(st, [128, 1024], "wm_k%d" % i, dma=True) for i in range(NIN)]
            v_ = [S.tile(st, [128, 1024], "wm_v%d" % i, dma=True) for i in range(NIN)]
            lw_ = [S.tile(st, [128, 1024], "wm_lw%d" % i, dma=True) for i in range(NIN)]
            a_ = [S.tile(st, [128, 1024], "wm_a%d" % i, dma=True) for i in range(NIN)]
            kk = S.tile(st, [128, 1024], "wm_kkn")
            km = S.tile(st, [128, 1024], "wm_km")
            kka = S.tile(st, [128, 1024], "wm_kka")
            cum = S.tile(st, [128, 1024], "wm_cum")
            e1 = S.tile(st, [128, 1024], "wm_e1")
            e1x = e1
            e2 = S.tile(st, [128, 1024], "wm_e2")
            Ab = S.tile(st, [128, 1024], "wm_Ab")
            Bb = S.tile(st, [128, 1024], "wm_Bb")
            Kb = S.tile(st, [128, 1024], "wm_Kb")
            Rb = S.tile(st, [128, 1024], "wm_Rb")
            Bt = S.tile(st, [128, 1024], "wm_Bt")
            Kt = S.tile(st, [128, 1024], "wm_Kt")
            AbT = S.tile(st, [64, 16, 128], "wm_AbT")
            BbT = S.tile(st, [64, 16, 128], "wm_BbT")
            KbT = S.tile(st, [64, 16, 128], "wm_KbT")
            RbT = S.tile(st, [64, 16, 128], "wm_RbT")
            gC = S.tile(st, [64, 16], "wm_gC")
            sm = S.tile(st, [128, 64], "wm_sm")
            ST = [S.tile(st, [64, 64], "wm_ST%d" % h) for h in range(16)]
            ysb = [S.tile(st, [128, 1024], "wm_y%d" % i, dma=True) for i in range(2)]
            G = 8
            P_ = [[S.tile(st, [128, 128], "wm_P%d_%d" % (s, i)) for i in range(2)] for s in range(G)]
            PT_ = [[S.tile(st, [128, 128], "wm_PT%d_%d" % (s, i)) for i in range(2)] for s in range(G)]
            W_ = [[S.tile(st, [128, 128], "wm_W%d_%d" % (s, i)) for i in range(2)] for s in range(G)]
            ArbT = [S.tile(st, [128, 128], "wm_ArbT%d" % s) for s in range(G)]
            ArkT = [S.tile(st, [128, 128], "wm_ArkT%d" % s) for s in range(G)]
            MT = [S.tile(st, [64, 64], "wm_MT%d" % s) for s in range(G)]
            GT = [S.tile(st, [64, 128], "wm_GT%d" % s) for s in range(G)]
            hs = 0
            ci = 0
            for seq in range(nseq):
                for h in range(16):
                    S.memset(ST[h][:, :], 0.0)
                for c in range(nb):
                    i = ci % NIN
                    ci += 1
                    r0 = seq * SEQ + c * 128
                    rt, kt, vt, lwt, at = r_[i], k_[i], v_[i], lw_[i], a_[i]
                    S.dma(rt[:, :], dv(self.rw_r[r0:r0 + 128, :]))
                    S.dma(kt[:, :], dv(self.rw_k[r0:r0 + 128, :]))
                    S.dma(vt[:, :], dv(self.rw_v[r0:r0 + 128, :]))
                    S.dma(lwt[:, :], dv(self.rw_lw[r0:r0 + 128, :]))
                    S.dma(at[:, :], dv(self.rw_a[r0:r0 + 128, :]))
                    S.tt(kk[:, :], kt[:, :], kk_c[:, :], ALU.mult, eng="gpsimd")
                    S.tt(e1x[:, :], kk[:, :], kk[:, :], ALU.mult, eng="gpsimd")
                    S.red(sm[:, 0:16], v3(e1x))
                    S.act(sm[:, 0:16], sm[:, 0:16], AF.Sqrt, bias=self.eps_tile(1e-12), scale=1.0)
                    S.recip(sm[:, 16:32], sm[:, 0:16])
                    S.tt(v3(kk), v3(kk), sm.v(sm.h[:, 16:32].unsqueeze(2).to_broadcast([128, 16, 64])), ALU.mult)
                    S.stt(km[:, :], at[:, :], -1.0, ka_c[:, :], ALU.add, ALU.mult)
                    S.stt(km[:, :], km[:, :], 1.0, kt[:, :], ALU.add, ALU.mult)
                    S.tt(kka[:, :], kk[:, :], at[:, :], ALU.mult, eng="gpsimd")
                    S.tt(e1x[:, :], rt[:, :], km[:, :], ALU.mult, eng="gpsimd")
                    S.tt(e1x[:, :], e1x[:, :], rk_c[:, :], ALU.mult, eng="gpsimd")
                    S.red(sm[:, 32:48], v3(e1x))
                    for half in range(2):
                        hsl = slice(half * 512, (half + 1) * 512)
                        psc = self.psum()
                        S.mm(psc[:, :], tri[:, :], lwt[:, hsl])
                        S.copy(cum[:, hsl], psc[:, :], eng="scalar")
                        pst = self.psum()
                        S.mm(pst[:, :], ones[:, :], lwt[:, hsl])
                        S.tt(e2[:, hsl], pst[:, :], cum[:, hsl], ALU.subtract)
                    S.act(e2[:, :], e2[:, :], AF.Exp)
                    S.tt(Bt[:, :], kka[:, :], e2[:, :], ALU.mult, eng="gpsimd")
                    S.tt(Kt[:, :], km[:, :], e2[:, :], ALU.mult, eng="gpsimd")
                    S.act(e1[:, :], cum[:, :], AF.Exp)
                    S.tt(Rb[:, :], rt[:, :], e1[:, :], ALU.mult, eng="gpsimd")
                    S.act(e1[:, :], cum[:, :], AF.Exp, scale=-1.0)
                    S.tt(Bb[:, :], kka[:, :], e1[:, :], ALU.mult, eng="gpsimd")
                    S.tt(Kb[:, :], km[:, :], e1[:, :], ALU.mult, eng="gpsimd")
                    S.tt(e2[:, :], cum[:, :], lwt[:, :], ALU.subtract)
                    S.act(e2[:, :], e2[:, :], AF.Exp)
                    S.stt(Ab[:, :], kk[:, :], -1.0, e2[:, :], ALU.mult, ALU.mult)
                    psg = self.psum()
                    for h in range(16):
                        S.mm(psg[0:64, h:h + 1], lwt[:, h * 64:(h + 1) * 64], ones[:, 0:1])
                    S.act(gC[:, :], psg[0:64, 0:16], AF.Exp)
                    for src, dstT in ((Ab, AbT), (Bb, BbT), (Kb, KbT), (Rb, RbT)):
                        for g in range(4):
                            pT = self.psum()
                            for j in range(4):
                                h = g * 4 + j
                                S.tr(pT[0:64, j * 128:(j + 1) * 128], src[:, h * 64:(h + 1) * 64], self.ident[:, :])
                            S.copy(dstT.v(dstT.h[:, g * 4:(g + 1) * 4, :].rearrange("p a t -> p (a t)")), pT[0:64, :],
                                   eng=("scalar" if g % 2 == 0 else "vector"))
                    y = ysb[c % 2]
                    for hg in range(16 // G):
                        H = [(hg * G + j, j) for j in range(G)]
                        for h, s in H:
                            ps1 = self.psum()
                            S.mm(ps1[:, 0:128], AbT[:, h, :], BbT[:, h, :])
                            S.tt(P_[s][0][:, :], ps1[:, 0:128], mgt[:, :], ALU.mult)
                        for h, s in H:
                            ps2 = self.psum()
                            S.mm(ps2[:, 0:128], BbT[:, h, :], AbT[:, h, :])
                            S.mm(ps2[:, 128:256], BbT[:, h, :], RbT[:, h, :])
                            S.tt(PT_[s][0][:, :], ps2[:, 0:128], mlt[:, :], ALU.mult)
                            S.tt(ArbT[s][:, :], ps2[:, 128:256], tri[:, :], ALU.mult)
                        for h, s in H:
                            ps3 = self.psum()
                            S.mm(ps3[:, 0:128], KbT[:, h, :], AbT[:, h, :])
                            S.mm(ps3[:, 128:256], KbT[:, h, :], RbT[:, h, :])
                            S.tt(PT_[s][1][:, :], ps3[:, 0:128], mlt[:, :], ALU.mult)
                            S.tt(ArkT[s][:, :], ps3[:, 128:256], tri[:, :], ALU.mult)
                        for h, s in H:
                            hc = slice(h * 64, (h + 1) * 64)
                            ps4 = self.psum()
                            S.mm(ps4[:, 0:64], PT_[s][1][:, :], vt[:, hc])
                            S.copy(W_[s][0][:, 0:64], Ab[:, hc], eng="gpsimd")
                            S.copy(W_[s][0][:, 64:128], ps4[:, 0:64], eng="scalar")
                        for lv in range(7):
                            a, b = lv % 2, (lv + 1) % 2
                            for h, s in H:
                                psw = self.psum()
                                S.mm(psw[:, 0:128], PT_[s][a][:, :], W_[s][a][:, :])
                                S.tt(W_[s][b][:, :], W_[s][a][:, :], psw[:, 0:128], ALU.add)
                            if lv < 6:
                                for h, s in H:
                                    psq = self.psum()
                                    S.mm(psq[:, 0:128], P_[s][a][:, :], PT_[s][a][:, :])
                                    if lv < 5:
                                        S.mm(psq[:, 128:256], PT_[s][a][:, :], P_[s][a][:, :])
                                    S.copy(PT_[s][b][:, :], psq[:, 0:128], eng="scalar")
                                    if lv < 5:
                                        S.copy(P_[s][b][:, :], psq[:, 128:256], eng="scalar")
                        for h, s in H:
                            hc = slice(h * 64, (h + 1) * 64)
                            psM = self.psum()
                            S.mm(psM[0:64, 0:64], W_[s][1][:, 0:64], Bt[:, hc])
                            S.copy(MT[s][:, :], psM[0:64, 0:64], eng="scalar")
                        for h, s in H:
                            psG = self.psum()
                            S.mm(psG[0:64, 0:128], W_[s][1][:, 0:64], ArbT[s][:, :])
                            S.tt(GT[s][:, :], psG[0:64, 0:128], RbT[:, h, :], ALU.add)
                        for h, s in H:
                            hc = slice(h * 64, (h + 1) * 64)
                            psY = self.psum()
                            S.mm(psY[:, 0:64], ArbT[s][:, :], W_[s][1][:, 64:128], start=True, stop=False)
                            S.mm(psY[:, 0:64], ArkT[s][:, :], vt[:, hc], start=False, stop=False)
                            S.mm(psY[:, 0:64], GT[s][:, :], ST[h][:, :], start=False, stop=True)
                            S.copy(y[:, hc], psY[:, 0:64], eng="scalar")
                        for h, s in H:
                            hc = slice(h * 64, (h + 1) * 64)
                            psS = self.psum()
                            S.mm(psS[0:64, 0:64], Bt[:, hc], W_[s][1][:, 64:128], start=True, stop=False)
                            S.mm(psS[0:64, 0:64], Kt[:, hc], vt[:, hc], start=False, stop=False)
                            S.mm(psS[0:64, 0:64], MT[s][:, :], ST[h][:, :], start=False, stop=True)
                            S.stt(ST[h][:, :], ST[h][:, :], gC[:, h:h + 1], psS[0:64, 0:64], ALU.mult, ALU.add)
                    S.red(sm[:, 0:16], v3(y))
                    S.ts(sm[:, 0:16], sm[:, 0:16], -1.0 / 64, None, op0=ALU.mult)
                    S.tt(v3(y), v3(y), sm.v(sm.h[:, 0:16].unsqueeze(2).to_broadcast([128, 16, 64])), ALU.add)
                    S.tt(e1x[:, :], y[:, :], y[:, :], ALU.mult, eng="gpsimd")
                    S.red(sm[:, 16:32], v3(e1x))
                    S.act(sm[:, 16:32], sm[:, 16:32], AF.Sqrt, bias=self.eps_tile(64e-5), scale=1.0 / 64)
                    S.recip(sm[:, 48:64], sm[:, 16:32])
                    S.tt(v3(y), v3(y), sm.v(sm.h[:, 48:64].unsqueeze(2).to_broadcast([128, 16, 64])), ALU.mult)
                    S.tt(y[:, :], y[:, :], gg_c[:, :], ALU.mult, eng="gpsimd")
                    S.tt(y[:, :], y[:, :], gb_c[:, :], ALU.add, eng="gpsimd")
                    S.tt(v3(e1x), v3(vt), sm.v(sm.h[:, 32:48].unsqueeze(2).to_broadcast([128, 16, 64])), ALU.mult)
                    S.tt(y[:, :], y[:, :], e1x[:, :], ALU.add, eng="gpsimd")
                    S.dma(dv(self.o_s[r0:r0 + 128, :]), y[:, :], q="gpsimd")
            S.barrier()

    def ret_alloc(self):
        self.rt_qT = self.scratch("rt_qT", [4, 2, 128, TOK])
        self.rt_kT = self.scratch("rt_kT", [4, 2, 128, TOK])
        self.rt_v = self.scratch("rt_v", [TOK, D])

    def ret_proj(self, x_src):
        S = self.S
        w = self.din["ret_w_in"]
        with ExitStack() as st:
            xtok = [S.tile(st, [128, 4, 1024], "rp_xtok%d" % i, dma=True) for i in range(2)]
            xT = [S.tile(st, [128, 8, 512], "rp_xT%d" % i) for i in range(2)]
            wt = [S.tile(st, [128, 8, 512], "rp_w%d" % i, dma=True) for i in range(3)]
            stg = [S.tile(st, [128, 512], "rp_stg%d" % i, dma=True) for i in range(4)]
            cs = [S.tile(st, [128, 2, 512], "rp_cs%d" % i, dma=True) for i in range(2)]
            tmp = [S.tile(st, [128, 512], "rp_tmp%d" % i) for i in range(4)]
            wi = 0
            si = 0
            ntile = self.nblk // 4
            for ti in range(ntile):
                t0 = ti * 512
                p0 = t0 % SEQ
                xk = xtok[ti % 2]
                xt_ = xT[ti % 2]
                self.load_xT(x_src, t0, xk, xt_)
                c = cs[ti % 2]
                S.dma(c[:, 0, :], dv(self.din["ret_cosT"][:, p0:p0 + 512]))
                S.dma(c[:, 1, :], dv(self.din["ret_sinT"][:, p0:p0 + 512]))
                for which, dst, scl in ((0, self.rt_qT, 1.0), (1, self.rt_kT, 256 ** -0.5)):
                    for grp in range(2):
                        wtile = wt[wi % 3]
                        wi += 1
                        self.load_w(wtile, w, which * 1024 + grp * 512, 512)
                        for j in range(2):
                            h = grp * 2 + j
                            ps1 = self.psum()
                            self.mm_feat(ps1, wtile, j * 256, 128, xt_)
                            ps2 = self.psum()
                            self.mm_feat(ps2, wtile, j * 256 + 128, 128, xt_)
                            a, b2, c3, d4 = tmp
                            S.stt(a[:, :], ps1[:, :], scl, c[:, 0, :], ALU.mult, ALU.mult)
                            S.stt(b2[:, :], ps2[:, :], scl, c[:, 1, :], ALU.mult, ALU.mult)
                            S.stt(c3[:, :], ps2[:, :], scl, c[:, 0, :], ALU.mult, ALU.mult)
                            S.stt(d4[:, :], ps1[:, :], scl, c[:, 1, :], ALU.mult, ALU.mult)
                            s1 = stg[si % 4]
                            s2 = stg[(si + 1) % 4]
                            si += 2
                            S.tt(s1[:, :], a[:, :], b2[:, :], ALU.subtract, eng="gpsimd")
                            S.tt(s2[:, :], c3[:, :], d4[:, :], ALU.add, eng="gpsimd")
                            S.dma(dv(dst[h, 0, :, t0:t0 + 512]), s1[:, :], q="gpsimd")
                            S.dma(dv(dst[h, 1, :, t0:t0 + 512]), s2[:, :], q="gpsimd")
                for c0, dst in ((2048, self.rt_v), (3072, self.g_s)):
                    for grp in range(2):
                        wtile = wt[wi % 3]
                        wi += 1
                        self.load_w(wtile, w, c0 + grp * 512, 512)
                        for b in range(4):
                            ps = self.psum()
                            self.mm_tok(ps, xt_, b, wtile, 0, 512)
                            sg = stg[si % 4]
                            si += 1
                            S.copy(sg[:, :], ps[:, :], eng=("vector" if si % 2 == 0 else "scalar"))
                            S.dma(dv(dst[t0 + b * 128:t0 + (b + 1) * 128, grp * 512:(grp + 1) * 512]), sg[:, :], q="gpsimd")
            S.barrier()

    def ret_mix(self):
        S = self.S
        nseq = max(1, self.nblk // NB)
        nb = min(NB, self.nblk)
        gs = [1.0 - 2.0 ** (-5.0 - h) for h in range(4)]
        with ExitStack() as st:
            QT = [[S.tile(st, [128, 2, 512], "rm_QT%d_%d" % (h, i), dma=True) for i in range(2)] for h in range(4)]
            KT = [[S.tile(st, [128, 2, 512], "rm_KT%d_%d" % (h, i), dma=True) for i in range(2)] for h in range(4)]
            VV = [[S.tile(st, [128, 4, 256], "rm_V%d_%d" % (h, i), dma=True) for i in range(2)] for h in range(4)]
            R = [S.tile(st, [128, 2, 256], "rm_R%d" % h) for h in range(4)]
            dpT = [S.tile(st, [128, 128], "rm_dp%d" % h, dma=True) for h in range(4)]
            xz = S.tile(st, [128, 8], "rm_xz", dma=True)
            S.dma(xz[:, :], dv(self.din["ret_xz"][:, :]))
            for h in range(4):
                S.dma(dpT[h][:, :], dv(self.din["ret_dpT"][h, :, :]))
            inT = [S.tile(st, [128, 128], "rm_inT%d" % i) for i in range(4)]
            kz = [S.tile(st, [128, 256], "rm_kz%d" % i) for i in range(4)]
            osb = [S.tile(st, [128, 256], "rm_o%d" % i, dma=True) for i in range(4)]
            cnt = 0
            for seq in range(nseq):
                for h in range(4):
                    S.memset(R[h][:, :, :], 0.0)
                for grp in range(nb // 4):
                    t0 = seq * SEQ + grp * 512
                    par = grp % 2
                    for h in range(4):
                        S.dma(QT[h][par][:, :, :], dv(self.rt_qT[h, :, :, t0:t0 + 512].rearrange("c p t -> p c t")))
                        S.dma(KT[h][par][:, :, :], dv(self.rt_kT[h, :, :, t0:t0 + 512].rearrange("c p t -> p c t")))
                        S.dma(VV[h][par][:, :, :],
                              dv(self.rt_v[t0:t0 + 512, h * 256:(h + 1) * 256].rearrange("(c p) e -> p c e", p=128)))
                    for n in range(4):
                        cs = slice(n * 128, (n + 1) * 128)
                        for h in range(4):
                            q_, k_, v_ = QT[h][par], KT[h][par], VV[h][par]
                            it = inT[cnt % 4]
                            kzt = kz[cnt % 4]
                            ot = osb[cnt % 4]
                            cnt += 1
                            ps_in = self.psum()
                            for dc in range(2):
                                S.mm(ps_in[:, 0:128], k_[:, dc, cs], q_[:, dc, cs], start=(dc == 0), stop=(dc == 1))
                            S.tt(it[:, :], ps_in[:, 0:128], dpT[h][:, :], ALU.mult)
                            ps_o = self.psum()
                            S.mm(ps_o[:, 0:256], it[:, :], v_[:, n, :], start=True, stop=False)
                            for dc in range(2):
                                S.mm(ps_o[:, 0:256], q_[:, dc, cs], R[h][:, dc, :], start=False, stop=(dc == 1))
                            S.act(ot[:, :], ps_o[:, 0:256], AF.Copy, scale=xz[:, h:h + 1])
                            r0 = t0 + n * 128
                            S.dma(dv(self.o_s[r0:r0 + 128, h * 256:(h + 1) * 256]), ot[:, :], q="gpsimd")
                            ps_k = self.psum()
                            for dc in range(2):
                                S.tr(ps_k[:, dc * 128:(dc + 1) * 128], k_[:, dc, cs], self.ident[:, :])
                            S.act(kzt[:, :], ps_k[:, 0:256], AF.Copy, scale=xz[:, 4 + h:5 + h])
                            ps_r = self.psum()
                            for dc in range(2):
                                S.mm(ps_r[:, dc * 256:(dc + 1) * 256], kzt[:, dc * 128:(dc + 1) * 128], v_[:, n, :])
                            Rf = R[h].v(R[h].h[:, :, :].rearrange("p c e -> p (c e)"))
                            S.stt(Rf, Rf, float(gs[h] ** 128), ps_r[:, :], ALU.mult, ALU.add)
            S.barrier()

    def ret_post_alloc(self, st):
        S = self.S
        gng = S.tile(st, [128, 1024], "rpo_g", dma=True)
        S.dma(gng[:, :], dv(self.din["ret_gn_g"].rearrange("(o n) -> o n", o=1).partition_broadcast(128)))
        sq = S.tile(st, [128, 1024], "rpo_sq")
        ss = [S.tile(st, [128, 8], "rpo_ss%d" % i) for i in range(2)]
        return (gng, sq, ss)

    def ret_post(self, extra, blk, ot, gt):
        S = self.S
        gng, sq, ss = extra
        s = ss[blk % 2]
        S.tt(sq[:, :], ot[:, :], ot[:, :], ALU.mult, eng="gpsimd")
        S.red(s[:, 0:4], sq.v(sq.h[:, :].rearrange("p (a c) -> p a c", c=256)))
        S.act(s[:, 0:4], s[:, 0:4], AF.Sqrt, bias=self.eps_tile(1e-6), scale=1.0 / 256)
        S.recip(s[:, 4:8], s[:, 0:4])
        o3 = ot.v(ot.h[:, :].rearrange("p (a c) -> p a c", c=256))
        S.tt(o3, o3, s.v(s.h[:, 4:8].unsqueeze(2).to_broadcast([128, 4, 256])), ALU.mult)
        S.tt(ot[:, :], ot[:, :], gng[:, :], ALU.mult, eng="gpsimd")

    def run_layers(self, layers, dbg=""):
        first = True
        for L in layers:
            src = self.x_in if first else self.out
            first = False
            name = ("fox", "dsa", "rwkv", "ret")[L]
            getattr(self, name + "_alloc")()
            if "noproj" not in dbg:
                getattr(self, name + "_proj")(src)
            if "nomix" not in dbg:
                getattr(self, name + "_mix")()
            if "noout" not in dbg:
                post = (self.ret_post_alloc, self.ret_post) if L == 3 else None
                self.phase_out(L, src, self.din[name + "_w_out"], post=post)
        self.S.barrier()


WEIGHT_SHAPES = {
    'ln_g': (4, 1024), 'ln_b': (4, 1024), 'fox_w_in': (1024, 4104), 'fox_b_f': (8,), 'fox_w_out': (1024, 1024),
    'dsa_w_in': (1024, 2792), 'dsa_kv_norm_g': (128,), 'dsa_w_uk': (8, 128, 96), 'dsa_w_uv': (8, 128, 128),
    'dsa_w_out': (1024, 1024), 'rwkv_mu': (6, 1024), 'rwkv_w_in': (1024, 4096), 'rwkv_w0': (1024,),
    'rwkv_w_lora_a': (1024, 64), 'rwkv_w_lora_b': (64, 1024), 'rwkv_a0': (1024,), 'rwkv_a_lora_a': (1024, 64),
    'rwkv_a_lora_b': (64, 1024), 'rwkv_k_k': (1024,), 'rwkv_k_a': (1024,), 'rwkv_r_k': (16, 64),
    'rwkv_gn_g': (1024,), 'rwkv_gn_b': (1024,), 'rwkv_w_out': (1024, 1024), 'ret_w_in': (1024, 4096),
    'ret_gn_g': (1024,), 'ret_w_out': (1024, 1024),
}


def make_consts():
    c = {}
    c["ident"] = np.eye(128, dtype=np.float32)
    i = np.arange(128)
    c["mask_le"] = (i[:, None] <= i[None, :]).astype(np.float32)
    c["mask_lt"] = (i[:, None] < i[None, :]).astype(np.float32)
    c["mask_ge"] = (i[:, None] >= i[None, :]).astype(np.float32)
    c["mask_gt"] = (i[:, None] > i[None, :]).astype(np.float32)
    inv = 1.0 / (10000.0 ** (np.arange(0, 256, 2, dtype=np.float32) / np.float32(256)))
    ang = np.arange(4096, dtype=np.float32)[:, None] * inv[None, :].astype(np.float32)
    c["ret_cosT"] = np.ascontiguousarray(np.cos(ang).T.astype(np.float32))
    c["ret_sinT"] = np.ascontiguousarray(np.sin(ang).T.astype(np.float32))
    lg = np.log1p(-(2.0 ** (-5.0 - np.arange(4, dtype=np.float64))))
    pos = np.arange(128, dtype=np.float64)
    dp = np.zeros((4, 128, 128), np.float32)
    xz = np.zeros((128, 8), np.float32)
    for h in range(4):
        dp[h] = (np.exp(-(pos[:, None] + 1.0) * lg[h]) * (pos[:, None] <= pos[None, :])).astype(np.float32)
        xz[:, h] = np.exp((pos + 1.0) * lg[h])
        xz[:, 4 + h] = np.exp((127.0 - pos) * lg[h])
    c["ret_dpT"] = dp
    c["ret_xz"] = xz
    def rope_fm(rot):
        inv = 1.0 / (np.float32(500000.0) ** (np.arange(0, rot, 2, dtype=np.float32) / np.float32(rot)))
        ang = np.arange(4096, dtype=np.float32)[:, None] * inv[None, :].astype(np.float32)
        cs, sn = np.cos(ang).T.astype(np.float32), np.sin(ang).T.astype(np.float32)
        return np.ascontiguousarray(np.concatenate([cs, cs], 0)), np.ascontiguousarray(np.concatenate([-sn, sn], 0))
    c["dsa_ropeA_c"], c["dsa_ropeA_s"] = rope_fm(32)
    ic, isn = rope_fm(16)
    c["dsa_ropeI_c"] = np.ascontiguousarray(np.concatenate([ic, np.ones_like(ic)], 0))
    c["dsa_ropeI_s"] = np.ascontiguousarray(np.concatenate([isn, np.zeros_like(isn)], 0))
    c["negmask"] = np.where(i[None, :] <= i[:, None], 0.0, -1e30).astype(np.float32)
    return c


_CACHE = {}


def build_program(layers=(0, 1, 2, 3), nblk=TOK // 128, dbg=""):
    key = (tuple(layers), nblk, dbg)
    if key in _CACHE:
        return _CACHE[key]
    consts = make_consts()
    nc = bass.Bass("TRN2", target_bir_lowering=False)
    with ExitStack() as st:
        k = K(nc, st, WEIGHT_SHAPES, {n: a.shape for n, a in consts.items()}, nblk=nblk)
        k.run_layers(layers, dbg)
        print("instructions", k.S.n_ins, "waits", k.S.n_wait)
    _CACHE[key] = (nc, consts)
    return nc, consts


def kernel(**inputs):
    x = np.ascontiguousarray(np.asarray(inputs["x"], dtype=np.float32))
    nc, consts = build_program()
    base = {n: np.ascontiguousarray(np.asarray(inputs[n], dtype=np.float32)) for n in WEIGHT_SHAPES}
    base.update(consts)
    in_maps = []
    for c in range(8):
        m = dict(base)
        m["x"] = x[2 * c:2 * c + 2].reshape(TOK, D)
        in_maps.append(m)
    res = run_bass_kernel_spmd(nc, in_maps, core_ids=list(range(8)))
    out = np.stack([r["out"].reshape(NSEQ, SEQ, D) for r in res.results], axis=0).reshape(16, SEQ, D)
    return out.astype(np.float32)
```

```python
import numpy as np
from contextlib import ExitStack
import concourse.bass as bass
import concourse.mybir as mybir
from concourse.bass_utils import run_bass_kernel_spmd

F32 = mybir.dt.float32
ALU = mybir.AluOpType
AF = mybir.ActivationFunctionType
AX = mybir.AxisListType


class Res:
    __slots__ = ("w", "r", "dsem", "excl")

    def __init__(self):
        self.w = None
        self.r = {}
        self.dsem = None
        self.excl = False


class V:
    __slots__ = ("ap", "res")

    def __init__(self, ap, res):
        self.ap = ap
        self.res = res


class Tile:
    def __init__(self, handle, res=None):
        self.h = handle
        self.res = res if res is not None else Res()

    def __getitem__(self, key):
        return V(self.h[key], self.res)

    def v(self, ap):
        return V(ap, self.res)


class Sched:
    ENG = ("tensor", "vector", "scalar", "gpsimd", "sync")

    def __init__(self, nc, stack, n_dma_sems=92):
        self.nc = nc
        self.stack = stack
        self.e = {"tensor": nc.tensor, "vector": nc.vector, "scalar": nc.scalar,
                  "gpsimd": nc.gpsimd, "sync": nc.sync}
        self.sem = {}
        self.tot = {}
        for e in self.ENG:
            self.sem[e] = stack.enter_context(nc.semaphore("s_" + e))
            self.tot[e] = 0
        self.bar = stack.enter_context(nc.semaphore("s_bar"))
        self.bar_n = 0
        self.free_dma = []
        for i in range(n_dma_sems):
            k = "d%d" % i
            self.sem[k] = stack.enter_context(nc.semaphore(k))
            self.tot[k] = 0
            self.free_dma.append(k)
        self.known = {e: {} for e in self.ENG}
        self.n_ins = 0
        self.n_wait = 0

    def tile(self, stack, shape, name, dtype=F32, dma=False):
        self.n_tiles = getattr(self, "n_tiles", 0) + 1
        h = stack.enter_context(self.nc.sbuf_tensor("%s_%d" % (name, self.n_tiles), list(shape), dtype))
        t = Tile(h)
        if dma:
            t.res.dsem = {}
            stack.callback(self._release_dsems, t.res)
        return t

    def _release_dsems(self, res):
        for k in res.dsem.values():
            self.free_dma.append(k)
        res.dsem = {}

    def _dsem(self, res, q):
        kind = "sw" if q == "gpsimd" else "hw"
        k = res.dsem.get(kind)
        if k is None:
            k = self.free_dma.pop()
            res.dsem[kind] = k
        return k

    def psum(self, stack, name, shape=(128, 512), dtype=F32):
        h = stack.enter_context(self.nc.psum_tensor(name, list(shape), dtype))
        t = Tile(h)
        t.res.excl = True
        return t

    def _waits(self, eng, reads, writes):
        deps = {}
        for v in reads:
            if v is None or v.res is None:
                continue
            w = v.res.w
            if w is not None:
                if deps.get(w[0], 0) < w[1]:
                    deps[w[0]] = w[1]
            if v.res.excl:
                for k, val in v.res.r.items():
                    if k != eng and deps.get(k, 0) < val:
                        deps[k] = val
        for v in writes:
            if v is None or v.res is None:
                continue
            w = v.res.w
            if w is not None:
                if deps.get(w[0], 0) < w[1]:
                    deps[w[0]] = w[1]
            for k, val in v.res.r.items():
                if deps.get(k, 0) < val:
                    deps[k] = val
        kn = self.known[eng]
        for k, val in deps.items():
            if k[0] == "d":
                val = self.tot[k]
            elif eng == "tensor" and k == "tensor":
                continue
            if kn.get(k, 0) >= val:
                continue
            kn[k] = val
            self.e[eng].wait_ge(self.sem[k], val)
            self.n_wait += 1

    def emit(self, eng, reads, writes, fn):
        self._waits(eng, reads, writes)
        ins = fn(self.e[eng])
        self.tot[eng] += 1
        n = self.tot[eng]
        ins.then_inc(self.sem[eng], 1)
        self.n_ins += 1
        for v in reads:
            if v is not None and v.res is not None:
                if v.res.r.get(eng, 0) < n:
                    v.res.r[eng] = n
        for v in writes:
            if v is not None and v.res is not None:
                v.res.w = (eng, n)
                v.res.r = {}

    def dma(self, out, in_, q="sync", **kw):
        self._waits(q, [in_], [out])
        k = None
        if out.res is not None and out.res.dsem is not None:
            k = self._dsem(out.res, q)
        elif in_.res is not None and in_.res.dsem is not None:
            k = self._dsem(in_.res, q)
        assert k is not None, "dma needs a resource with a dma semaphore"
        ins = self.e[q].dma_start(out=out.ap, in_=in_.ap, **kw)
        self.tot[k] += 16
        n = self.tot[k]
        ins.then_inc(self.sem[k], 16)
        self.n_ins += 1
        if in_.res is not None:
            if in_.res.r.get(k, 0) < n:
                in_.res.r[k] = n
        if out.res is not None:
            out.res.w = (k, n)
            out.res.r = {}

    def barrier(self):
        sy = self.e["sync"]
        kn = self.known["sync"]
        for k, val in self.tot.items():
            if val > 0 and kn.get(k, 0) < val:
                sy.wait_ge(self.sem[k], val)
                kn[k] = val
        self.bar_n += 1
        sy.sem_inc(self.bar, 1)
        for e in self.ENG:
            if e != "sync":
                self.e[e].wait_ge(self.bar, self.bar_n)
            for k, val in self.tot.items():
                self.known[e][k] = val

    def mm(self, out, lhsT, rhs, start=True, stop=True):
        self.emit("tensor", [lhsT, rhs], [out],
                  lambda e: e.matmul(out.ap, lhsT.ap, rhs.ap, start=start, stop=stop))

    def tr(self, out, in_, ident):
        self.emit("tensor", [in_, ident], [out],
                  lambda e: e.transpose(out.ap, in_.ap, ident.ap))

    def act(self, out, in_, func, bias=None, scale=1.0, eng="scalar"):
        rd = [in_]
        kw = {}
        if isinstance(bias, V):
            rd.append(bias)
            kw["bias"] = bias.ap
        elif bias is not None:
            kw["bias"] = bias
        if isinstance(scale, V):
            rd.append(scale)
            kw["scale"] = scale.ap
        else:
            kw["scale"] = scale
        self.emit(eng, rd, [out], lambda e: e.activation(out.ap, in_.ap, func, **kw))

    def tt(self, out, in0, in1, op, eng="vector"):
        self.emit(eng, [in0, in1], [out],
                  lambda e: e.tensor_tensor(out.ap, in0.ap, in1.ap, op))

    def ts(self, out, in0, s1, s2=None, op0=ALU.mult, op1=None, eng="vector"):
        rd = [in0]
        a1 = s1
        a2 = s2
        if isinstance(s1, V):
            rd.append(s1)
            a1 = s1.ap
        if isinstance(s2, V):
            rd.append(s2)
            a2 = s2.ap
        if op1 is None:
            self.emit(eng, rd, [out], lambda e: e.tensor_scalar(out.ap, in0.ap, a1, None, op0))
        else:
            self.emit(eng, rd, [out], lambda e: e.tensor_scalar(out.ap, in0.ap, a1, a2, op0, op1))

    def stt(self, out, in0, scalar, in1, op0, op1, eng="vector"):
        rd = [in0, in1]
        a = scalar
        if isinstance(scalar, V):
            rd.append(scalar)
            a = scalar.ap
        self.emit(eng, rd, [out],
                  lambda e: e.scalar_tensor_tensor(out.ap, in0.ap, a, in1.ap, op0, op1))

    def copy(self, out, in_, eng="vector"):
        if eng == "scalar":
            self.emit(eng, [in_], [out], lambda e: e.copy(out.ap, in_.ap))
        else:
            self.emit(eng, [in_], [out], lambda e: e.tensor_copy(out.ap, in_.ap))

    def red(self, out, in_, op=ALU.add, axis=AX.X, eng="vector"):
        self.emit(eng, [in_], [out], lambda e: e.tensor_reduce(out.ap, in_.ap, axis, op))

    def memset(self, out, val, eng="vector"):
        self.emit(eng, [], [out], lambda e: e.memset(out.ap, val))

    def recip(self, out, in_):
        self.emit("vector", [in_], [out], lambda e: e.reciprocal(out.ap, in_.ap))

    def vmax(self, out, in_):
        self.emit("vector", [in_], [out], lambda e: e.max(out.ap, in_.ap))

    def match_replace(self, out, in_to_replace, in_values, imm):
        self.emit("vector", [in_to_replace, in_values], [out],
                  lambda e: e.match_replace(out.ap, in_to_replace.ap, in_values.ap, imm))

D = 1024
SEQ = 4096
NSEQ = 2
TOK = NSEQ * SEQ
NB = SEQ // 128
LN_EPS = 1e-5
DN_ALPHA = (2 * 4) ** 0.25


def dv(ap):
    return V(ap, None)


class K:
    def __init__(self, nc, st, weights_meta, consts_meta, nblk=TOK // 128):
        self.nc = nc
        self.st = st
        self.S = Sched(nc, st)
        self.nblk = nblk
        self.din = {}
        for name, shape in list(weights_meta.items()) + list(consts_meta.items()):
            self.din[name] = nc.dram_tensor(name, list(shape), F32, kind="ExternalInput").ap()
        self.x_in = nc.dram_tensor("x", [TOK, D], F32, kind="ExternalInput").ap()
        self.out = nc.dram_tensor("out", [TOK, D], F32, kind="ExternalOutput").ap()
        self.o_s = nc.dram_tensor("o_s", [TOK, D], F32).ap()
        self.g_s = nc.dram_tensor("g_s", [TOK, D], F32).ap()
        S = self.S
        self.ps = [S.psum(st, "psb%d" % i) for i in range(8)]
        self.ps_i = 0
        self.pa_i = 0
        self.ident = S.tile(st, [128, 128], "ident", dma=True)
        S.dma(self.ident[:, :], dv(self.din["ident"][:, :]))
        self._eps = {}
        for val in (LN_EPS, 1.0, 1e-6, 64e-5, 1e-12, 0.0, -1.0):
            t = S.tile(st, [128, 1], "eps")
            S.memset(t[:, :], float(val))
            self._eps[val] = t

    def psum(self):
        p = self.ps[2 + self.ps_i % 6]
        self.ps_i += 1
        return p

    def psum_acc(self):
        p = self.ps[self.pa_i % 2]
        self.pa_i += 1
        return p

    def stop(self, n):
        import os
        if int(os.environ.get("KSTOP", "99")) == n:
            self.S.barrier()
            return True
        return False

    def scratch(self, name, shape):
        return self.nc.dram_tensor(name, list(shape), F32).ap()

    def load_xT(self, src, t0, xtok, xT, nb=4):
        S = self.S
        S.dma(xtok[:, 0:nb, :], dv(src[t0:t0 + nb * 128, :].rearrange("(b p) d -> p b d", p=128)))
        for kc in range(8):
            ps = self.psum()
            for b in range(nb):
                S.tr(ps[:, b * 128:(b + 1) * 128], xtok[:, b, kc * 128:(kc + 1) * 128], self.ident[:, :])
            S.copy(xT[:, kc, 0:nb * 128], ps[:, 0:nb * 128], eng=("vector" if kc % 2 == 0 else "scalar"))

    def load_w(self, wt, w_ap, c0, n, q="sync"):
        self.S.dma(wt[:, :, 0:n], dv(w_ap.rearrange("(kc p) n -> p kc n", p=128)[:, :, c0:c0 + n]), q=q)

    def mm_feat(self, ps, wt, c0, m, xT, ntok=512):
        for kc in range(8):
            self.S.mm(ps[0:m, 0:ntok], wt[:, kc, c0:c0 + m], xT[:, kc, 0:ntok], start=(kc == 0), stop=(kc == 7))

    def mm_tok(self, ps, xT, blk, wt, c0, n):
        for kc in range(8):
            self.S.mm(ps[:, 0:n], xT[:, kc, blk * 128:(blk + 1) * 128], wt[:, kc, c0:c0 + n], start=(kc == 0), stop=(kc == 7))

    def phase_out(self, layer, x_src, w_out, post=None):
        S = self.S
        with ExitStack() as st:
            wo = S.tile(st, [128, 8, 1024], "wo", dma=True)
            S.dma(wo[:, :, :], dv(w_out.rearrange("(kc p) n -> p kc n", p=128)))
            lng = S.tile(st, [128, 1024], "lng", dma=True)
            lnb = S.tile(st, [128, 1024], "lnb", dma=True)
            S.dma(lng[:, :], dv(self.din["ln_g"][layer:layer + 1, :].partition_broadcast(128)))
            S.dma(lnb[:, :], dv(self.din["ln_b"][layer:layer + 1, :].partition_broadcast(128)))
            NBUF = 2
            ot = [S.tile(st, [128, 1024], "po_o%d" % i, dma=True) for i in range(NBUF)]
            gt = [S.tile(st, [128, 1024], "po_g%d" % i, dma=True) for i in range(NBUF)]
            xt = [S.tile(st, [128, 1024], "po_x%d" % i, dma=True) for i in range(NBUF)]
            zT = [S.tile(st, [128, 8, 128], "po_zT%d" % i) for i in range(NBUF)]
            yt = [S.tile(st, [128, 1024], "po_y%d" % i, dma=True) for i in range(NBUF)]
            sq = S.tile(st, [128, 1024], "po_sq")
            stat = [S.tile(st, [128, 8], "po_st%d" % i) for i in range(NBUF)]
            extra = post[0](st) if post is not None else None
            for blk in range(self.nblk):
                i = blk % NBUF
                r0 = blk * 128
                S.dma(ot[i][:, :], dv(self.o_s[r0:r0 + 128, :]))
                S.dma(gt[i][:, :], dv(self.g_s[r0:r0 + 128, :]))
                S.dma(xt[i][:, :], dv(x_src[r0:r0 + 128, :]))
                if post is not None:
                    post[1](extra, blk, ot[i], gt[i])
                S.act(gt[i][:, :], gt[i][:, :], AF.Silu)
                S.tt(ot[i][:, :], ot[i][:, :], gt[i][:, :], ALU.mult, eng="gpsimd")
                for half in range(2):
                    ps = self.psum()
                    for j in range(4):
                        kc = half * 4 + j
                        S.tr(ps[:, j * 128:(j + 1) * 128], ot[i][:, kc * 128:(kc + 1) * 128], self.ident[:, :])
                    S.copy(zT[i][:, half * 4:(half + 1) * 4, :],
                           ps.v(ps.h[:, :].rearrange("p (j t) -> p j t", t=128)), eng=("vector" if half == 0 else "scalar"))
                y = yt[i]
                for half in range(2):
                    ps = self.psum()
                    for kc in range(8):
                        S.mm(ps[:, :], zT[i][:, kc, :], wo[:, kc, half * 512:(half + 1) * 512], start=(kc == 0), stop=(kc == 7))
                    S.stt(y[:, half * 512:(half + 1) * 512], xt[i][:, half * 512:(half + 1) * 512], DN_ALPHA, ps[:, :], ALU.mult, ALU.add)
                sv = stat[i]
                S.red(sv[:, 0:1], y[:, :])
                S.ts(sv[:, 1:2], sv[:, 0:1], -1.0 / D, None, op0=ALU.mult)
                S.ts(y[:, :], y[:, :], sv[:, 1:2], None, op0=ALU.add)
                S.tt(sq[:, :], y[:, :], y[:, :], ALU.mult, eng="gpsimd")
                S.red(sv[:, 2:3], sq[:, :])
                S.act(sv[:, 3:4], sv[:, 2:3], AF.Sqrt, bias=self.eps_tile(LN_EPS), scale=1.0 / D)
                S.recip(sv[:, 4:5], sv[:, 3:4])
                S.stt(y[:, :], y[:, :], sv[:, 4:5], lng[:, :], ALU.mult, ALU.mult)
                S.tt(y[:, :], y[:, :], lnb[:, :], ALU.add, eng="gpsimd")
                S.dma(dv(self.out[r0:r0 + 128, :]), y[:, :], q="gpsimd")
            S.barrier()

    def eps_tile(self, val):
        return self._eps[val][:, 0:1]

    def fox_alloc(self):
        self.fx_qT = self.scratch("fx_qT", [8, 128, TOK])
        self.fx_kT = self.scratch("fx_kT", [8, 128, TOK])
        self.fx_v = self.scratch("fx_v", [TOK, D])
        self.fx_c = self.scratch("fx_c", [NSEQ, 128, NB * 8])
        self.fx_cr = self.scratch("fx_cr", [NSEQ, 128, NB * 8])

    def fox_proj(self, x_src):
        S = self.S
        w = self.din["fox_w_in"]
        with ExitStack() as st:
            xtok = [S.tile(st, [128, 4, 1024], "fp_xtok%d" % i, dma=True) for i in range(2)]
            xT = [S.tile(st, [128, 8, 512], "fp_xT%d" % i) for i in range(2)]
            wt = [S.tile(st, [128, 8, 512], "fp_w%d" % i, dma=True) for i in range(3)]
            wf = S.tile(st, [128, 8, 8], "fp_wf", dma=True)
            self.load_w(wf, w, 3072, 8)
            stg = [S.tile(st, [128, 512], "fp_stg%d" % i, dma=True) for i in range(4)]
            lf = S.tile(st, [128, NB, 8], "fp_lf")
            bf = S.tile(st, [128, 8], "fp_bf", dma=True)
            S.dma(bf[:, :], dv(self.din["fox_b_f"].rearrange("(o n) -> o n", o=1).partition_broadcast(128)))
            tri = S.tile(st, [128, 128], "fp_tri", dma=True)
            S.dma(tri[:, :], dv(self.din["mask_le"][:, :]))
            ones = S.tile(st, [128, 128], "fp_ones")
            S.memset(ones[:, :], 1.0)
            cw = S.tile(st, [128, NB, 8], "fp_cw", dma=True)
            cr = S.tile(st, [128, NB, 8], "fp_cr", dma=True)
            wi = 0
            si = 0
            ntile = self.nblk // 4
            for ti in range(ntile):
                t0 = ti * 512
                seq, tl = divmod(ti, NB // 4)
                xk = xtok[ti % 2]
                xt_ = xT[ti % 2]
                self.load_xT(x_src, t0, xk, xt_)
                for which, dst in ((0, self.fx_qT), (1, self.fx_kT)):
                    for grp in range(2):
                        wtile = wt[wi % 3]
                        wi += 1
                        self.load_w(wtile, w, which * 1024 + grp * 512, 512)
                        for j in range(4):
                            h = grp * 4 + j
                            ps = self.psum()
                            self.mm_feat(ps, wtile, j * 128, 128, xt_)
                            sg = stg[si % 4]
                            si += 1
                            S.copy(sg[:, :], ps[:, :], eng=("vector" if si % 2 == 0 else "scalar"))
                            S.dma(dv(dst[h, :, t0:t0 + 512]), sg[:, :], q="gpsimd")
                for c0, dst in ((2048, self.fx_v), (3080, self.g_s)):
                    for grp in range(2):
                        wtile = wt[wi % 3]
                        wi += 1
                        self.load_w(wtile, w, c0 + grp * 512, 512)
                        for b in range(4):
                            ps = self.psum()
                            self.mm_tok(ps, xt_, b, wtile, 0, 512)
                            sg = stg[si % 4]
                            si += 1
                            S.copy(sg[:, :], ps[:, :], eng=("vector" if si % 2 == 0 else "scalar"))
                            S.dma(dv(dst[t0 + b * 128:t0 + (b + 1) * 128, grp * 512:(grp + 1) * 512]), sg[:, :], q="gpsimd")
                ps = self.psum()
                for b in range(4):
                    for kc in range(8):
                        S.mm(ps[:, b * 8:(b + 1) * 8], xt_[:, kc, b * 128:(b + 1) * 128], wf[:, kc, :], start=(kc == 0), stop=(kc == 7))
                for b in range(4):
                    S.tt(lf[:, tl * 4 + b, :], ps[:, b * 8:(b + 1) * 8], bf[:, :], ALU.add)
                if tl == NB // 4 - 1 or ti == ntile - 1:
                    lf2 = lf.v(lf.h[:, :, :].rearrange("p b h -> p (b h)"))
                    S.act(lf2, lf2, AF.Exp, scale=-1.0)
                    S.act(lf2, lf2, AF.Ln, bias=self.eps_tile(1.0), scale=1.0)
                    S.ts(lf2, lf2, -1.0, None, op0=ALU.mult)
                    psw = self.psum()
                    S.mm(psw[:, 0:NB * 8], tri[:, :], lf2)
                    pst = self.psum()
                    S.mm(pst[:, 0:NB * 8], ones[:, :], lf2)
                    S.memset(cr[:, 0, :], 0.0)
                    for b in range(1, NB):
                        S.tt(cr[:, b, :], cr[:, b - 1, :], pst[:, (b - 1) * 8:b * 8], ALU.add)
                    S.tt(cw.v(cw.h[:, :, :].rearrange("p b h -> p (b h)")), psw[:, 0:NB * 8],
                         cr.v(cr.h[:, :, :].rearrange("p b h -> p (b h)")), ALU.add)
                    S.dma(dv(self.fx_c[seq, :, :]), cw.v(cw.h[:, :, :].rearrange("p b h -> p (b h)")), q="gpsimd")
                    S.dma(dv(self.fx_cr[seq, :, :]), cr.v(cr.h[:, :, :].rearrange("p b h -> p (b h)")), q="gpsimd")
            S.barrier()

    def fox_mix(self):
        S = self.S
        nseq = max(1, self.nblk // NB)
        nb = min(NB, self.nblk)
        SCALE = 128 ** -0.5
        with ExitStack() as st:
            KT = [S.tile(st, [128, SEQ], "fm_KT%d" % i, dma=True) for i in range(2)]
            QT = [S.tile(st, [128, SEQ], "fm_QT%d" % i, dma=True) for i in range(2)]
            VA = [S.tile(st, [128, NB, 132], "fm_VA%d" % i, dma=True) for i in range(2)]
            OS = [S.tile(st, [128, NB, 128], "fm_OS%d" % i, dma=True) for i in range(2)]
            cw = S.tile(st, [128, NB, 8], "fm_cw", dma=True)
            cr = S.tile(st, [128, NB, 8], "fm_cr", dma=True)
            bias = [S.tile(st, [128, NB, NB], "fm_bias%d" % i) for i in range(2)]
            PT = [S.tile(st, [128, 128], "fm_PT%d" % i) for i in range(12)]
            rc = [S.tile(st, [128, 1], "fm_rc%d" % i) for i in range(2)]
            mle = S.tile(st, [128, 128], "fm_mle", dma=True)
            S.dma(mle[:, :], dv(self.din["mask_le"][:, :]))
            for i in range(2):
                S.memset(VA[i][:, :, 128:129], 1.0)
            it = 0
            pti = 0
            for seq in range(nseq):
                S.dma(cw.v(cw.h[:, :, :].rearrange("p b h -> p (b h)")), dv(self.fx_c[seq, :, :]))
                S.dma(cr.v(cr.h[:, :, :].rearrange("p b h -> p (b h)")), dv(self.fx_cr[seq, :, :]))
                for h in range(8):
                    i = it % 2
                    it += 1
                    c0 = seq * SEQ
                    S.dma(KT[i][:, 0:nb * 128], dv(self.fx_kT[h, :, c0:c0 + nb * 128]))
                    S.dma(QT[i][:, 0:nb * 128], dv(self.fx_qT[h, :, c0:c0 + nb * 128]))
                    S.dma(VA[i][:, 0:nb, 0:128],
                          dv(self.fx_v[c0:c0 + nb * 128, h * 128:(h + 1) * 128].rearrange("(b p) d -> p b d", p=128)))
                    bt = bias[i]
                    for qb in range(nb):
                        S.ts(bt[:, qb, 0:qb + 1], cw[:, 0:qb + 1, h], -1.0, cr[:, qb, h:h + 1], op0=ALU.mult, op1=ALU.add)
                    groups = []
                    for qb in range(nb):
                        nk = qb + 1
                        for g0 in range(0, nk, 4):
                            groups.append((qb, g0, min(nk, g0 + 4)))
                    state = {}

                    def stage1(gi):
                        nonlocal pti
                        qb, g0, g1 = groups[gi]
                        pss = self.psum()
                        pts = [PT[(pti + j) % 12] for j in range(4)]
                        pti += 4
                        for kb in range(g0, g1):
                            S.mm(pss[:, (kb - g0) * 128:(kb - g0 + 1) * 128], KT[i][:, kb * 128:(kb + 1) * 128],
                                 QT[i][:, qb * 128:(qb + 1) * 128])
                        for kb in range(g0, g1):
                            S.act(pts[kb - g0][:, :], pss[:, (kb - g0) * 128:(kb - g0 + 1) * 128], AF.Exp,
                                  bias=bt[:, qb, kb:kb + 1], scale=SCALE)
                        if g1 == qb + 1:
                            S.tt(pts[qb - g0][:, :], pts[qb - g0][:, :], mle[:, :], ALU.mult, eng="gpsimd")
                        state[gi] = pts

                    def stage2(gi):
                        qb, g0, g1 = groups[gi]
                        nk = qb + 1
                        pts = state.pop(gi)
                        if g0 == 0:
                            state["pso"] = self.psum_acc()
                        pso = state["pso"]
                        for kb in range(g0, g1):
                            S.mm(pso[:, 0:129], pts[kb - g0][:, :], VA[i][:, kb, 0:129], start=(kb == 0), stop=(kb == nk - 1))
                        if g1 == nk:
                            r = rc[qb % 2]
                            S.recip(r[:, :], pso[:, 128:129])
                            S.ts(OS[i][:, qb, :], pso[:, 0:128], r[:, 0:1], None, op0=ALU.mult)

                    LA = 2
                    for gi in range(len(groups) + LA):
                        if gi < len(groups):
                            stage1(gi)
                        if gi - LA >= 0:
                            stage2(gi - LA)
                    S.dma(dv(self.o_s[c0:c0 + nb * 128, h * 128:(h + 1) * 128].rearrange("(b p) d -> p b d", p=128)),
                          OS[i][:, 0:nb, :], q="gpsimd")
            S.barrier()

    def dsa_alloc(self):
        self.ds_qlT = self.scratch("ds_qlT", [8, 128, TOK])
        self.ds_qcT = self.scratch("ds_qcT", [8, 96, TOK])
        self.ds_kcT = self.scratch("ds_kcT", [96, TOK])
        self.ds_ckv = self.scratch("ds_ckv", [TOK, 128])
        self.ds_ckvT = self.scratch("ds_ckvT", [128, TOK])
        self.ds_wi = self.scratch("ds_wi", [TOK, 16])

    def dsa_proj(self, x_src):
        S = self.S
        w = self.din["dsa_w_in"]
        wr = w.rearrange("(kc p) n -> p kc n", p=128)
        with ExitStack() as st:
            xtok = [S.tile(st, [128, 4, 1024], "dp_xtok%d" % i, dma=True) for i in range(2)]
            xT = [S.tile(st, [128, 8, 512], "dp_xT%d" % i) for i in range(2)]
            wt = [S.tile(st, [128, 8, 512], "dp_w%d" % i, dma=True) for i in range(2)]
            wsm = S.tile(st, [128, 8, 744], "dp_wsm", dma=True)
            S.dma(wsm[:, :, :], dv(wr[:, :, 1024:1768]))
            wpq = S.tile(st, [128, 8, 8, 32], "dp_wpq", dma=True)
            wpi = S.tile(st, [128, 8, 8, 32], "dp_wpi", dma=True)
            wpk = S.tile(st, [128, 8, 64], "dp_wpk", dma=True)
            S.memset(wpi[:, :, :, :], 0.0)
            S.memset(wpk[:, :, :], 0.0)
            for h in range(8):
                S.dma(wpq[:, :, h, 0:16], dv(wr[:, :, h * 128 + 16:h * 128 + 32]))
                S.dma(wpq[:, :, h, 16:32], dv(wr[:, :, h * 128:h * 128 + 16]))
                S.dma(wpi[:, :, h, 0:8], dv(wr[:, :, 1184 + h * 64 + 8:1184 + h * 64 + 16]))
                S.dma(wpi[:, :, h, 8:16], dv(wr[:, :, 1184 + h * 64:1184 + h * 64 + 8]))
            S.dma(wpk[:, :, 0:16], dv(wr[:, :, 1152 + 16:1152 + 32]))
            S.dma(wpk[:, :, 16:32], dv(wr[:, :, 1152:1152 + 16]))
            S.dma(wpk[:, :, 32:40], dv(wr[:, :, 1696 + 8:1696 + 16]))
            S.dma(wpk[:, :, 40:48], dv(wr[:, :, 1696:1696 + 8]))
            wuk = S.tile(st, [128, 8, 96], "dp_wuk", dma=True)
            S.dma(wuk[:, :, :], dv(self.din["dsa_w_uk"].rearrange("h c n -> c h n")))
            wukT = S.tile(st, [96, 8, 128], "dp_wukT")
            for h in range(8):
                ps = self.psum()
                S.tr(ps[0:96, 0:128], wuk[:, h, :], self.ident[:, :])
                S.copy(wukT[:, h, :], ps[0:96, 0:128])
            if self.stop(1):
                return
            kvg = S.tile(st, [128, 128], "dp_kvg", dma=True)
            S.dma(kvg[:, :], dv(self.din["dsa_kv_norm_g"].rearrange("(o n) -> o n", o=1).partition_broadcast(128)))
            ropeA = [S.tile(st, [32, 2, 512], "dp_ropeA%d" % i, dma=True) for i in range(2)]
            ropeI = [S.tile(st, [32, 2, 512], "dp_ropeI%d" % i, dma=True) for i in range(2)]
            stg = [S.tile(st, [128, 512], "dp_stg%d" % i, dma=True) for i in range(4)]
            qn = [S.tile(st, [96, 512], "dp_qn%d" % i) for i in range(2)]
            qc = [S.tile(st, [96, 512], "dp_qc%d" % i, dma=True) for i in range(3)]
            t32 = [S.tile(st, [32, 512], "dp_t32%d" % i) for i in range(2)]
            csb = [S.tile(st, [128, 128], "dp_csb%d" % i, dma=True) for i in range(2)]
            csq = S.tile(st, [128, 128], "dp_csq")
            cst = [S.tile(st, [128, 8], "dp_cst%d" % i) for i in range(2)]
            cT = [S.tile(st, [128, 128], "dp_cT%d" % i, dma=True) for i in range(2)]
            wis = [S.tile(st, [128, 16], "dp_wi%d" % i, dma=True) for i in range(2)]
            wi_ = 0
            si = 0
            qi_ = 0
            ntile = self.nblk // 4
            CI = (64 ** -0.5) * (8 ** -0.5)
            for ti in range(ntile):
                t0 = ti * 512
                p0 = t0 % SEQ
                xk = xtok[ti % 2]
                xt_ = xT[ti % 2]
                self.load_xT(x_src, t0, xk, xt_)
                rA = ropeA[ti % 2]
                rI = ropeI[ti % 2]
                S.dma(rA[:, 0, :], dv(self.din["dsa_ropeA_c"][:, p0:p0 + 512]))
                S.dma(rA[:, 1, :], dv(self.din["dsa_ropeA_s"][:, p0:p0 + 512]))
                S.dma(rI[:, 0, :], dv(self.din["dsa_ropeI_c"][:, p0:p0 + 512]))
                S.dma(rI[:, 1, :], dv(self.din["dsa_ropeI_s"][:, p0:p0 + 512]))

                def rope32(dst, psA, psB):
                    a, b2 = t32
                    S.tt(a[:, :], psA[0:32, :], rA[:, 0, :], ALU.mult)
                    S.tt(b2[:, :], psB[0:32, :], rA[:, 1, :], ALU.mult)
                    S.tt(dst, a[:, :], b2[:, :], ALU.add, eng="gpsimd")

                if self.stop(2):
                    return
                for grp in range(2):
                    wtile = wt[wi_ % 2]
                    wi_ += 1
                    self.load_w(wtile, w, grp * 512, 512)
                    for j in range(4):
                        h = grp * 4 + j
                        ps = self.psum()
                        self.mm_feat(ps, wtile, j * 128 + 32, 96, xt_)
                        qnt = qn[h % 2]
                        S.copy(qnt[:, :], ps[0:96, :], eng="scalar")
                        ps2 = self.psum()
                        S.mm(ps2[:, :], wukT[:, h, :], qnt[:, :])
                        sg = stg[si % 4]
                        si += 1
                        S.copy(sg[:, :], ps2[:, :])
                        S.dma(dv(self.ds_qlT[h, :, t0:t0 + 512]), sg[:, :], q="gpsimd")
                        if self.stop(21):
                            return
                        qct = qc[qi_ % 3]
                        qi_ += 1
                        psA = self.psum()
                        self.mm_feat(psA, wtile, j * 128, 32, xt_)
                        psB = self.psum()
                        for kc in range(8):
                            S.mm(psB[0:32, :], wpq[:, kc, h, :], xt_[:, kc, :], start=(kc == 0), stop=(kc == 7))
                        rope32(qct[0:32, :], psA, psB)
                        S.dma(dv(self.ds_qcT[h, 64:96, t0:t0 + 512]), qct[0:32, :], q="gpsimd")
                        if self.stop(22):
                            return
                        qct = qc[qi_ % 3]
                        qi_ += 1
                        psC = self.psum()
                        self.mm_feat(psC, wsm, 160 + h * 64, 64, xt_)
                        psD = self.psum()
                        for kc in range(8):
                            S.mm(psD[0:32, :], wpi[:, kc, h, :], xt_[:, kc, :], start=(kc == 0), stop=(kc == 7))
                        if self.stop(23):
                            return
                        S.copy(qct[0:64, :], psC[0:64, :], eng="scalar")
                        if self.stop(24):
                            return
                        a, b2 = t32
                        S.tt(a[0:32, :], psC[0:32, :], rI[:, 0, :], ALU.mult)
                        S.tt(b2[0:32, :], psD[0:32, :], rI[:, 1, :], ALU.mult)
                        if self.stop(25):
                            return
                        S.tt(qct[0:32, :], a[0:32, :], b2[0:32, :], ALU.add, eng="gpsimd")
                        if self.stop(26):
                            return
                        S.dma(dv(self.ds_qcT[h, 0:64, t0:t0 + 512]), qct[0:64, :], q="gpsimd")
                if self.stop(3):
                    return
                qct = qc[qi_ % 3]
                qi_ += 1
                psA = self.psum()
                self.mm_feat(psA, wsm, 128, 32, xt_)
                psB = self.psum()
                for kc in range(8):
                    S.mm(psB[0:32, :], wpk[:, kc, 0:32], xt_[:, kc, :], start=(kc == 0), stop=(kc == 7))
                rope32(qct[0:32, :], psA, psB)
                S.dma(dv(self.ds_kcT[64:96, t0:t0 + 512]), qct[0:32, :], q="gpsimd")
                qct = qc[qi_ % 3]
                qi_ += 1
                psC = self.psum()
                self.mm_feat(psC, wsm, 672, 64, xt_)
                psD = self.psum()
                for kc in range(8):
                    S.mm(psD[0:32, :], wpk[:, kc, 32:64], xt_[:, kc, :], start=(kc == 0), stop=(kc == 7))
                S.copy(qct[0:64, :], psC[0:64, :], eng="scalar")
                a, b2 = t32
                S.tt(a[0:32, :], psC[0:32, :], rI[:, 0, :], ALU.mult)
                S.tt(b2[0:32, :], psD[0:32, :], rI[:, 1, :], ALU.mult)
                S.tt(qct[0:32, :], a[0:32, :], b2[0:32, :], ALU.add, eng="gpsimd")
                S.dma(dv(self.ds_kcT[0:64, t0:t0 + 512]), qct[0:64, :], q="gpsimd")
                if self.stop(4):
                    return
                for b in range(4):
                    r0 = t0 + b * 128
                    ps = self.psum()
                    self.mm_tok(ps, xt_, b, wsm, 0, 128)
                    c = csb[b % 2]
                    sv = cst[b % 2]
                    S.copy(c[:, :], ps[:, 0:128], eng="scalar")
                    S.tt(csq[:, :], c[:, :], c[:, :], ALU.mult, eng="gpsimd")
                    S.red(sv[:, 0:1], csq[:, :])
                    S.act(sv[:, 1:2], sv[:, 0:1], AF.Sqrt, bias=self.eps_tile(1e-6), scale=1.0 / 128)
                    S.recip(sv[:, 2:3], sv[:, 1:2])
                    S.stt(c[:, :], c[:, :], sv[:, 2:3], kvg[:, :], ALU.mult, ALU.mult)
                    S.dma(dv(self.ds_ckv[r0:r0 + 128, :]), c[:, :], q="gpsimd")
                    psT = self.psum()
                    S.tr(psT[:, 0:128], c[:, :], self.ident[:, :])
                    ct = cT[b % 2]
                    S.copy(ct[:, :], psT[:, 0:128], eng="scalar")
                    S.dma(dv(self.ds_ckvT[:, r0:r0 + 128]), ct[:, :], q="gpsimd")
                    psw = self.psum()
                    self.mm_tok(psw, xt_, b, wsm, 736, 8)
                    wv = wis[b % 2]
                    S.copy(wv[:, 8:16], psw[:, 0:8])
                    S.stt(wv[:, 0:8], wv[:, 8:16], -1.0, wv[:, 8:16], ALU.mult, ALU.max)
                    S.ts(wv[:, 0:8], wv[:, 0:8], CI, None, op0=ALU.mult)
                    S.act(wv[:, 8:16], wv[:, 8:16], AF.Sign)
                    S.dma(dv(self.ds_wi[r0:r0 + 128, :]), wv[:, :], q="gpsimd")
                if self.stop(5):
                    return
                for grp in range(2):
                    wtile = wt[wi_ % 2]
                    wi_ += 1
                    self.load_w(wtile, w, 1768 + grp * 512, 512)
                    for b in range(4):
                        ps = self.psum()
                        self.mm_tok(ps, xt_, b, wtile, 0, 512)
                        sg = stg[si % 4]
                        si += 1
                        S.copy(sg[:, :], ps[:, :], eng=("vector" if si % 2 == 0 else "scalar"))
                        S.dma(dv(self.g_s[t0 + b * 128:t0 + (b + 1) * 128, grp * 512:(grp + 1) * 512]), sg[:, :], q="gpsimd")
            S.barrier()

    def dsa_mix(self):
        S = self.S
        nseq = max(1, self.nblk // NB)
        nb = min(NB, self.nblk)
        SCALE = 128 ** -0.5
        with ExitStack() as st:
            kc_ = S.tile(st, [96, SEQ], "dm_kc", dma=True)
            ckvT = S.tile(st, [128, SEQ], "dm_ckvT", dma=True)
            ckvA = S.tile(st, [128, NB, 132], "dm_ckvA", dma=True)
            S.memset(ckvA[:, :, 128:129], 1.0)
            wuv = S.tile(st, [128, 8, 128], "dm_wuv", dma=True)
            S.dma(wuv[:, :, :], dv(self.din["dsa_w_uv"].rearrange("h c d -> c h d")))
            negm = S.tile(st, [128, 128], "dm_negm", dma=True)
            S.dma(negm[:, :], dv(self.din["negmask"][:, :]))
            qc = [S.tile(st, [96, 8, 128], "dm_qc%d" % i, dma=True) for i in range(2)]
            ql = [S.tile(st, [128, 8, 128], "dm_ql%d" % i, dma=True) for i in range(2)]
            wi = [S.tile(st, [128, 16], "dm_wi%d" % i, dma=True) for i in range(2)]
            I_ = [S.tile(st, [128, SEQ], "dm_I%d" % i) for i in range(2)]
            Wk = S.tile(st, [128, SEQ], "dm_Wk")
            MT = [S.tile(st, [128, NB, 128], "dm_MT%d" % i) for i in range(2)]
            m8 = [S.tile(st, [128, 8], "dm_m8%d" % i) for i in range(2)]
            tmp = [S.tile(st, [128, 512], "dm_tmp%d" % i) for i in range(3)]
            PT = [S.tile(st, [128, 512], "dm_PT%d" % i) for i in range(3)]
            rc = [S.tile(st, [128, 1], "dm_rc%d" % i) for i in range(2)]
            raws = [S.tile(st, [128, 132], "dm_raw%d" % i) for i in range(2)]
            olat = [S.tile(st, [128, 128], "dm_ol%d" % i) for i in range(2)]
            olT = [S.tile(st, [128, 128], "dm_olT%d" % i) for i in range(2)]
            osb = [S.tile(st, [128, 1024], "dm_osb%d" % i, dma=True) for i in range(2)]
            cnt = {"tmi": 0, "pti": 0, "hi": 0}

            def prep(seq, qb):
                i = qb % 2
                c0 = seq * SEQ
                r0 = c0 + qb * 128
                L = (qb + 1) * 128
                S.dma(qc[i][:, :, :], dv(self.ds_qcT[:, :, r0:r0 + 128].rearrange("h d t -> d h t")))
                S.dma(ql[i][:, :, :], dv(self.ds_qlT[:, :, r0:r0 + 128].rearrange("h d t -> d h t")))
                S.dma(wi[i][:, :], dv(self.ds_wi[r0:r0 + 128, :]))
                It = I_[i]
                for k0 in range(0, L, 512):
                    n = min(512, L - k0)
                    for h in range(8):
                        ps = self.psum()
                        S.mm(ps[:, 0:n], qc[i][0:64, h, :], kc_[0:64, k0:k0 + n])
                        t = tmp[cnt["tmi"] % 3]
                        cnt["tmi"] += 1
                        S.act(t[:, 0:n], ps[:, 0:n], AF.Relu, scale=wi[i][:, h:h + 1])
                        if h == 0:
                            S.ts(It[:, k0:k0 + n], t[:, 0:n], wi[i][:, 8:9], None, op0=ALU.mult)
                        else:
                            S.stt(It[:, k0:k0 + n], t[:, 0:n], wi[i][:, 8 + h:9 + h], It[:, k0:k0 + n], ALU.mult, ALU.add)
                S.tt(It[:, qb * 128:L], It[:, qb * 128:L], negm[:, :], ALU.add, eng="gpsimd")

            def topk_rounds(qb, ra, rb):
                if qb < 2:
                    return
                It = I_[qb % 2]
                L = (qb + 1) * 128
                for r in range(ra, rb):
                    src = It if r == 0 else Wk
                    m = m8[r % 2]
                    S.vmax(m[:, :], src[:, 0:L])
                    if r < 31:
                        S.match_replace(Wk[:, 0:L], m[:, :], src[:, 0:L], -1e30)

            def finish(qb):
                i = qb % 2
                It = I_[i]
                nk = qb + 1
                L = nk * 128
                if qb >= 2:
                    S.ts(Wk[:, 0:L], It[:, 0:L], m8[1][:, 7:8], None, op0=ALU.is_ge)
                else:
                    S.ts(Wk[:, 0:L], It[:, 0:L], -1e29, None, op0=ALU.is_ge)
                mt = MT[i]
                for g0 in range(0, nk, 4):
                    g1 = min(nk, g0 + 4)
                    ps = self.psum()
                    for kb in range(g0, g1):
                        S.tr(ps[:, (kb - g0) * 128:(kb - g0 + 1) * 128], Wk[:, kb * 128:(kb + 1) * 128], self.ident[:, :])
                    S.copy(mt.v(mt.h[:, g0:g1, :].rearrange("p a t -> p (a t)")), ps[:, 0:(g1 - g0) * 128], eng="scalar")

            def a_stage1(qb, h, g0, g1):
                i = qb % 2
                mt = MT[i]
                n = (g1 - g0) * 128
                pss = self.psum()
                for kb in range(g0, g1):
                    o_ = pss[:, (kb - g0) * 128:(kb - g0 + 1) * 128]
                    S.mm(o_, ckvT[:, kb * 128:(kb + 1) * 128], ql[i][:, h, :], start=True, stop=False)
                    S.mm(o_, kc_[64:96, kb * 128:(kb + 1) * 128], qc[i][64:96, h, :], start=False, stop=True)
                pt = PT[cnt["pti"] % 3]
                cnt["pti"] += 1
                S.act(pt[:, 0:n], pss[:, 0:n], AF.Exp, scale=SCALE)
                S.tt(pt[:, 0:n], pt[:, 0:n], mt.v(mt.h[:, g0:g1, :].rearrange("p a t -> p (a t)")), ALU.mult, eng="gpsimd")
                return pt

            def a_stage2(seq, qb, h, g0, g1, pt):
                i = qb % 2
                nk = qb + 1
                ob = osb[i]
                if g0 == 0:
                    cnt["pso"] = self.psum_acc()
                pso = cnt["pso"]
                for kb in range(g0, g1):
                    S.mm(pso[:, 0:129], pt[:, (kb - g0) * 128:(kb - g0 + 1) * 128], ckvA[:, kb, 0:129],
                         start=(kb == 0), stop=(kb == nk - 1))
                if g1 != nk:
                    return
                hi = cnt["hi"]
                cnt["hi"] += 1
                r = rc[hi % 2]
                ol = olat[hi % 2]
                olt = olT[hi % 2]
                raw = raws[hi % 2]
                S.copy(raw[:, 0:1], pso[:, 128:129], eng="scalar")
                S.tt(raw[:, 1:2], raw[:, 0:1], self.eps_tile(-1.0), ALU.pow, eng="gpsimd")
                S.act(ol[:, :], pso[:, 0:128], AF.Copy, scale=raw[:, 1:2])
                psT = self.psum()
                S.tr(psT[:, 0:128], ol[:, :], self.ident[:, :])
                S.copy(olt[:, :], psT[:, 0:128], eng="scalar")
                ps2 = self.psum()
                S.mm(ps2[:, 0:128], olt[:, :], wuv[:, h, :])
                S.copy(ob[:, h * 128:(h + 1) * 128], ps2[:, 0:128], eng="scalar")
                if h == 7:
                    r0 = seq * SEQ + qb * 128
                    S.dma(dv(self.o_s[r0:r0 + 128, :]), ob[:, :], q="gpsimd")

            def attention(seq, qb, nxt):
                nk = qb + 1
                groups = [(h, g0, min(nk, g0 + 4)) for h in range(8) for g0 in range(0, nk, 4)]
                LA = 2
                pend = {}
                for gi in range(len(groups) + LA):
                    if gi < len(groups):
                        h, g0, g1 = groups[gi]
                        pend[gi] = a_stage1(qb, h, g0, g1)
                    gj = gi - LA
                    if gj >= 0:
                        h, g0, g1 = groups[gj]
                        a_stage2(seq, qb, h, g0, g1, pend.pop(gj))
                        if g1 == nk and nxt:
                            topk_rounds(qb + 1, 4 * h, 4 * h + 4)

            for seq in range(nseq):
                c0 = seq * SEQ
                S.dma(kc_[:, 0:nb * 128], dv(self.ds_kcT[:, c0:c0 + nb * 128]))
                S.dma(ckvT[:, 0:nb * 128], dv(self.ds_ckvT[:, c0:c0 + nb * 128]))
                S.dma(ckvA[:, 0:nb, 0:128], dv(self.ds_ckv[c0:c0 + nb * 128, :].rearrange("(b p) d -> p b d", p=128)))
                prep(seq, 0)
                topk_rounds(0, 0, 32)
                finish(0)
                for qb in range(nb):
                    nxt = qb + 1 < nb
                    if nxt:
                        prep(seq, qb + 1)
                    attention(seq, qb, nxt)
                    if nxt:
                        finish(qb + 1)
            S.barrier()

    def rwkv_alloc(self):
        self.rw_r = self.scratch("rw_r", [TOK, D])
        self.rw_k = self.scratch("rw_k", [TOK, D])
        self.rw_v = self.scratch("rw_v", [TOK, D])
        self.rw_lw = self.scratch("rw_lw", [TOK, D])
        self.rw_a = self.scratch("rw_a", [TOK, D])

    def bcast_row(self, st, name, ap1d):
        t = self.S.tile(st, [128, 1024], name, dma=True)
        self.S.dma(t[:, :], dv(ap1d.rearrange("(o n) -> o n", o=1).partition_broadcast(128)))
        return t

    def rwkv_proj(self, x_src):
        S = self.S
        w = self.din["rwkv_w_in"]
        with ExitStack() as st:
            xtok = [S.tile(st, [128, 4, 1024], "wp_xtok%d" % i, dma=True) for i in range(2)]
            xT = [S.tile(st, [128, 8, 512], "wp_xT%d" % i) for i in range(2)]
            dT = S.tile(st, [128, 8, 512], "wp_dT")
            xm = [S.tile(st, [128, 8, 512], "wp_xm%d" % i) for i in range(2)]
            halo = S.tile(st, [128, 8, 1], "wp_halo")
            wt = [S.tile(st, [128, 8, 512], "wp_w%d" % i, dma=True) for i in range(2)]
            stg = [S.tile(st, [128, 512], "wp_stg%d" % i, dma=True) for i in range(4)]
            mu48 = S.tile(st, [48, 128], "wp_mu48", dma=True)
            S.dma(mu48[:, :], dv(self.din["rwkv_mu"].rearrange("i (kc p) -> (i kc) p", p=128)))
            muT = S.tile(st, [128, 48], "wp_muT")
            ps = self.psum()
            S.tr(ps[:, 0:48], mu48[:, :], self.ident[0:48, 0:48])
            S.copy(muT[:, :], ps[:, 0:48])
            wla = S.tile(st, [128, 8, 64], "wp_wla", dma=True)
            ala = S.tile(st, [128, 8, 64], "wp_ala", dma=True)
            S.dma(wla[:, :, :], dv(self.din["rwkv_w_lora_a"].rearrange("(kc p) n -> p kc n", p=128)))
            S.dma(ala[:, :, :], dv(self.din["rwkv_a_lora_a"].rearrange("(kc p) n -> p kc n", p=128)))
            wlb = S.tile(st, [64, 1024], "wp_wlb", dma=True)
            alb = S.tile(st, [64, 1024], "wp_alb", dma=True)
            S.dma(wlb[:, :], dv(self.din["rwkv_w_lora_b"][:, :]))
            S.dma(alb[:, :], dv(self.din["rwkv_a_lora_b"][:, :]))
            w0 = self.bcast_row(st, "wp_w0", self.din["rwkv_w0"])
            a0 = self.bcast_row(st, "wp_a0", self.din["rwkv_a0"])
            hl = [S.tile(st, [64, 512], "wp_hl%d" % i) for i in range(2)]
            wi_ = 0
            si = 0
            xi_ = 0
            ntile = self.nblk // 4
            NEG = -float(np.exp(-0.5))
            for ti in range(ntile):
                t0 = ti * 512
                xk = xtok[ti % 2]
                xt_ = xT[ti % 2]
                if t0 % SEQ == 0:
                    S.memset(halo[:, :, :], 0.0)
                self.load_xT(x_src, t0, xk, xt_)
                S.tt(dT[:, :, 1:512], xt_[:, :, 0:511], xt_[:, :, 1:512], ALU.subtract)
                S.tt(dT[:, :, 0:1], halo[:, :, :], xt_[:, :, 0:1], ALU.subtract)
                S.copy(halo[:, :, :], xt_[:, :, 511:512], eng="gpsimd")

                def mix(i):
                    nonlocal xi_
                    t = xm[xi_ % 2]
                    xi_ += 1
                    for kc in range(8):
                        S.stt(t[:, kc, :], dT[:, kc, :], muT[:, i * 8 + kc:i * 8 + kc + 1], xt_[:, kc, :], ALU.mult, ALU.add)
                    return t

                for i, c0, dst in ((0, 0, self.rw_r), (2, 1024, self.rw_k), (3, 2048, self.rw_v), (5, 3072, self.g_s)):
                    xmt = mix(i)
                    for grp in range(2):
                        wtile = wt[wi_ % 2]
                        wi_ += 1
                        self.load_w(wtile, w, c0 + grp * 512, 512)
                        for b in range(4):
                            ps = self.psum()
                            self.mm_tok(ps, xmt, b, wtile, 0, 512)
                            sg = stg[si % 4]
                            si += 1
                            S.copy(sg[:, :], ps[:, :], eng=("vector" if si % 2 == 0 else "scalar"))
                            S.dma(dv(dst[t0 + b * 128:t0 + (b + 1) * 128, grp * 512:(grp + 1) * 512]), sg[:, :], q="gpsimd")
                for i, la, lb, bias_t, dst, mul in ((1, wla, wlb, w0, self.rw_lw, NEG), (4, ala, alb, a0, self.rw_a, None)):
                    xmt = mix(i)
                    ps = self.psum()
                    for kc in range(8):
                        S.mm(ps[0:64, :], la[:, kc, :], xmt[:, kc, :], start=(kc == 0), stop=(kc == 7))
                    h_ = hl[i % 2]
                    if i == 1:
                        S.act(h_[:, :], ps[0:64, :], AF.Tanh)
                    else:
                        S.copy(h_[:, :], ps[0:64, :], eng="scalar")
                    for b in range(4):
                        for half in range(2):
                            ps2 = self.psum()
                            S.mm(ps2[:, :], h_[:, b * 128:(b + 1) * 128], lb[:, half * 512:(half + 1) * 512])
                            sg = stg[si % 4]
                            si += 1
                            S.tt(sg[:, :], ps2[:, :], bias_t[:, half * 512:(half + 1) * 512], ALU.add)
                            S.act(sg[:, :], sg[:, :], AF.Sigmoid)
                            if mul is not None:
                                S.ts(sg[:, :], sg[:, :], mul, None, op0=ALU.mult, eng="gpsimd")
                            S.dma(dv(dst[t0 + b * 128:t0 + (b + 1) * 128, half * 512:(half + 1) * 512]), sg[:, :], q="gpsimd")
            S.barrier()

    def rwkv_mix(self):
        S = self.S
        nseq = max(1, self.nblk // NB)
        nb = min(NB, self.nblk)

        def v3(t):
            return t.v(t.h[:, :].rearrange("p (a c) -> p a c", c=64))

        with ExitStack() as st:
            kk_c = self.bcast_row(st, "wm_kk", self.din["rwkv_k_k"])
            ka_c = self.bcast_row(st, "wm_ka", self.din["rwkv_k_a"])
            rk_c = self.bcast_row(st, "wm_rk", self.din["rwkv_r_k"].rearrange("h n -> (h n)"))
            gg_c = self.bcast_row(st, "wm_gg", self.din["rwkv_gn_g"])
            gb_c = self.bcast_row(st, "wm_gb", self.din["rwkv_gn_b"])
            tri = S.tile(st, [128, 128], "wm_tri", dma=True)
            mlt = S.tile(st, [128, 128], "wm_mlt", dma=True)
            mgt = S.tile(st, [128, 128], "wm_mgt", dma=True)
            S.dma(tri[:, :], dv(self.din["mask_le"][:, :]))
            S.dma(mlt[:, :], dv(self.din["mask_lt"][:, :]))
            S.dma(mgt[:, :], dv(self.din["mask_gt"][:, :]))
            ones = S.tile(st, [128, 128], "wm_ones")
            S.memset(ones[:, :], 1.0)
            NIN = 2
            r_ = [S.tile(st, [128, 1024], "wm_r%d" % i, dma=True) for i in range(NIN)]
            k_ = [S.tile(st, [128, 1024], "wm_k%d" % i, dma=True) for i in range(NIN)]
            v_ = [S.tile(st, [128, 1024], "wm_v%d" % i, dma=True) for i in range(NIN)]
            lw_ = [S.tile(st, [128, 1024], "wm_lw%d" % i, dma=True) for i in range(NIN)]
            a_ = [S.tile(st, [128, 1024], "wm_a%d" % i, dma=True) for i in range(NIN)]
            kk = S.tile(st, [128, 1024], "wm_kkn")
            km = S.tile(st, [128, 1024], "wm_km")
            kka = S.tile(st, [128, 1024], "wm_kka")
            cum = S.tile(st, [128, 1024], "wm_cum")
            e1 = S.tile(st, [128, 1024], "wm_e1")
            e1x = e1
            e2 = S.tile(st, [128, 1024], "wm_e2")
            Ab = S.tile(st, [128, 1024], "wm_Ab")
            Bb = S.tile(st, [128, 1024], "wm_Bb")
            Kb = S.tile(st, [128, 1024], "wm_Kb")
            Rb = S.tile(st, [128, 1024], "wm_Rb")
            Bt = S.tile(st, [128, 1024], "wm_Bt")
            Kt = S.tile(st, [128, 1024], "wm_Kt")
            AbT = S.tile(st, [64, 16, 128], "wm_AbT")
            BbT = S.tile(st, [64, 16, 128], "wm_BbT")
            KbT = S.tile(st, [64, 16, 128], "wm_KbT")
            RbT = S.tile(st, [64, 16, 128], "wm_RbT")
            gC = S.tile(st, [64, 16], "wm_gC")
            sm = S.tile(st, [128, 64], "wm_sm")
            ST = [S.tile(st, [64, 64], "wm_ST%d" % h) for h in range(16)]
            ysb = [S.tile(st, [128, 1024], "wm_y%d" % i, dma=True) for i in range(2)]
            G = 8
            P_ = [[S.tile(st, [128, 128], "wm_P%d_%d" % (s, i)) for i in range(2)] for s in range(G)]
            PT_ = [[S.tile(st, [128, 128], "wm_PT%d_%d" % (s, i)) for i in range(2)] for s in range(G)]
            W_ = [[S.tile(st, [128, 128], "wm_W%d_%d" % (s, i)) for i in range(2)] for s in range(G)]
            ArbT = [S.tile(st, [128, 128], "wm_ArbT%d" % s) for s in range(G)]
            ArkT = [S.tile(st, [128, 128], "wm_ArkT%d" % s) for s in range(G)]
            MT = [S.tile(st, [64, 64], "wm_MT%d" % s) for s in range(G)]
            GT = [S.tile(st, [64, 128], "wm_GT%d" % s) for s in range(G)]
            hs = 0
            ci = 0
            for seq in range(nseq):
                for h in range(16):
                    S.memset(ST[h][:, :], 0.0)
                for c in range(nb):
                    i = ci % NIN
                    ci += 1
                    r0 = seq * SEQ + c * 128
                    rt, kt, vt, lwt, at = r_[i], k_[i], v_[i], lw_[i], a_[i]
                    S.dma(rt[:, :], dv(self.rw_r[r0:r0 + 128, :]))
                    S.dma(kt[:, :], dv(self.rw_k[r0:r0 + 128, :]))
                    S.dma(vt[:, :], dv(self.rw_v[r0:r0 + 128, :]))
                    S.dma(lwt[:, :], dv(self.rw_lw[r0:r0 + 128, :]))
                    S.dma(at[:, :], dv(self.rw_a[r0:r0 + 128, :]))
                    S.tt(kk[:, :], kt[:, :], kk_c[:, :], ALU.mult, eng="gpsimd")
                    S.tt(e1x[:, :], kk[:, :], kk[:, :], ALU.mult, eng="gpsimd")
                    S.red(sm[:, 0:16], v3(e1x))
                    S.act(sm[:, 0:16], sm[:, 0:16], AF.Sqrt, bias=self.eps_tile(1e-12), scale=1.0)
                    S.recip(sm[:, 16:32], sm[:, 0:16])
                    S.tt(v3(kk), v3(kk), sm.v(sm.h[:, 16:32].unsqueeze(2).to_broadcast([128, 16, 64])), ALU.mult)
                    S.stt(km[:, :], at[:, :], -1.0, ka_c[:, :], ALU.add, ALU.mult)
                    S.stt(km[:, :], km[:, :], 1.0, kt[:, :], ALU.add, ALU.mult)
                    S.tt(kka[:, :], kk[:, :], at[:, :], ALU.mult, eng="gpsimd")
                    S.tt(e1x[:, :], rt[:, :], km[:, :], ALU.mult, eng="gpsimd")
                    S.tt(e1x[:, :], e1x[:, :], rk_c[:, :], ALU.mult, eng="gpsimd")
                    S.red(sm[:, 32:48], v3(e1x))
                    for half in range(2):
                        hsl = slice(half * 512, (half + 1) * 512)
                        psc = self.psum()
                        S.mm(psc[:, :], tri[:, :], lwt[:, hsl])
                        S.copy(cum[:, hsl], psc[:, :], eng="scalar")
                        pst = self.psum()
                        S.mm(pst[:, :], ones[:, :], lwt[:, hsl])
                        S.tt(e2[:, hsl], pst[:, :], cum[:, hsl], ALU.subtract)
                    S.act(e2[:, :], e2[:, :], AF.Exp)
                    S.tt(Bt[:, :], kka[:, :], e2[:, :], ALU.mult, eng="gpsimd")
                    S.tt(Kt[:, :], km[:, :], e2[:, :], ALU.mult, eng="gpsimd")
                    S.act(e1[:, :], cum[:, :], AF.Exp)
                    S.tt(Rb[:, :], rt[:, :], e1[:, :], ALU.mult, eng="gpsimd")
                    S.act(e1[:, :], cum[:, :], AF.Exp, scale=-1.0)
                    S.tt(Bb[:, :], kka[:, :], e1[:, :], ALU.mult, eng="gpsimd")
                    S.tt(Kb[:, :], km[:, :], e1[:, :], ALU.mult, eng="gpsimd")
                    S.tt(e2[:, :], cum[:, :], lwt[:, :], ALU.subtract)
                    S.act(e2[:, :], e2[:, :], AF.Exp)
                    S.stt(Ab[:, :], kk[:, :], -1.0, e2[:, :], ALU.mult, ALU.mult)
                    psg = self.psum()
                    for h in range(16):
                        S.mm(psg[0:64, h:h + 1], lwt[:, h * 64:(h + 1) * 64], ones[:, 0:1])
                    S.act(gC[:, :], psg[0:64, 0:16], AF.Exp)
                    for src, dstT in ((Ab, AbT), (Bb, BbT), (Kb, KbT), (Rb, RbT)):
                        for g in range(4):
                            pT = self.psum()
                            for j in range(4):
                                h = g * 4 + j
                                S.tr(pT[0:64, j * 128:(j + 1) * 128], src[:, h * 64:(h + 1) * 64], self.ident[:, :])
                            S.copy(dstT.v(dstT.h[:, g * 4:(g + 1) * 4, :].rearrange("p a t -> p (a t)")), pT[0:64, :],
                                   eng=("scalar" if g % 2 == 0 else "vector"))
                    y = ysb[c % 2]
                    for hg in range(16 // G):
                        H = [(hg * G + j, j) for j in range(G)]
                        for h, s in H:
                            ps1 = self.psum()
                            S.mm(ps1[:, 0:128], AbT[:, h, :], BbT[:, h, :])
                            S.tt(P_[s][0][:, :], ps1[:, 0:128], mgt[:, :], ALU.mult)
                        for h, s in H:
                            ps2 = self.psum()
                            S.mm(ps2[:, 0:128], BbT[:, h, :], AbT[:, h, :])
                            S.mm(ps2[:, 128:256], BbT[:, h, :], RbT[:, h, :])
                            S.tt(PT_[s][0][:, :], ps2[:, 0:128], mlt[:, :], ALU.mult)
                            S.tt(ArbT[s][:, :], ps2[:, 128:256], tri[:, :], ALU.mult)
                        for h, s in H:
                            ps3 = self.psum()
                            S.mm(ps3[:, 0:128], KbT[:, h, :], AbT[:, h, :])
                            S.mm(ps3[:, 128:256], KbT[:, h, :], RbT[:, h, :])
                            S.tt(PT_[s][1][:, :], ps3[:, 0:128], mlt[:, :], ALU.mult)
                            S.tt(ArkT[s][:, :], ps3[:, 128:256], tri[:, :], ALU.mult)
                        for h, s in H:
                            hc = slice(h * 64, (h + 1) * 64)
                            ps4 = self.psum()
                            S.mm(ps4[:, 0:64], PT_[s][1][:, :], vt[:, hc])
                            S.copy(W_[s][0][:, 0:64], Ab[:, hc], eng="gpsimd")
                            S.copy(W_[s][0][:, 64:128], ps4[:, 0:64], eng="scalar")
                        for lv in range(7):
                            a, b = lv % 2, (lv + 1) % 2
                            for h, s in H:
                                psw = self.psum()
                                S.mm(psw[:, 0:128], PT_[s][a][:, :], W_[s][a][:, :])
                                S.tt(W_[s][b][:, :], W_[s][a][:, :], psw[:, 0:128], ALU.add)
                            if lv < 6:
                                for h, s in H:
                                    psq = self.psum()
                                    S.mm(psq[:, 0:128], P_[s][a][:, :], PT_[s][a][:, :])
                                    if lv < 5:
                                        S.mm(psq[:, 128:256], PT_[s][a][:, :], P_[s][a][:, :])
                                    S.copy(PT_[s][b][:, :], psq[:, 0:128], eng="scalar")
                                    if lv < 5:
                                        S.copy(P_[s][b][:, :], psq[:, 128:256], eng="scalar")
                        for h, s in H:
                            hc = slice(h * 64, (h + 1) * 64)
                            psM = self.psum()
                            S.mm(psM[0:64, 0:64], W_[s][1][:, 0:64], Bt[:, hc])
                            S.copy(MT[s][:, :], psM[0:64, 0:64], eng="scalar")
                        for h, s in H:
                            psG = self.psum()
                            S.mm(psG[0:64, 0:128], W_[s][1][:, 0:64], ArbT[s][:, :])
                            S.tt(GT[s][:, :], psG[0:64, 0:128], RbT[:, h, :], ALU.add)
                        for h, s in H:
                            hc = slice(h * 64, (h + 1) * 64)
                            psY = self.psum()
                            S.mm(psY[:, 0:64], ArbT[s][:, :], W_[s][1][:, 64:128], start=True, stop=False)
                            S.mm(psY[:, 0:64], ArkT[s][:, :], vt[:, hc], start=False, stop=False)
                            S.mm(psY[:, 0:64], GT[s][:, :], ST[h][:, :], start=False, stop=True)
                            S.copy(y[:, hc], psY[:, 0:64], eng="scalar")
                        for h, s in H:
                            hc = slice(h * 64, (h + 1) * 64)
                            psS = self.psum()
                            S.mm(psS[0:64, 0:64], Bt[:, hc], W_[s][1][:, 64:128], start=True, stop=False)
                            S.mm(psS[0:64, 0:64], Kt[:, hc], vt[:, hc], start=False, stop=False)
                            S.mm(psS[0:64, 0:64], MT[s][:, :], ST[h][:, :], start=False, stop=True)
                            S.stt(ST[h][:, :], ST[h][:, :], gC[:, h:h + 1], psS[0:64, 0:64], ALU.mult, ALU.add)
                    S.red(sm[:, 0:16], v3(y))
                    S.ts(sm[:, 0:16], sm[:, 0:16], -1.0 / 64, None, op0=ALU.mult)
                    S.tt(v3(y), v3(y), sm.v(sm.h[:, 0:16].unsqueeze(2).to_broadcast([128, 16, 64])), ALU.add)
                    S.tt(e1x[:, :], y[:, :], y[:, :], ALU.mult, eng="gpsimd")
                    S.red(sm[:, 16:32], v3(e1x))
                    S.act(sm[:, 16:32], sm[:, 16:32], AF.Sqrt, bias=self.eps_tile(64e-5), scale=1.0 / 64)
                    S.recip(sm[:, 48:64], sm[:, 16:32])
                    S.tt(v3(y), v3(y), sm.v(sm.h[:, 48:64].unsqueeze(2).to_broadcast([128, 16, 64])), ALU.mult)
                    S.tt(y[:, :], y[:, :], gg_c[:, :], ALU.mult, eng="gpsimd")
                    S.tt(y[:, :], y[:, :], gb_c[:, :], ALU.add, eng="gpsimd")
                    S.tt(v3(e1x), v3(vt), sm.v(sm.h[:, 32:48].unsqueeze(2).to_broadcast([128, 16, 64])), ALU.mult)
                    S.tt(y[:, :], y[:, :], e1x[:, :], ALU.add, eng="gpsimd")
                    S.dma(dv(self.o_s[r0:r0 + 128, :]), y[:, :], q="gpsimd")
            S.barrier()

    def ret_alloc(self):
        self.rt_qT = self.scratch("rt_qT", [4, 2, 128, TOK])
        self.rt_kT = self.scratch("rt_kT", [4, 2, 128, TOK])
        self.rt_v = self.scratch("rt_v", [TOK, D])

    def ret_proj(self, x_src):
        S = self.S
        w = self.din["ret_w_in"]
        with ExitStack() as st:
            xtok = [S.tile(st, [128, 4, 1024], "rp_xtok%d" % i, dma=True) for i in range(2)]
            xT = [S.tile(st, [128, 8, 512], "rp_xT%d" % i) for i in range(2)]
            wt = [S.tile(st, [128, 8, 512], "rp_w%d" % i, dma=True) for i in range(3)]
            stg = [S.tile(st, [128, 512], "rp_stg%d" % i, dma=True) for i in range(4)]
            cs = [S.tile(st, [128, 2, 512], "rp_cs%d" % i, dma=True) for i in range(2)]
            tmp = [S.tile(st, [128, 512], "rp_tmp%d" % i) for i in range(4)]
            wi = 0
            si = 0
            ntile = self.nblk // 4
            for ti in range(ntile):
                t0 = ti * 512
                p0 = t0 % SEQ
                xk = xtok[ti % 2]
                xt_ = xT[ti % 2]
                self.load_xT(x_src, t0, xk, xt_)
                c = cs[ti % 2]
                S.dma(c[:, 0, :], dv(self.din["ret_cosT"][:, p0:p0 + 512]))
                S.dma(c[:, 1, :], dv(self.din["ret_sinT"][:, p0:p0 + 512]))
                for which, dst, scl in ((0, self.rt_qT, 1.0), (1, self.rt_kT, 256 ** -0.5)):
                    for grp in range(2):
                        wtile = wt[wi % 3]
                        wi += 1
                        self.load_w(wtile, w, which * 1024 + grp * 512, 512)
                        for j in range(2):
                            h = grp * 2 + j
                            ps1 = self.psum()
                            self.mm_feat(ps1, wtile, j * 256, 128, xt_)
                            ps2 = self.psum()
                            self.mm_feat(ps2, wtile, j * 256 + 128, 128, xt_)
                            a, b2, c3, d4 = tmp
                            S.stt(a[:, :], ps1[:, :], scl, c[:, 0, :], ALU.mult, ALU.mult)
                            S.stt(b2[:, :], ps2[:, :], scl, c[:, 1, :], ALU.mult, ALU.mult)
                            S.stt(c3[:, :], ps2[:, :], scl, c[:, 0, :], ALU.mult, ALU.mult)
                            S.stt(d4[:, :], ps1[:, :], scl, c[:, 1, :], ALU.mult, ALU.mult)
                            s1 = stg[si % 4]
                            s2 = stg[(si + 1) % 4]
                            si += 2
                            S.tt(s1[:, :], a[:, :], b2[:, :], ALU.subtract, eng="gpsimd")
                            S.tt(s2[:, :], c3[:, :], d4[:, :], ALU.add, eng="gpsimd")
                            S.dma(dv(dst[h, 0, :, t0:t0 + 512]), s1[:, :], q="gpsimd")
                            S.dma(dv(dst[h, 1, :, t0:t0 + 512]), s2[:, :], q="gpsimd")
                for c0, dst in ((2048, self.rt_v), (3072, self.g_s)):
                    for grp in range(2):
                        wtile = wt[wi % 3]
                        wi += 1
                        self.load_w(wtile, w, c0 + grp * 512, 512)
                        for b in range(4):
                            ps = self.psum()
                            self.mm_tok(ps, xt_, b, wtile, 0, 512)
                            sg = stg[si % 4]
                            si += 1
                            S.copy(sg[:, :], ps[:, :], eng=("vector" if si % 2 == 0 else "scalar"))
                            S.dma(dv(dst[t0 + b * 128:t0 + (b + 1) * 128, grp * 512:(grp + 1) * 512]), sg[:, :], q="gpsimd")
            S.barrier()

    def ret_mix(self):
        S = self.S
        nseq = max(1, self.nblk // NB)
        nb = min(NB, self.nblk)
        gs = [1.0 - 2.0 ** (-5.0 - h) for h in range(4)]
        with ExitStack() as st:
            QT = [[S.tile(st, [128, 2, 512], "rm_QT%d_%d" % (h, i), dma=True) for i in range(2)] for h in range(4)]
            KT = [[S.tile(st, [128, 2, 512], "rm_KT%d_%d" % (h, i), dma=True) for i in range(2)] for h in range(4)]
            VV = [[S.tile(st, [128, 4, 256], "rm_V%d_%d" % (h, i), dma=True) for i in range(2)] for h in range(4)]
            R = [S.tile(st, [128, 2, 256], "rm_R%d" % h) for h in range(4)]
            dpT = [S.tile(st, [128, 128], "rm_dp%d" % h, dma=True) for h in range(4)]
            xz = S.tile(st, [128, 8], "rm_xz", dma=True)
            S.dma(xz[:, :], dv(self.din["ret_xz"][:, :]))
            for h in range(4):
                S.dma(dpT[h][:, :], dv(self.din["ret_dpT"][h, :, :]))
            inT = [S.tile(st, [128, 128], "rm_inT%d" % i) for i in range(4)]
            kz = [S.tile(st, [128, 256], "rm_kz%d" % i) for i in range(4)]
            osb = [S.tile(st, [128, 256], "rm_o%d" % i, dma=True) for i in range(4)]
            cnt = 0
            for seq in range(nseq):
                for h in range(4):
                    S.memset(R[h][:, :, :], 0.0)
                for grp in range(nb // 4):
                    t0 = seq * SEQ + grp * 512
                    par = grp % 2
                    for h in range(4):
                        S.dma(QT[h][par][:, :, :], dv(self.rt_qT[h, :, :, t0:t0 + 512].rearrange("c p t -> p c t")))
                        S.dma(KT[h][par][:, :, :], dv(self.rt_kT[h, :, :, t0:t0 + 512].rearrange("c p t -> p c t")))
                        S.dma(VV[h][par][:, :, :],
                              dv(self.rt_v[t0:t0 + 512, h * 256:(h + 1) * 256].rearrange("(c p) e -> p c e", p=128)))
                    for n in range(4):
                        cs = slice(n * 128, (n + 1) * 128)
                        for h in range(4):
                            q_, k_, v_ = QT[h][par], KT[h][par], VV[h][par]
                            it = inT[cnt % 4]
                            kzt = kz[cnt % 4]
                            ot = osb[cnt % 4]
                            cnt += 1
                            ps_in = self.psum()
                            for dc in range(2):
                                S.mm(ps_in[:, 0:128], k_[:, dc, cs], q_[:, dc, cs], start=(dc == 0), stop=(dc == 1))
                            S.tt(it[:, :], ps_in[:, 0:128], dpT[h][:, :], ALU.mult)
                            ps_o = self.psum()
                            S.mm(ps_o[:, 0:256], it[:, :], v_[:, n, :], start=True, stop=False)
                            for dc in range(2):
                                S.mm(ps_o[:, 0:256], q_[:, dc, cs], R[h][:, dc, :], start=False, stop=(dc == 1))
                            S.act(ot[:, :], ps_o[:, 0:256], AF.Copy, scale=xz[:, h:h + 1])
                            r0 = t0 + n * 128
                            S.dma(dv(self.o_s[r0:r0 + 128, h * 256:(h + 1) * 256]), ot[:, :], q="gpsimd")
                            ps_k = self.psum()
                            for dc in range(2):
                                S.tr(ps_k[:, dc * 128:(dc + 1) * 128], k_[:, dc, cs], self.ident[:, :])
                            S.act(kzt[:, :], ps_k[:, 0:256], AF.Copy, scale=xz[:, 4 + h:5 + h])
                            ps_r = self.psum()
                            for dc in range(2):
                                S.mm(ps_r[:, dc * 256:(dc + 1) * 256], kzt[:, dc * 128:(dc + 1) * 128], v_[:, n, :])
                            Rf = R[h].v(R[h].h[:, :, :].rearrange("p c e -> p (c e)"))
                            S.stt(Rf, Rf, float(gs[h] ** 128), ps_r[:, :], ALU.mult, ALU.add)
            S.barrier()

    def ret_post_alloc(self, st):
        S = self.S
        gng = S.tile(st, [128, 1024], "rpo_g", dma=True)
        S.dma(gng[:, :], dv(self.din["ret_gn_g"].rearrange("(o n) -> o n", o=1).partition_broadcast(128)))
        sq = S.tile(st, [128, 1024], "rpo_sq")
        ss = [S.tile(st, [128, 8], "rpo_ss%d" % i) for i in range(2)]
        return (gng, sq, ss)

    def ret_post(self, extra, blk, ot, gt):
        S = self.S
        gng, sq, ss = extra
        s = ss[blk % 2]
        S.tt(sq[:, :], ot[:, :], ot[:, :], ALU.mult, eng="gpsimd")
        S.red(s[:, 0:4], sq.v(sq.h[:, :].rearrange("p (a c) -> p a c", c=256)))
        S.act(s[:, 0:4], s[:, 0:4], AF.Sqrt, bias=self.eps_tile(1e-6), scale=1.0 / 256)
        S.recip(s[:, 4:8], s[:, 0:4])
        o3 = ot.v(ot.h[:, :].rearrange("p (a c) -> p a c", c=256))
        S.tt(o3, o3, s.v(s.h[:, 4:8].unsqueeze(2).to_broadcast([128, 4, 256])), ALU.mult)
        S.tt(ot[:, :], ot[:, :], gng[:, :], ALU.mult, eng="gpsimd")

    def run_layers(self, layers, dbg=""):
        first = True
        for L in layers:
            src = self.x_in if first else self.out
            first = False
            name = ("fox", "dsa", "rwkv", "ret")[L]
            getattr(self, name + "_alloc")()
            if "noproj" not in dbg:
                getattr(self, name + "_proj")(src)
            if "nomix" not in dbg:
                getattr(self, name + "_mix")()
            if "noout" not in dbg:
                post = (self.ret_post_alloc, self.ret_post) if L == 3 else None
                self.phase_out(L, src, self.din[name + "_w_out"], post=post)
        self.S.barrier()


WEIGHT_SHAPES = {
    'ln_g': (4, 1024), 'ln_b': (4, 1024), 'fox_w_in': (1024, 4104), 'fox_b_f': (8,), 'fox_w_out': (1024, 1024),
    'dsa_w_in': (1024, 2792), 'dsa_kv_norm_g': (128,), 'dsa_w_uk': (8, 128, 96), 'dsa_w_uv': (8, 128, 128),
    'dsa_w_out': (1024, 1024), 'rwkv_mu': (6, 1024), 'rwkv_w_in': (1024, 4096), 'rwkv_w0': (1024,),
    'rwkv_w_lora_a': (1024, 64), 'rwkv_w_lora_b': (64, 1024), 'rwkv_a0': (1024,), 'rwkv_a_lora_a': (1024, 64),
    'rwkv_a_lora_b': (64, 1024), 'rwkv_k_k': (1024,), 'rwkv_k_a': (1024,), 'rwkv_r_k': (16, 64),
    'rwkv_gn_g': (1024,), 'rwkv_gn_b': (1024,), 'rwkv_w_out': (1024, 1024), 'ret_w_in': (1024, 4096),
    'ret_gn_g': (1024,), 'ret_w_out': (1024, 1024),
}


def make_consts():
    c = {}
    c["ident"] = np.eye(128, dtype=np.float32)
    i = np.arange(128)
    c["mask_le"] = (i[:, None] <= i[None, :]).astype(np.float32)
    c["mask_lt"] = (i[:, None] < i[None, :]).astype(np.float32)
    c["mask_ge"] = (i[:, None] >= i[None, :]).astype(np.float32)
    c["mask_gt"] = (i[:, None] > i[None, :]).astype(np.float32)
    inv = 1.0 / (10000.0 ** (np.arange(0, 256, 2, dtype=np.float32) / np.float32(256)))
    ang = np.arange(4096, dtype=np.float32)[:, None] * inv[None, :].astype(np.float32)
    c["ret_cosT"] = np.ascontiguousarray(np.cos(ang).T.astype(np.float32))
    c["ret_sinT"] = np.ascontiguousarray(np.sin(ang).T.astype(np.float32))
    lg = np.log1p(-(2.0 ** (-5.0 - np.arange(4, dtype=np.float64))))
    pos = np.arange(128, dtype=np.float64)
    dp = np.zeros((4, 128, 128), np.float32)
    xz = np.zeros((128, 8), np.float32)
    for h in range(4):
        dp[h] = (np.exp(-(pos[:, None] + 1.0) * lg[h]) * (pos[:, None] <= pos[None, :])).astype(np.float32)
        xz[:, h] = np.exp((pos + 1.0) * lg[h])
        xz[:, 4 + h] = np.exp((127.0 - pos) * lg[h])
    c["ret_dpT"] = dp
    c["ret_xz"] = xz
    def rope_fm(rot):
        inv = 1.0 / (np.float32(500000.0) ** (np.arange(0, rot, 2, dtype=np.float32) / np.float32(rot)))
        ang = np.arange(4096, dtype=np.float32)[:, None] * inv[None, :].astype(np.float32)
        cs, sn = np.cos(ang).T.astype(np.float32), np.sin(ang).T.astype(np.float32)
        return np.ascontiguousarray(np.concatenate([cs, cs], 0)), np.ascontiguousarray(np.concatenate([-sn, sn], 0))
    c["dsa_ropeA_c"], c["dsa_ropeA_s"] = rope_fm(32)
    ic, isn = rope_fm(16)
    c["dsa_ropeI_c"] = np.ascontiguousarray(np.concatenate([ic, np.ones_like(ic)], 0))
    c["dsa_ropeI_s"] = np.ascontiguousarray(np.concatenate([isn, np.zeros_like(isn)], 0))
    c["negmask"] = np.where(i[None, :] <= i[:, None], 0.0, -1e30).astype(np.float32)
    return c


_CACHE = {}


def build_program(layers=(0, 1, 2, 3), nblk=TOK // 128, dbg=""):
    key = (tuple(layers), nblk, dbg)
    if key in _CACHE:
        return _CACHE[key]
    consts = make_consts()
    nc = bass.Bass("TRN2", target_bir_lowering=False)
    with ExitStack() as st:
        k = K(nc, st, WEIGHT_SHAPES, {n: a.shape for n, a in consts.items()}, nblk=nblk)
        k.run_layers(layers, dbg)
        print("instructions", k.S.n_ins, "waits", k.S.n_wait)
    _CACHE[key] = (nc, consts)
    return nc, consts


def kernel(**inputs):
    x = np.ascontiguousarray(np.asarray(inputs["x"], dtype=np.float32))
    nc, consts = build_program()
    base = {n: np.ascontiguousarray(np.asarray(inputs[n], dtype=np.float32)) for n in WEIGHT_SHAPES}
    base.update(consts)
    in_maps = []
    for c in range(8):
        m = dict(base)
        m["x"] = x[2 * c:2 * c + 2].reshape(TOK, D)
        in_maps.append(m)
    res = run_bass_kernel_spmd(nc, in_maps, core_ids=list(range(8)))
    out = np.stack([r["out"].reshape(NSEQ, SEQ, D) for r in res.results], axis=0).reshape(16, SEQ, D)
    return out.astype(np.float32)
```

```python
import numpy as np
from contextlib import ExitStack
import concourse.bass as bass
import concourse.mybir as mybir
from concourse.bass_utils import run_bass_kernel_spmd

F32 = mybir.dt.float32
ALU = mybir.AluOpType
AF = mybir.ActivationFunctionType
AX = mybir.AxisListType


class Res:
    __slots__ = ("w", "r", "dsem", "excl")

    def __init__(self):
        self.w = None
        self.r = {}
        self.dsem = None
        self.excl = False


class V:
    __slots__ = ("ap", "res")

    def __init__(self, ap, res):
        self.ap = ap
        self.res = res


class Tile:
    def __init__(self, handle, res=None):
        self.h = handle
        self.res = res if res is not None else Res()

    def __getitem__(self, key):
        return V(self.h[key], self.res)

    def v(self, ap):
        return V(ap, self.res)


class Sched:
    ENG = ("tensor", "vector", "scalar", "gpsimd", "sync")

    def __init__(self, nc, stack, n_dma_sems=92):
        self.nc = nc
        self.stack = stack
        self.e = {"tensor": nc.tensor, "vector": nc.vector, "scalar": nc.scalar,
                  "gpsimd": nc.gpsimd, "sync": nc.sync}
        self.sem = {}
        self.tot = {}
        for e in self.ENG:
            self.sem[e] = stack.enter_context(nc.semaphore("s_" + e))
            self.tot[e] = 0
        self.bar = stack.enter_context(nc.semaphore("s_bar"))
        self.bar_n = 0
        self.free_dma = {"hw": [], "sw": []}
        self.sem_kind = {}
        for i in range(n_dma_sems):
            k = "d%d" % i
            self.sem[k] = stack.enter_context(nc.semaphore(k))
            self.tot[k] = 0
            kind = "sw" if i < 30 else "hw"
            self.sem_kind[k] = kind
            self.free_dma[kind].append(k)
        self.known = {e: {} for e in self.ENG}
        self.n_ins = 0
        self.n_wait = 0

    def tile(self, stack, shape, name, dtype=F32, dma=False):
        self.n_tiles = getattr(self, "n_tiles", 0) + 1
        h = stack.enter_context(self.nc.sbuf_tensor("%s_%d" % (name, self.n_tiles), list(shape), dtype))
        t = Tile(h)
        if dma:
            t.res.dsem = {}
            stack.callback(self._release_dsems, t.res)
        return t

    def _release_dsems(self, res):
        for k in res.dsem.values():
            self.free_dma[self.sem_kind[k]].append(k)
        res.dsem = {}

    def _dsem(self, res, q):
        kind = "sw" if q == "gpsimd" else "hw"
        k = res.dsem.get(kind)
        if k is None:
            k = self.free_dma[kind].pop()
            res.dsem[kind] = k
        return k

    def psum(self, stack, name, shape=(128, 512), dtype=F32):
        h = stack.enter_context(self.nc.psum_tensor(name, list(shape), dtype))
        t = Tile(h)
        t.res.excl = True
        return t

    def _waits(self, eng, reads, writes):
        deps = {}
        for v in reads:
            if v is None or v.res is None:
                continue
            w = v.res.w
            if w is not None:
                if deps.get(w[0], 0) < w[1]:
                    deps[w[0]] = w[1]
            if v.res.excl:
                for k, val in v.res.r.items():
                    if k != eng and deps.get(k, 0) < val:
                        deps[k] = val
        for v in writes:
            if v is None or v.res is None:
                continue
            w = v.res.w
            if w is not None:
                if deps.get(w[0], 0) < w[1]:
                    deps[w[0]] = w[1]
            for k, val in v.res.r.items():
                if deps.get(k, 0) < val:
                    deps[k] = val
        kn = self.known[eng]
        for k, val in deps.items():
            if k[0] == "d":
                val = self.tot[k]
            elif eng == "tensor" and k == "tensor":
                continue
            if kn.get(k, 0) >= val:
                continue
            kn[k] = val
            self.e[eng].wait_ge(self.sem[k], val)
            self.n_wait += 1

    def emit(self, eng, reads, writes, fn):
        self._waits(eng, reads, writes)
        ins = fn(self.e[eng])
        self.tot[eng] += 1
        n = self.tot[eng]
        ins.then_inc(self.sem[eng], 1)
        self.n_ins += 1
        for v in reads:
            if v is not None and v.res is not None:
                if v.res.r.get(eng, 0) < n:
                    v.res.r[eng] = n
        for v in writes:
            if v is not None and v.res is not None:
                v.res.w = (eng, n)
                v.res.r = {}

    def dma(self, out, in_, q="sync", **kw):
        self._waits(q, [in_], [out])
        k = None
        if out.res is not None and out.res.dsem is not None:
            k = self._dsem(out.res, q)
        elif in_.res is not None and in_.res.dsem is not None:
            k = self._dsem(in_.res, q)
        assert k is not None, "dma needs a resource with a dma semaphore"
        ins = self.e[q].dma_start(out=out.ap, in_=in_.ap, **kw)
        self.tot[k] += 16
        n = self.tot[k]
        ins.then_inc(self.sem[k], 16)
        self.n_ins += 1
        if in_.res is not None:
            if in_.res.r.get(k, 0) < n:
                in_.res.r[k] = n
        if out.res is not None:
            out.res.w = (k, n)
            out.res.r = {}

    def barrier(self):
        sy = self.e["sync"]
        kn = self.known["sync"]
        for k, val in self.tot.items():
            if val > 0 and kn.get(k, 0) < val:
                sy.wait_ge(self.sem[k], val)
                kn[k] = val
        self.bar_n += 1
        sy.sem_inc(self.bar, 1)
        for e in self.ENG:
            if e != "sync":
                self.e[e].wait_ge(self.bar, self.bar_n)
            for k, val in self.tot.items():
                self.known[e][k] = val

    def mm(self, out, lhsT, rhs, start=True, stop=True):
        self.emit("tensor", [lhsT, rhs], [out],
                  lambda e: e.matmul(out.ap, lhsT.ap, rhs.ap, start=start, stop=stop))

    def tr(self, out, in_, ident):
        self.emit("tensor", [in_, ident], [out],
                  lambda e: e.transpose(out.ap, in_.ap, ident.ap))

    def act(self, out, in_, func, bias=None, scale=1.0, eng="scalar"):
        rd = [in_]
        kw = {}
        if isinstance(bias, V):
            rd.append(bias)
            kw["bias"] = bias.ap
        elif bias is not None:
            kw["bias"] = bias
        if isinstance(scale, V):
            rd.append(scale)
            kw["scale"] = scale.ap
        else:
            kw["scale"] = scale
        self.emit(eng, rd, [out], lambda e: e.activation(out.ap, in_.ap, func, **kw))

    def tt(self, out, in0, in1, op, eng="vector"):
        self.emit(eng, [in0, in1], [out],
                  lambda e: e.tensor_tensor(out.ap, in0.ap, in1.ap, op))

    def ts(self, out, in0, s1, s2=None, op0=ALU.mult, op1=None, eng="vector"):
        rd = [in0]
        a1 = s1
        a2 = s2
        if isinstance(s1, V):
            rd.append(s1)
            a1 = s1.ap
        if isinstance(s2, V):
            rd.append(s2)
            a2 = s2.ap
        if op1 is None:
            self.emit(eng, rd, [out], lambda e: e.tensor_scalar(out.ap, in0.ap, a1, None, op0))
        else:
            self.emit(eng, rd, [out], lambda e: e.tensor_scalar(out.ap, in0.ap, a1, a2, op0, op1))

    def stt(self, out, in0, scalar, in1, op0, op1, eng="vector"):
        rd = [in0, in1]
        a = scalar
        if isinstance(scalar, V):
            rd.append(scalar)
            a = scalar.ap
        self.emit(eng, rd, [out],
                  lambda e: e.scalar_tensor_tensor(out.ap, in0.ap, a, in1.ap, op0, op1))

    def copy(self, out, in_, eng="vector"):
        if eng == "scalar":
            self.emit(eng, [in_], [out], lambda e: e.copy(out.ap, in_.ap))
        else:
            self.emit(eng, [in_], [out], lambda e: e.tensor_copy(out.ap, in_.ap))

    def red(self, out, in_, op=ALU.add, axis=AX.X, eng="vector"):
        self.emit(eng, [in_], [out], lambda e: e.tensor_reduce(out.ap, in_.ap, axis, op))

    def memset(self, out, val, eng="vector"):
        self.emit(eng, [], [out], lambda e: e.memset(out.ap, val))

    def recip(self, out, in_):
        self.emit("vector", [in_], [out], lambda e: e.reciprocal(out.ap, in_.ap))

    def vmax(self, out, in_):
        self.emit("vector", [in_], [out], lambda e: e.max(out.ap, in_.ap))

    def match_replace(self, out, in_to_replace, in_values, imm):
        self.emit("vector", [in_to_replace, in_values], [out],
                  lambda e: e.match_replace(out.ap, in_to_replace.ap, in_values.ap, imm))

D = 1024
SEQ = 4096
NSEQ = 2
TOK = NSEQ * SEQ
NB = SEQ // 128
LN_EPS = 1e-5
DN_ALPHA = (2 * 4) ** 0.25


def dv(ap):
    return V(ap, None)


class K:
    def __init__(self, nc, st, weights_meta, consts_meta, nblk=TOK // 128):
        self.nc = nc
        self.st = st
        self.S = Sched(nc, st)
        self.nblk = nblk
        self.din = {}
        for name, shape in list(weights_meta.items()) + list(consts_meta.items()):
            self.din[name] = nc.dram_tensor(name, list(shape), F32, kind="ExternalInput").ap()
        self.x_in = nc.dram_tensor("x", [TOK, D], F32, kind="ExternalInput").ap()
        self.out = nc.dram_tensor("out", [TOK, D], F32, kind="ExternalOutput").ap()
        self.o_s = nc.dram_tensor("o_s", [TOK, D], F32).ap()
        self.g_s = nc.dram_tensor("g_s", [TOK, D], F32).ap()
        S = self.S
        self.ps = [S.psum(st, "psb%d" % i) for i in range(8)]
        self.ps_i = 0
        self.pa_i = 0
        self.ident = S.tile(st, [128, 128], "ident", dma=True)
        S.dma(self.ident[:, :], dv(self.din["ident"][:, :]))
        self._eps = {}
        for val in (LN_EPS, 1.0, 1e-6, 64e-5, 1e-12, 0.0, -1.0):
            t = S.tile(st, [128, 1], "eps")
            S.memset(t[:, :], float(val))
            self._eps[val] = t

    def psum(self):
        p = self.ps[2 + self.ps_i % 6]
        self.ps_i += 1
        return p

    def psum_acc(self):
        p = self.ps[self.pa_i % 2]
        self.pa_i += 1
        return p

    def stop(self, n):
        import os
        if int(os.environ.get("KSTOP", "99")) == n:
            self.S.barrier()
            return True
        return False

    def scratch(self, name, shape):
        return self.nc.dram_tensor(name, list(shape), F32).ap()

    def load_xT(self, src, t0, xtok, xT, nb=4):
        S = self.S
        S.dma(xtok[:, 0:nb, :], dv(src[t0:t0 + nb * 128, :].rearrange("(b p) d -> p b d", p=128)))
        for kc in range(8):
            ps = self.psum()
            for b in range(nb):
                S.tr(ps[:, b * 128:(b + 1) * 128], xtok[:, b, kc * 128:(kc + 1) * 128], self.ident[:, :])
            S.copy(xT[:, kc, 0:nb * 128], ps[:, 0:nb * 128], eng=("vector" if kc % 2 == 0 else "scalar"))

    def load_w(self, wt, w_ap, c0, n, q="sync"):
        self.S.dma(wt[:, :, 0:n], dv(w_ap.rearrange("(kc p) n -> p kc n", p=128)[:, :, c0:c0 + n]), q=q)

    def mm_feat(self, ps, wt, c0, m, xT, ntok=512):
        for kc in range(8):
            self.S.mm(ps[0:m, 0:ntok], wt[:, kc, c0:c0 + m], xT[:, kc, 0:ntok], start=(kc == 0), stop=(kc == 7))

    def mm_tok(self, ps, xT, blk, wt, c0, n):
        for kc in range(8):
            self.S.mm(ps[:, 0:n], xT[:, kc, blk * 128:(blk + 1) * 128], wt[:, kc, c0:c0 + n], start=(kc == 0), stop=(kc == 7))

    def phase_out(self, layer, x_src, w_out, post=None):
        S = self.S
        with ExitStack() as st:
            wo = S.tile(st, [128, 8, 1024], "wo", dma=True)
            S.dma(wo[:, :, :], dv(w_out.rearrange("(kc p) n -> p kc n", p=128)))
            lng = S.tile(st, [128, 1024], "lng", dma=True)
            lnb = S.tile(st, [128, 1024], "lnb", dma=True)
            S.dma(lng[:, :], dv(self.din["ln_g"][layer:layer + 1, :].partition_broadcast(128)))
            S.dma(lnb[:, :], dv(self.din["ln_b"][layer:layer + 1, :].partition_broadcast(128)))
            NBUF = 2
            ot = [S.tile(st, [128, 1024], "po_o%d" % i, dma=True) for i in range(NBUF)]
            gt = [S.tile(st, [128, 1024], "po_g%d" % i, dma=True) for i in range(NBUF)]
            xt = [S.tile(st, [128, 1024], "po_x%d" % i, dma=True) for i in range(NBUF)]
            zT = [S.tile(st, [128, 8, 128], "po_zT%d" % i) for i in range(NBUF)]
            yt = [S.tile(st, [128, 1024], "po_y%d" % i, dma=True) for i in range(NBUF)]
            sq = S.tile(st, [128, 1024], "po_sq")
            stat = [S.tile(st, [128, 8], "po_st%d" % i) for i in range(NBUF)]
            extra = post[0](st) if post is not None else None
            for blk in range(self.nblk):
                i = blk % NBUF
                r0 = blk * 128
                S.dma(ot[i][:, :], dv(self.o_s[r0:r0 + 128, :]))
                S.dma(gt[i][:, :], dv(self.g_s[r0:r0 + 128, :]))
                S.dma(xt[i][:, :], dv(x_src[r0:r0 + 128, :]))
                if post is not None:
                    post[1](extra, blk, ot[i], gt[i])
                S.act(gt[i][:, :], gt[i][:, :], AF.Silu)
                S.tt(ot[i][:, :], ot[i][:, :], gt[i][:, :], ALU.mult, eng="gpsimd")
                for half in range(2):
                    ps = self.psum()
                    for j in range(4):
                        kc = half * 4 + j
                        S.tr(ps[:, j * 128:(j + 1) * 128], ot[i][:, kc * 128:(kc + 1) * 128], self.ident[:, :])
                    S.copy(zT[i][:, half * 4:(half + 1) * 4, :],
                           ps.v(ps.h[:, :].rearrange("p (j t) -> p j t", t=128)), eng=("vector" if half == 0 else "scalar"))
                y = yt[i]
                for half in range(2):
                    ps = self.psum()
                    for kc in range(8):
                        S.mm(ps[:, :], zT[i][:, kc, :], wo[:, kc, half * 512:(half + 1) * 512], start=(kc == 0), stop=(kc == 7))
                    S.stt(y[:, half * 512:(half + 1) * 512], xt[i][:, half * 512:(half + 1) * 512], DN_ALPHA, ps[:, :], ALU.mult, ALU.add)
                sv = stat[i]
                S.red(sv[:, 0:1], y[:, :])
                S.ts(sv[:, 1:2], sv[:, 0:1], -1.0 / D, None, op0=ALU.mult)
                S.ts(y[:, :], y[:, :], sv[:, 1:2], None, op0=ALU.add)
                S.tt(sq[:, :], y[:, :], y[:, :], ALU.mult, eng="gpsimd")
                S.red(sv[:, 2:3], sq[:, :])
                S.act(sv[:, 3:4], sv[:, 2:3], AF.Sqrt, bias=self.eps_tile(LN_EPS), scale=1.0 / D)
                S.recip(sv[:, 4:5], sv[:, 3:4])
                S.stt(y[:, :], y[:, :], sv[:, 4:5], lng[:, :], ALU.mult, ALU.mult)
                S.tt(y[:, :], y[:, :], lnb[:, :], ALU.add, eng="gpsimd")
                S.dma(dv(self.out[r0:r0 + 128, :]), y[:, :], q="gpsimd")
            S.barrier()

    def eps_tile(self, val):
        return self._eps[val][:, 0:1]

    def fox_alloc(self):
        self.fx_qT = self.scratch("fx_qT", [8, 128, TOK])
        self.fx_kT = self.scratch("fx_kT", [8, 128, TOK])
        self.fx_v = self.scratch("fx_v", [TOK, D])
        self.fx_c = self.scratch("fx_c", [NSEQ, 128, NB * 8])
        self.fx_cr = self.scratch("fx_cr", [NSEQ, 128, NB * 8])

    def fox_proj(self, x_src):
        S = self.S
        w = self.din["fox_w_in"]
        with ExitStack() as st:
            xtok = [S.tile(st, [128, 4, 1024], "fp_xtok%d" % i, dma=True) for i in range(2)]
            xT = [S.tile(st, [128, 8, 512], "fp_xT%d" % i) for i in range(2)]
            wt = [S.tile(st, [128, 8, 512], "fp_w%d" % i, dma=True) for i in range(3)]
            wf = S.tile(st, [128, 8, 8], "fp_wf", dma=True)
            self.load_w(wf, w, 3072, 8)
            stg = [S.tile(st, [128, 512], "fp_stg%d" % i, dma=True) for i in range(4)]
            lf = S.tile(st, [128, NB, 8], "fp_lf")
            bf = S.tile(st, [128, 8], "fp_bf", dma=True)
            S.dma(bf[:, :], dv(self.din["fox_b_f"].rearrange("(o n) -> o n", o=1).partition_broadcast(128)))
            tri = S.tile(st, [128, 128], "fp_tri", dma=True)
            S.dma(tri[:, :], dv(self.din["mask_le"][:, :]))
            ones = S.tile(st, [128, 128], "fp_ones")
            S.memset(ones[:, :], 1.0)
            cw = S.tile(st, [128, NB, 8], "fp_cw", dma=True)
            cr = S.tile(st, [128, NB, 8], "fp_cr", dma=True)
            wi = 0
            si = 0
            ntile = self.nblk // 4
            for ti in range(ntile):
                t0 = ti * 512
                seq, tl = divmod(ti, NB // 4)
                xk = xtok[ti % 2]
                xt_ = xT[ti % 2]
                self.load_xT(x_src, t0, xk, xt_)
                for which, dst in ((0, self.fx_qT), (1, self.fx_kT)):
                    for grp in range(2):
                        wtile = wt[wi % 3]
                        wi += 1
                        self.load_w(wtile, w, which * 1024 + grp * 512, 512)
                        for j in range(4):
                            h = grp * 4 + j
                            ps = self.psum()
                            self.mm_feat(ps, wtile, j * 128, 128, xt_)
                            sg = stg[si % 4]
                            si += 1
                            S.copy(sg[:, :], ps[:, :], eng=("vector" if si % 2 == 0 else "scalar"))
                            S.dma(dv(dst[h, :, t0:t0 + 512]), sg[:, :], q="gpsimd")
                for c0, dst in ((2048, self.fx_v), (3080, self.g_s)):
                    for grp in range(2):
                        wtile = wt[wi % 3]
                        wi += 1
                        self.load_w(wtile, w, c0 + grp * 512, 512)
                        for b in range(4):
                            ps = self.psum()
                            self.mm_tok(ps, xt_, b, wtile, 0, 512)
                            sg = stg[si % 4]
                            si += 1
                            S.copy(sg[:, :], ps[:, :], eng=("vector" if si % 2 == 0 else "scalar"))
                            S.dma(dv(dst[t0 + b * 128:t0 + (b + 1) * 128, grp * 512:(grp + 1) * 512]), sg[:, :], q="gpsimd")
                ps = self.psum()
                for b in range(4):
                    for kc in range(8):
                        S.mm(ps[:, b * 8:(b + 1) * 8], xt_[:, kc, b * 128:(b + 1) * 128], wf[:, kc, :], start=(kc == 0), stop=(kc == 7))
                for b in range(4):
                    S.tt(lf[:, tl * 4 + b, :], ps[:, b * 8:(b + 1) * 8], bf[:, :], ALU.add)
                if tl == NB // 4 - 1 or ti == ntile - 1:
                    lf2 = lf.v(lf.h[:, :, :].rearrange("p b h -> p (b h)"))
                    S.act(lf2, lf2, AF.Exp, scale=-1.0)
                    S.act(lf2, lf2, AF.Ln, bias=self.eps_tile(1.0), scale=1.0)
                    S.ts(lf2, lf2, -1.0, None, op0=ALU.mult)
                    psw = self.psum()
                    S.mm(psw[:, 0:NB * 8], tri[:, :], lf2)
                    pst = self.psum()
                    S.mm(pst[:, 0:NB * 8], ones[:, :], lf2)
                    S.memset(cr[:, 0, :], 0.0)
                    for b in range(1, NB):
                        S.tt(cr[:, b, :], cr[:, b - 1, :], pst[:, (b - 1) * 8:b * 8], ALU.add)
                    S.tt(cw.v(cw.h[:, :, :].rearrange("p b h -> p (b h)")), psw[:, 0:NB * 8],
                         cr.v(cr.h[:, :, :].rearrange("p b h -> p (b h)")), ALU.add)
                    S.dma(dv(self.fx_c[seq, :, :]), cw.v(cw.h[:, :, :].rearrange("p b h -> p (b h)")), q="gpsimd")
                    S.dma(dv(self.fx_cr[seq, :, :]), cr.v(cr.h[:, :, :].rearrange("p b h -> p (b h)")), q="gpsimd")
            S.barrier()

    def fox_mix(self):
        S = self.S
        nseq = max(1, self.nblk // NB)
        nb = min(NB, self.nblk)
        SCALE = 128 ** -0.5
        with ExitStack() as st:
            KT = [S.tile(st, [128, SEQ], "fm_KT%d" % i, dma=True) for i in range(2)]
            QT = [S.tile(st, [128, SEQ], "fm_QT%d" % i, dma=True) for i in range(2)]
            VA = [S.tile(st, [128, NB, 132], "fm_VA%d" % i, dma=True) for i in range(2)]
            OS = [S.tile(st, [128, NB, 128], "fm_OS%d" % i, dma=True) for i in range(2)]
            cw = S.tile(st, [128, NB, 8], "fm_cw", dma=True)
            cr = S.tile(st, [128, NB, 8], "fm_cr", dma=True)
            bias = [S.tile(st, [128, NB, NB], "fm_bias%d" % i) for i in range(2)]
            PT = [S.tile(st, [128, 128], "fm_PT%d" % i) for i in range(12)]
            rc = [S.tile(st, [128, 1], "fm_rc%d" % i) for i in range(2)]
            mle = S.tile(st, [128, 128], "fm_mle", dma=True)
            S.dma(mle[:, :], dv(self.din["mask_le"][:, :]))
            for i in range(2):
                S.memset(VA[i][:, :, 128:129], 1.0)
            it = 0
            pti = 0
            for seq in range(nseq):
                S.dma(cw.v(cw.h[:, :, :].rearrange("p b h -> p (b h)")), dv(self.fx_c[seq, :, :]))
                S.dma(cr.v(cr.h[:, :, :].rearrange("p b h -> p (b h)")), dv(self.fx_cr[seq, :, :]))
                for h in range(8):
                    i = it % 2
                    it += 1
                    c0 = seq * SEQ
                    S.dma(KT[i][:, 0:nb * 128], dv(self.fx_kT[h, :, c0:c0 + nb * 128]))
                    S.dma(QT[i][:, 0:nb * 128], dv(self.fx_qT[h, :, c0:c0 + nb * 128]))
                    S.dma(VA[i][:, 0:nb, 0:128],
                          dv(self.fx_v[c0:c0 + nb * 128, h * 128:(h + 1) * 128].rearrange("(b p) d -> p b d", p=128)))
                    bt = bias[i]
                    for qb in range(nb):
                        S.ts(bt[:, qb, 0:qb + 1], cw[:, 0:qb + 1, h], -1.0, cr[:, qb, h:h + 1], op0=ALU.mult, op1=ALU.add)
                    groups = []
                    for qb in range(nb):
                        nk = qb + 1
                        for g0 in range(0, nk, 4):
                            groups.append((qb, g0, min(nk, g0 + 4)))
                    state = {}

                    def stage1(gi):
                        nonlocal pti
                        qb, g0, g1 = groups[gi]
                        pss = self.psum()
                        pts = [PT[(pti + j) % 12] for j in range(4)]
                        pti += 4
                        for kb in range(g0, g1):
                            S.mm(pss[:, (kb - g0) * 128:(kb - g0 + 1) * 128], KT[i][:, kb * 128:(kb + 1) * 128],
                                 QT[i][:, qb * 128:(qb + 1) * 128])
                        for kb in range(g0, g1):
                            S.act(pts[kb - g0][:, :], pss[:, (kb - g0) * 128:(kb - g0 + 1) * 128], AF.Exp,
                                  bias=bt[:, qb, kb:kb + 1], scale=SCALE)
                        if g1 == qb + 1:
                            S.tt(pts[qb - g0][:, :], pts[qb - g0][:, :], mle[:, :], ALU.mult, eng="gpsimd")
                        state[gi] = pts

                    def stage2(gi):
                        qb, g0, g1 = groups[gi]
                        nk = qb + 1
                        pts = state.pop(gi)
                        if g0 == 0:
                            state["pso"] = self.psum_acc()
                        pso = state["pso"]
                        for kb in range(g0, g1):
                            S.mm(pso[:, 0:129], pts[kb - g0][:, :], VA[i][:, kb, 0:129], start=(kb == 0), stop=(kb == nk - 1))
                        if g1 == nk:
                            r = rc[qb % 2]
                            S.recip(r[:, :], pso[:, 128:129])
                            S.ts(OS[i][:, qb, :], pso[:, 0:128], r[:, 0:1], None, op0=ALU.mult)

                    LA = 2
                    for gi in range(len(groups) + LA):
                        if gi < len(groups):
                            stage1(gi)
                        if gi - LA >= 0:
                            stage2(gi - LA)
                    S.dma(dv(self.o_s[c0:c0 + nb * 128, h * 128:(h + 1) * 128].rearrange("(b p) d -> p b d", p=128)),
                          OS[i][:, 0:nb, :], q="gpsimd")
            S.barrier()

    def dsa_alloc(self):
        self.ds_qlT = self.scratch("ds_qlT", [8, 128, TOK])
        self.ds_qcT = self.scratch("ds_qcT", [8, 96, TOK])
        self.ds_kcT = self.scratch("ds_kcT", [96, TOK])
        self.ds_ckv = self.scratch("ds_ckv", [TOK, 128])
        self.ds_ckvT = self.scratch("ds_ckvT", [128, TOK])
        self.ds_wi = self.scratch("ds_wi", [TOK, 16])

    def dsa_proj(self, x_src):
        S = self.S
        w = self.din["dsa_w_in"]
        wr = w.rearrange("(kc p) n -> p kc n", p=128)
        with ExitStack() as st:
            xtok = [S.tile(st, [128, 4, 1024], "dp_xtok%d" % i, dma=True) for i in range(2)]
            xT = [S.tile(st, [128, 8, 512], "dp_xT%d" % i) for i in range(2)]
            wt = [S.tile(st, [128, 8, 512], "dp_w%d" % i, dma=True) for i in range(2)]
            wsm = S.tile(st, [128, 8, 744], "dp_wsm", dma=True)
            S.dma(wsm[:, :, :], dv(wr[:, :, 1024:1768]))
            wpq = S.tile(st, [128, 8, 8, 32], "dp_wpq", dma=True)
            wpi = S.tile(st, [128, 8, 8, 32], "dp_wpi", dma=True)
            wpk = S.tile(st, [128, 8, 64], "dp_wpk", dma=True)
            S.memset(wpi[:, :, :, :], 0.0)
            S.memset(wpk[:, :, :], 0.0)
            for h in range(8):
                S.dma(wpq[:, :, h, 0:16], dv(wr[:, :, h * 128 + 16:h * 128 + 32]))
                S.dma(wpq[:, :, h, 16:32], dv(wr[:, :, h * 128:h * 128 + 16]))
                S.dma(wpi[:, :, h, 0:8], dv(wr[:, :, 1184 + h * 64 + 8:1184 + h * 64 + 16]))
                S.dma(wpi[:, :, h, 8:16], dv(wr[:, :, 1184 + h * 64:1184 + h * 64 + 8]))
            S.dma(wpk[:, :, 0:16], dv(wr[:, :, 1152 + 16:1152 + 32]))
            S.dma(wpk[:, :, 16:32], dv(wr[:, :, 1152:1152 + 16]))
            S.dma(wpk[:, :, 32:40], dv(wr[:, :, 1696 + 8:1696 + 16]))
            S.dma(wpk[:, :, 40:48], dv(wr[:, :, 1696:1696 + 8]))
            wuk = S.tile(st, [128, 8, 96], "dp_wuk", dma=True)
            S.dma(wuk[:, :, :], dv(self.din["dsa_w_uk"].rearrange("h c n -> c h n")))
            wukT = S.tile(st, [96, 8, 128], "dp_wukT")
            for h in range(8):
                ps = self.psum()
                S.tr(ps[0:96, 0:128], wuk[:, h, :], self.ident[:, :])
                S.copy(wukT[:, h, :], ps[0:96, 0:128])
            if self.stop(1):
                return
            kvg = S.tile(st, [128, 128], "dp_kvg", dma=True)
            S.dma(kvg[:, :], dv(self.din["dsa_kv_norm_g"].rearrange("(o n) -> o n", o=1).partition_broadcast(128)))
            ropeA = [S.tile(st, [32, 2, 512], "dp_ropeA%d" % i, dma=True) for i in range(2)]
            ropeI = [S.tile(st, [32, 2, 512], "dp_ropeI%d" % i, dma=True) for i in range(2)]
            stg = [S.tile(st, [128, 512], "dp_stg%d" % i, dma=True) for i in range(4)]
            qn = [S.tile(st, [96, 512], "dp_qn%d" % i) for i in range(2)]
            qc = [S.tile(st, [96, 512], "dp_qc%d" % i, dma=True) for i in range(3)]
            t32 = [S.tile(st, [32, 512], "dp_t32%d" % i) for i in range(2)]
            csb = [S.tile(st, [128, 128], "dp_csb%d" % i, dma=True) for i in range(2)]
            csq = S.tile(st, [128, 128], "dp_csq")
            cst = [S.tile(st, [128, 8], "dp_cst%d" % i) for i in range(2)]
            cT = [S.tile(st, [128, 128], "dp_cT%d" % i, dma=True) for i in range(2)]
            wis = [S.tile(st, [128, 16], "dp_wi%d" % i, dma=True) for i in range(2)]
            wi_ = 0
            si = 0
            qi_ = 0
            ntile = self.nblk // 4
            CI = (64 ** -0.5) * (8 ** -0.5)
            for ti in range(ntile):
                t0 = ti * 512
                p0 = t0 % SEQ
                xk = xtok[ti % 2]
                xt_ = xT[ti % 2]
                self.load_xT(x_src, t0, xk, xt_)
                rA = ropeA[ti % 2]
                rI = ropeI[ti % 2]
                S.dma(rA[:, 0, :], dv(self.din["dsa_ropeA_c"][:, p0:p0 + 512]))
                S.dma(rA[:, 1, :], dv(self.din["dsa_ropeA_s"][:, p0:p0 + 512]))
                S.dma(rI[:, 0, :], dv(self.din["dsa_ropeI_c"][:, p0:p0 + 512]))
                S.dma(rI[:, 1, :], dv(self.din["dsa_ropeI_s"][:, p0:p0 + 512]))

                def rope32(dst, psA, psB):
                    a, b2 = t32
                    S.tt(a[:, :], psA[0:32, :], rA[:, 0, :], ALU.mult)
                    S.tt(b2[:, :], psB[0:32, :], rA[:, 1, :], ALU.mult)
                    S.tt(dst, a[:, :], b2[:, :], ALU.add, eng="gpsimd")

                if self.stop(2):
                    return
                for grp in range(2):
                    wtile = wt[wi_ % 2]
                    wi_ += 1
                    self.load_w(wtile, w, grp * 512, 512)
                    for j in range(4):
                        h = grp * 4 + j
                        ps = self.psum()
                        self.mm_feat(ps, wtile, j * 128 + 32, 96, xt_)
                        qnt = qn[h % 2]
                        S.copy(qnt[:, :], ps[0:96, :], eng="scalar")
                        ps2 = self.psum()
                        S.mm(ps2[:, :], wukT[:, h, :], qnt[:, :])
                        sg = stg[si % 4]
                        si += 1
                        S.copy(sg[:, :], ps2[:, :])
                        S.dma(dv(self.ds_qlT[h, :, t0:t0 + 512]), sg[:, :], q="gpsimd")
                        if self.stop(21):
                            return
                        qct = qc[qi_ % 3]
                        qi_ += 1
                        psA = self.psum()
                        self.mm_feat(psA, wtile, j * 128, 32, xt_)
                        psB = self.psum()
                        for kc in range(8):
                            S.mm(psB[0:32, :], wpq[:, kc, h, :], xt_[:, kc, :], start=(kc == 0), stop=(kc == 7))
                        rope32(qct[0:32, :], psA, psB)
                        S.dma(dv(self.ds_qcT[h, 64:96, t0:t0 + 512]), qct[0:32, :], q="gpsimd")
                        if self.stop(22):
                            return
                        qct = qc[qi_ % 3]
                        qi_ += 1
                        psC = self.psum()
                        self.mm_feat(psC, wsm, 160 + h * 64, 64, xt_)
                        psD = self.psum()
                        for kc in range(8):
                            S.mm(psD[0:32, :], wpi[:, kc, h, :], xt_[:, kc, :], start=(kc == 0), stop=(kc == 7))
                        if self.stop(23):
                            return
                        S.copy(qct[0:64, :], psC[0:64, :], eng="scalar")
                        if self.stop(24):
                            return
                        a, b2 = t32
                        S.tt(a[0:32, :], psC[0:32, :], rI[:, 0, :], ALU.mult)
                        S.tt(b2[0:32, :], psD[0:32, :], rI[:, 1, :], ALU.mult)
                        if self.stop(25):
                            return
                        S.tt(qct[0:32, :], a[0:32, :], b2[0:32, :], ALU.add, eng="gpsimd")
                        if self.stop(26):
                            return
                        S.dma(dv(self.ds_qcT[h, 0:64, t0:t0 + 512]), qct[0:64, :], q="gpsimd")
                if self.stop(3):
                    return
                qct = qc[qi_ % 3]
                qi_ += 1
                psA = self.psum()
                self.mm_feat(psA, wsm, 128, 32, xt_)
                psB = self.psum()
                for kc in range(8):
                    S.mm(psB[0:32, :], wpk[:, kc, 0:32], xt_[:, kc, :], start=(kc == 0), stop=(kc == 7))
                rope32(qct[0:32, :], psA, psB)
                S.dma(dv(self.ds_kcT[64:96, t0:t0 + 512]), qct[0:32, :], q="gpsimd")
                qct = qc[qi_ % 3]
                qi_ += 1
                psC = self.psum()
                self.mm_feat(psC, wsm, 672, 64, xt_)
                psD = self.psum()
                for kc in range(8):
                    S.mm(psD[0:32, :], wpk[:, kc, 32:64], xt_[:, kc, :], start=(kc == 0), stop=(kc == 7))
                S.copy(qct[0:64, :], psC[0:64, :], eng="scalar")
                a, b2 = t32
                S.tt(a[0:32, :], psC[0:32, :], rI[:, 0, :], ALU.mult)
                S.tt(b2[0:32, :], psD[0:32, :], rI[:, 1, :], ALU.mult)
                S.tt(qct[0:32, :], a[0:32, :], b2[0:32, :], ALU.add, eng="gpsimd")
                S.dma(dv(self.ds_kcT[0:64, t0:t0 + 512]), qct[0:64, :], q="gpsimd")
                if self.stop(4):
                    return
                for b in range(4):
                    r0 = t0 + b * 128
                    ps = self.psum()
                    self.mm_tok(ps, xt_, b, wsm, 0, 128)
                    c = csb[b % 2]
                    sv = cst[b % 2]
                    S.copy(c[:, :], ps[:, 0:128], eng="scalar")
                    S.tt(csq[:, :], c[:, :], c[:, :], ALU.mult, eng="gpsimd")
                    S.red(sv[:, 0:1], csq[:, :])
                    S.act(sv[:, 1:2], sv[:, 0:1], AF.Sqrt, bias=self.eps_tile(1e-6), scale=1.0 / 128)
                    S.recip(sv[:, 2:3], sv[:, 1:2])
                    S.stt(c[:, :], c[:, :], sv[:, 2:3], kvg[:, :], ALU.mult, ALU.mult)
                    S.dma(dv(self.ds_ckv[r0:r0 + 128, :]), c[:, :], q="gpsimd")
                    psT = self.psum()
                    S.tr(psT[:, 0:128], c[:, :], self.ident[:, :])
                    ct = cT[b % 2]
                    S.copy(ct[:, :], psT[:, 0:128], eng="scalar")
                    S.dma(dv(self.ds_ckvT[:, r0:r0 + 128]), ct[:, :], q="gpsimd")
                    psw = self.psum()
                    self.mm_tok(psw, xt_, b, wsm, 736, 8)
                    wv = wis[b % 2]
                    S.copy(wv[:, 8:16], psw[:, 0:8])
                    S.stt(wv[:, 0:8], wv[:, 8:16], -1.0, wv[:, 8:16], ALU.mult, ALU.max)
                    S.ts(wv[:, 0:8], wv[:, 0:8], CI, None, op0=ALU.mult)
                    S.act(wv[:, 8:16], wv[:, 8:16], AF.Sign)
                    S.dma(dv(self.ds_wi[r0:r0 + 128, :]), wv[:, :], q="gpsimd")
                if self.stop(5):
                    return
                for grp in range(2):
                    wtile = wt[wi_ % 2]
                    wi_ += 1
                    self.load_w(wtile, w, 1768 + grp * 512, 512)
                    for b in range(4):
                        ps = self.psum()
                        self.mm_tok(ps, xt_, b, wtile, 0, 512)
                        sg = stg[si % 4]
                        si += 1
                        S.copy(sg[:, :], ps[:, :], eng=("vector" if si % 2 == 0 else "scalar"))
                        S.dma(dv(self.g_s[t0 + b * 128:t0 + (b + 1) * 128, grp * 512:(grp + 1) * 512]), sg[:, :], q="gpsimd")
            S.barrier()

    def dsa_mix(self):
        S = self.S
        nseq = max(1, self.nblk // NB)
        nb = min(NB, self.nblk)
        SCALE = 128 ** -0.5
        with ExitStack() as st:
            kc_ = S.tile(st, [96, SEQ], "dm_kc", dma=True)
            ckvT = S.tile(st, [128, SEQ], "dm_ckvT", dma=True)
            ckvA = S.tile(st, [128, NB, 132], "dm_ckvA", dma=True)
            S.memset(ckvA[:, :, 128:129], 1.0)
            wuv = S.tile(st, [128, 8, 128], "dm_wuv", dma=True)
            S.dma(wuv[:, :, :], dv(self.din["dsa_w_uv"].rearrange("h c d -> c h d")))
            negm = S.tile(st, [128, 128], "dm_negm", dma=True)
            S.dma(negm[:, :], dv(self.din["negmask"][:, :]))
            qc = [S.tile(st, [96, 8, 128], "dm_qc%d" % i, dma=True) for i in range(2)]
            ql = [S.tile(st, [128, 8, 128], "dm_ql%d" % i, dma=True) for i in range(2)]
            wi = [S.tile(st, [128, 16], "dm_wi%d" % i, dma=True) for i in range(2)]
            I_ = [S.tile(st, [128, SEQ], "dm_I%d" % i) for i in range(2)]
            Wk = S.tile(st, [128, SEQ], "dm_Wk")
            MT = [S.tile(st, [128, NB, 128], "dm_MT%d" % i) for i in range(2)]
            m8 = [S.tile(st, [128, 8], "dm_m8%d" % i) for i in range(2)]
            tmp = [S.tile(st, [128, 512], "dm_tmp%d" % i) for i in range(3)]
            PT = [S.tile(st, [128, 512], "dm_PT%d" % i) for i in range(3)]
            rc = [S.tile(st, [128, 1], "dm_rc%d" % i) for i in range(2)]
            raws = [S.tile(st, [128, 132], "dm_raw%d" % i) for i in range(2)]
            olat = [S.tile(st, [128, 128], "dm_ol%d" % i) for i in range(2)]
            olT = [S.tile(st, [128, 128], "dm_olT%d" % i) for i in range(2)]
            osb = [S.tile(st, [128, 1024], "dm_osb%d" % i, dma=True) for i in range(2)]
            cnt = {"tmi": 0, "pti": 0, "hi": 0}

            def prep(seq, qb):
                i = qb % 2
                c0 = seq * SEQ
                r0 = c0 + qb * 128
                L = (qb + 1) * 128
                S.dma(qc[i][:, :, :], dv(self.ds_qcT[:, :, r0:r0 + 128].rearrange("h d t -> d h t")))
                S.dma(ql[i][:, :, :], dv(self.ds_qlT[:, :, r0:r0 + 128].rearrange("h d t -> d h t")))
                S.dma(wi[i][:, :], dv(self.ds_wi[r0:r0 + 128, :]))
                It = I_[i]
                for k0 in range(0, L, 512):
                    n = min(512, L - k0)
                    for h in range(8):
                        ps = self.psum()
                        S.mm(ps[:, 0:n], qc[i][0:64, h, :], kc_[0:64, k0:k0 + n])
                        t = tmp[cnt["tmi"] % 3]
                        cnt["tmi"] += 1
                        S.act(t[:, 0:n], ps[:, 0:n], AF.Relu, scale=wi[i][:, h:h + 1])
                        if h == 0:
                            S.ts(It[:, k0:k0 + n], t[:, 0:n], wi[i][:, 8:9], None, op0=ALU.mult)
                        else:
                            S.stt(It[:, k0:k0 + n], t[:, 0:n], wi[i][:, 8 + h:9 + h], It[:, k0:k0 + n], ALU.mult, ALU.add)
                S.tt(It[:, qb * 128:L], It[:, qb * 128:L], negm[:, :], ALU.add, eng="gpsimd")

            def topk_rounds(qb, ra, rb):
                if qb < 2:
                    return
                It = I_[qb % 2]
                L = (qb + 1) * 128
                for r in range(ra, rb):
                    src = It if r == 0 else Wk
                    m = m8[r % 2]
                    S.vmax(m[:, :], src[:, 0:L])
                    if r < 31:
                        S.match_replace(Wk[:, 0:L], m[:, :], src[:, 0:L], -1e30)

            def finish(qb):
                i = qb % 2
                It = I_[i]
                nk = qb + 1
                L = nk * 128
                if qb >= 2:
                    S.ts(Wk[:, 0:L], It[:, 0:L], m8[1][:, 7:8], None, op0=ALU.is_ge)
                else:
                    S.ts(Wk[:, 0:L], It[:, 0:L], -1e29, None, op0=ALU.is_ge)
                mt = MT[i]
                for g0 in range(0, nk, 4):
                    g1 = min(nk, g0 + 4)
                    ps = self.psum()
                    for kb in range(g0, g1):
                        S.tr(ps[:, (kb - g0) * 128:(kb - g0 + 1) * 128], Wk[:, kb * 128:(kb + 1) * 128], self.ident[:, :])
                    S.copy(mt.v(mt.h[:, g0:g1, :].rearrange("p a t -> p (a t)")), ps[:, 0:(g1 - g0) * 128], eng="scalar")

            def a_stage1(qb, h, g0, g1):
                i = qb % 2
                mt = MT[i]
                n = (g1 - g0) * 128
                pss = self.psum()
                for kb in range(g0, g1):
                    o_ = pss[:, (kb - g0) * 128:(kb - g0 + 1) * 128]
                    S.mm(o_, ckvT[:, kb * 128:(kb + 1) * 128], ql[i][:, h, :], start=True, stop=False)
                    S.mm(o_, kc_[64:96, kb * 128:(kb + 1) * 128], qc[i][64:96, h, :], start=False, stop=True)
                pt = PT[cnt["pti"] % 3]
                cnt["pti"] += 1
                S.act(pt[:, 0:n], pss[:, 0:n], AF.Exp, scale=SCALE)
                S.tt(pt[:, 0:n], pt[:, 0:n], mt.v(mt.h[:, g0:g1, :].rearrange("p a t -> p (a t)")), ALU.mult, eng="gpsimd")
                return pt

            def a_stage2(seq, qb, h, g0, g1, pt):
                i = qb % 2
                nk = qb + 1
                ob = osb[i]
                if g0 == 0:
                    cnt["pso"] = self.psum_acc()
                pso = cnt["pso"]
                for kb in range(g0, g1):
                    S.mm(pso[:, 0:129], pt[:, (kb - g0) * 128:(kb - g0 + 1) * 128], ckvA[:, kb, 0:129],
                         start=(kb == 0), stop=(kb == nk - 1))
                if g1 != nk:
                    return
                hi = cnt["hi"]
                cnt["hi"] += 1
                r = rc[hi % 2]
                ol = olat[hi % 2]
                olt = olT[hi % 2]
                raw = raws[hi % 2]
                S.copy(raw[:, 0:1], pso[:, 128:129], eng="scalar")
                S.tt(raw[:, 1:2], raw[:, 0:1], self.eps_tile(-1.0), ALU.pow, eng="gpsimd")
                S.act(ol[:, :], pso[:, 0:128], AF.Copy, scale=raw[:, 1:2])
                psT = self.psum()
                S.tr(psT[:, 0:128], ol[:, :], self.ident[:, :])
                S.copy(olt[:, :], psT[:, 0:128], eng="scalar")
                ps2 = self.psum()
                S.mm(ps2[:, 0:128], olt[:, :], wuv[:, h, :])
                S.copy(ob[:, h * 128:(h + 1) * 128], ps2[:, 0:128], eng="scalar")
                if h == 7:
                    r0 = seq * SEQ + qb * 128
                    S.dma(dv(self.o_s[r0:r0 + 128, :]), ob[:, :], q="gpsimd")

            def attention(seq, qb, nxt):
                nk = qb + 1
                groups = [(h, g0, min(nk, g0 + 4)) for h in range(8) for g0 in range(0, nk, 4)]
                LA = 2
                pend = {}
                for gi in range(len(groups) + LA):
                    if gi < len(groups):
                        h, g0, g1 = groups[gi]
                        pend[gi] = a_stage1(qb, h, g0, g1)
                    gj = gi - LA
                    if gj >= 0:
                        h, g0, g1 = groups[gj]
                        a_stage2(seq, qb, h, g0, g1, pend.pop(gj))
                        if g1 == nk and nxt:
                            topk_rounds(qb + 1, 4 * h, 4 * h + 4)

            for seq in range(nseq):
                c0 = seq * SEQ
                S.dma(kc_[:, 0:nb * 128], dv(self.ds_kcT[:, c0:c0 + nb * 128]))
                S.dma(ckvT[:, 0:nb * 128], dv(self.ds_ckvT[:, c0:c0 + nb * 128]))
                S.dma(ckvA[:, 0:nb, 0:128], dv(self.ds_ckv[c0:c0 + nb * 128, :].rearrange("(b p) d -> p b d", p=128)))
                prep(seq, 0)
                topk_rounds(0, 0, 32)
                finish(0)
                for qb in range(nb):
                    nxt = qb + 1 < nb
                    if nxt:
                        prep(seq, qb + 1)
                    attention(seq, qb, nxt)
                    if nxt:
                        finish(qb + 1)
            S.barrier()

    def rwkv_alloc(self):
        self.rw_r = self.scratch("rw_r", [TOK, D])
        self.rw_k = self.scratch("rw_k", [TOK, D])
        self.rw_v = self.scratch("rw_v", [TOK, D])
        self.rw_lw = self.scratch("rw_lw", [TOK, D])
        self.rw_a = self.scratch("rw_a", [TOK, D])

    def bcast_row(self, st, name, ap1d):
        t = self.S.tile(st, [128, 1024], name, dma=True)
        self.S.dma(t[:, :], dv(ap1d.rearrange("(o n) -> o n", o=1).partition_broadcast(128)))
        return t

    def rwkv_proj(self, x_src):
        S = self.S
        w = self.din["rwkv_w_in"]
        with ExitStack() as st:
            xtok = [S.tile(st, [128, 4, 1024], "wp_xtok%d" % i, dma=True) for i in range(2)]
            xT = [S.tile(st, [128, 8, 512], "wp_xT%d" % i) for i in range(2)]
            dT = S.tile(st, [128, 8, 512], "wp_dT")
            xm = [S.tile(st, [128, 8, 512], "wp_xm%d" % i) for i in range(2)]
            halo = S.tile(st, [128, 8, 1], "wp_halo")
            wt = [S.tile(st, [128, 8, 512], "wp_w%d" % i, dma=True) for i in range(2)]
            stg = [S.tile(st, [128, 512], "wp_stg%d" % i, dma=True) for i in range(4)]
            mu48 = S.tile(st, [48, 128], "wp_mu48", dma=True)
            S.dma(mu48[:, :], dv(self.din["rwkv_mu"].rearrange("i (kc p) -> (i kc) p", p=128)))
            muT = S.tile(st, [128, 48], "wp_muT")
            ps = self.psum()
            S.tr(ps[:, 0:48], mu48[:, :], self.ident[0:48, 0:48])
            S.copy(muT[:, :], ps[:, 0:48])
            wla = S.tile(st, [128, 8, 64], "wp_wla", dma=True)
            ala = S.tile(st, [128, 8, 64], "wp_ala", dma=True)
            S.dma(wla[:, :, :], dv(self.din["rwkv_w_lora_a"].rearrange("(kc p) n -> p kc n", p=128)))
            S.dma(ala[:, :, :], dv(self.din["rwkv_a_lora_a"].rearrange("(kc p) n -> p kc n", p=128)))
            wlb = S.tile(st, [64, 1024], "wp_wlb", dma=True)
            alb = S.tile(st, [64, 1024], "wp_alb", dma=True)
            S.dma(wlb[:, :], dv(self.din["rwkv_w_lora_b"][:, :]))
            S.dma(alb[:, :], dv(self.din["rwkv_a_lora_b"][:, :]))
            w0 = self.bcast_row(st, "wp_w0", self.din["rwkv_w0"])
            a0 = self.bcast_row(st, "wp_a0", self.din["rwkv_a0"])
            hl = [S.tile(st, [64, 512], "wp_hl%d" % i) for i in range(2)]
            wi_ = 0
            si = 0
            xi_ = 0
            ntile = self.nblk // 4
            NEG = -float(np.exp(-0.5))
            for ti in range(ntile):
                t0 = ti * 512
                xk = xtok[ti % 2]
                xt_ = xT[ti % 2]
                if t0 % SEQ == 0:
                    S.memset(halo[:, :, :], 0.0)
                self.load_xT(x_src, t0, xk, xt_)
                S.tt(dT[:, :, 1:512], xt_[:, :, 0:511], xt_[:, :, 1:512], ALU.subtract)
                S.tt(dT[:, :, 0:1], halo[:, :, :], xt_[:, :, 0:1], ALU.subtract)
                S.copy(halo[:, :, :], xt_[:, :, 511:512], eng="gpsimd")

                def mix(i):
                    nonlocal xi_
                    t = xm[xi_ % 2]
                    xi_ += 1
                    for kc in range(8):
                        S.stt(t[:, kc, :], dT[:, kc, :], muT[:, i * 8 + kc:i * 8 + kc + 1], xt_[:, kc, :], ALU.mult, ALU.add)
                    return t

                for i, c0, dst in ((0, 0, self.rw_r), (2, 1024, self.rw_k), (3, 2048, self.rw_v), (5, 3072, self.g_s)):
                    xmt = mix(i)
                    for grp in range(2):
                        wtile = wt[wi_ % 2]
                        wi_ += 1
                        self.load_w(wtile, w, c0 + grp * 512, 512)
                        for b in range(4):
                            ps = self.psum()
                            self.mm_tok(ps, xmt, b, wtile, 0, 512)
                            sg = stg[si % 4]
                            si += 1
                            S.copy(sg[:, :], ps[:, :], eng=("vector" if si % 2 == 0 else "scalar"))
                            S.dma(dv(dst[t0 + b * 128:t0 + (b + 1) * 128, grp * 512:(grp + 1) * 512]), sg[:, :], q="gpsimd")
                for i, la, lb, bias_t, dst, mul in ((1, wla, wlb, w0, self.rw_lw, NEG), (4, ala, alb, a0, self.rw_a, None)):
                    xmt = mix(i)
                    ps = self.psum()
                    for kc in range(8):
                        S.mm(ps[0:64, :], la[:, kc, :], xmt[:, kc, :], start=(kc == 0), stop=(kc == 7))
                    h_ = hl[i % 2]
                    if i == 1:
                        S.act(h_[:, :], ps[0:64, :], AF.Tanh)
                    else:
                        S.copy(h_[:, :], ps[0:64, :], eng="scalar")
                    for b in range(4):
                        for half in range(2):
                            ps2 = self.psum()
                            S.mm(ps2[:, :], h_[:, b * 128:(b + 1) * 128], lb[:, half * 512:(half + 1) * 512])
                            sg = stg[si % 4]
                            si += 1
                            S.tt(sg[:, :], ps2[:, :], bias_t[:, half * 512:(half + 1) * 512], ALU.add)
                            S.act(sg[:, :], sg[:, :], AF.Sigmoid)
                            if mul is not None:
                                S.ts(sg[:, :], sg[:, :], mul, None, op0=ALU.mult, eng="gpsimd")
                            S.dma(dv(dst[t0 + b * 128:t0 + (b + 1) * 128, half * 512:(half + 1) * 512]), sg[:, :], q="gpsimd")
            S.barrier()

    def rwkv_mix(self):
        S = self.S
        nseq = max(1, self.nblk // NB)
        nb = min(NB, self.nblk)

        def v3(t):
            return t.v(t.h[:, :].rearrange("p (a c) -> p a c", c=64))

        with ExitStack() as st:
            kk_c = self.bcast_row(st, "wm_kk", self.din["rwkv_k_k"])
            ka_c = self.bcast_row(st, "wm_ka", self.din["rwkv_k_a"])
            rk_c = self.bcast_row(st, "wm_rk", self.din["rwkv_r_k"].rearrange("h n -> (h n)"))
            gg_c = self.bcast_row(st, "wm_gg", self.din["rwkv_gn_g"])
            gb_c = self.bcast_row(st, "wm_gb", self.din["rwkv_gn_b"])
            tri = S.tile(st, [128, 128], "wm_tri", dma=True)
            mlt = S.tile(st, [128, 128], "wm_mlt", dma=True)
            mgt = S.tile(st, [128, 128], "wm_mgt", dma=True)
            S.dma(tri[:, :], dv(self.din["mask_le"][:, :]))
            S.dma(mlt[:, :], dv(self.din["mask_lt"][:, :]))
            S.dma(mgt[:, :], dv(self.din["mask_gt"][:, :]))
            ones = S.tile(st, [128, 128], "wm_ones")
            S.memset(ones[:, :], 1.0)
            NIN = 2
            r_ = [S.tile(st, [128, 1024], "wm_r%d" % i, dma=True) for i in range(NIN)]
            k_ = [S.tile(st, [128, 1024], "wm_k%d" % i, dma=True) for i in range(NIN)]
            v_ = [S.tile(st, [128, 1024], "wm_v%d" % i, dma=True) for i in range(NIN)]
            lw_ = [S.tile(st, [128, 1024], "wm_lw%d" % i, dma=True) for i in range(NIN)]
            a_ = [S.tile(st, [128, 1024], "wm_a%d" % i, dma=True) for i in range(NIN)]
            kk = S.tile(st, [128, 1024], "wm_kkn")
            km = S.tile(st, [128, 1024], "wm_km")
            kka = S.tile(st, [128, 1024], "wm_kka")
            cum = S.tile(st, [128, 1024], "wm_cum")
            e1 = S.tile(st, [128, 1024], "wm_e1")
            e1x = e1
            e2 = S.tile(st, [128, 1024], "wm_e2")
            Ab = S.tile(st, [128, 1024], "wm_Ab")
            Bb = S.tile(st, [128, 1024], "wm_Bb")
            Kb = S.tile(st, [128, 1024], "wm_Kb")
            Rb = S.tile(st, [128, 1024], "wm_Rb")
            Bt = S.tile(st, [128, 1024], "wm_Bt")
            Kt = S.tile(st, [128, 1024], "wm_Kt")
            AbT = S.tile(st, [64, 16, 128], "wm_AbT")
            BbT = S.tile(st, [64, 16, 128], "wm_BbT")
            KbT = S.tile(st, [64, 16, 128], "wm_KbT")
            RbT = S.tile(st, [64, 16, 128], "wm_RbT")
            gC = S.tile(st, [64, 16], "wm_gC")
            sm = S.tile(st, [128, 64], "wm_sm")
            ST = [S.tile(st, [64, 64], "wm_ST%d" % h) for h in range(16)]
            ysb = [S.tile(st, [128, 1024], "wm_y%d" % i, dma=True) for i in range(2)]
            G = 8
            P_ = [[S.tile(st, [128, 128], "wm_P%d_%d" % (s, i)) for i in range(2)] for s in range(G)]
            PT_ = [[S.tile(st, [128, 128], "wm_PT%d_%d" % (s, i)) for i in range(2)] for s in range(G)]
            W_ = [[S.tile(st, [128, 128], "wm_W%d_%d" % (s, i)) for i in range(2)] for s in range(G)]
            ArbT = [S.tile(st, [128, 128], "wm_ArbT%d" % s) for s in range(G)]
            ArkT = [S.tile(st, [128, 128], "wm_ArkT%d" % s) for s in range(G)]
            MT = [S.tile(st, [64, 64], "wm_MT%d" % s) for s in range(G)]
            GT = [S.tile(st, [64, 128], "wm_GT%d" % s) for s in range(G)]
            hs = 0
            ci = 0
            for seq in range(nseq):
                for h in range(16):
                    S.memset(ST[h][:, :], 0.0)
                for c in range(nb):
                    i = ci % NIN
                    ci += 1
                    r0 = seq * SEQ + c * 128
                    rt, kt, vt, lwt, at = r_[i], k_[i], v_[i], lw_[i], a_[i]
                    S.dma(rt[:, :], dv(self.rw_r[r0:r0 + 128, :]))
                    S.dma(kt[:, :], dv(self.rw_k[r0:r0 + 128, :]))
                    S.dma(vt[:, :], dv(self.rw_v[r0:r0 + 128, :]))
                    S.dma(lwt[:, :], dv(self.rw_lw[r0:r0 + 128, :]))
                    S.dma(at[:, :], dv(self.rw_a[r0:r0 + 128, :]))
                    S.tt(kk[:, :], kt[:, :], kk_c[:, :], ALU.mult, eng="gpsimd")
                    S.tt(e1x[:, :], kk[:, :], kk[:, :], ALU.mult, eng="gpsimd")
                    S.red(sm[:, 0:16], v3(e1x))
                    S.act(sm[:, 0:16], sm[:, 0:16], AF.Sqrt, bias=self.eps_tile(1e-12), scale=1.0)
                    S.recip(sm[:, 16:32], sm[:, 0:16])
                    S.tt(v3(kk), v3(kk), sm.v(sm.h[:, 16:32].unsqueeze(2).to_broadcast([128, 16, 64])), ALU.mult)
                    S.stt(km[:, :], at[:, :], -1.0, ka_c[:, :], ALU.add, ALU.mult)
                    S.stt(km[:, :], km[:, :], 1.0, kt[:, :], ALU.add, ALU.mult)
                    S.tt(kka[:, :], kk[:, :], at[:, :], ALU.mult, eng="gpsimd")
                    S.tt(e1x[:, :], rt[:, :], km[:, :], ALU.mult, eng="gpsimd")
                    S.tt(e1x[:, :], e1x[:, :], rk_c[:, :], ALU.mult, eng="gpsimd")
                    S.red(sm[:, 32:48], v3(e1x))
                    for half in range(2):
                        hsl = slice(half * 512, (half + 1) * 512)
                        psc = self.psum()
                        S.mm(psc[:, :], tri[:, :], lwt[:, hsl])
                        S.copy(cum[:, hsl], psc[:, :], eng="scalar")
                        pst = self.psum()
                        S.mm(pst[:, :], ones[:, :], lwt[:, hsl])
                        S.tt(e2[:, hsl], pst[:, :], cum[:, hsl], ALU.subtract)
                    S.act(e2[:, :], e2[:, :], AF.Exp)
                    S.tt(Bt[:, :], kka[:, :], e2[:, :], ALU.mult, eng="gpsimd")
                    S.tt(Kt[:, :], km[:, :], e2[:, :], ALU.mult, eng="gpsimd")
                    S.act(e1[:, :], cum[:, :], AF.Exp)
                    S.tt(Rb[:, :], rt[:, :], e1[:, :], ALU.mult, eng="gpsimd")
                    S.act(e1[:, :], cum[:, :], AF.Exp, scale=-1.0)
                    S.tt(Bb[:, :], kka[:, :], e1[:, :], ALU.mult, eng="gpsimd")
                    S.tt(Kb[:, :], km[:, :], e1[:, :], ALU.mult, eng="gpsimd")
                    S.tt(e2[:, :], cum[:, :], lwt[:, :], ALU.subtract)
                    S.act(e2[:, :], e2[:, :], AF.Exp)
                    S.stt(Ab[:, :], kk[:, :], -1.0, e2[:, :], ALU.mult, ALU.mult)
                    psg = self.psum()
                    for h in range(16):
                        S.mm(psg[0:64, h:h + 1], lwt[:, h * 64:(h + 1) * 64], ones[:, 0:1])
                    S.act(gC[:, :], psg[0:64, 0:16], AF.Exp)
                    for src, dstT in ((Ab, AbT), (Bb, BbT), (Kb, KbT), (Rb, RbT)):
                        for g in range(4):
                            pT = self.psum()
                            for j in range(4):
                                h = g * 4 + j
                                S.tr(pT[0:64, j * 128:(j + 1) * 128], src[:, h * 64:(h + 1) * 64], self.ident[:, :])
                            S.copy(dstT.v(dstT.h[:, g * 4:(g + 1) * 4, :].rearrange("p a t -> p (a t)")), pT[0:64, :],
                                   eng=("scalar" if g % 2 == 0 else "vector"))
                    y = ysb[c % 2]
                    for hg in range(16 // G):
                        H = [(hg * G + j, j) for j in range(G)]
                        for h, s in H:
                            ps1 = self.psum()
                            S.mm(ps1[:, 0:128], AbT[:, h, :], BbT[:, h, :])
                            S.tt(P_[s][0][:, :], ps1[:, 0:128], mgt[:, :], ALU.mult)
                        for h, s in H:
                            ps2 = self.psum()
                            S.mm(ps2[:, 0:128], BbT[:, h, :], AbT[:, h, :])
                            S.mm(ps2[:, 128:256], BbT[:, h, :], RbT[:, h, :])
                            S.tt(PT_[s][0][:, :], ps2[:, 0:128], mlt[:, :], ALU.mult)
                            S.tt(ArbT[s][:, :], ps2[:, 128:256], tri[:, :], ALU.mult)
                        for h, s in H:
                            ps3 = self.psum()
                            S.mm(ps3[:, 0:128], KbT[:, h, :], AbT[:, h, :])
                            S.mm(ps3[:, 128:256], KbT[:, h, :], RbT[:, h, :])
                            S.tt(PT_[s][1][:, :], ps3[:, 0:128], mlt[:, :], ALU.mult)
                            S.tt(ArkT[s][:, :], ps3[:, 128:256], tri[:, :], ALU.mult)
                        for h, s in H:
                            hc = slice(h * 64, (h + 1) * 64)
                            ps4 = self.psum()
                            S.mm(ps4[:, 0:64], PT_[s][1][:, :], vt[:, hc])
                            S.copy(W_[s][0][:, 0:64], Ab[:, hc], eng="gpsimd")
                            S.copy(W_[s][0][:, 64:128], ps4[:, 0:64], eng="scalar")
                        for lv in range(7):
                            a, b = lv % 2, (lv + 1) % 2
                            for h, s in H:
                                psw = self.psum()
                                S.mm(psw[:, 0:128], PT_[s][a][:, :], W_[s][a][:, :])
                                S.tt(W_[s][b][:, :], W_[s][a][:, :], psw[:, 0:128], ALU.add)
                            if lv < 6:
                                for h, s in H:
                                    psq = self.psum()
                                    S.mm(psq[:, 0:128], P_[s][a][:, :], PT_[s][a][:, :])
                                    if lv < 5:
                                        S.mm(psq[:, 128:256], PT_[s][a][:, :], P_[s][a][:, :])
                                    S.copy(PT_[s][b][:, :], psq[:, 0:128], eng="scalar")
                                    if lv < 5:
                                        S.copy(P_[s][b][:, :], psq[:, 128:256], eng="scalar")
                        for h, s in H:
                            hc = slice(h * 64, (h + 1) * 64)
                            psM = self.psum()
                            S.mm(psM[0:64, 0:64], W_[s][1][:, 0:64], Bt[:, hc])
                            S.copy(MT[s][:, :], psM[0:64, 0:64], eng="scalar")
                        for h, s in H:
                            psG = self.psum()
                            S.mm(psG[0:64, 0:128], W_[s][1][:, 0:64], ArbT[s][:, :])
                            S.tt(GT[s][:, :], psG[0:64, 0:128], RbT[:, h, :], ALU.add)
                        for h, s in H:
                            hc = slice(h * 64, (h + 1) * 64)
                            psY = self.psum()
                            S.mm(psY[:, 0:64], ArbT[s][:, :], W_[s][1][:, 64:128], start=True, stop=False)
                            S.mm(psY[:, 0:64], ArkT[s][:, :], vt[:, hc], start=False, stop=False)
                            S.mm(psY[:, 0:64], GT[s][:, :], ST[h][:, :], start=False, stop=True)
                            S.copy(y[:, hc], psY[:, 0:64], eng="scalar")
                        for h, s in H:
                            hc = slice(h * 64, (h + 1) * 64)
                            psS = self.psum()
                            S.mm(psS[0:64, 0:64], Bt[:, hc], W_[s][1][:, 64:128], start=True, stop=False)
                            S.mm(psS[0:64, 0:64], Kt[:, hc], vt[:, hc], start=False, stop=False)
                            S.mm(psS[0:64, 0:64], MT[s][:, :], ST[h][:, :], start=False, stop=True)
                            S.stt(ST[h][:, :], ST[h][:, :], gC[:, h:h + 1], psS[0:64, 0:64], ALU.mult, ALU.add)
                    S.red(sm[:, 0:16], v3(y))
                    S.ts(sm[:, 0:16], sm[:, 0:16], -1.0 / 64, None, op0=ALU.mult)
                    S.tt(v3(y), v3(y), sm.v(sm.h[:, 0:16].unsqueeze(2).to_broadcast([128, 16, 64])), ALU.add)
                    S.tt(e1x[:, :], y[:, :], y[:, :], ALU.mult, eng="gpsimd")
                    S.red(sm[:, 16:32], v3(e1x))
                    S.act(sm[:, 16:32], sm[:, 16:32], AF.Sqrt, bias=self.eps_tile(64e-5), scale=1.0 / 64)
                    S.recip(sm[:, 48:64], sm[:, 16:32])
                    S.tt(v3(y), v3(y), sm.v(sm.h[:, 48:64].unsqueeze(2).to_broadcast([128, 16, 64])), ALU.mult)
                    S.tt(y[:, :], y[:, :], gg_c[:, :], ALU.mult, eng="gpsimd")
                    S.tt(y[:, :], y[:, :], gb_c[:, :], ALU.add, eng="gpsimd")
                    S.tt(v3(e1x), v3(vt), sm.v(sm.h[:, 32:48].unsqueeze(2).to_broadcast([128, 16, 64])), ALU.mult)
                    S.tt(y[:, :], y[:, :], e1x[:, :], ALU.add, eng="gpsimd")
                    S.dma(dv(self.o_s[r0:r0 + 128, :]), y[:, :], q="gpsimd")
            S.barrier()

    def ret_alloc(self):
        self.rt_qT = self.scratch("rt_qT", [4, 2, 128, TOK])
        self.rt_kT = self.scratch("rt_kT", [4, 2, 128, TOK])
        self.rt_v = self.scratch("rt_v", [TOK, D])

    def ret_proj(self, x_src):
        S = self.S
        w = self.din["ret_w_in"]
        with ExitStack() as st:
            xtok = [S.tile(st, [128, 4, 1024], "rp_xtok%d" % i, dma=True) for i in range(2)]
            xT = [S.tile(st, [128, 8, 512], "rp_xT%d" % i) for i in range(2)]
            wt = [S.tile(st, [128, 8, 512], "rp_w%d" % i, dma=True) for i in range(3)]
            stg = [S.tile(st, [128, 512], "rp_stg%d" % i, dma=True) for i in range(4)]
            cs = [S.tile(st, [128, 2, 512], "rp_cs%d" % i, dma=True) for i in range(2)]
            tmp = [S.tile(st, [128, 512], "rp_tmp%d" % i) for i in range(4)]
            wi = 0
            si = 0
            ntile = self.nblk // 4
            for ti in range(ntile):
                t0 = ti * 512
                p0 = t0 % SEQ
                xk = xtok[ti % 2]
                xt_ = xT[ti % 2]
                self.load_xT(x_src, t0, xk, xt_)
                c = cs[ti % 2]
                S.dma(c[:, 0, :], dv(self.din["ret_cosT"][:, p0:p0 + 512]))
                S.dma(c[:, 1, :], dv(self.din["ret_sinT"][:, p0:p0 + 512]))
                for which, dst, scl in ((0, self.rt_qT, 1.0), (1, self.rt_kT, 256 ** -0.5)):
                    for grp in range(2):
                        wtile = wt[wi % 3]
                        wi += 1
                        self.load_w(wtile, w, which * 1024 + grp * 512, 512)
                        for j in range(2):
                            h = grp * 2 + j
                            ps1 = self.psum()
                            self.mm_feat(ps1, wtile, j * 256, 128, xt_)
                            ps2 = self.psum()
                            self.mm_feat(ps2, wtile, j * 256 + 128, 128, xt_)
                            a, b2, c3, d4 = tmp
                            S.stt(a[:, :], ps1[:, :], scl, c[:, 0, :], ALU.mult, ALU.mult)
                            S.stt(b2[:, :], ps2[:, :], scl, c[:, 1, :], ALU.mult, ALU.mult)
                            S.stt(c3[:, :], ps2[:, :], scl, c[:, 0, :], ALU.mult, ALU.mult)
                            S.stt(d4[:, :], ps1[:, :], scl, c[:, 1, :], ALU.mult, ALU.mult)
                            s1 = stg[si % 4]
                            s2 = stg[(si + 1) % 4]
                            si += 2
                            S.tt(s1[:, :], a[:, :], b2[:, :], ALU.subtract, eng="gpsimd")
                            S.tt(s2[:, :], c3[:, :], d4[:, :], ALU.add, eng="gpsimd")
                            S.dma(dv(dst[h, 0, :, t0:t0 + 512]), s1[:, :], q="gpsimd")
                            S.dma(dv(dst[h, 1, :, t0:t0 + 512]), s2[:, :], q="gpsimd")
                for c0, dst in ((2048, self.rt_v), (3072, self.g_s)):
                    for grp in range(2):
                        wtile = wt[wi % 3]
                        wi += 1
                        self.load_w(wtile, w, c0 + grp * 512, 512)
                        for b in range(4):
                            ps = self.psum()
                            self.mm_tok(ps, xt_, b, wtile, 0, 512)
                            sg = stg[si % 4]
                            si += 1
                            S.copy(sg[:, :], ps[:, :], eng=("vector" if si % 2 == 0 else "scalar"))
                            S.dma(dv(dst[t0 + b * 128:t0 + (b + 1) * 128, grp * 512:(grp + 1) * 512]), sg[:, :], q="gpsimd")
            S.barrier()

    def ret_mix(self):
        S = self.S
        nseq = max(1, self.nblk // NB)
        nb = min(NB, self.nblk)
        gs = [1.0 - 2.0 ** (-5.0 - h) for h in range(4)]
        with ExitStack() as st:
            QT = [[S.tile(st, [128, 2, 512], "rm_QT%d_%d" % (h, i), dma=True) for i in range(2)] for h in range(4)]
            KT = [[S.tile(st, [128, 2, 512], "rm_KT%d_%d" % (h, i), dma=True) for i in range(2)] for h in range(4)]
            VV = [[S.tile(st, [128, 4, 256], "rm_V%d_%d" % (h, i), dma=True) for i in range(2)] for h in range(4)]
            R = [S.tile(st, [128, 2, 256], "rm_R%d" % h) for h in range(4)]
            dpT = [S.tile(st, [128, 128], "rm_dp%d" % h, dma=True) for h in range(4)]
            xz = S.tile(st, [128, 8], "rm_xz", dma=True)
            S.dma(xz[:, :], dv(self.din["ret_xz"][:, :]))
            for h in range(4):
                S.dma(dpT[h][:, :], dv(self.din["ret_dpT"][h, :, :]))
            inT = [S.tile(st, [128, 128], "rm_inT%d" % i) for i in range(4)]
            kz = [S.tile(st, [128, 256], "rm_kz%d" % i) for i in range(4)]
            osb = [S.tile(st, [128, 256], "rm_o%d" % i, dma=True) for i in range(4)]
            cnt = 0
            for seq in range(nseq):
                for h in range(4):
                    S.memset(R[h][:, :, :], 0.0)
                for grp in range(nb // 4):
                    t0 = seq * SEQ + grp * 512
                    par = grp % 2
                    for h in range(4):
                        S.dma(QT[h][par][:, :, :], dv(self.rt_qT[h, :, :, t0:t0 + 512].rearrange("c p t -> p c t")))
                        S.dma(KT[h][par][:, :, :], dv(self.rt_kT[h, :, :, t0:t0 + 512].rearrange("c p t -> p c t")))
                        S.dma(VV[h][par][:, :, :],
                              dv(self.rt_v[t0:t0 + 512, h * 256:(h + 1) * 256].rearrange("(c p) e -> p c e", p=128)))
                    for n in range(4):
                        cs = slice(n * 128, (n + 1) * 128)
                        for h in range(4):
                            q_, k_, v_ = QT[h][par], KT[h][par], VV[h][par]
                            it = inT[cnt % 4]
                            kzt = kz[cnt % 4]
                            ot = osb[cnt % 4]
                            cnt += 1
                            ps_in = self.psum()
                            for dc in range(2):
                                S.mm(ps_in[:, 0:128], k_[:, dc, cs], q_[:, dc, cs], start=(dc == 0), stop=(dc == 1))
                            S.tt(it[:, :], ps_in[:, 0:128], dpT[h][:, :], ALU.mult)
                            ps_o = self.psum()
                            S.mm(ps_o[:, 0:256], it[:, :], v_[:, n, :], start=True, stop=False)
                            for dc in range(2):
                                S.mm(ps_o[:, 0:256], q_[:, dc, cs], R[h][:, dc, :], start=False, stop=(dc == 1))
                            S.act(ot[:, :], ps_o[:, 0:256], AF.Copy, scale=xz[:, h:h + 1])
                            r0 = t0 + n * 128
                            S.dma(dv(self.o_s[r0:r0 + 128, h * 256:(h + 1) * 256]), ot[:, :], q="gpsimd")
                            ps_k = self.psum()
                            for dc in range(2):
                                S.tr(ps_k[:, dc * 128:(dc + 1) * 128], k_[:, dc, cs], self.ident[:, :])
                            S.act(kzt[:, :], ps_k[:, 0:256], AF.Copy, scale=xz[:, 4 + h:5 + h])
                            ps_r = self.psum()
                            for dc in range(2):
                                S.mm(ps_r[:, dc * 256:(dc + 1) * 256], kzt[:, dc * 128:(dc + 1) * 128], v_[:, n, :])
                            Rf = R[h].v(R[h].h[:, :, :].rearrange("p c e -> p (c e)"))
                            S.stt(Rf, Rf, float(gs[h] ** 128), ps_r[:, :], ALU.mult, ALU.add)
            S.barrier()

    def ret_post_alloc(self, st):
        S = self.S
        gng = S.tile(st, [128, 1024], "rpo_g", dma=True)
        S.dma(gng[:, :], dv(self.din["ret_gn_g"].rearrange("(o n) -> o n", o=1).partition_broadcast(128)))
        sq = S.tile(st, [128, 1024], "rpo_sq")
        ss = [S.tile(st, [128, 8], "rpo_ss%d" % i) for i in range(2)]
        return (gng, sq, ss)

    def ret_post(self, extra, blk, ot, gt):
        S = self.S
        gng, sq, ss = extra
        s = ss[blk % 2]
        S.tt(sq[:, :], ot[:, :], ot[:, :], ALU.mult, eng="gpsimd")
        S.red(s[:, 0:4], sq.v(sq.h[:, :].rearrange("p (a c) -> p a c", c=256)))
        S.act(s[:, 0:4], s[:, 0:4], AF.Sqrt, bias=self.eps_tile(1e-6), scale=1.0 / 256)
        S.recip(s[:, 4:8], s[:, 0:4])
        o3 = ot.v(ot.h[:, :].rearrange("p (a c) -> p a c", c=256))
        S.tt(o3, o3, s.v(s.h[:, 4:8].unsqueeze(2).to_broadcast([128, 4, 256])), ALU.mult)
        S.tt(ot[:, :], ot[:, :], gng[:, :], ALU.mult, eng="gpsimd")

    def run_layers(self, layers, dbg=""):
        first = True
        for L in layers:
            src = self.x_in if first else self.out
            first = False
            name = ("fox", "dsa", "rwkv", "ret")[L]
            getattr(self, name + "_alloc")()
            if "noproj" not in dbg:
                getattr(self, name + "_proj")(src)
            if "nomix" not in dbg:
                getattr(self, name + "_mix")()
            if "noout" not in dbg:
                post = (self.ret_post_alloc, self.ret_post) if L == 3 else None
                self.phase_out(L, src, self.din[name + "_w_out"], post=post)
        self.S.barrier()


WEIGHT_SHAPES = {
    'ln_g': (4, 1024), 'ln_b': (4, 1024), 'fox_w_in': (1024, 4104), 'fox_b_f': (8,), 'fox_w_out': (1024, 1024),
    'dsa_w_in': (1024, 2792), 'dsa_kv_norm_g': (128,), 'dsa_w_uk': (8, 128, 96), 'dsa_w_uv': (8, 128, 128),
    'dsa_w_out': (1024, 1024), 'rwkv_mu': (6, 1024), 'rwkv_w_in': (1024, 4096), 'rwkv_w0': (1024,),
    'rwkv_w_lora_a': (1024, 64), 'rwkv_w_lora_b': (64, 1024), 'rwkv_a0': (1024,), 'rwkv_a_lora_a': (1024, 64),
    'rwkv_a_lora_b': (64, 1024), 'rwkv_k_k': (1024,), 'rwkv_k_a': (1024,), 'rwkv_r_k': (16, 64),
    'rwkv_gn_g': (1024,), 'rwkv_gn_b': (1024,), 'rwkv_w_out': (1024, 1024), 'ret_w_in': (1024, 4096),
    'ret_gn_g': (1024,), 'ret_w_out': (1024, 1024),
}


def make_consts():
    c = {}
    c["ident"] = np.eye(128, dtype=np.float32)
    i = np.arange(128)
    c["mask_le"] = (i[:, None] <= i[None, :]).astype(np.float32)
    c["mask_lt"] = (i[:, None] < i[None, :]).astype(np.float32)
    c["mask_ge"] = (i[:, None] >= i[None, :]).astype(np.float32)
    c["mask_gt"] = (i[:, None] > i[None, :]).astype(np.float32)
    inv = 1.0 / (10000.0 ** (np.arange(0, 256, 2, dtype=np.float32) / np.float32(256)))
    ang = np.arange(4096, dtype=np.float32)[:, None] * inv[None, :].astype(np.float32)
    c["ret_cosT"] = np.ascontiguousarray(np.cos(ang).T.astype(np.float32))
    c["ret_sinT"] = np.ascontiguousarray(np.sin(ang).T.astype(np.float32))
    lg = np.log1p(-(2.0 ** (-5.0 - np.arange(4, dtype=np.float64))))
    pos = np.arange(128, dtype=np.float64)
    dp = np.zeros((4, 128, 128), np.float32)
    xz = np.zeros((128, 8), np.float32)
    for h in range(4):
        dp[h] = (np.exp(-(pos[:, None] + 1.0) * lg[h]) * (pos[:, None] <= pos[None, :])).astype(np.float32)
        xz[:, h] = np.exp((pos + 1.0) * lg[h])
        xz[:, 4 + h] = np.exp((127.0 - pos) * lg[h])
    c["ret_dpT"] = dp
    c["ret_xz"] = xz
    def rope_fm(rot):
        inv = 1.0 / (np.float32(500000.0) ** (np.arange(0, rot, 2, dtype=np.float32) / np.float32(rot)))
        ang = np.arange(4096, dtype=np.float32)[:, None] * inv[None, :].astype(np.float32)
        cs, sn = np.cos(ang).T.astype(np.float32), np.sin(ang).T.astype(np.float32)
        return np.ascontiguousarray(np.concatenate([cs, cs], 0)), np.ascontiguousarray(np.concatenate([-sn, sn], 0))
    c["dsa_ropeA_c"], c["dsa_ropeA_s"] = rope_fm(32)
    ic, isn = rope_fm(16)
    c["dsa_ropeI_c"] = np.ascontiguousarray(np.concatenate([ic, np.ones_like(ic)], 0))
    c["dsa_ropeI_s"] = np.ascontiguousarray(np.concatenate([isn, np.zeros_like(isn)], 0))
    c["negmask"] = np.where(i[None, :] <= i[:, None], 0.0, -1e30).astype(np.float32)
    return c


_CACHE = {}


def build_program(layers=(0, 1, 2, 3), nblk=TOK // 128, dbg=""):
    key = (tuple(layers), nblk, dbg)
    if key in _CACHE:
        return _CACHE[key]
    consts = make_consts()
    nc = bass.Bass("TRN2", target_bir_lowering=False)
    with ExitStack() as st:
        k = K(nc, st, WEIGHT_SHAPES, {n: a.shape for n, a in consts.items()}, nblk=nblk)
        k.run_layers(layers, dbg)
        print("instructions", k.S.n_ins, "waits", k.S.n_wait)
    _CACHE[key] = (nc, consts)
    return nc, consts


def kernel(**inputs):
    x = np.ascontiguousarray(np.asarray(inputs["x"], dtype=np.float32))
    nc, consts = build_program()
    base = {n: np.ascontiguousarray(np.asarray(inputs[n], dtype=np.float32)) for n in WEIGHT_SHAPES}
    base.update(consts)
    in_maps = []
    for c in range(8):
        m = dict(base)
        m["x"] = x[2 * c:2 * c + 2].reshape(TOK, D)
        in_maps.append(m)
    res = run_bass_kernel_spmd(nc, in_maps, core_ids=list(range(8)))
    out = np.stack([r["out"].reshape(NSEQ, SEQ, D) for r in res.results], axis=0).reshape(16, SEQ, D)
    return out.astype(np.float32)
```
